# Optimizing a Trainium2 kernel written in Bass

```python
import jax, jax.numpy as jnp
from jax import lax
import numpy as np

D_MODEL = 2048
BATCH = 4
SEQ = 2048
DEPTH = 1
DEC_BATCH = 128
DEC_SEQ = 8
PAST_LEN = 16384
PAGE_SIZE = 128

HEAD_SIZE = 64
RWKV_WIDTH = D_MODEL // 2
RWKV_HEADS = RWKV_WIDTH // HEAD_SIZE
DECAY_LORA = D_MODEL // 32
AAA_LORA = D_MODEL // 32
GATE_LORA = D_MODEL // 16
CHUNK = 128
MLP_WIDTH = D_MODEL // 2
MLP_GROUPS = 8
MLP_GROUP_DIM = MLP_WIDTH // MLP_GROUPS
N_BRANCH = 2
D_FF = ((-(-8 * D_MODEL // 3)) + 255) // 256 * 256
C_SHIFT = 3 * RWKV_WIDTH + DECAY_LORA + AAA_LORA + GATE_LORA
C_IN = C_SHIFT + 2 * MLP_WIDTH + N_BRANCH * D_MODEL
NORM_EPS = 1e-6
GN_EPS = 64e-5
LN_EPS = 1e-5

kernel_name = "rwkv7_chunk_gmlp_gated_hybrid_step"


def rms_norm(x, g):
    xf = x.astype(jnp.float32)
    return xf * lax.rsqrt(jnp.mean(xf * xf, -1, keepdims=True) + NORM_EPS) * g.astype(jnp.float32)


def layer_norm(x, g, b):
    mu = jnp.mean(x, -1, keepdims=True)
    var = jnp.mean(jnp.square(x - mu), -1, keepdims=True)
    return (x - mu) * lax.rsqrt(var + LN_EPS) * g.astype(jnp.float32) + b.astype(jnp.float32)


def wkv7_scan(state, r, decay, k, v, kk, kka):
    def step(S, inp):
        r_t, w_t, k_t, v_t, kk_t, kka_t = inp
        sa = jnp.einsum('bhvk,bhk->bhv', S, -kk_t)
        S = S * w_t[:, :, None, :] + sa[..., None] * kka_t[:, :, None, :] + v_t[..., None] * k_t[:, :, None, :]
        o = jnp.einsum('bhvk,bhk->bhv', S, r_t)
        return S, o
    xs = tuple(jnp.swapaxes(a, 0, 1) for a in (r, decay, k, v, kk, kka))
    S, o = lax.scan(step, state, xs)
    return S, jnp.swapaxes(o, 0, 1)


def hybrid_layer(x, c, shift_state, wkv_state,
                 w_ada, b_ada, norm_mix_g, w_in, mu_shift, w0, w_decay_up, a0, w_aaa_up,
                 w_gate_up, k_k, k_a, r_k, gn_g, gn_b, ln_v_g, ln_v_b, w_spatial, b_spatial,
                 w_branch_a, w_branch_b, w_out, norm_ffn_g, w_ffn_in, w_ffn_out):
    f32 = jnp.float32
    dt = x.dtype
    B, T, _ = x.shape
    H, N = RWKV_HEADS, HEAD_SIZE
    mod = (jax.nn.silu(c.astype(f32)) @ w_ada.astype(f32) + b_ada.astype(f32)).reshape(B, 6, D_MODEL)
    shift_m, scale_m, gate_m = mod[:, 0], mod[:, 1], mod[:, 2]
    shift_f, scale_f, gate_f = mod[:, 3], mod[:, 4], mod[:, 5]

    h = (rms_norm(x, norm_mix_g) * (1.0 + scale_m[:, None]) + shift_m[:, None]).astype(dt)
    z = h @ w_in
    z_rw, z_u, z_v, z_gate = jnp.split(z, [C_SHIFT, C_SHIFT + MLP_WIDTH, C_SHIFT + 2 * MLP_WIDTH], axis=-1)

    prev = jnp.concatenate([shift_state[:, None].astype(dt), z_rw[:, :-1]], axis=1)
    zs = (z_rw + (prev - z_rw) * mu_shift.astype(dt)).astype(f32)
    new_shift = z_rw[:, -1]
    r, k, v, wd, ad, gd = jnp.split(zs, [RWKV_WIDTH, 2 * RWKV_WIDTH, 3 * RWKV_WIDTH,
                                         3 * RWKV_WIDTH + DECAY_LORA,
                                         3 * RWKV_WIDTH + DECAY_LORA + AAA_LORA], axis=-1)
    w_raw = w0.astype(f32) + jnp.tanh(wd) @ w_decay_up.astype(f32)
    decay = jnp.exp(-jnp.exp(-jax.nn.softplus(-w_raw) - 0.5))
    a = jax.nn.sigmoid(a0.astype(f32) + ad @ w_aaa_up.astype(f32))
    g = jax.nn.sigmoid(gd) @ w_gate_up.astype(f32)
    kk = (k * k_k.astype(f32)).reshape(B, T, H, N)
    kk = kk / jnp.maximum(jnp.sqrt(jnp.sum(kk * kk, -1, keepdims=True)), 1e-12)
    k = k * (1.0 + (a - 1.0) * k_a.astype(f32))
    r4, k4, v4 = r.reshape(B, T, H, N), k.reshape(B, T, H, N), v.reshape(B, T, H, N)
    a4, w4 = a.reshape(B, T, H, N), decay.reshape(B, T, H, N)
    S, o = wkv7_scan(wkv_state.astype(f32), r4, w4, k4, v4, kk, kk * a4)
    mu = jnp.mean(o, -1, keepdims=True)
    var = jnp.mean(jnp.square(o - mu), -1, keepdims=True)
    o = ((o - mu) * lax.rsqrt(var + GN_EPS)).reshape(B, T, RWKV_WIDTH) * gn_g.astype(f32) + gn_b.astype(f32)
    bonus = jnp.sum(r4 * k4 * r_k.astype(f32), -1, keepdims=True) * v4
    o_a = (o + bonus.reshape(B, T, RWKV_WIDTH)) * g
    y_a = o_a.astype(dt) @ w_branch_a

    u = jax.nn.gelu(z_u.astype(f32))
    vn = layer_norm(jax.nn.gelu(z_v.astype(f32)), ln_v_g, ln_v_b)
    n_chunks = -(-T // CHUNK)
    pad = n_chunks * CHUNK - T
    vp = jnp.pad(vn, ((0, 0), (0, pad), (0, 0))).reshape(B, n_chunks, CHUNK, MLP_GROUPS, MLP_GROUP_DIM)
    causal = jnp.tril(jnp.ones((CHUNK, CHUNK), dtype=bool))
    ws = jnp.where(causal[None], w_spatial.astype(f32), 0.0)
    mixed = jnp.einsum('gts,bcsgd->bctgd', ws, vp) + jnp.transpose(b_spatial.astype(f32))[:, :, None]
    mixed = mixed.reshape(B, n_chunks * CHUNK, MLP_WIDTH)[:, :T]
    y_b = (u * mixed).astype(dt) @ w_branch_b

    gates = jax.nn.sigmoid(z_gate.astype(f32)).reshape(B, T, N_BRANCH, D_MODEL)
    merged = gates[:, :, 0] * y_a.astype(f32) + gates[:, :, 1] * y_b.astype(f32)
    mix_out = merged.astype(dt) @ w_out
    x = x + (gate_m[:, None] * mix_out.astype(f32)).astype(dt)

    h2 = (rms_norm(x, norm_ffn_g) * (1.0 + scale_f[:, None]) + shift_f[:, None]).astype(dt)
    gt, up = jnp.split(h2 @ w_ffn_in, [D_FF], axis=-1)
    f = (jax.nn.silu(gt) * up) @ w_ffn_out
    x = x + (gate_f[:, None] * f.astype(f32)).astype(dt)
    return x, S.astype(wkv_state.dtype), new_shift, vn.astype(dt)


def setup_inputs(seed: int = 0) -> dict:
    key = jax.random.key(seed)
    ks = iter(jax.random.split(key, 48))
    nrm = lambda shape, s: jax.random.normal(next(ks), shape, jnp.float32) * s
    L, D = DEPTH, D_MODEL
    return {
        "x_prompt": nrm((BATCH, SEQ, D), 1.0),
        "x_sample": nrm((DEC_BATCH, DEC_SEQ, D), 1.0),
        "state_wkv": nrm((L, DEC_BATCH, RWKV_HEADS, HEAD_SIZE, HEAD_SIZE), 0.5),
        "state_shift": nrm((L, DEC_BATCH, C_SHIFT), 1.0),
        "c_prompt": nrm((BATCH, D), 1.0),
        "c_sample": nrm((DEC_BATCH, D), 1.0),
        "w_ada": nrm((L, D, 6 * D), 0.5 * D ** -0.5),
        "b_ada": nrm((L, 6 * D), 0.02),
        "norm_mix_g": 1.0 + nrm((L, D), 0.02),
        "w_in": nrm((L, D, C_IN), D ** -0.5),
        "mu_shift": jax.random.uniform(next(ks), (L, C_SHIFT), jnp.float32),
        "w0": jax.random.uniform(next(ks), (L, RWKV_WIDTH), jnp.float32, -6.5, -1.5),
        "w_decay_up": nrm((L, DECAY_LORA, RWKV_WIDTH), 0.1 * DECAY_LORA ** -0.5),
        "a0": nrm((L, RWKV_WIDTH), 0.1),
        "w_aaa_up": nrm((L, AAA_LORA, RWKV_WIDTH), 0.1 * AAA_LORA ** -0.5),
        "w_gate_up": nrm((L, GATE_LORA, RWKV_WIDTH), GATE_LORA ** -0.5),
        "k_k": 0.85 + nrm((L, RWKV_WIDTH), 0.05),
        "k_a": 1.0 + nrm((L, RWKV_WIDTH), 0.05),
        "r_k": nrm((L, RWKV_HEADS, HEAD_SIZE), 0.1),
        "gn_g": 1.0 + nrm((L, RWKV_WIDTH), 0.02),
        "gn_b": nrm((L, RWKV_WIDTH), 0.02),
        "ln_v_g": 1.0 + nrm((L, MLP_WIDTH), 0.02),
        "ln_v_b": nrm((L, MLP_WIDTH), 0.02),
        "w_spatial": nrm((L, MLP_GROUPS, CHUNK, CHUNK), 0.5 * CHUNK ** -0.5),
        "b_spatial": 1.0 + nrm((L, MLP_GROUPS, CHUNK), 0.02),
        "w_branch_a": nrm((L, RWKV_WIDTH, D), RWKV_WIDTH ** -0.5),
        "w_branch_b": nrm((L, MLP_WIDTH, D), MLP_WIDTH ** -0.5),
        "w_out": nrm((L, D, D), D ** -0.5),
        "norm_ffn_g": 1.0 + nrm((L, D), 0.02),
        "w_ffn_in": nrm((L, D, 2 * D_FF), D ** -0.5),
        "w_ffn_out": nrm((L, D_FF, D), D_FF ** -0.5),
        "norm_final_g": 1.0 + nrm((D,), 0.02),
    }


def reference(x_prompt, x_sample, state_wkv, state_shift, c_prompt, c_sample,
              w_ada, b_ada, norm_mix_g, w_in, mu_shift, w0, w_decay_up, a0, w_aaa_up,
              w_gate_up, k_k, k_a, r_k, gn_g, gn_b, ln_v_g, ln_v_b, w_spatial, b_spatial,
              w_branch_a, w_branch_b, w_out, norm_ffn_g, w_ffn_in, w_ffn_out, norm_final_g):
    xp, xs = x_prompt, x_sample
    Bp = x_prompt.shape[0]
    wkv_p, shift_p, wkv_s, shift_s, v_s = [], [], [], [], []
    for l in range(DEPTH):
        lw = (w_ada[l], b_ada[l], norm_mix_g[l], w_in[l], mu_shift[l], w0[l], w_decay_up[l], a0[l],
              w_aaa_up[l], w_gate_up[l], k_k[l], k_a[l], r_k[l], gn_g[l], gn_b[l], ln_v_g[l], ln_v_b[l],
              w_spatial[l], b_spatial[l], w_branch_a[l], w_branch_b[l], w_out[l], norm_ffn_g[l],
              w_ffn_in[l], w_ffn_out[l])
        zero_shift = jnp.zeros((Bp, C_SHIFT), x_prompt.dtype)
        zero_wkv = jnp.zeros((Bp, RWKV_HEADS, HEAD_SIZE, HEAD_SIZE), state_wkv.dtype)
        xp, Sp, shp, _ = hybrid_layer(xp, c_prompt, zero_shift, zero_wkv, *lw)
        xs, Ss, shs, vs = hybrid_layer(xs, c_sample, state_shift[l], state_wkv[l], *lw)
        wkv_p.append(Sp); shift_p.append(shp)
        wkv_s.append(Ss); shift_s.append(shs); v_s.append(vs)
    y_prompt = rms_norm(xp, norm_final_g).astype(x_prompt.dtype)
    y_sample = rms_norm(xs, norm_final_g).astype(x_sample.dtype)
    return (y_prompt, y_sample, jnp.stack(wkv_p), jnp.stack(shift_p),
            jnp.stack(wkv_s), jnp.stack(shift_s), jnp.stack(v_s))
```

```python
import numpy as np
import concourse.bass as bass
import concourse.mybir as mybir
from concourse.bass_utils import run_bass_kernel_spmd
from contextlib import ExitStack

F32 = mybir.dt.float32
F32R = mybir.dt.float32r
BF16 = mybir.dt.bfloat16
AF = mybir.ActivationFunctionType
ALU = mybir.AluOpType
AX = mybir.AxisListType

EPOCH = 8192
D = 2048
T = 1152
TP = 1024
DFF = 5632
CIN = 9472
NCST = 10 * 128 + 16


class Prog:
    def __init__(self, nc):
        self.nc = nc
        self.eng = {"pe": nc.tensor, "act": nc.scalar, "dve": nc.vector,
                    "pool": nc.gpsimd, "sp": nc.sync}
        self.cnt = {e: 0 for e in self.eng}
        self.sems = {e: [] for e in self.eng}
        self.seen = {e: {} for e in self.eng}
        self.last_w = {}
        self.readers = {}
        self.dma_sems = {}
        self.pend_r = {e: [] for e in self.eng}
        self.pend_w = {e: [] for e in self.eng}
        self.nsem = 0
        self.ninstr = {e: 0 for e in self.eng}
        self.rec = None
        self.m_eng = {e: 0.0 for e in self.eng}
        self.m_key = {}

    def record(self, gen):
        self.rec = []
        for _ in gen:
            pass
        r, self.rec = self.rec, None
        return r

    def schedule(self, streams):
        from collections import Counter
        DUR = {"pe": 0.2, "act": 0.25, "dve": 0.22, "pool": 0.5, "sp": 0.05}
        LAT = 0.3
        isps = lambda k: isinstance(k, str) and k.startswith("psb")
        units = []
        for st in streams:
            us, cur = [], []
            for o in st:
                cur.append(o)
                if o[5]:
                    us.append(cur)
                    cur = []
            if cur:
                us.append(cur)
            units.append(us)
        pend_r = [Counter(k for u in us for o in u for k in o[3] if not isps(k)) for us in units]
        pend_w = [Counter(k for u in us for o in u for k in o[4] if not isps(k)) for us in units]
        idx = [0] * len(units)
        while True:
            best = None
            for j, us in enumerate(units):
                if idx[j] >= len(us):
                    continue
                u = us[idx[j]]
                keys = [k for o in u for k in (o[3] + o[4])]
                rk_ = [k for o in u for k in o[3] if not isps(k)]
                wk_ = [k for o in u for k in o[4] if not isps(k)]
                if any(pend_w[i][k] > 0 for k in rk_ for i in range(j)) or \
                   any(pend_w[i][k] > 0 or pend_r[i][k] > 0 for k in wk_ for i in range(j)):
                    continue
                F = u[0][1]
                t_ready = max([self.m_key.get(k, 0.0) for k in keys] + [0.0])
                start = max(self.m_eng[F], t_ready)
                if best is None or start < best[0]:
                    best = (start, j, u, keys, F)
            if best is None:
                break
            start, j, u, keys, F = best
            t = start
            for o in u:
                if o[0] == "op":
                    self.op(o[1], o[2], o[3], o[4], o[5])
                    t += DUR[o[1]]
                else:
                    out, in_, semres, kw = o[2]
                    self.dma(o[1], out, in_, o[3], o[4], semres, **kw)
                    t += DUR["sp"]
            self.m_eng[F] = t
            fin = t + LAT + (2.0 if u[0][0] == "dma" else 0.0)
            for o in u:
                for k in o[4] + [k2 for k2 in o[3] if isps(k2)]:
                    self.m_key[k] = fin
            for o in u:
                for k in o[3]:
                    if not isps(k):
                        pend_r[j][k] -= 1
                for k in o[4]:
                    if not isps(k):
                        pend_w[j][k] -= 1
            idx[j] += 1

    def _newsem(self, name):
        self.nsem += 1
        return self.nc.alloc_semaphore(name)

    def _deps(self, F, reads, writes):
        deps = {}

        def add(tok, same_ok):
            if tok is None:
                return
            key, sem, val, eng = tok
            if eng == F and F == "pe":
                return
            if val > deps.get(key, (None, 0))[1]:
                deps[key] = (sem, val)

        for r in reads:
            add(self.last_w.get(r), True)
        for w in writes:
            add(self.last_w.get(w), True)
            for t in self.readers.get(w, ()):
                add(t, False)
        return deps

    def _emit_waits(self, F, deps):
        e = self.eng[F]
        for key, (sem, val) in deps.items():
            if self.seen[F].get(key, 0) >= val:
                continue
            e.wait_ge(sem, val)
            self.seen[F][key] = val

    def _register(self, tok, reads, writes):
        for r in reads:
            lst = self.readers.setdefault(r, [])
            lst[:] = [t for t in lst if t[0] != tok[0]]
            lst.append(tok)
        for w in writes:
            self.last_w[w] = tok
            self.readers[w] = []

    def op(self, F, fn, reads=(), writes=(), signal=True):
        reads = list(reads)
        writes = list(writes)
        if self.rec is not None:
            self.rec.append(("op", F, fn, reads, writes, signal))
            return None
        ex = [r for r in reads if isinstance(r, str) and r.startswith("psb")]
        if ex:
            reads = [r for r in reads if r not in ex]
            writes = writes + [r for r in ex if r not in writes]
        deps = self._deps(F, reads, writes)
        self._emit_waits(F, deps)
        ins = fn()
        self.ninstr[F] += 1
        if not signal:
            self.pend_r[F] += reads
            self.pend_w[F] += writes
            return ins
        i = self.cnt[F]
        self.cnt[F] += 1
        ep = i // EPOCH
        while len(self.sems[F]) <= ep:
            self.sems[F].append(self._newsem(f"s_{F}_{len(self.sems[F])}"))
        sem = self.sems[F][ep]
        val = i % EPOCH + 1
        ins.then_inc(sem, 1)
        tok = ((F, ep), sem, val, F)
        self._register(tok, reads + self.pend_r[F], writes + self.pend_w[F])
        self.pend_r[F] = []
        self.pend_w[F] = []
        return ins

    def dma(self, Q, out, in_, reads=(), writes=(), semres=None, **kw):
        reads = list(reads)
        writes = list(writes)
        if self.rec is not None:
            self.rec.append(("dma", Q, (out, in_, semres, kw), reads, writes, True))
            return None
        if semres is None:
            semres = (writes + reads)[0]
        deps = self._deps("dma", reads, writes)
        self._emit_waits(Q, deps)
        if semres not in self.dma_sems:
            self.dma_sems[semres] = [self._newsem(f"d_{len(self.dma_sems)}"), 0]
        ent = self.dma_sems[semres]
        ent[1] += 16
        ins = self.eng[Q].dma_start(out=out, in_=in_, **kw)
        ins.then_inc(ent[0], 16)
        self.ninstr[Q] += 1
        tok = (("dma", semres), ent[0], ent[1], "dma")
        self._register(tok, reads, writes)
        return ins

    def barrier(self):
        for F, e in self.eng.items():
            for semres, (sem, val) in self.dma_sems.items():
                key = ("dma", semres)
                if val > self.seen[F].get(key, 0):
                    e.wait_ge(sem, val)
                    self.seen[F][key] = val
            for E in self.eng:
                if E == F or self.cnt[E] == 0:
                    continue
                i = self.cnt[E] - 1
                key = (E, i // EPOCH)
                val = i % EPOCH + 1
                if val > self.seen[F].get(key, 0):
                    e.wait_ge(self.sems[E][i // EPOCH], val)
                    self.seen[F][key] = val

    def finish(self, F="sp"):
        e = self.eng[F]
        for semres, (sem, val) in self.dma_sems.items():
            if val > 0:
                e.wait_ge(sem, val)
        for E in self.eng:
            if self.cnt[E] > 0:
                i = self.cnt[E] - 1
                e.wait_ge(self.sems[E][i // EPOCH], i % EPOCH + 1)


_UC = [0]


def _uname(name):
    _UC[0] += 1
    return f"t{_UC[0]}_{name}"


class _Arena:
    def __init__(self):
        self.ap = None
        self.free = []

    def init(self, nc, name="arena", dt=F32, n=None):
        if n is None:
            nbytes = int(nc.sbuf_bytes_remaining) - 1024
            n = nbytes // 4
        self.ap = nc.alloc_sbuf_tensor(name, [128, n], dt).ap()
        self.free = [(0, n)]
        self.n = n

    def alloc(self, words):
        words = (words + 15) // 16 * 16
        for i, (st, sz) in enumerate(self.free):
            if sz >= words:
                if sz == words:
                    self.free.pop(i)
                else:
                    self.free[i] = (st + words, sz - words)
                return st, words
        raise MemoryError(f"arena full: need {words} words, free={self.free}")

    def release(self, st, words):
        self.free.append((st, words))
        self.free.sort()
        out = []
        for a, b in self.free:
            if out and out[-1][0] + out[-1][1] == a:
                out[-1] = (out[-1][0], out[-1][1] + b)
            else:
                out.append((a, b))
        self.free = out


_AR = _Arena()
_ARR = _Arena()


class _Tile:
    def __init__(self, shape, dt):
        self.shape = list(shape)
        self.dt = dt

    def __enter__(self):
        esz = 2 if self.dt == BF16 else 4
        per = 1
        for d in self.shape[1:]:
            per *= d
        words = (per * esz + 3) // 4
        self.ar = _ARR if self.dt == F32R else _AR
        self.st, self.words = self.ar.alloc(words)
        v = self.ar.ap[:, self.st:self.st + words]
        if self.dt == BF16:
            v = v.bitcast(self.dt)
        v = v[:, 0:per]
        if len(self.shape) == 3:
            v = v.rearrange("p (a b) -> p a b", b=self.shape[2])
        elif len(self.shape) == 4:
            v = v.rearrange("p (a b c) -> p a b c", b=self.shape[2], c=self.shape[3])
        if self.shape[0] != 128:
            v = v[0:self.shape[0]]
        self._ap = v
        return self

    def ap(self):
        return self._ap

    def __exit__(self, *a):
        self.ar.release(self.st, self.words)
        return False


def _sbt(nc, name, shape, dt):
    return _Tile(shape, dt)


def r32(ap):
    return ap.bitcast(F32)


class _Stop(Exception):
    pass


def build_nc(stop=None):
    nc = bass.Bass("TRN2", target_bir_lowering=False)
    P = Prog(nc)
    DR = {}
    try:
        _build(nc, P, DR, stop)
    except _Stop:
        pass
    print("ninstr", P.ninstr, "nsem", P.nsem, flush=True)
    return nc


def _build(nc, P, DR, stop):
    def ckpt(tag, dumps):
        if stop != tag:
            return
        for name, ap, keys in dumps:
            d = nc.dram_tensor("dbg_" + name, list(ap.shape), ap.dtype, kind="ExternalOutput").ap()
            P.dma("sp", d, ap, reads=keys, semres=("dbg", name))
        P.finish("sp")
        raise _Stop()

    cpa = lambda o, i: (lambda: nc.scalar.copy(o, i))
    cpv = lambda o, i: (lambda: nc.vector.tensor_copy(o, i))
    rcp = lambda o, i: (lambda: nc.vector.reciprocal(o, i))
    scn = lambda o, d0, d1, init, o0, o1: (lambda: nc.vector.tensor_tensor_scan(o, d0, d1, init, o0, o1))

    def din(name, shape):
        DR[name] = nc.dram_tensor(name, list(shape), F32, kind="ExternalInput").ap()

    def dout(name, shape):
        DR[name] = nc.dram_tensor(name, list(shape), F32, kind="ExternalOutput").ap()

    din("xT_own", [D, T]); din("xT_prev", [D, TP]); din("cT", [D, 17]); din("flag", [128, 1])
    din("s0T", [8, 128, 16 * 64]); din("sshT", [128, 26 * 16])
    din("w_ada", [D, 6 * D]); din("badaT", [128, 96])
    din("gvec", [128, 48])
    din("w_in", [D, CIN]); din("svec", [128, 26 + 9 * 8])
    din("lora_up", [128, 1024]); din("w_gate_up", [128, 1024])
    din("ln_g", [1024]); din("ln_b", [1024])
    din("w_spT", [8, 128, 128]); din("w_spTs", [8, 128, 128]); din("bsp", [8, 128]); din("bsps", [8, 128])
    din("w_branch_a", [1024, D]); din("w_branch_b", [1024, D]); din("w_out", [D, D])
    din("w_ffn_in", [D, 2 * DFF]); din("w_ffn_out", [DFF, D]); din("cst", [128, NCST])
    DR["cstb"] = nc.dram_tensor("cstb", [128, 1152], BF16, kind="ExternalInput").ap()
    dout("yT", [D, T]); dout("wkv_p", [8, 128, 128]); dout("shp", [128, 26])
    dout("wkv_s", [8, 128, 16 * 64]); dout("shs", [128, 26 * 16]); dout("v_s", [128, 1024])

    def sbp(name, shape, dt=F32):
        return nc.alloc_sbuf_tensor(_uname(name), list(shape), dt).ap()

    cst = sbp("cst", [128, NCST])
    P.dma("sp", cst, DR["cst"], writes=["cst"])
    ident = cst[:, 0:128]; m_sl = cst[:, 128:256]; m_gt = cst[:, 256:384]; m_le = cst[:, 384:512]
    ms_sl = cst[:, 512:640]; ms_gt = cst[:, 640:768]; ms_le = cst[:, 768:896]
    mreset = cst[:, 896:1024]; blk1 = cst[:, 1024:1152]; ones = cst[:, 1152:1280]
    maskTB = cst[:, 1280:1296]
    cstb = sbp("cstb", [128, 1152], BF16)
    blk1b = cstb[:, 1024:1152]
    P.dma("sp", cstb, DR["cstb"], writes=["cstb"])
    moff = [cstb[:, l * 128:(l + 1) * 128] for l in range(4)]
    moffT = [cstb[:, 512 + l * 128:512 + (l + 1) * 128] for l in range(4)]
    onesR = sbp("onesR", [128, 128], F32R)
    P.op("dve", cpv(onesR, ones), ["cst"], ["onesR"])
    epsc = sbp("epsc", [128, 8])
    P.op("dve", lambda: nc.vector.memset(epsc[:, 0:1], 1e-6), [], ["epsc"])
    P.op("dve", lambda: nc.vector.memset(epsc[:, 1:2], 64e-5), [], ["epsc"])
    P.op("dve", lambda: nc.vector.memset(epsc[:, 2:3], 1e-5), [], ["epsc"])
    P.op("dve", lambda: nc.vector.memset(epsc[:, 3:4], 1.0), [], ["epsc"])
    P.op("dve", lambda: nc.vector.memset(epsc[:, 4:5], -0.5), [], ["epsc"])
    flag = sbp("flag", [128, 1]); P.dma("sp", flag, DR["flag"], writes=["flag"])
    gvec = sbp("gvec", [128, 48]); P.dma("sp", gvec, DR["gvec"], writes=["gvec"])
    svec = sbp("svec", [128, 114]); P.dma("sp", svec[:, 0:98], DR["svec"], writes=["svec"])
    P.op("dve", lambda: nc.vector.tensor_scalar(svec[:, 98:114], svec[:, 26:42], -1.0, None, ALU.mult, ALU.bypass), ["svec"], ["svec"])
    P.op("dve", lambda: nc.vector.tensor_scalar(svec[:, 58:66], svec[:, 50:58], -1.0, 1.0, ALU.mult, ALU.add), ["svec"], ["svec"])
    zeros = sbp("zeros", [128, 128])
    P.op("dve", lambda: nc.vector.memset(zeros, 0.0), [], ["zeros"])
    muT = svec[:, 0:26]
    sv = lambda i, hp: svec[:, 26 + 8 * i + hp: 26 + 8 * i + hp + 1]
    badaT = sbp("badaT", [128, 96]); P.dma("sp", badaT, DR["badaT"], writes=["badaT"])
    modT = sbp("modT", [128, 96, 17])
    MODK = [("modT", k) for k in range(6)]
    gm = sbp("gm", [128, 16, 17]); gf = sbp("gf", [128, 16, 17])
    WB = [sbp(f"WB{i}", [128, 16, 256], BF16) for i in range(2)]
    wbc = [0]

    def wslot():
        i = wbc[0] % 2
        wbc[0] += 1
        return WB[i], f"WB{i}"

    psb = [nc.alloc_psum_tensor(f"psb{i}", [128, 512], F32).ap() for i in range(8)]
    bc = [0]; qc = [0]

    BPOOL = [[0, 1, 2, 3]]
    QPOOL = [[2, 3, 4, 5, 6, 7]]

    def bank():
        pool_ = BPOOL[0]
        i = pool_[bc[0] % len(pool_)]
        bc[0] += 1
        return psb[i], f"psb{i}"

    def quart():
        pool_ = QPOOL[0]
        i = pool_[qc[0] % len(pool_)]
        qc[0] += 1
        return psb[i][:, 0:128], f"psb{i}"

    V = lambda fn, r, w: P.op("dve", fn, r, w)
    A = lambda fn, r, w: P.op("act", fn, r, w)

    def MM(out, lhsT, rhs, start, stop, r, w, signal=True):
        return P.op("pe", lambda: nc.tensor.matmul(out, lhsT=lhsT, rhs=rhs, start=start, stop=stop), r, w, signal)

    def TR(out, in_, r, w):
        return P.op("pe", lambda: nc.tensor.transpose(out, in_, ident), list(r) + ["cst"], w)

    tt = lambda o, a, b, op: (lambda: nc.vector.tensor_tensor(o, a, b, op))
    ts = lambda o, a, s1, s2, o0, o1: (lambda: nc.vector.tensor_scalar(o, a, s1, s2, o0, o1))
    stt = lambda o, a, s, b, o0, o1: (lambda: nc.vector.scalar_tensor_tensor(o, a, s, b, o0, o1))
    act = lambda o, i, f, **kw: (lambda: nc.scalar.activation(o, i, f, **kw))

    def load_w(src_ap, nk, ncols, dst=None, key=None):
        if dst is None:
            dst, key = wslot()
        P.dma("pool", dst[:, 0:nk, 0:ncols], src_ap.rearrange("(kc p) n -> p kc n", p=128), writes=[key])
        return dst, key

    _ARR.init(nc, "arenaR", F32R, 10624)
    _AR.init(nc)
    scb = sbp("scb", [128, 16, 17], BF16)
    with ExitStack() as es:
        cTt = es.enter_context(_sbt(nc, "cTt", [128, 16, 17], F32)).ap()
        P.dma("sp", cTt, DR["cT"].rearrange("(kc p) n -> p kc n", p=128), writes=["cTt"])
        A(act(scb, cTt, AF.Silu), ["cTt"], ["scb"])
        P.barrier()

    def ada_block(blk):
        wa_, wak = load_w(DR["w_ada"][:, blk * 256:(blk + 1) * 256], 16, 256)
        for jj in range(2):
            j = blk * 2 + jj
            pb, pk = bank()
            for kc in range(16):
                MM(pb[:, 0:17], wa_[:, kc, jj * 128:(jj + 1) * 128], scb[:, kc, :], kc == 0, kc == 15,
                   [wak, "scb"], [pk], signal=(kc == 15))
            A(act(modT[:, j, :], pb[:, 0:17], AF.Identity, bias=badaT[:, j:j + 1]), [pk, "badaT"], [("modT", j // 16)])

    for blk in range(16):
        ada_block(blk)
    for dc in range(16):
        V(ts(gm[:, dc, :], modT[:, 16 + dc, :], 1.0, gvec[:, dc:dc + 1], ALU.add, ALU.mult), [("modT", 1), "gvec"], ["gm"])
    ckpt("A", [("modT", modT, MODK), ("gm", gm, ["gm"])])

    def rms_rstd(es_tiles, src_fn, n, srckeys):
        sq, rstd, tmpn = es_tiles
        pb, pk = bank()
        for dc in range(16):
            s = dc % 2
            A(act(sq[s][:, 0:n], src_fn(dc), AF.Square), srckeys(dc), [f"sq{s}"])
            MM(pb[:, 0:n], onesR, sq[s][:, 0:n], dc == 0, dc == 15, [f"sq{s}", "onesR"], [pk], signal=True)
        A(act(tmpn[:, 0:n], pb[:, 0:n], AF.Sqrt, bias=epsc[:, 0:1], scale=1.0 / D), [pk, "epsc"], ["rstd"])
        V(rcp(rstd[:, 0:n], tmpn[:, 0:n]), ["rstd"], ["rstd"])
        return rstd

    def build_hT(es, hT, xsrc, ncols, g_t, shift_base, with_sample):
        xg = es.enter_context(_sbt(nc, "xg", [128, 16, 512], F32)).ap()
        sq = [es.enter_context(_sbt(nc, f"sq{i}", [128, 512], F32R)).ap() for i in range(2)]
        rstd = es.enter_context(_sbt(nc, "rstd", [128, 512], F32)).ap()
        tmpn = es.enter_context(_sbt(nc, "tmpn", [128, 512], F32)).ap()
        tq = [es.enter_context(_sbt(nc, f"tq{i}", [128, 512], F32)).ap() for i in range(2)]
        for g0 in range(0, ncols, 512):
            n = min(512, ncols - g0)
            for dc in range(16):
                P.dma("sp", xg[:, dc, 0:n], xsrc[dc * 128:(dc + 1) * 128, g0:g0 + n], writes=[("xg", dc)])
            rms_rstd((sq, rstd, tmpn), lambda dc: xg[:, dc, 0:n], n, lambda dc: [("xg", dc)])
            for dc in range(16):
                s = dc % 2
                if with_sample and g0 >= TP:
                    b3 = lambda a: a.unsqueeze(2).broadcast_to([128, 16, 8])
                    v3 = lambda a: a.rearrange("p (b t) -> p b t", t=8)
                    V(tt(v3(tq[s][:, 0:n]), v3(xg[:, dc, 0:n]), b3(g_t[:, dc, 1:17]), ALU.mult), [("xg", dc), "gm", "gf"], [f"tq{s}"])
                    V(tt(tq[s][:, 0:n], tq[s][:, 0:n], rstd[:, 0:n], ALU.mult), [f"tq{s}", "rstd"], [f"tq{s}"])
                    V(tt(v3(hT[:, dc, g0:g0 + n]), v3(tq[s][:, 0:n]), b3(modT[:, shift_base + dc, 1:17]), ALU.add),
                      [f"tq{s}", *MODK], [("hT", dc)])
                else:
                    V(stt(tq[s][:, 0:n], xg[:, dc, 0:n], g_t[:, dc, 0:1], rstd[:, 0:n], ALU.mult, ALU.mult),
                      [("xg", dc), "gm", "gf", "rstd"], [f"tq{s}"])
                    A(act(hT[:, dc, g0:g0 + n], tq[s][:, 0:n], AF.Identity, bias=modT[:, shift_base + dc, 0:1]),
                      [f"tq{s}", *MODK], [("hT", dc)])

    def dense_fm(wt, wkey, wcol0, nk, act_fn, act_keys, c0, n, consumer):
        pb, pk = bank()
        for kc in range(nk):
            MM(pb[:, 0:n], wt[:, kc, wcol0:wcol0 + 128], act_fn(kc)[:, c0:c0 + n], kc == 0, kc == nk - 1,
               [wkey] + act_keys(kc), [pk], signal=(kc == nk - 1))
        consumer(pb[:, 0:n], pk)

    es_mix = ExitStack()
    es_mp = ExitStack()
    mpa = lambda name, shape, dt=F32: es_mp.enter_context(_sbt(nc, name, list(shape), dt)).ap()
    lora_d = mpa("lora_d", [128, 1024], BF16); lora_a = mpa("lora_a", [128, 1024], BF16)
    P.op("dve", lambda: nc.vector.memset(lora_d[64:128, :], 0.0), [], ["lora_up"])
    P.op("dve", lambda: nc.vector.memset(lora_a[0:64, :], 0.0), [], ["lora_up"])
    P.dma("pool", lora_d[0:64, :], DR["lora_up"][0:64, :], writes=["lora_up"])
    P.dma("pool", lora_a[64:128, :], DR["lora_up"][64:128, :], writes=["lora_up"])
    wgu = mpa("wgu", [128, 1024], BF16); P.dma("pool", wgu, DR["w_gate_up"], writes=["wgu"])
    sshT = mpa("sshT", [128, 26, 16]); P.dma("sp", sshT, DR["sshT"].rearrange("p (j b) -> p j b", b=16), writes=["sshT"])
    Hs = mpa("Hs", [128, 8, 128], F32R)
    hlast = mpa("hlast", [128, 16, 2], BF16)
    shp = mpa("shp", [128, 26]); shs = mpa("shs", [128, 26, 16])
    hT = es_mix.enter_context(_sbt(nc, "hT", [128, 16, T], BF16)).ap()
    hT_fn = lambda kc: hT[:, kc, :]
    hT_keys = lambda kc: [("hT", kc)]
    oaT = es_mix.enter_context(_sbt(nc, "oaT", [128, 8, T], BF16)).ap()

    def mixer_pass(own):
        ncols = T if own else TP
        ntiles = 9 if own else 8
        with ExitStack() as es:
            build_hT(es, hT, DR["xT_own"] if own else DR["xT_prev"], ncols, gm, 0, own)
        if not own:
            V(cpv(hlast, hT[:, :, TP - 2:TP]), [("hT", k) for k in range(16)], ["hlast"])
        P.barrier()
        ckpt("B1" if not own else "B2", [("hT", hT, [("hT", k) for k in range(16)])])
        with ExitStack() as es:
            al = lambda name, shape, dt=F32: es.enter_context(_sbt(nc, name, list(shape), dt)).ap()
            lor = al("lor", [128, T], BF16); sg = al("sg", [128, T], BF16)
            zraw = [al(f"zraw{i}", [128, 1 + T]) for i in range(1)]
            zrs = [al(f"zrs{i}", [128, 16, 8]) for i in range(1)]
            big = al("big", [128, 16, 64])
            dtl = [big[:, 0:8, :].rearrange("p b v -> p (b v)")]
            zs = [[al(f"zs{s}_{w}", [128, T]) for w in range(3)] for s in range(1)]
            wrkv = [al(f"wrkv{i}", [128, 16, 3, 128], BF16) for i in range(1)]
            zcnt = [0]

            def shift_evac(j, dst, dstkey, wt, wkey, wcol0):
                zi = 0
                zcnt[0] += 1
                zr, zk = zraw[zi], f"zraw{zi}"
                mu = muT[:, j:j + 1]
                if own:
                    pb, pk = bank()
                    for kc in range(16):
                        MM(pb[:, 0:2], wt[:, kc, wcol0:wcol0 + 128], hlast[:, kc, :], kc == 0, kc == 15, [wkey, "hlast"], [pk], signal=(kc == 15))
                    A(act(zr[:, 0:1], pb[:, 1:2], AF.Identity, scale=flag[:, 0:1]), [pk, "flag"], [zk])
                else:
                    V(lambda: nc.vector.memset(zr[:, 0:1], 0.0), [], [zk])
                for g0 in range(0, TP, 512):
                    def cons(ps, pk, g0=g0):
                        A(cpa(zr[:, 1 + g0:1 + g0 + 512], ps), [pk], [zk])
                        d = dtl[0]; dk = "big"
                        V(tt(d, zr[:, g0:g0 + 512], ps, ALU.subtract), [zk, pk], [dk])
                        V(stt(dst[:, g0:g0 + 512], d, mu, ps, ALU.mult, ALU.add), [dk, pk, "svec"], [dstkey])
                    dense_fm(wt, wkey, wcol0, 16, hT_fn, hT_keys, g0, 512, cons)
                if own:
                    V(cpv(shp[:, j:j + 1], zr[:, TP:TP + 1]), [zk], ["shp"])

                    def cons_s(ps, pk):
                        z3 = zrs[zi]; z3k = f"zrs{zi}"
                        p3 = ps.rearrange("p (b t) -> p b t", t=8)
                        A(cpa(z3[:, :, 1:8], p3[:, :, 0:7]), [pk], [z3k])
                        V(cpv(z3[:, :, 0:1], sshT[:, j, :].unsqueeze(2)), ["sshT"], [z3k])
                        V(cpv(shs[:, j, :].unsqueeze(2), p3[:, :, 7:8]), [pk], ["shs"])
                        d = dtl[0][:, 0:128]
                        V(tt(d, z3.rearrange("p b t -> p (b t)"), ps, ALU.subtract), [z3k, pk], ["big"])
                        V(stt(dst[:, TP:T], d, mu, ps, ALU.mult, ALU.add), ["big", pk, "svec"], [dstkey])
                    dense_fm(wt, wkey, wcol0, 16, hT_fn, hT_keys, TP, 128, cons_s)

            wl, wlk = load_w(DR["w_in"][:, 3072:3328], 16, 256)
            zl = zs[0][0]
            shift_evac(24, zl, "zs0_0", wl, wlk, 0)
            A(act(lor[0:64, 0:ncols], zl[0:64, 0:ncols], AF.Tanh), ["zs0_0"], ["lor"])
            V(cpv(lor[64:128, 0:ncols], zl[64:128, 0:ncols]), ["zs0_0"], ["lor"])
            if own:
                shift_evac(25, zl, "zs0_0", wl, wlk, 128)
                A(act(sg, zl, AF.Sigmoid), ["zs0_0"], ["sg"])

            ckpt("L1" if not own else "L2", [("lor", lor, ["lor"]), ("sg", sg, ["sg"])])
            NS = 1
            DBN_ = {"AbTA", "AbTB", "BtTA", "BtTB", "KtTA", "KtTB", "RbT", "Bt_tm", "Kt_tm", "VeA", "VeB", "YA0", "YB0", "G", "bon"}
            pt = {}
            for nm in ["oT", "o2", "kk2b", "rkb"]:
                pt[nm] = [al(f"p_{nm}0", [128, 128], BF16)]
            for nm in ["lw", "cum", "aa", "kk", "kkn", "kp", "G", "Gi", "Gm1", "bon",
                       "mean", "msq", "varp", "cen", "otm"]:
                pt[nm] = [al(f"p_{nm}{s}", [128, 128]) for s in range(2 if nm in DBN_ else 1)]
            for nm in ["AbT", "BtT", "KtT", "RbT", "Bt_tm", "Kt_tm", "VeA", "VeB", "UeA", "UeB", "WT",
                       "AbTA", "AbTB", "BtTA", "BtTB", "KtTA", "KtTB", "WTA", "WTB", "RbTA", "RbTB",
                       "sX0", "sX1", "sXT0", "sXT1", "YA0", "YA1", "LakA", "MrbA", "MrkA",
                       "YB0", "YB1", "LakB", "MrbB", "MrkB", "DTfA", "DTfB"]:
                pt[nm] = [al(f"p_{nm}{s}", [128, 128], F32R) for s in range(2 if nm in DBN_ else 1)]
            for hh_ in ("A", "B"):
                for nm in ["X0", "X1", "XT0", "XT1", "D0", "D1", "DT0", "DT1", "Pm", "Qm", "Lo0", "Lo1", "Lo2", "Lo3", "LoT0", "LoT1", "LoT2", "Lb", "LTb"]:
                    pt[nm + hh_] = [al(f"p_{nm}{hh_}{s}", [128, 128], BF16) for s in range(NS)]
            for nm in ["VeA", "VeB", "UeA", "UeB", "AbTA", "AbTB", "BtTA", "BtTB", "KtTA", "KtTB", "WTA", "WTB", "RbTA", "RbTB"]:
                for s in range(len(pt[nm])):
                    V((lambda a: (cpv(a, zeros)))(pt[nm][s]), ["zeros"], [(nm, s)])
            if own:
                h0r = al("h0r", [128, 16, 64], F32R)
                hsn = al("hsn", [128, 16, 64])
                u1 = al("u1", [128, 64]); o0 = al("o0", [128, 64])
                Ublk = al("Ublk", [128, 16, 64], F32R); Vblk = al("Vblk", [128, 16, 64], F32R)
            setc = [0]

            for hp in range(8):
                if not own:
                    for blk_ in range(16 + hp * 4, 16 + hp * 4 + 4):
                        ada_block(blk_)
                    if hp == 7:
                        for dc_ in range(16):
                            V(ts(gf[:, dc_, :], modT[:, 64 + dc_, :], 1.0, gvec[:, 16 + dc_:17 + dc_], ALU.add, ALU.mult), [("modT", 4), "gvec"], ["gf"])
                wi = 0
                wr, wrk = wrkv[wi], f"wrkv{wi}"
                which = [0, 1, 2] if own else [1, 2]
                for w in which:
                    c = w * 1024 + hp * 128
                    P.dma("pool", wr[:, :, w, :], DR["w_in"][:, c:c + 128].rearrange("(kc p) n -> p kc n", p=128), writes=[wrk])
                Z = zs[0]
                for w in which:
                    shift_evac(w * 8 + hp, Z[w], f"zs0_{w}", wr[:, :, w, :], wrk, 0)
                zr_, zk_, zv_ = Z
                kr, kk_, kv = [f"zs0_{w}" for w in range(3)]
                if own:
                    A(act(Hs[:, hp, :], r32(Hs[:, hp, :]), AF.Identity, scale=flag[:, 0:1]), [("Hs", hp), "flag"], [("Hs", hp)])
                else:
                    V((lambda a: (cpv(a, zeros)))(Hs[:, hp, :]), ["zeros"], [("Hs", hp)])
                ch = slice(hp * 128, (hp + 1) * 128)
                if hp == 0 and not own:
                    ckpt("T0", [("zs1", zs[0][1], ["zs0_1"]), ("zs2", zs[0][2], ["zs0_2"]), ("Hs", r32(Hs), [("Hs", 0)])])
                DBN = {"AbTA", "AbTB", "BtTA", "BtTB", "KtTA", "KtTB", "RbT", "Bt_tm", "Kt_tm", "VeA", "VeB", "YA0", "YB0", "G", "bon"}

                def env(ti):
                    samp = own and ti == 8
                    par = ti % 2
                    K = lambda nm: (nm, par if nm in DBN else 0)
                    X = lambda nm: pt[nm][par if nm in DBN else 0]
                    cs = slice(ti * 128, (ti + 1) * 128)
                    Msl, Mgt, Mle = (ms_sl, ms_gt, ms_le) if samp else (m_sl, m_gt, m_le)
                    return samp, K, X, cs, Msl, Mgt, Mle

                def prep_gen(ti):
                    samp, K, X, cs, Msl, Mgt, Mle = env(ti)
                    q, qk = quart()
                    MM(q, lora_d[:, ch], lor[:, cs], True, True, ["lora_up", "lor"], [qk])
                    A(act(X("lw"), q, AF.Exp, bias=svec[:, 98 + hp:99 + hp], scale=-1.0), [qk, "svec"], [K("lw")])
                    A(act(X("lw"), X("lw"), AF.Ln, bias=epsc[:, 3:4]), [K("lw"), "epsc"], [K("lw")])
                    A(act(X("lw"), X("lw"), AF.Exp, bias=epsc[:, 4:5], scale=-1.0), [K("lw"), "epsc"], [K("lw")])
                    yield
                    V(scn(X("cum"), mreset if samp else ones, X("lw"), 0.0, ALU.mult, ALU.add),
                      [K("lw"), "cst"], [K("cum")])
                    q, qk = quart()
                    MM(q, lora_a[:, ch], lor[:, cs], True, True, ["lora_up", "lor"], [qk])
                    A(act(X("aa"), q, AF.Exp, bias=svec[:, 106 + hp:107 + hp], scale=-1.0), [qk, "svec"], [K("aa")])
                    A(act(X("aa"), X("aa"), AF.Ln, bias=epsc[:, 3:4]), [K("aa"), "epsc"], [K("aa")])
                    A(act(X("aa"), X("aa"), AF.Exp, scale=-1.0), [K("aa")], [K("aa")])
                    yield
                    V(ts(X("kk"), zk_[:, cs], sv(2, hp), None, ALU.mult, ALU.bypass), [kk_, "svec"], [K("kk")])
                    A(act(X("kk2b"), X("kk"), AF.Square), [K("kk")], [K("kk2b")])
                    q, qk = quart()
                    MM(q, blk1b, X("kk2b"), True, True, ["cstb", K("kk2b")], [qk])
                    V(ts(X("Gm1"), q, 1e-19, None, ALU.max, ALU.bypass), [qk], [K("Gm1")])
                    A(act(X("Gm1"), X("Gm1"), AF.Ln), [K("Gm1")], [K("Gm1")])
                    A(act(X("Gm1"), X("Gm1"), AF.Exp, scale=-0.5), [K("Gm1")], [K("Gm1")])
                    V(tt(X("kkn"), X("kk"), X("Gm1"), ALU.mult), [K("kk"), K("Gm1")], [K("kkn")])
                    yield
                    V(ts(X("kp"), X("aa"), sv(3, hp), sv(4, hp), ALU.mult, ALU.add), [K("aa"), "svec"], [K("kp")])
                    V(tt(X("kp"), zk_[:, cs], X("kp"), ALU.mult), [kk_, K("kp")], [K("kp")])
                    A(act(X("G"), X("cum"), AF.Exp, scale=-1.0), [K("cum")], [K("G")])
                    A(act(X("Gi"), X("cum"), AF.Exp), [K("cum")], [K("Gi")])
                    V(tt(X("cum"), X("cum"), X("lw"), ALU.subtract), [K("cum"), K("lw")], [K("cum")])
                    A(act(X("Gm1"), X("cum"), AF.Exp, scale=-1.0), [K("cum")], [K("Gm1")])
                    yield
                    V(stt(X("AbT"), X("kkn"), -1.0, X("Gm1"), ALU.mult, ALU.mult), [K("kkn"), K("Gm1")], [K("AbT")])
                    V(tt(X("kk"), X("kkn"), X("aa"), ALU.mult), [K("kkn"), K("aa")], [K("kk")])
                    V(tt(X("BtT"), X("kk"), X("Gi"), ALU.mult), [K("kk"), K("Gi")], [K("BtT")])
                    V(tt(X("KtT"), X("kp"), X("Gi"), ALU.mult), [K("kp"), K("Gi")], [K("KtT")])
                    for nm in ("AbT", "BtT", "KtT"):
                        A((lambda nm=nm: (cpa(X(nm + "A")[0:64, :], r32(X(nm))[0:64, :])))(), [K(nm)], [K(nm + "A")])
                        V((lambda nm=nm: (cpv(X(nm + "B")[64:128, :], r32(X(nm))[64:128, :])))(), [K(nm)], [K(nm + "B")])
                    if own:
                        V(tt(X("RbT"), zr_[:, cs], X("G"), ALU.mult), [kr, K("G")], [K("RbT")])
                        if samp:
                            A(cpa(X("RbTA")[0:64, :], r32(X("RbT"))[0:64, :]), [K("RbT")], [K("RbTA")])
                            V(cpv(X("RbTB")[64:128, :], r32(X("RbT"))[64:128, :]), [K("RbT")], [K("RbTB")])
                        V(stt(X("rkb"), zr_[:, cs], sv(7, hp), X("kp"), ALU.mult, ALU.mult), [kr, "svec", K("kp")], [K("rkb")])
                        q, qk = quart()
                        MM(q, blk1b, X("rkb"), True, True, ["cstb", K("rkb")], [qk])
                        V(tt(X("bon"), q, zv_[:, cs], ALU.mult), [qk, kv], [K("bon")])
                    yield
                    q, qk = quart()
                    TR(q, r32(X("AbT")), [K("AbT")], [qk])
                    A(cpa(X("YA0")[:, 0:64], q[:, 0:64]), [qk], [K("YA0")])
                    A(cpa(X("YB0")[:, 64:128], q[:, 64:128]), [qk], [K("YB0")])
                    q, qk = quart()
                    TR(q, r32(X("BtT")), [K("BtT")], [qk])
                    A(cpa(X("Bt_tm"), q), [qk], [K("Bt_tm")])
                    yield
                    q, qk = quart()
                    TR(q, r32(X("KtT")), [K("KtT")], [qk])
                    A(cpa(X("Kt_tm"), q), [qk], [K("Kt_tm")])
                    q, qk = quart()
                    TR(q, zv_[:, cs], [kv], [qk])
                    A(cpa(X("VeA")[:, 0:64], q[:, 0:64]), [qk], [K("VeA")])
                    A(cpa(X("VeB")[:, 64:128], q[:, 64:128]), [qk], [K("VeB")])

                def scan_gens(ti):
                    samp, K, X, cs, Msl, Mgt, Mle = env(ti)
                    Yfin = YF.setdefault(ti, {})

                    def head_gen(hh, p0):
                        ps_ = slice(p0, p0 + 64)
                        dhalf = slice(0, 64) if hh == "A" else slice(64, 128)
                        ohalf = slice(64, 128) if hh == "A" else slice(0, 64)
                        Ve = X("Ve" + hh); Vek = K("Ve" + hh)
                        HX = lambda nm: X(nm + hh)
                        HK = lambda nm: K(nm + hh)
                        q1, qk1 = quart()
                        MM(q1, X("BtT" + hh), X("AbT" + hh), True, True, [K("BtT" + hh), K("AbT" + hh)], [qk1])
                        q2, qk2 = quart()
                        MM(q2, X("AbT" + hh), X("BtT" + hh), True, True, [K("BtT" + hh), K("AbT" + hh)], [qk2])

                        def side_lak():
                            q, qk = quart()
                            MM(q, X("KtT" + hh), X("AbT" + hh), True, True, [K("KtT" + hh), K("AbT" + hh)], [qk])
                            V(tt(X("Lak" + hh), q, Msl, ALU.mult), [qk, "cst"], [K("Lak" + hh)])

                        def side_z():
                            q, qk = quart()
                            MM(q[:, 0:64], X("Lak" + hh), Ve[:, dhalf], True, True, [K("Lak" + hh), Vek], [qk])
                            A(cpa(X("Y" + hh + "0")[:, ohalf], q[:, 0:64]), [qk], [K("Y" + hh + "0")])

                        def side_m():
                            if own:
                                q, qk = quart()
                                MM(q, X("BtT" + hh), X("RbT"), True, True, [K("BtT" + hh), K("RbT")], [qk])
                                V(tt(X("Mrb" + hh), q, Mle, ALU.mult), [qk, "cst"], [K("Mrb" + hh)])
                                q, qk = quart()
                                MM(q, X("KtT" + hh), X("RbT"), True, True, [K("KtT" + hh), K("RbT")], [qk])
                                V(tt(X("Mrk" + hh), q, Mle, ALU.mult), [qk, "cst"], [K("Mrk" + hh)])

                        if samp:
                            V(tt(X("sXT0"), q1, ms_sl, ALU.mult), [qk1, "cst"], [K("sXT0")])
                            V(tt(X("sX0"), q2, ms_gt, ALU.mult), [qk2, "cst"], [K("sX0")])
                            yield
                            side_lak()
                            yield
                            side_z()
                            yield
                            side_m()
                            yield
                            nsteps = 3
                            for i in range(nsteps):
                                a, b = i % 2, (i + 1) % 2
                                Xa, XTa, Ya = X(f"sX{a}"), X(f"sXT{a}"), X(f"Y{hh}{a}")
                                Xb, XTb, Yb = X(f"sX{b}"), X(f"sXT{b}"), X(f"Y{hh}{b}")
                                q, qk = quart()
                                MM(q, XTa, Ya, True, True, [K(f"sXT{a}"), K(f"Y{hh}{a}")], [qk])
                                V(tt(Yb, q, r32(Ya), ALU.add), [qk, K(f"Y{hh}{a}")], [K(f"Y{hh}{b}")])
                                if i < nsteps - 1:
                                    q, qk = quart()
                                    MM(q, XTa, Xa, True, True, [K(f"sXT{a}"), K(f"sX{a}")], [qk])
                                    qq, qqk = quart()
                                    MM(qq, Xa, XTa, True, True, [K(f"sXT{a}"), K(f"sX{a}")], [qqk])
                                    A(cpa(Xb, q), [qk], [K(f"sX{b}")])
                                    A(cpa(XTb, qq), [qqk], [K(f"sXT{b}")])
                            Yf, Yfk = X(f"Y{hh}1"), K(f"Y{hh}1")
                            Yfin[hh] = (Yf, Yfk)
                            q, qk = quart()
                            TR(q, r32(Yf), [Yfk], [qk])
                            A(cpa(X("WT")[ps_, :], q[ps_, :]), [qk], [K("WT")])
                            V(cpv(X("WT" + hh)[ps_, :], q[ps_, :]), [qk], [K("WT" + hh)])
                            return
                        A(cpa(HX("LTb"), q1), [qk1], [HK("LTb")])
                        A(cpa(HX("Lb"), q2), [qk2], [HK("Lb")])
                        V(tt(HX("XT0"), HX("LTb"), ms_sl, ALU.mult), [HK("LTb"), "cst"], [HK("XT0")])
                        V(tt(HX("X0"), HX("Lb"), ms_gt, ALU.mult), [HK("Lb"), "cst"], [HK("X0")])
                        V(tt(HX("D0"), HX("X0"), ident, ALU.add), [HK("X0"), "cst"], [HK("D0")])
                        V(tt(HX("DT0"), HX("XT0"), ident, ALU.add), [HK("XT0"), "cst"], [HK("DT0")])

                        def mask_l(l):
                            if l < 3:
                                V(tt(HX(f"LoT{l}"), HX("LTb"), moffT[l], ALU.mult), [HK("LTb"), "cstb"], [HK(f"LoT{l}")])
                            V(tt(HX(f"Lo{l}"), HX("Lb"), moff[l], ALU.mult), [HK("Lb"), "cstb"], [HK(f"Lo{l}")])

                        side = [side_lak, lambda: mask_l(0), side_z, lambda: mask_l(1), side_m, lambda: mask_l(2), lambda: mask_l(3)]
                        cur = 0
                        for i in range(2):
                            a, b = i % 2, (i + 1) % 2
                            q, qk = quart()
                            MM(q, HX(f"XT{a}"), HX(f"X{a}"), True, True, [HK(f"XT{a}"), HK(f"X{a}")], [qk])
                            qq, qqk = quart()
                            MM(qq, HX(f"X{a}"), HX(f"XT{a}"), True, True, [HK(f"XT{a}"), HK(f"X{a}")], [qqk])
                            A(cpa(HX(f"X{b}"), q), [qk], [HK(f"X{b}")])
                            A(cpa(HX(f"XT{b}"), qq), [qqk], [HK(f"XT{b}")])
                            if side:
                                side.pop(0)()
                            yield
                            q, qk = quart()
                            MM(q, HX(f"XT{b}"), HX(f"D{cur}"), True, True, [HK(f"XT{b}"), HK(f"D{cur}")], [qk])
                            qq, qqk = quart()
                            MM(qq, HX(f"D{cur}"), HX(f"XT{b}"), True, True, [HK(f"XT{b}"), HK(f"D{cur}")], [qqk])
                            V(tt(HX(f"D{1 - cur}"), q, HX(f"D{cur}"), ALU.add), [qk, HK(f"D{cur}")], [HK(f"D{1 - cur}")])
                            V(tt(HX(f"DT{1 - cur}"), qq, HX(f"DT{cur}"), ALU.add), [qqk, HK(f"DT{cur}")], [HK(f"DT{1 - cur}")])
                            cur = 1 - cur
                            if side:
                                side.pop(0)()
                            yield
                        for l in range(4):
                            last = (l == 3)
                            while side and l + 2 > 7 - len(side):
                                side.pop(0)()
                            if not last:
                                q, qk = quart()
                                MM(q, HX(f"LoT{l}"), HX(f"D{cur}"), True, True, [HK(f"LoT{l}"), HK(f"D{cur}")], [qk])
                                A(cpa(HX("Pm"), q), [qk], [HK("Pm")])
                            qq, qqk = quart()
                            MM(qq, HX(f"Lo{l}"), HX(f"DT{cur}"), True, True, [HK(f"Lo{l}"), HK(f"DT{cur}")], [qqk])
                            A(cpa(HX("Qm"), qq), [qqk], [HK("Qm")])
                            if side:
                                side.pop(0)()
                            yield
                            if not last:
                                q, qk = quart()
                                MM(q, HX(f"DT{cur}"), HX("Pm"), True, True, [HK(f"DT{cur}"), HK("Pm")], [qk])
                                V(tt(HX(f"D{1 - cur}"), q, HX(f"D{cur}"), ALU.add), [qk, HK(f"D{cur}")], [HK(f"D{1 - cur}")])
                            qq, qqk = quart()
                            MM(qq, HX(f"D{cur}"), HX("Qm"), True, True, [HK(f"D{cur}"), HK("Qm")], [qqk])
                            if last:
                                V(tt(HX("DTf"), qq, HX(f"DT{cur}"), ALU.add), [qqk, HK(f"DT{cur}")], [HK("DTf")])
                            else:
                                V(tt(HX(f"DT{1 - cur}"), qq, HX(f"DT{cur}"), ALU.add), [qqk, HK(f"DT{cur}")], [HK(f"DT{1 - cur}")])
                            cur = 1 - cur
                            yield
                        while side:
                            side.pop(0)()
                        DT, DTk = HX("DTf"), HK("DTf")
                        Y0, Y0k = X(f"Y{hh}0"), K(f"Y{hh}0")
                        q, qk = quart()
                        MM(q, Y0, DT, True, True, [Y0k, DTk], [qk])
                        A(cpa(X("WT")[ps_, :], q[ps_, :]), [qk], [K("WT")])
                        q, qk = quart()
                        MM(q[:, 0:64], DT, Y0[:, ohalf], True, True, [Y0k, DTk], [qk])
                        A(cpa(X(f"Y{hh}1")[:, ohalf], q[:, 0:64]), [qk], [K(f"Y{hh}1")])
                        Yfin[hh] = (X(f"Y{hh}1"), K(f"Y{hh}1"))

                    if samp:
                        def seq():
                            for g_ in (head_gen("A", 0), head_gen("B", 64)):
                                for _ in g_:
                                    yield
                        return [seq()]
                    return [head_gen("A", 0), head_gen("B", 64)]

                def chainpost_gen(ti):
                    samp, K, X, cs, Msl, Mgt, Mle = env(ti)
                    Yfin = YF[ti]
                    hk = ("Hs", hp)
                    Hh = Hs[:, hp, :]
                    if not samp:
                        q, qk = quart()
                        MM(q, X("WT"), Hh, True, True, [K("WT"), hk], [qk])
                        V(tt(X("UeA")[:, 0:64], q[:, 0:64], r32(Yfin["A"][0][:, 64:128]), ALU.add), [qk, Yfin["A"][1]], [K("UeA")])
                        V(tt(X("UeB")[:, 64:128], q[:, 64:128], r32(Yfin["B"][0][:, 0:64]), ALU.add), [qk, Yfin["B"][1]], [K("UeB")])
                        if own:
                            q, qk = quart()
                            MM(q, Hh, X("RbT"), True, False, [hk, K("RbT")], [qk], signal=False)
                            MM(q, X("UeA"), X("MrbA"), False, False, [K("UeA"), K("MrbA")], [qk], signal=False)
                            MM(q, X("UeB"), X("MrbB"), False, False, [K("UeB"), K("MrbB")], [qk], signal=False)
                            MM(q, X("VeA"), X("MrkA"), False, False, [K("VeA"), K("MrkA")], [qk], signal=False)
                            MM(q, X("VeB"), X("MrkB"), False, True, [K("VeB"), K("MrkB")], [qk])
                            A(cpa(X("oT"), q), [qk], [K("oT")])
                        q, qk = quart()
                        MM(q[:, 0:64], X("Bt_tm"), X("UeA")[:, 0:64], True, False, [K("Bt_tm"), K("UeA")], [qk], signal=False)
                        MM(q[:, 0:64], X("Kt_tm"), X("VeA")[:, 0:64], False, True, [K("Kt_tm"), K("VeA")], [qk], signal=False)
                        MM(q[:, 64:128], X("Bt_tm"), X("UeB")[:, 64:128], True, False, [K("Bt_tm"), K("UeB")], [qk], signal=False)
                        MM(q[:, 64:128], X("Kt_tm"), X("VeB")[:, 64:128], False, True, [K("Kt_tm"), K("VeB")], [qk])
                        GC = X("G")[:, 127:128]
                        for ps_, hf in ((slice(0, 64), slice(0, 64)), (slice(64, 128), slice(64, 128))):
                            A(act(X("msq")[ps_, 0:64], r32(Hh[ps_, hf]), AF.Identity, scale=GC[ps_, :]), [hk, K("G")], [K("msq")])
                            V(stt(Hh[ps_, hf], q[ps_, hf], GC[ps_, :], X("msq")[ps_, 0:64], ALU.mult, ALU.add), [qk, K("G"), K("msq")], [hk])
                        if own and ti == 7:
                            P.dma("sp", DR["wkv_p"][hp], r32(Hh), reads=[hk], semres=("Hs", hp))
                    else:
                        P.dma("sp", big, DR["s0T"][hp].rearrange("p (b v) -> p b v", v=64), writes=["big"])
                        V(cpv(h0r, big), ["big"], ["h0r"])
                        b3m = maskTB.unsqueeze(2).broadcast_to([128, 16, 64])
                        G3 = X("G").rearrange("p (b t) -> p b t", t=8)[:, :, 7:8]
                        for hh, p0 in (("A", 0), ("B", 64)):
                            ps_ = slice(p0, p0 + 64)
                            dhalf = slice(0, 64) if hh == "A" else slice(64, 128)
                            ohalf = slice(64, 128) if hh == "A" else slice(0, 64)
                            Ue, Uek = X("Ue" + hh), K("Ue" + hh)
                            Ve, Vek = X("Ve" + hh), K("Ve" + hh)
                            for src, srck, dstv, dstk in ((X("WT" + hh), K("WT" + hh), u1, "u1"), (X("RbT" + hh), K("RbT" + hh), o0, "o0")):
                                for nb in range(2):
                                    pb, pk = bank()
                                    MM(pb, src, h0r[:, nb * 8:(nb + 1) * 8, :].rearrange("p b v -> p (b v)"), True, True, [srck, "h0r"], [pk])
                                    V(tt(big[:, nb * 8:(nb + 1) * 8, :], pb.rearrange("p (b v) -> p b v", v=64), b3m[:, nb * 8:(nb + 1) * 8, :], ALU.mult),
                                      [pk, "cst"], ["big"])
                                V(lambda dstv=dstv: nc.vector.tensor_reduce(dstv, big.rearrange("p b v -> p v b"), AX.X, ALU.add), ["big"], [dstk])
                            V(tt(Ue[:, dhalf], u1, r32(Yfin[hh][0][:, ohalf]), ALU.add), ["u1", Yfin[hh][1]], [Uek])
                            q, qk = quart()
                            MM(q[:, 0:64], X("Mrb" + hh), Ue[:, dhalf], True, False, [K("Mrb" + hh), Uek], [qk], signal=False)
                            MM(q[:, 0:64], X("Mrk" + hh), Ve[:, dhalf], False, True, [K("Mrk" + hh), Vek], [qk])
                            V(tt(X("otm")[:, dhalf], q[:, 0:64], o0, ALU.add), [qk, "o0"], [K("otm")])
                            ub = r32(Ue[:, dhalf]).unsqueeze(1).broadcast_to([128, 16, 64])
                            vb = r32(Ve[:, dhalf]).unsqueeze(1).broadcast_to([128, 16, 64])
                            V(tt(Ublk, ub, b3m, ALU.mult), [Uek, "cst"], ["Ublk"])
                            V(tt(Vblk, vb, b3m, ALU.mult), [Vek, "cst"], ["Vblk"])
                            for nb in range(2):
                                pb, pk = bank()
                                bs = slice(nb * 8, (nb + 1) * 8)
                                MM(pb, X("Bt_tm"), Ublk[:, bs, :].rearrange("p b v -> p (b v)"), True, False, [K("Bt_tm"), "Ublk"], [pk], signal=False)
                                MM(pb, X("Kt_tm"), Vblk[:, bs, :].rearrange("p b v -> p (b v)"), False, True, [K("Kt_tm"), "Vblk"], [pk])
                                V(tt(hsn[ps_, bs, :], pb[ps_, :].rearrange("p (b v) -> p b v", v=64), r32(h0r)[ps_, bs, :], ALU.add), [pk, "h0r"], ["hsn"])
                                V(tt(hsn[ps_, bs, :], hsn[ps_, bs, :], G3[ps_, bs, :].broadcast_to([64, 8, 64]), ALU.mult), ["hsn", K("G")], ["hsn"])
                        P.dma("sp", DR["wkv_s"][hp].rearrange("p (b v) -> p b v", v=64), hsn, reads=["hsn"], semres="hsn")
                        q, qk = quart()
                        TR(q, X("otm"), [K("otm")], [qk])
                        A(cpa(X("oT"), q), [qk], [K("oT")])
                    yield
                    if own:
                        q, qk = quart()
                        MM(q, blk1b, X("oT"), True, True, ["cstb", K("oT")], [qk])
                        A(act(X("o2"), X("oT"), AF.Square), [K("oT")], [K("o2")])
                        q2, qk2 = quart()
                        MM(q2, blk1b, X("o2"), True, True, ["cstb", K("o2")], [qk2])
                        A(act(X("mean"), q, AF.Identity, scale=1.0 / 64), [qk], [K("mean")])
                        V(tt(X("msq"), X("mean"), X("mean"), ALU.mult), [K("mean")], [K("msq")])
                        V(stt(X("varp"), q2, 1.0 / 64, X("msq"), ALU.mult, ALU.subtract), [qk2, K("msq")], [K("varp")])
                        A(act(X("varp"), X("varp"), AF.Ln, bias=epsc[:, 1:2]), [K("varp"), "epsc"], [K("varp")])
                        A(act(X("varp"), X("varp"), AF.Exp, scale=-0.5), [K("varp")], [K("varp")])
                        yield
                        V(tt(X("cen"), X("oT"), X("mean"), ALU.subtract), [K("oT"), K("mean")], [K("cen")])
                        V(tt(X("cen"), X("cen"), X("varp"), ALU.mult), [K("cen"), K("varp")], [K("cen")])
                        V(ts(X("cen"), X("cen"), sv(5, hp), sv(6, hp), ALU.mult, ALU.add), [K("cen"), "svec"], [K("cen")])
                        V(tt(X("cen"), X("cen"), X("bon"), ALU.add), [K("cen"), K("bon")], [K("cen")])
                        q, qk = quart()
                        MM(q, wgu[:, ch], sg[:, cs], True, True, ["wgu", "sg"], [qk])
                        V(tt(oaT[:, hp, cs], X("cen"), q, ALU.mult), [K("cen"), qk], [("oaT", hp)])
                    if hp == 0 and ti == 0:
                        ckpt("T1" if not own else "T2", [(nm, r32(pt[nm][0]) if pt[nm][0].dtype == F32R else pt[nm][0], [(nm, 0)]) for nm in
                                   ["lw", "cum", "aa", "kkn", "kp", "G", "AbT", "BtT", "KtT", "Bt_tm", "VeA", "VeB", "WT", "UeA", "UeB", "LakA", "YA0", "YA1", "YB1", "oT", "cen"]]
                             + [("Hs", r32(Hs), [("Hs", k) for k in range(8)]), ("zs1", zs[0][1], ["zs0_1"]), ("zs2", zs[0][2], ["zs0_2"])])

                YF = {}

                def rec(gen, qpool, bpool=(0,)):
                    QPOOL[0] = list(qpool)
                    BPOOL[0] = list(bpool)
                    r_ = P.record(gen)
                    QPOOL[0] = [2, 3, 4, 5, 6, 7]
                    BPOOL[0] = [0, 1, 2, 3]
                    return r_

                P.schedule([rec(prep_gen(0), (1,))])
                prev = None
                for ti in range(ntiles):
                    streams = []
                    if prev is not None:
                        streams.append(rec(chainpost_gen(prev), (2, 3)))
                    hg = scan_gens(ti)
                    if len(hg) == 1:
                        streams.append(rec(hg[0], (4, 5, 6, 7)))
                    else:
                        streams.append(rec(hg[0], (4, 5)))
                        streams.append(rec(hg[1], (6, 7)))
                    if ti + 1 < ntiles:
                        streams.append(rec(prep_gen(ti + 1), (1,)))
                    P.schedule(streams)
                    prev = ti
                P.schedule([rec(chainpost_gen(prev), (2, 3))])
            P.barrier()
            ckpt("C1" if not own else "C2", [("Hs", r32(Hs), [("Hs", k) for k in range(8)]), ("oaT", oaT, [("oaT", k) for k in range(8)])])

    mixer_pass(False)
    mixer_pass(True)
    P.dma("sp", DR["shp"], shp, reads=["shp"], semres="shp")
    P.dma("sp", DR["shs"].rearrange("p (j b) -> p j b", b=16), shs, reads=["shs"], semres="shs")
    P.barrier()
    es_mp.close()

    prodT = es_mix.enter_context(_sbt(nc, "prodT", [128, 8, T], BF16)).ap()
    with ExitStack() as es:
        al = lambda name, shape, dt=F32: es.enter_context(_sbt(nc, name, list(shape), dt)).ap()
        wsm = al("wsm", [128, 16, 128], BF16)
        with ExitStack() as es2:
            wsf = es2.enter_context(_sbt(nc, "wsf", [128, 8, 128], F32)).ap()
            P.dma("sp", wsf, DR["w_spT"].rearrange("g s t -> s g t"), writes=["wsf"])
            for g in range(8):
                V(tt(wsm[:, g, :], wsf[:, g, :], m_le, ALU.mult), ["wsf", "cst"], ["wsm"])
            P.dma("sp", wsf, DR["w_spTs"].rearrange("g s t -> s g t"), writes=["wsf"])
            for g in range(8):
                V(tt(wsm[:, 8 + g, :], wsf[:, g, :], ms_le, ALU.mult), ["wsf", "cst"], ["wsm"])
            P.barrier()
        gv = al("gv", [128, 9, 1024], BF16)
        vnb = gv
        lngb = al("lngb", [128, 1024]); lnbb = al("lnbb", [128, 1024])
        P.dma("sp", lngb, DR["ln_g"].partition_broadcast(128), writes=["lngb"])
        P.dma("sp", lnbb, DR["ln_b"].partition_broadcast(128), writes=["lnbb"])
        bspb = al("bspb", [128, 8, 128]); bspsb = al("bspsb", [128, 8, 128])
        P.dma("sp", bspb, DR["bsp"].rearrange("g t -> (g t)").partition_broadcast(128).rearrange("p (g t) -> p g t", t=128), writes=["bspb"])
        P.dma("sp", bspsb, DR["bsps"].rearrange("g t -> (g t)").partition_broadcast(128).rearrange("p (g t) -> p g t", t=128), writes=["bspsb"])
        for blk in range(4):
            wt, wk = load_w(DR["w_in"][:, 4352 + blk * 256: 4352 + (blk + 1) * 256], 16, 256)
            for ti in range(9):
                pb, pk = bank()
                for kc in range(16):
                    MM(pb[:, 0:256], hT[:, kc, ti * 128:(ti + 1) * 128], wt[:, kc, 0:256], kc == 0, kc == 15, [wk, ("hT", kc)], [pk], signal=(kc == 15))
                A(act(gv[:, ti, blk * 256:(blk + 1) * 256], pb[:, 0:256], AF.Gelu_apprx_tanh), [pk], [("gv", ti)])
        st6 = al("st6", [128, 12]); mv = al("mv", [128, 2]); vtmp = [al(f"vtmp{i}", [128, 1024]) for i in range(1)]
        for ti in range(9):
            s = 0
            V(lambda: nc.vector.bn_stats(st6[:, 0:6], gv[:, ti, 0:512]), [("gv", ti)], ["st6"])
            V(lambda: nc.vector.bn_stats(st6[:, 6:12], gv[:, ti, 512:1024]), [("gv", ti)], ["st6"])
            V(lambda: nc.vector.bn_aggr(mv, st6), ["st6"], ["mv"])
            A(act(mv[:, 1:2], mv[:, 1:2], AF.Sqrt, bias=epsc[:, 2:3]), ["mv", "epsc"], ["mv"])
            V(rcp(mv[:, 1:2], mv[:, 1:2]), ["mv"], ["mv"])
            V(ts(vtmp[s], gv[:, ti, :], mv[:, 0:1], mv[:, 1:2], ALU.subtract, ALU.mult), [("gv", ti), "mv"], [f"vtmp{s}"])
            V(tt(vtmp[s], vtmp[s], lngb, ALU.mult), [f"vtmp{s}", "lngb"], [f"vtmp{s}"])
            V(tt(vtmp[s], vtmp[s], lnbb, ALU.add), [f"vtmp{s}", "lnbb"], [f"vtmp{s}"])
            A(cpa(vnb[:, ti, :], vtmp[s]), [f"vtmp{s}"], [("gv", ti)])
            if ti == 8:
                P.dma("sp", DR["v_s"], vtmp[s], reads=[f"vtmp{s}"], semres=f"vtmp{s}")
        uT = [al(f"uT{i}", [128, T]) for i in range(1)]
        mx = [al(f"mx{i}", [128, 128]) for i in range(2)]
        for blk in range(4):
            wt, wk = load_w(DR["w_in"][:, 3328 + blk * 256: 3328 + (blk + 1) * 256], 16, 256)
            for jj in range(2):
                g = blk * 2 + jj
                us, uk = uT[0], "uT0"
                for g0 in range(0, T, 384):
                    dense_fm(wt, wk, jj * 128, 16, hT_fn, hT_keys, g0, 384,
                             lambda ps, pk, g0=g0: A(act(us[:, g0:g0 + 384], ps, AF.Gelu_apprx_tanh), [pk], [uk]))
                for ti in range(9):
                    wsel = g if ti < 8 else 8 + g
                    bsel = bspb if ti < 8 else bspsb
                    q, qk = quart()
                    MM(q, vnb[:, ti, g * 128:(g + 1) * 128], wsm[:, wsel, :], True, True, [("gv", ti), "wsm"], [qk])
                    m_ = mx[ti % 2]; mk = f"mx{ti % 2}"
                    V(tt(m_, q, bsel[:, g, :], ALU.add), [qk, "bspb", "bspsb"], [mk])
                    V(tt(prodT[:, g, ti * 128:(ti + 1) * 128], m_, us[:, ti * 128:(ti + 1) * 128], ALU.mult), [mk, uk], [("prodT", g)])
        P.barrier()
        ckpt("D", [("prodT", prodT, [("prodT", k) for k in range(8)]), ("gv", gv, [("gv", k) for k in range(9)])])

    es_mg = ExitStack()
    merged = es_mg.enter_context(_sbt(nc, "merged", [128, 16, T], BF16)).ap()
    with ExitStack() as es:
        al = lambda name, shape, dt=F32: es.enter_context(_sbt(nc, name, list(shape), dt)).ap()
        wba = al("wba", [128, 8, 256], BF16)
        sa = [al(f"sa{i}", [128, 384]) for i in range(1)]
        sb_ = [al(f"sb{i}", [128, 384]) for i in range(1)]
        wbb = al("wbb", [128, 8, 256], BF16)
        cnt = 0
        for blk in range(8):
            wta, wka = wslot()
            wtb, wkb = wslot()
            P.dma("pool", wta[:, :, 0:256], DR["w_in"][:, 5376 + blk * 256:5376 + (blk + 1) * 256].rearrange("(kc p) n -> p kc n", p=128), writes=[wka])
            P.dma("pool", wtb[:, :, 0:256], DR["w_in"][:, 7424 + blk * 256:7424 + (blk + 1) * 256].rearrange("(kc p) n -> p kc n", p=128), writes=[wkb])
            P.dma("pool", wba, DR["w_branch_a"][:, blk * 256:(blk + 1) * 256].rearrange("(kc p) n -> p kc n", p=128), writes=["wba"])
            P.dma("pool", wbb, DR["w_branch_b"][:, blk * 256:(blk + 1) * 256].rearrange("(kc p) n -> p kc n", p=128), writes=["wbb"])
            for jj in range(2):
                dc = blk * 2 + jj
                for g0 in range(0, T, 384):
                    s = 0
                    cs = slice(g0, g0 + 384)
                    dense_fm(wta, wka, jj * 128, 16, hT_fn, hT_keys, g0, 384,
                             lambda ps, pk: A(act(sa[s], ps, AF.Sigmoid), [pk], [f"sa{s}"]))
                    dense_fm(wtb, wkb, jj * 128, 16, hT_fn, hT_keys, g0, 384,
                             lambda ps, pk: A(act(sb_[s], ps, AF.Sigmoid), [pk], [f"sb{s}"]))
                    dense_fm(wba, "wba", jj * 128, 8, lambda kc: oaT[:, kc, :], lambda kc: [("oaT", kc)], g0, 384,
                             lambda ps, pk: V(tt(sa[s], sa[s], ps, ALU.mult), [f"sa{s}", pk], [f"sa{s}"]))
                    dense_fm(wbb, "wbb", jj * 128, 8, lambda kc: prodT[:, kc, :], lambda kc: [("prodT", kc)], g0, 384,
                             lambda ps, pk: V(tt(sb_[s], sb_[s], ps, ALU.mult), [f"sb{s}", pk], [f"sb{s}"]))
                    V(tt(merged[:, dc, cs], sa[s], sb_[s], ALU.add), [f"sa{s}", f"sb{s}"], [("merged", dc)])
        P.barrier()
        ckpt("E", [("merged", merged, [("merged", k) for k in range(16)])])
    es_mix.close()
    es_x = ExitStack()
    x1 = es_x.enter_context(_sbt(nc, "x1", [128, 16, T], F32)).ap()

    def resid_evac(ps, pk, dc, g0, n, gate_row):
        npr = max(0, min(TP, g0 + n) - g0)
        if npr > 0:
            V(stt(x1[:, dc, g0:g0 + npr], ps[:, 0:npr], modT[:, gate_row + dc, 0:1], x1[:, dc, g0:g0 + npr], ALU.mult, ALU.add),
              [pk, *MODK, ("x1", dc)], [("x1", dc)])
        if g0 + n > TP:
            a0 = max(g0, TP)
            v3 = lambda a: a.rearrange("p (b t) -> p b t", t=8)
            nb0 = (a0 - TP) // 8
            nb = (g0 + n - a0) // 8
            gb = modT[:, gate_row + dc, 1 + nb0:1 + nb0 + nb].unsqueeze(2).broadcast_to([128, nb, 8])
            V(tt(v3(ps[:, a0 - g0:n]), v3(ps[:, a0 - g0:n]), gb, ALU.mult), [pk, *MODK], [pk])
            V(tt(x1[:, dc, a0:g0 + n], ps[:, a0 - g0:n], x1[:, dc, a0:g0 + n], ALU.add), [pk, ("x1", dc)], [("x1", dc)])

    for blk in range(8):
        wt, wk = load_w(DR["w_out"][:, blk * 256:(blk + 1) * 256], 16, 256)
        for jj in range(2):
            dc = blk * 2 + jj
            P.dma("sp", x1[:, dc, :], DR["xT_own"][dc * 128:(dc + 1) * 128, :], writes=[("x1", dc)])
            for g0 in range(0, T, 384):
                dense_fm(wt, wk, jj * 128, 16, lambda kc: merged[:, kc, :], lambda kc: [("merged", kc)], g0, 384,
                         lambda ps, pk, dc=dc, g0=g0: resid_evac(ps, pk, dc, g0, 384, 32))
    P.barrier()
    ckpt("F", [("x1", x1, [("x1", k) for k in range(16)])])
    es_mg.close()

    with ExitStack() as es:
        al = lambda name, shape, dt=F32: es.enter_context(_sbt(nc, name, list(shape), dt)).ap()
        h2T = al("h2T", [128, 16, T], BF16)
        sq = [al(f"sq{i}", [128, 512], F32R) for i in range(2)]
        rstd = al("rstd", [128, 512]); tmpn = rstd
        tq = [al(f"tq{i}", [128, 512]) for i in range(2)]
        for g0 in range(0, T, 512):
            n = min(512, T - g0)
            rms_rstd((sq, rstd, tmpn), lambda dc: x1[:, dc, g0:g0 + n], n, lambda dc: [("x1", dc)])
            for dc in range(16):
                s = dc % 2
                if g0 >= TP:
                    b3 = lambda a: a.unsqueeze(2).broadcast_to([128, 16, 8])
                    v3 = lambda a: a.rearrange("p (b t) -> p b t", t=8)
                    V(tt(v3(tq[s][:, 0:n]), v3(x1[:, dc, g0:g0 + n]), b3(gf[:, dc, 1:17]), ALU.mult), [("x1", dc), "gf"], [f"tq{s}"])
                    V(tt(tq[s][:, 0:n], tq[s][:, 0:n], rstd[:, 0:n], ALU.mult), [f"tq{s}", "rstd"], [f"tq{s}"])
                    V(tt(v3(h2T[:, dc, g0:g0 + n]), v3(tq[s][:, 0:n]), b3(modT[:, 48 + dc, 1:17]), ALU.add), [f"tq{s}", *MODK], [("h2T", dc)])
                else:
                    V(stt(tq[s][:, 0:n], x1[:, dc, g0:g0 + n], gf[:, dc, 0:1], rstd[:, 0:n], ALU.mult, ALU.mult), [("x1", dc), "gf", "rstd"], [f"tq{s}"])
                    A(act(h2T[:, dc, g0:g0 + n], tq[s][:, 0:n], AF.Identity, bias=modT[:, 48 + dc, 0:1]), [f"tq{s}", *MODK], [("h2T", dc)])
        actT = al("actT", [128, 4, T], BF16)
        sl = [al(f"sl{i}", [128, 384]) for i in range(1)]
        wfo = [al(f"wfo{i}", [128, 4, 256], BF16) for i in range(2)]
        cnt = 0
        for qd in range(11):
            for jj in range(4):
                j = qd * 4 + jj
                wt, wk = wslot()
                P.dma("pool", wt[:, :, 0:128], DR["w_ffn_in"][:, j * 128:(j + 1) * 128].rearrange("(kc p) n -> p kc n", p=128), writes=[wk])
                P.dma("pool", wt[:, :, 128:256], DR["w_ffn_in"][:, DFF + j * 128:DFF + (j + 1) * 128].rearrange("(kc p) n -> p kc n", p=128), writes=[wk])
                for g0 in range(0, T, 384):
                    s = 0
                    dense_fm(wt, wk, 0, 16, lambda kc: h2T[:, kc, :], lambda kc: [("h2T", kc)], g0, 384,
                             lambda ps, pk: A(act(sl[s], ps, AF.Silu), [pk], [f"sl{s}"]))
                    dense_fm(wt, wk, 128, 16, lambda kc: h2T[:, kc, :], lambda kc: [("h2T", kc)], g0, 384,
                             lambda ps, pk, g0=g0, jj=jj: V(tt(actT[:, jj, g0:g0 + 384], sl[s], ps, ALU.mult), [f"sl{s}", pk], [("actT", jj)]))
            for blk in range(8):
                wo, wok = wfo[blk % 2], f"wfo{blk % 2}"
                P.dma("pool", wo, DR["w_ffn_out"][qd * 512:(qd + 1) * 512, blk * 256:(blk + 1) * 256].rearrange("(kc p) n -> p kc n", p=128), writes=[wok])
                for jj2 in range(2):
                    dc = blk * 2 + jj2
                    for g0 in range(0, T, 384):
                        dense_fm(wo, wok, jj2 * 128, 4, lambda kc: actT[:, kc, :], lambda kc: [("actT", kc)], g0, 384,
                                 lambda ps, pk, dc=dc, g0=g0: resid_evac(ps, pk, dc, g0, 384, 80))
        yo = tq
        for g0 in range(0, T, 512):
            n = min(512, T - g0)
            rms_rstd((sq, rstd, tmpn), lambda dc: x1[:, dc, g0:g0 + n], n, lambda dc: [("x1", dc)])
            for dc in range(16):
                s = dc % 2
                V(stt(yo[s][:, 0:n], x1[:, dc, g0:g0 + n], gvec[:, 32 + dc:33 + dc], rstd[:, 0:n], ALU.mult, ALU.mult), [("x1", dc), "gvec", "rstd"], [f"tq{s}"])
                P.dma("sp", DR["yT"][dc * 128:(dc + 1) * 128, g0:g0 + n], yo[s][:, 0:n], reads=[f"tq{s}"], semres=f"tq{s}")
        P.finish("sp")
    es_x.close()


def _consts():
    i = np.arange(128)
    r, c = i[:, None], i[None, :]
    same = (r // 8) == (c // 8)
    f = lambda m: m.astype(np.float32)
    parts = [np.eye(128, dtype=np.float32), f(r < c), f(r > c), f(r <= c),
             f((r < c) & same), f((r > c) & same), f((r <= c) & same),
             f(np.broadcast_to((c % 8) != 0, (128, 128))), f((r // 64) == (c // 64)), np.ones((128, 128), np.float32),
             f((r // 8) == np.arange(16)[None, :])]
    return np.ascontiguousarray(np.concatenate(parts, axis=1))


def _constsb():
    import ml_dtypes
    i = np.arange(128)
    r, c = i[:, None], i[None, :]
    parts = []
    for b in (8, 16, 32, 64):
        parts.append(((r // (2 * b)) == (c // (2 * b))) & ((r // b) != (c // b)) & (r > c))
    parts = parts + [p.T for p in parts]
    parts.append((r // 64) == (c // 64))
    return np.ascontiguousarray(np.concatenate(parts, axis=1).astype(np.float32).astype(ml_dtypes.bfloat16))


def _col(v, n):
    return np.ascontiguousarray(np.asarray(v, np.float32).reshape(n, 128).T)


_NC_CACHE = {}


def _prep(x_prompt, x_sample, state_wkv, state_shift, c_prompt, c_sample,
           w_ada, b_ada, norm_mix_g, w_in, mu_shift, w0, w_decay_up, a0, w_aaa_up,
           w_gate_up, k_k, k_a, r_k, gn_g, gn_b, ln_v_g, ln_v_b, w_spatial, b_spatial,
           w_branch_a, w_branch_b, w_out, norm_ffn_g, w_ffn_in, w_ffn_out, norm_final_g):
    f = lambda a: np.ascontiguousarray(np.asarray(a, np.float32))
    x_prompt, x_sample = f(x_prompt), f(x_sample)
    state_wkv, state_shift = f(state_wkv)[0], f(state_shift)[0]
    c_prompt, c_sample = f(c_prompt), f(c_sample)
    shared = {
        "w_ada": f(w_ada)[0], "badaT": _col(f(b_ada)[0], 96),
        "gvec": np.concatenate([_col(f(norm_mix_g)[0], 16), _col(f(norm_ffn_g)[0], 16), _col(f(norm_final_g), 16)], 1),
        "w_in": f(w_in)[0],
        "lora_up": np.ascontiguousarray(np.concatenate([f(w_decay_up)[0], f(w_aaa_up)[0]], 0)),
        "w_gate_up": f(w_gate_up)[0], "ln_g": f(ln_v_g)[0], "ln_b": f(ln_v_b)[0],
        "w_branch_a": f(w_branch_a)[0], "w_branch_b": f(w_branch_b)[0], "w_out": f(w_out)[0],
        "w_ffn_in": f(w_ffn_in)[0], "w_ffn_out": f(w_ffn_out)[0], "cst": _consts(), "cstb": _constsb(),
    }
    mu = f(mu_shift)[0]
    ka = f(k_a)[0]
    vecs = [f(w0)[0], f(a0)[0], f(k_k)[0], ka, ka, f(gn_g)[0], f(gn_b)[0], f(r_k)[0].reshape(-1), ka]
    sv = [_col(mu, 26)] + [_col(v, 8) for v in vecs]
    shared["svec"] = np.ascontiguousarray(np.concatenate(sv, 1))
    wsp = f(w_spatial)[0]
    shared["w_spT"] = np.ascontiguousarray(wsp.transpose(0, 2, 1))
    blkT = np.zeros((8, 128, 128), np.float32)
    for b in range(16):
        blkT[:, b * 8:(b + 1) * 8, b * 8:(b + 1) * 8] = wsp[:, :8, :8].transpose(0, 2, 1)
    shared["w_spTs"] = blkT
    bsp = f(b_spatial)[0]
    shared["bsp"] = bsp
    shared["bsps"] = np.ascontiguousarray(np.tile(bsp[:, :8], (1, 16)))
    in_maps = []
    for c in range(8):
        b, half = c // 2, c % 2
        xs = x_sample[16 * c:16 * (c + 1)].reshape(128, D)
        xo = np.concatenate([x_prompt[b, half * 1024:(half + 1) * 1024], xs], 0)
        xp = x_prompt[b, 0:1024]
        cc = np.concatenate([c_prompt[b:b + 1], c_sample[16 * c:16 * (c + 1)]], 0)
        sw = state_wkv[16 * c:16 * (c + 1)]
        s0T = sw.reshape(16, 8, 2, 64, 64).transpose(1, 2, 4, 0, 3).reshape(8, 128, 16 * 64)
        ssh = state_shift[16 * c:16 * (c + 1)]
        sshT = ssh.reshape(16, 26, 128).transpose(2, 1, 0).reshape(128, 26 * 16)
        m = dict(shared)
        m.update({"xT_own": np.ascontiguousarray(xo.T), "xT_prev": np.ascontiguousarray(xp.T),
                  "cT": np.ascontiguousarray(cc.T), "flag": np.full((128, 1), float(half), np.float32),
                  "s0T": np.ascontiguousarray(s0T), "sshT": np.ascontiguousarray(sshT)})
        in_maps.append(m)
    return in_maps


def kernel(x_prompt, x_sample, state_wkv, state_shift, c_prompt, c_sample,
           w_ada, b_ada, norm_mix_g, w_in, mu_shift, w0, w_decay_up, a0, w_aaa_up,
           w_gate_up, k_k, k_a, r_k, gn_g, gn_b, ln_v_g, ln_v_b, w_spatial, b_spatial,
           w_branch_a, w_branch_b, w_out, norm_ffn_g, w_ffn_in, w_ffn_out, norm_final_g):
    in_maps = _prep(x_prompt, x_sample, state_wkv, state_shift, c_prompt, c_sample,
                    w_ada, b_ada, norm_mix_g, w_in, mu_shift, w0, w_decay_up, a0, w_aaa_up,
                    w_gate_up, k_k, k_a, r_k, gn_g, gn_b, ln_v_g, ln_v_b, w_spatial, b_spatial,
                    w_branch_a, w_branch_b, w_out, norm_ffn_g, w_ffn_in, w_ffn_out, norm_final_g)
    if "nc" not in _NC_CACHE:
        _NC_CACHE["nc"] = build_nc()
    res = run_bass_kernel_spmd(_NC_CACHE["nc"], in_maps, core_ids=list(range(8)))
    R = res.results
    y_prompt = np.zeros((4, 2048, D), np.float32); y_sample = np.zeros((128, 8, D), np.float32)
    wkv_p = np.zeros((1, 4, 16, 64, 64), np.float32); shift_p = np.zeros((1, 4, 3328), np.float32)
    wkv_s = np.zeros((1, 128, 16, 64, 64), np.float32); shift_s = np.zeros((1, 128, 3328), np.float32)
    v_s = np.zeros((1, 128, 8, 1024), np.float32)
    for c in range(8):
        b, half = c // 2, c % 2
        yT = R[c]["yT"]
        y_prompt[b, half * 1024:(half + 1) * 1024] = yT[:, :1024].T
        y_sample[16 * c:16 * (c + 1)] = yT[:, 1024:].T.reshape(16, 8, D)
        if half == 1:
            hp = R[c]["wkv_p"]
            for p in range(8):
                for h2 in range(2):
                    wkv_p[0, b, 2 * p + h2] = hp[p, h2 * 64:(h2 + 1) * 64, h2 * 64:(h2 + 1) * 64].T
            shift_p[0, b] = R[c]["shp"].T.reshape(-1)
        ws = R[c]["wkv_s"].reshape(8, 2, 64, 16, 64)
        wkv_s[0, 16 * c:16 * (c + 1)] = ws.transpose(3, 0, 1, 4, 2).reshape(16, 16, 64, 64)
        shift_s[0, 16 * c:16 * (c + 1)] = R[c]["shs"].reshape(128, 26, 16).transpose(2, 1, 0).reshape(16, 3328)
        v_s[0, 16 * c:16 * (c + 1)] = R[c]["v_s"].reshape(16, 8, 1024)
    return (y_prompt, y_sample, wkv_p, shift_p, wkv_s, shift_s, v_s)
```

```python
import numpy as np
import concourse.bass as bass
import concourse.mybir as mybir
from concourse.bass_utils import run_bass_kernel_spmd
from contextlib import ExitStack

F32 = mybir.dt.float32
F32R = mybir.dt.float32r
BF16 = mybir.dt.bfloat16
AF = mybir.ActivationFunctionType
ALU = mybir.AluOpType
AX = mybir.AxisListType

EPOCH = 8192
D = 2048
T = 1152
TP = 1024
DFF = 5632
CIN = 9472
NCST = 10 * 128 + 16


class Prog:
    def __init__(self, nc):
        self.nc = nc
        self.eng = {"pe": nc.tensor, "act": nc.scalar, "dve": nc.vector,
                    "pool": nc.gpsimd, "sp": nc.sync}
        self.cnt = {e: 0 for e in self.eng}
        self.sems = {e: [] for e in self.eng}
        self.seen = {e: {} for e in self.eng}
        self.last_w = {}
        self.readers = {}
        self.dma_sems = {}
        self.pend_r = {e: [] for e in self.eng}
        self.pend_w = {e: [] for e in self.eng}
        self.nsem = 0
        self.ninstr = {e: 0 for e in self.eng}
        self.rec = None
        self.m_eng = {e: 0.0 for e in self.eng}
        self.m_key = {}

    def record(self, gen):
        self.rec = []
        for _ in gen:
            pass
        r, self.rec = self.rec, None
        return r

    def schedule(self, streams):
        from collections import Counter
        DUR = {"pe": 0.2, "act": 0.25, "dve": 0.22, "pool": 0.5, "sp": 0.05}
        LAT = 0.3
        isps = lambda k: isinstance(k, str) and k.startswith("psb")
        units = []
        for st in streams:
            us, cur = [], []
            for o in st:
                cur.append(o)
                if o[5]:
                    us.append(cur)
                    cur = []
            if cur:
                us.append(cur)
            units.append(us)
        pend_r = [Counter(k for u in us for o in u for k in o[3] if not isps(k)) for us in units]
        pend_w = [Counter(k for u in us for o in u for k in o[4] if not isps(k)) for us in units]
        idx = [0] * len(units)
        while True:
            best = None
            for j, us in enumerate(units):
                if idx[j] >= len(us):
                    continue
                u = us[idx[j]]
                keys = [k for o in u for k in (o[3] + o[4])]
                rk_ = [k for o in u for k in o[3] if not isps(k)]
                wk_ = [k for o in u for k in o[4] if not isps(k)]
                if any(pend_w[i][k] > 0 for k in rk_ for i in range(j)) or \
                   any(pend_w[i][k] > 0 or pend_r[i][k] > 0 for k in wk_ for i in range(j)):
                    continue
                F = u[0][1]
                t_ready = max([self.m_key.get(k, 0.0) for k in keys] + [0.0])
                start = max(self.m_eng[F], t_ready)
                if best is None or start < best[0]:
                    best = (start, j, u, keys, F)
            if best is None:
                break
            start, j, u, keys, F = best
            t = start
            for o in u:
                if o[0] == "op":
                    self.op(o[1], o[2], o[3], o[4], o[5])
                    t += DUR[o[1]]
                else:
                    out, in_, semres, kw = o[2]
                    self.dma(o[1], out, in_, o[3], o[4], semres, **kw)
                    t += DUR["sp"]
            self.m_eng[F] = t
            fin = t + LAT + (2.0 if u[0][0] == "dma" else 0.0)
            for o in u:
                for k in o[4] + [k2 for k2 in o[3] if isps(k2)]:
                    self.m_key[k] = fin
            for o in u:
                for k in o[3]:
                    if not isps(k):
                        pend_r[j][k] -= 1
                for k in o[4]:
                    if not isps(k):
                        pend_w[j][k] -= 1
            idx[j] += 1

    def _newsem(self, name):
        self.nsem += 1
        return self.nc.alloc_semaphore(name)

    def _deps(self, F, reads, writes):
        deps = {}

        def add(tok, same_ok):
            if tok is None:
                return
            key, sem, val, eng = tok
            if eng == F and F == "pe":
                return
            if val > deps.get(key, (None, 0))[1]:
                deps[key] = (sem, val)

        for r in reads:
            add(self.last_w.get(r), True)
        for w in writes:
            add(self.last_w.get(w), True)
            for t in self.readers.get(w, ()):
                add(t, False)
        return deps

    def _emit_waits(self, F, deps):
        e = self.eng[F]
        for key, (sem, val) in deps.items():
            if self.seen[F].get(key, 0) >= val:
                continue
            e.wait_ge(sem, val)
            self.seen[F][key] = val

    def _register(self, tok, reads, writes):
        for r in reads:
            lst = self.readers.setdefault(r, [])
            lst[:] = [t for t in lst if t[0] != tok[0]]
            lst.append(tok)
        for w in writes:
            self.last_w[w] = tok
            self.readers[w] = []

    def op(self, F, fn, reads=(), writes=(), signal=True):
        reads = list(reads)
        writes = list(writes)
        if self.rec is not None:
            self.rec.append(("op", F, fn, reads, writes, signal))
            return None
        ex = [r for r in reads if isinstance(r, str) and r.startswith("psb")]
        if ex:
            reads = [r for r in reads if r not in ex]
            writes = writes + [r for r in ex if r not in writes]
        deps = self._deps(F, reads, writes)
        self._emit_waits(F, deps)
        ins = fn()
        self.ninstr[F] += 1
        if not signal:
            self.pend_r[F] += reads
            self.pend_w[F] += writes
            return ins
        i = self.cnt[F]
        self.cnt[F] += 1
        ep = i // EPOCH
        while len(self.sems[F]) <= ep:
            self.sems[F].append(self._newsem(f"s_{F}_{len(self.sems[F])}"))
        sem = self.sems[F][ep]
        val = i % EPOCH + 1
        ins.then_inc(sem, 1)
        tok = ((F, ep), sem, val, F)
        self._register(tok, reads + self.pend_r[F], writes + self.pend_w[F])
        self.pend_r[F] = []
        self.pend_w[F] = []
        return ins

    def dma(self, Q, out, in_, reads=(), writes=(), semres=None, **kw):
        reads = list(reads)
        writes = list(writes)
        if self.rec is not None:
            self.rec.append(("dma", Q, (out, in_, semres, kw), reads, writes, True))
            return None
        if semres is None:
            semres = (writes + reads)[0]
        deps = self._deps("dma", reads, writes)
        self._emit_waits(Q, deps)
        if semres not in self.dma_sems:
            self.dma_sems[semres] = [self._newsem(f"d_{len(self.dma_sems)}"), 0]
        ent = self.dma_sems[semres]
        ent[1] += 16
        ins = self.eng[Q].dma_start(out=out, in_=in_, **kw)
        ins.then_inc(ent[0], 16)
        self.ninstr[Q] += 1
        tok = (("dma", semres), ent[0], ent[1], "dma")
        self._register(tok, reads, writes)
        return ins

    def barrier(self):
        for F, e in self.eng.items():
            for semres, (sem, val) in self.dma_sems.items():
                key = ("dma", semres)
                if val > self.seen[F].get(key, 0):
                    e.wait_ge(sem, val)
                    self.seen[F][key] = val
            for E in self.eng:
                if E == F or self.cnt[E] == 0:
                    continue
                i = self.cnt[E] - 1
                key = (E, i // EPOCH)
                val = i % EPOCH + 1
                if val > self.seen[F].get(key, 0):
                    e.wait_ge(self.sems[E][i // EPOCH], val)
                    self.seen[F][key] = val

    def finish(self, F="sp"):
        e = self.eng[F]
        for semres, (sem, val) in self.dma_sems.items():
            if val > 0:
                e.wait_ge(sem, val)
        for E in self.eng:
            if self.cnt[E] > 0:
                i = self.cnt[E] - 1
                e.wait_ge(self.sems[E][i // EPOCH], i % EPOCH + 1)


_UC = [0]


def _uname(name):
    _UC[0] += 1
    return f"t{_UC[0]}_{name}"


class _Arena:
    def __init__(self):
        self.ap = None
        self.free = []

    def init(self, nc, name="arena", dt=F32, n=None):
        if n is None:
            nbytes = int(nc.sbuf_bytes_remaining) - 1024
            n = nbytes // 4
        self.ap = nc.alloc_sbuf_tensor(name, [128, n], dt).ap()
        self.free = [(0, n)]
        self.n = n

    def alloc(self, words):
        words = (words + 15) // 16 * 16
        for i, (st, sz) in enumerate(self.free):
            if sz >= words:
                if sz == words:
                    self.free.pop(i)
                else:
                    self.free[i] = (st + words, sz - words)
                return st, words
        raise MemoryError(f"arena full: need {words} words, free={self.free}")

    def release(self, st, words):
        self.free.append((st, words))
        self.free.sort()
        out = []
        for a, b in self.free:
            if out and out[-1][0] + out[-1][1] == a:
                out[-1] = (out[-1][0], out[-1][1] + b)
            else:
                out.append((a, b))
        self.free = out


_AR = _Arena()
_ARR = _Arena()


class _Tile:
    def __init__(self, shape, dt):
        self.shape = list(shape)
        self.dt = dt

    def __enter__(self):
        esz = 2 if self.dt == BF16 else 4
        per = 1
        for d in self.shape[1:]:
            per *= d
        words = (per * esz + 3) // 4
        self.ar = _ARR if self.dt == F32R else _AR
        self.st, self.words = self.ar.alloc(words)
        v = self.ar.ap[:, self.st:self.st + words]
        if self.dt == BF16:
            v = v.bitcast(self.dt)
        v = v[:, 0:per]
        if len(self.shape) == 3:
            v = v.rearrange("p (a b) -> p a b", b=self.shape[2])
        elif len(self.shape) == 4:
            v = v.rearrange("p (a b c) -> p a b c", b=self.shape[2], c=self.shape[3])
        if self.shape[0] != 128:
            v = v[0:self.shape[0]]
        self._ap = v
        return self

    def ap(self):
        return self._ap

    def __exit__(self, *a):
        self.ar.release(self.st, self.words)
        return False


def _sbt(nc, name, shape, dt):
    return _Tile(shape, dt)


def r32(ap):
    return ap.bitcast(F32)


class _Stop(Exception):
    pass


def build_nc(stop=None):
    nc = bass.Bass("TRN2", target_bir_lowering=False)
    P = Prog(nc)
    DR = {}
    try:
        _build(nc, P, DR, stop)
    except _Stop:
        pass
    print("ninstr", P.ninstr, "nsem", P.nsem, flush=True)
    return nc


def _build(nc, P, DR, stop):
    def ckpt(tag, dumps):
        if stop != tag:
            return
        for name, ap, keys in dumps:
            d = nc.dram_tensor("dbg_" + name, list(ap.shape), ap.dtype, kind="ExternalOutput").ap()
            P.dma("sp", d, ap, reads=keys, semres=("dbg", name))
        P.finish("sp")
        raise _Stop()

    cpa = lambda o, i: (lambda: nc.scalar.copy(o, i))
    cpv = lambda o, i: (lambda: nc.vector.tensor_copy(o, i))
    rcp = lambda o, i: (lambda: nc.vector.reciprocal(o, i))
    scn = lambda o, d0, d1, init, o0, o1: (lambda: nc.vector.tensor_tensor_scan(o, d0, d1, init, o0, o1))

    def din(name, shape):
        DR[name] = nc.dram_tensor(name, list(shape), F32, kind="ExternalInput").ap()

    def dout(name, shape):
        DR[name] = nc.dram_tensor(name, list(shape), F32, kind="ExternalOutput").ap()

    din("xT_own", [D, T]); din("xT_prev", [D, TP]); din("cT", [D, 17]); din("flag", [128, 1])
    din("s0T", [8, 128, 16 * 64]); din("sshT", [128, 26 * 16])
    din("w_ada", [D, 6 * D]); din("badaT", [128, 96])
    din("gvec", [128, 48])
    din("w_in", [D, CIN]); din("svec", [128, 26 + 9 * 8])
    din("lora_up", [128, 1024]); din("w_gate_up", [128, 1024])
    din("ln_g", [1024]); din("ln_b", [1024])
    din("w_spT", [8, 128, 128]); din("w_spTs", [8, 128, 128]); din("bsp", [8, 128]); din("bsps", [8, 128])
    din("w_branch_a", [1024, D]); din("w_branch_b", [1024, D]); din("w_out", [D, D])
    din("w_ffn_in", [D, 2 * DFF]); din("w_ffn_out", [DFF, D]); din("cst", [128, NCST])
    DR["cstb"] = nc.dram_tensor("cstb", [128, 1152], BF16, kind="ExternalInput").ap()
    dout("yT", [D, T]); dout("wkv_p", [8, 128, 128]); dout("shp", [128, 26])
    dout("wkv_s", [8, 128, 16 * 64]); dout("shs", [128, 26 * 16]); dout("v_s", [128, 1024])

    def sbp(name, shape, dt=F32):
        return nc.alloc_sbuf_tensor(_uname(name), list(shape), dt).ap()

    cst = sbp("cst", [128, NCST])
    P.dma("sp", cst, DR["cst"], writes=["cst"])
    ident = cst[:, 0:128]; m_sl = cst[:, 128:256]; m_gt = cst[:, 256:384]; m_le = cst[:, 384:512]
    ms_sl = cst[:, 512:640]; ms_gt = cst[:, 640:768]; ms_le = cst[:, 768:896]
    mreset = cst[:, 896:1024]; blk1 = cst[:, 1024:1152]; ones = cst[:, 1152:1280]
    maskTB = cst[:, 1280:1296]
    cstb = sbp("cstb", [128, 1152], BF16)
    blk1b = cstb[:, 1024:1152]
    P.dma("sp", cstb, DR["cstb"], writes=["cstb"])
    moff = [cstb[:, l * 128:(l + 1) * 128] for l in range(4)]
    moffT = [cstb[:, 512 + l * 128:512 + (l + 1) * 128] for l in range(4)]
    onesR = sbp("onesR", [128, 128], F32R)
    P.op("dve", cpv(onesR, ones), ["cst"], ["onesR"])
    epsc = sbp("epsc", [128, 8])
    P.op("dve", lambda: nc.vector.memset(epsc[:, 0:1], 1e-6), [], ["epsc"])
    P.op("dve", lambda: nc.vector.memset(epsc[:, 1:2], 64e-5), [], ["epsc"])
    P.op("dve", lambda: nc.vector.memset(epsc[:, 2:3], 1e-5), [], ["epsc"])
    P.op("dve", lambda: nc.vector.memset(epsc[:, 3:4], 1.0), [], ["epsc"])
    P.op("dve", lambda: nc.vector.memset(epsc[:, 4:5], -0.5), [], ["epsc"])
    flag = sbp("flag", [128, 1]); P.dma("sp", flag, DR["flag"], writes=["flag"])
    gvec = sbp("gvec", [128, 48]); P.dma("sp", gvec, DR["gvec"], writes=["gvec"])
    svec = sbp("svec", [128, 114]); P.dma("sp", svec[:, 0:98], DR["svec"], writes=["svec"])
    P.op("dve", lambda: nc.vector.tensor_scalar(svec[:, 98:114], svec[:, 26:42], -1.0, None, ALU.mult, ALU.bypass), ["svec"], ["svec"])
    P.op("dve", lambda: nc.vector.tensor_scalar(svec[:, 58:66], svec[:, 50:58], -1.0, 1.0, ALU.mult, ALU.add), ["svec"], ["svec"])
    zeros = sbp("zeros", [128, 128])
    P.op("dve", lambda: nc.vector.memset(zeros, 0.0), [], ["zeros"])
    muT = svec[:, 0:26]
    sv = lambda i, hp: svec[:, 26 + 8 * i + hp: 26 + 8 * i + hp + 1]
    badaT = sbp("badaT", [128, 96]); P.dma("sp", badaT, DR["badaT"], writes=["badaT"])
    modT = sbp("modT", [128, 96, 17])
    MODK = [("modT", k) for k in range(6)]
    gm = sbp("gm", [128, 16, 17]); gf = sbp("gf", [128, 16, 17])
    WB = [sbp(f"WB{i}", [128, 16, 256], BF16) for i in range(2)]
    wbc = [0]

    def wslot():
        i = wbc[0] % 2
        wbc[0] += 1
        return WB[i], f"WB{i}"

    psb = [nc.alloc_psum_tensor(f"psb{i}", [128, 512], F32).ap() for i in range(8)]
    bc = [0]; qc = [0]

    BPOOL = [[0, 1, 2, 3]]
    QPOOL = [[2, 3, 4, 5, 6, 7]]

    def bank():
        pool_ = BPOOL[0]
        i = pool_[bc[0] % len(pool_)]
        bc[0] += 1
        return psb[i], f"psb{i}"

    def quart():
        pool_ = QPOOL[0]
        i = pool_[qc[0] % len(pool_)]
        qc[0] += 1
        return psb[i][:, 0:128], f"psb{i}"

    V = lambda fn, r, w: P.op("dve", fn, r, w)
    A = lambda fn, r, w: P.op("act", fn, r, w)

    def MM(out, lhsT, rhs, start, stop, r, w, signal=True):
        return P.op("pe", lambda: nc.tensor.matmul(out, lhsT=lhsT, rhs=rhs, start=start, stop=stop), r, w, signal)

    def TR(out, in_, r, w):
        return P.op("pe", lambda: nc.tensor.transpose(out, in_, ident), list(r) + ["cst"], w)

    tt = lambda o, a, b, op: (lambda: nc.vector.tensor_tensor(o, a, b, op))
    ts = lambda o, a, s1, s2, o0, o1: (lambda: nc.vector.tensor_scalar(o, a, s1, s2, o0, o1))
    stt = lambda o, a, s, b, o0, o1: (lambda: nc.vector.scalar_tensor_tensor(o, a, s, b, o0, o1))
    act = lambda o, i, f, **kw: (lambda: nc.scalar.activation(o, i, f, **kw))

    def load_w(src_ap, nk, ncols, dst=None, key=None):
        if dst is None:
            dst, key = wslot()
        P.dma("pool", dst[:, 0:nk, 0:ncols], src_ap.rearrange("(kc p) n -> p kc n", p=128), writes=[key])
        return dst, key

    _ARR.init(nc, "arenaR", F32R, 10624)
    _AR.init(nc)
    scb = sbp("scb", [128, 16, 17], BF16)
    with ExitStack() as es:
        cTt = es.enter_context(_sbt(nc, "cTt", [128, 16, 17], F32)).ap()
        P.dma("sp", cTt, DR["cT"].rearrange("(kc p) n -> p kc n", p=128), writes=["cTt"])
        A(act(scb, cTt, AF.Silu), ["cTt"], ["scb"])
        P.barrier()

    def ada_block(blk):
        wa_, wak = load_w(DR["w_ada"][:, blk * 256:(blk + 1) * 256], 16, 256)
        for jj in range(2):
            j = blk * 2 + jj
            pb, pk = bank()
            for kc in range(16):
                MM(pb[:, 0:17], wa_[:, kc, jj * 128:(jj + 1) * 128], scb[:, kc, :], kc == 0, kc == 15,
                   [wak, "scb"], [pk], signal=(kc == 15))
            A(act(modT[:, j, :], pb[:, 0:17], AF.Identity, bias=badaT[:, j:j + 1]), [pk, "badaT"], [("modT", j // 16)])

    for blk in range(16):
        ada_block(blk)
    for dc in range(16):
        V(ts(gm[:, dc, :], modT[:, 16 + dc, :], 1.0, gvec[:, dc:dc + 1], ALU.add, ALU.mult), [("modT", 1), "gvec"], ["gm"])
    ckpt("A", [("modT", modT, MODK), ("gm", gm, ["gm"])])

    def rms_rstd(es_tiles, src_fn, n, srckeys):
        sq, rstd, tmpn = es_tiles
        pb, pk = bank()
        for dc in range(16):
            s = dc % 2
            A(act(sq[s][:, 0:n], src_fn(dc), AF.Square), srckeys(dc), [f"sq{s}"])
            MM(pb[:, 0:n], onesR, sq[s][:, 0:n], dc == 0, dc == 15, [f"sq{s}", "onesR"], [pk], signal=True)
        A(act(tmpn[:, 0:n], pb[:, 0:n], AF.Sqrt, bias=epsc[:, 0:1], scale=1.0 / D), [pk, "epsc"], ["rstd"])
        V(rcp(rstd[:, 0:n], tmpn[:, 0:n]), ["rstd"], ["rstd"])
        return rstd

    def build_hT(es, hT, xsrc, ncols, g_t, shift_base, with_sample):
        xg = es.enter_context(_sbt(nc, "xg", [128, 16, 512], F32)).ap()
        sq = [es.enter_context(_sbt(nc, f"sq{i}", [128, 512], F32R)).ap() for i in range(2)]
        rstd = es.enter_context(_sbt(nc, "rstd", [128, 512], F32)).ap()
        tmpn = es.enter_context(_sbt(nc, "tmpn", [128, 512], F32)).ap()
        tq = [es.enter_context(_sbt(nc, f"tq{i}", [128, 512], F32)).ap() for i in range(2)]
        for g0 in range(0, ncols, 512):
            n = min(512, ncols - g0)
            for dc in range(16):
                P.dma("sp", xg[:, dc, 0:n], xsrc[dc * 128:(dc + 1) * 128, g0:g0 + n], writes=[("xg", dc)])
            rms_rstd((sq, rstd, tmpn), lambda dc: xg[:, dc, 0:n], n, lambda dc: [("xg", dc)])
            for dc in range(16):
                s = dc % 2
                if with_sample and g0 >= TP:
                    b3 = lambda a: a.unsqueeze(2).broadcast_to([128, 16, 8])
                    v3 = lambda a: a.rearrange("p (b t) -> p b t", t=8)
                    V(tt(v3(tq[s][:, 0:n]), v3(xg[:, dc, 0:n]), b3(g_t[:, dc, 1:17]), ALU.mult), [("xg", dc), "gm", "gf"], [f"tq{s}"])
                    V(tt(tq[s][:, 0:n], tq[s][:, 0:n], rstd[:, 0:n], ALU.mult), [f"tq{s}", "rstd"], [f"tq{s}"])
                    V(tt(v3(hT[:, dc, g0:g0 + n]), v3(tq[s][:, 0:n]), b3(modT[:, shift_base + dc, 1:17]), ALU.add),
                      [f"tq{s}", *MODK], [("hT", dc)])
                else:
                    V(stt(tq[s][:, 0:n], xg[:, dc, 0:n], g_t[:, dc, 0:1], rstd[:, 0:n], ALU.mult, ALU.mult),
                      [("xg", dc), "gm", "gf", "rstd"], [f"tq{s}"])
                    A(act(hT[:, dc, g0:g0 + n], tq[s][:, 0:n], AF.Identity, bias=modT[:, shift_base + dc, 0:1]),
                      [f"tq{s}", *MODK], [("hT", dc)])

    def dense_fm(wt, wkey, wcol0, nk, act_fn, act_keys, c0, n, consumer):
        pb, pk = bank()
        for kc in range(nk):
            MM(pb[:, 0:n], wt[:, kc, wcol0:wcol0 + 128], act_fn(kc)[:, c0:c0 + n], kc == 0, kc == nk - 1,
               [wkey] + act_keys(kc), [pk], signal=(kc == nk - 1))
        consumer(pb[:, 0:n], pk)

    es_mix = ExitStack()
    es_mp = ExitStack()
    mpa = lambda name, shape, dt=F32: es_mp.enter_context(_sbt(nc, name, list(shape), dt)).ap()
    lora_d = mpa("lora_d", [128, 1024], BF16); lora_a = mpa("lora_a", [128, 1024], BF16)
    P.op("dve", lambda: nc.vector.memset(lora_d[64:128, :], 0.0), [], ["lora_up"])
    P.op("dve", lambda: nc.vector.memset(lora_a[0:64, :], 0.0), [], ["lora_up"])
    P.dma("pool", lora_d[0:64, :], DR["lora_up"][0:64, :], writes=["lora_up"])
    P.dma("pool", lora_a[64:128, :], DR["lora_up"][64:128, :], writes=["lora_up"])
    wgu = mpa("wgu", [128, 1024], BF16); P.dma("pool", wgu, DR["w_gate_up"], writes=["wgu"])
    sshT = mpa("sshT", [128, 26, 16]); P.dma("sp", sshT, DR["sshT"].rearrange("p (j b) -> p j b", b=16), writes=["sshT"])
    Hs = mpa("Hs", [128, 8, 128], F32R)
    hlast = mpa("hlast", [128, 16, 2], BF16)
    shp = mpa("shp", [128, 26]); shs = mpa("shs", [128, 26, 16])
    hT = es_mix.enter_context(_sbt(nc, "hT", [128, 16, T], BF16)).ap()
    hT_fn = lambda kc: hT[:, kc, :]
    hT_keys = lambda kc: [("hT", kc)]
    oaT = es_mix.enter_context(_sbt(nc, "oaT", [128, 8, T], BF16)).ap()

    def mixer_pass(own):
        ncols = T if own else TP
        ntiles = 9 if own else 8
        with ExitStack() as es:
            build_hT(es, hT, DR["xT_own"] if own else DR["xT_prev"], ncols, gm, 0, own)
        if not own:
            V(cpv(hlast, hT[:, :, TP - 2:TP]), [("hT", k) for k in range(16)], ["hlast"])
        P.barrier()
        ckpt("B1" if not own else "B2", [("hT", hT, [("hT", k) for k in range(16)])])
        with ExitStack() as es:
            al = lambda name, shape, dt=F32: es.enter_context(_sbt(nc, name, list(shape), dt)).ap()
            lor = al("lor", [128, T], BF16); sg = al("sg", [128, T], BF16)
            zraw = [al(f"zraw{i}", [128, 1 + T]) for i in range(1)]
            zrs = [al(f"zrs{i}", [128, 16, 8]) for i in range(1)]
            big = al("big", [128, 16, 64])
            dtl = [big[:, 0:8, :].rearrange("p b v -> p (b v)")]
            zs = [[al(f"zs{s}_{w}", [128, T]) for w in range(3)] for s in range(1)]
            wrkv = [al(f"wrkv{i}", [128, 16, 3, 128], BF16) for i in range(1)]
            zcnt = [0]

            def shift_evac(j, dst, dstkey, wt, wkey, wcol0):
                zi = 0
                zcnt[0] += 1
                zr, zk = zraw[zi], f"zraw{zi}"
                mu = muT[:, j:j + 1]
                if own:
                    pb, pk = bank()
                    for kc in range(16):
                        MM(pb[:, 0:2], wt[:, kc, wcol0:wcol0 + 128], hlast[:, kc, :], kc == 0, kc == 15, [wkey, "hlast"], [pk], signal=(kc == 15))
                    A(act(zr[:, 0:1], pb[:, 1:2], AF.Identity, scale=flag[:, 0:1]), [pk, "flag"], [zk])
                else:
                    V(lambda: nc.vector.memset(zr[:, 0:1], 0.0), [], [zk])
                for g0 in range(0, TP, 512):
                    def cons(ps, pk, g0=g0):
                        A(cpa(zr[:, 1 + g0:1 + g0 + 512], ps), [pk], [zk])
                        d = dtl[0]; dk = "big"
                        V(tt(d, zr[:, g0:g0 + 512], ps, ALU.subtract), [zk, pk], [dk])
                        V(stt(dst[:, g0:g0 + 512], d, mu, ps, ALU.mult, ALU.add), [dk, pk, "svec"], [dstkey])
                    dense_fm(wt, wkey, wcol0, 16, hT_fn, hT_keys, g0, 512, cons)
                if own:
                    V(cpv(shp[:, j:j + 1], zr[:, TP:TP + 1]), [zk], ["shp"])

                    def cons_s(ps, pk):
                        z3 = zrs[zi]; z3k = f"zrs{zi}"
                        p3 = ps.rearrange("p (b t) -> p b t", t=8)
                        A(cpa(z3[:, :, 1:8], p3[:, :, 0:7]), [pk], [z3k])
                        V(cpv(z3[:, :, 0:1], sshT[:, j, :].unsqueeze(2)), ["sshT"], [z3k])
                        V(cpv(shs[:, j, :].unsqueeze(2), p3[:, :, 7:8]), [pk], ["shs"])
                        d = dtl[0][:, 0:128]
                        V(tt(d, z3.rearrange("p b t -> p (b t)"), ps, ALU.subtract), [z3k, pk], ["big"])
                        V(stt(dst[:, TP:T], d, mu, ps, ALU.mult, ALU.add), ["big", pk, "svec"], [dstkey])
                    dense_fm(wt, wkey, wcol0, 16, hT_fn, hT_keys, TP, 128, cons_s)

            wl, wlk = load_w(DR["w_in"][:, 3072:3328], 16, 256)
            zl = zs[0][0]
            shift_evac(24, zl, "zs0_0", wl, wlk, 0)
            A(act(lor[0:64, 0:ncols], zl[0:64, 0:ncols], AF.Tanh), ["zs0_0"], ["lor"])
            V(cpv(lor[64:128, 0:ncols], zl[64:128, 0:ncols]), ["zs0_0"], ["lor"])
            if own:
                shift_evac(25, zl, "zs0_0", wl, wlk, 128)
                A(act(sg, zl, AF.Sigmoid), ["zs0_0"], ["sg"])

            ckpt("L1" if not own else "L2", [("lor", lor, ["lor"]), ("sg", sg, ["sg"])])
            NS = 1
            DBN_ = {"AbTA", "AbTB", "BtTA", "BtTB", "KtTA", "KtTB", "RbT", "Bt_tm", "Kt_tm", "VeA", "VeB", "YA0", "YB0", "G", "bon"}
            pt = {}
            for nm in ["oT", "o2", "kk2b", "rkb"]:
                pt[nm] = [al(f"p_{nm}0", [128, 128], BF16)]
            for nm in ["lw", "cum", "aa", "kk", "kkn", "kp", "G", "Gi", "Gm1", "bon",
                       "mean", "msq", "varp", "cen", "otm"]:
                pt[nm] = [al(f"p_{nm}{s}", [128, 128]) for s in range(2 if nm in DBN_ else 1)]
            for nm in ["AbT", "BtT", "KtT", "RbT", "Bt_tm", "Kt_tm", "VeA", "VeB", "UeA", "UeB", "WT",
                       "AbTA", "AbTB", "BtTA", "BtTB", "KtTA", "KtTB", "WTA", "WTB", "RbTA", "RbTB",
                       "sX0", "sX1", "sXT0", "sXT1", "YA0", "YA1", "LakA", "MrbA", "MrkA",
                       "YB0", "YB1", "LakB", "MrbB", "MrkB", "DTfA", "DTfB"]:
                pt[nm] = [al(f"p_{nm}{s}", [128, 128], F32R) for s in range(2 if nm in DBN_ else 1)]
            for hh_ in ("A", "B"):
                for nm in ["X0", "X1", "XT0", "XT1", "D0", "D1", "DT0", "DT1", "Pm", "Qm", "Lo0", "Lo1", "Lo2", "Lo3", "LoT0", "LoT1", "LoT2", "Lb", "LTb"]:
                    pt[nm + hh_] = [al(f"p_{nm}{hh_}{s}", [128, 128], BF16) for s in range(NS)]
            for nm in ["VeA", "VeB", "UeA", "UeB", "AbTA", "AbTB", "BtTA", "BtTB", "KtTA", "KtTB", "WTA", "WTB", "RbTA", "RbTB"]:
                for s in range(len(pt[nm])):
                    V((lambda a: (cpv(a, zeros)))(pt[nm][s]), ["zeros"], [(nm, s)])
            if own:
                h0r = al("h0r", [128, 16, 64], F32R)
                hsn = al("hsn", [128, 16, 64])
                u1 = al("u1", [128, 64]); o0 = al("o0", [128, 64])
                Ublk = al("Ublk", [128, 16, 64], F32R); Vblk = al("Vblk", [128, 16, 64], F32R)
            setc = [0]

            for hp in range(8):
                if not own:
                    for blk_ in range(16 + hp * 4, 16 + hp * 4 + 4):
                        ada_block(blk_)
                    if hp == 7:
                        for dc_ in range(16):
                            V(ts(gf[:, dc_, :], modT[:, 64 + dc_, :], 1.0, gvec[:, 16 + dc_:17 + dc_], ALU.add, ALU.mult), [("modT", 4), "gvec"], ["gf"])
                wi = 0
                wr, wrk = wrkv[wi], f"wrkv{wi}"
                which = [0, 1, 2] if own else [1, 2]
                for w in which:
                    c = w * 1024 + hp * 128
                    P.dma("pool", wr[:, :, w, :], DR["w_in"][:, c:c + 128].rearrange("(kc p) n -> p kc n", p=128), writes=[wrk])
                Z = zs[0]
                for w in which:
                    shift_evac(w * 8 + hp, Z[w], f"zs0_{w}", wr[:, :, w, :], wrk, 0)
                zr_, zk_, zv_ = Z
                kr, kk_, kv = [f"zs0_{w}" for w in range(3)]
                if own:
                    A(act(Hs[:, hp, :], r32(Hs[:, hp, :]), AF.Identity, scale=flag[:, 0:1]), [("Hs", hp), "flag"], [("Hs", hp)])
                else:
                    V((lambda a: (cpv(a, zeros)))(Hs[:, hp, :]), ["zeros"], [("Hs", hp)])
                ch = slice(hp * 128, (hp + 1) * 128)
                if hp == 0 and not own:
                    ckpt("T0", [("zs1", zs[0][1], ["zs0_1"]), ("zs2", zs[0][2], ["zs0_2"]), ("Hs", r32(Hs), [("Hs", 0)])])
                DBN = {"AbTA", "AbTB", "BtTA", "BtTB", "KtTA", "KtTB", "RbT", "Bt_tm", "Kt_tm", "VeA", "VeB", "YA0", "YB0", "G", "bon"}

                def env(ti):
                    samp = own and ti == 8
                    par = ti % 2
                    K = lambda nm: (nm, par if nm in DBN else 0)
                    X = lambda nm: pt[nm][par if nm in DBN else 0]
                    cs = slice(ti * 128, (ti + 1) * 128)
                    Msl, Mgt, Mle = (ms_sl, ms_gt, ms_le) if samp else (m_sl, m_gt, m_le)
                    return samp, K, X, cs, Msl, Mgt, Mle

                def prep_gen(ti):
                    samp, K, X, cs, Msl, Mgt, Mle = env(ti)
                    q, qk = quart()
                    MM(q, lora_d[:, ch], lor[:, cs], True, True, ["lora_up", "lor"], [qk])
                    A(act(X("lw"), q, AF.Exp, bias=svec[:, 98 + hp:99 + hp], scale=-1.0), [qk, "svec"], [K("lw")])
                    A(act(X("lw"), X("lw"), AF.Ln, bias=epsc[:, 3:4]), [K("lw"), "epsc"], [K("lw")])
                    A(act(X("lw"), X("lw"), AF.Exp, bias=epsc[:, 4:5], scale=-1.0), [K("lw"), "epsc"], [K("lw")])
                    yield
                    V(scn(X("cum"), mreset if samp else ones, X("lw"), 0.0, ALU.mult, ALU.add),
                      [K("lw"), "cst"], [K("cum")])
                    q, qk = quart()
                    MM(q, lora_a[:, ch], lor[:, cs], True, True, ["lora_up", "lor"], [qk])
                    A(act(X("aa"), q, AF.Exp, bias=svec[:, 106 + hp:107 + hp], scale=-1.0), [qk, "svec"], [K("aa")])
                    A(act(X("aa"), X("aa"), AF.Ln, bias=epsc[:, 3:4]), [K("aa"), "epsc"], [K("aa")])
                    A(act(X("aa"), X("aa"), AF.Exp, scale=-1.0), [K("aa")], [K("aa")])
                    yield
                    V(ts(X("kk"), zk_[:, cs], sv(2, hp), None, ALU.mult, ALU.bypass), [kk_, "svec"], [K("kk")])
                    A(act(X("kk2b"), X("kk"), AF.Square), [K("kk")], [K("kk2b")])
                    q, qk = quart()
                    MM(q, blk1b, X("kk2b"), True, True, ["cstb", K("kk2b")], [qk])
                    V(ts(X("Gm1"), q, 1e-19, None, ALU.max, ALU.bypass), [qk], [K("Gm1")])
                    A(act(X("Gm1"), X("Gm1"), AF.Ln), [K("Gm1")], [K("Gm1")])
                    A(act(X("Gm1"), X("Gm1"), AF.Exp, scale=-0.5), [K("Gm1")], [K("Gm1")])
                    V(tt(X("kkn"), X("kk"), X("Gm1"), ALU.mult), [K("kk"), K("Gm1")], [K("kkn")])
                    yield
                    V(ts(X("kp"), X("aa"), sv(3, hp), sv(4, hp), ALU.mult, ALU.add), [K("aa"), "svec"], [K("kp")])
                    V(tt(X("kp"), zk_[:, cs], X("kp"), ALU.mult), [kk_, K("kp")], [K("kp")])
                    A(act(X("G"), X("cum"), AF.Exp, scale=-1.0), [K("cum")], [K("G")])
                    A(act(X("Gi"), X("cum"), AF.Exp), [K("cum")], [K("Gi")])
                    V(tt(X("cum"), X("cum"), X("lw"), ALU.subtract), [K("cum"), K("lw")], [K("cum")])
                    A(act(X("Gm1"), X("cum"), AF.Exp, scale=-1.0), [K("cum")], [K("Gm1")])
                    yield
                    V(stt(X("AbT"), X("kkn"), -1.0, X("Gm1"), ALU.mult, ALU.mult), [K("kkn"), K("Gm1")], [K("AbT")])
                    V(tt(X("kk"), X("kkn"), X("aa"), ALU.mult), [K("kkn"), K("aa")], [K("kk")])
                    V(tt(X("BtT"), X("kk"), X("Gi"), ALU.mult), [K("kk"), K("Gi")], [K("BtT")])
                    V(tt(X("KtT"), X("kp"), X("Gi"), ALU.mult), [K("kp"), K("Gi")], [K("KtT")])
                    for nm in ("AbT", "BtT", "KtT"):
                        A((lambda nm=nm: (cpa(X(nm + "A")[0:64, :], r32(X(nm))[0:64, :])))(), [K(nm)], [K(nm + "A")])
                        V((lambda nm=nm: (cpv(X(nm + "B")[64:128, :], r32(X(nm))[64:128, :])))(), [K(nm)], [K(nm + "B")])
                    if own:
                        V(tt(X("RbT"), zr_[:, cs], X("G"), ALU.mult), [kr, K("G")], [K("RbT")])
                        if samp:
                            A(cpa(X("RbTA")[0:64, :], r32(X("RbT"))[0:64, :]), [K("RbT")], [K("RbTA")])
                            V(cpv(X("RbTB")[64:128, :], r32(X("RbT"))[64:128, :]), [K("RbT")], [K("RbTB")])
                        V(stt(X("rkb"), zr_[:, cs], sv(7, hp), X("kp"), ALU.mult, ALU.mult), [kr, "svec", K("kp")], [K("rkb")])
                        q, qk = quart()
                        MM(q, blk1b, X("rkb"), True, True, ["cstb", K("rkb")], [qk])
                        V(tt(X("bon"), q, zv_[:, cs], ALU.mult), [qk, kv], [K("bon")])
                    yield
                    q, qk = quart()
                    TR(q, r32(X("AbT")), [K("AbT")], [qk])
                    A(cpa(X("YA0")[:, 0:64], q[:, 0:64]), [qk], [K("YA0")])
                    A(cpa(X("YB0")[:, 64:128], q[:, 64:128]), [qk], [K("YB0")])
                    q, qk = quart()
                    TR(q, r32(X("BtT")), [K("BtT")], [qk])
                    A(cpa(X("Bt_tm"), q), [qk], [K("Bt_tm")])
                    yield
                    q, qk = quart()
                    TR(q, r32(X("KtT")), [K("KtT")], [qk])
                    A(cpa(X("Kt_tm"), q), [qk], [K("Kt_tm")])
                    q, qk = quart()
                    TR(q, zv_[:, cs], [kv], [qk])
                    A(cpa(X("VeA")[:, 0:64], q[:, 0:64]), [qk], [K("VeA")])
                    A(cpa(X("VeB")[:, 64:128], q[:, 64:128]), [qk], [K("VeB")])

                def scan_gens(ti):
                    samp, K, X, cs, Msl, Mgt, Mle = env(ti)
                    Yfin = YF.setdefault(ti, {})

                    def head_gen(hh, p0):
                        ps_ = slice(p0, p0 + 64)
                        dhalf = slice(0, 64) if hh == "A" else slice(64, 128)
                        ohalf = slice(64, 128) if hh == "A" else slice(0, 64)
                        Ve = X("Ve" + hh); Vek = K("Ve" + hh)
                        HX = lambda nm: X(nm + hh)
                        HK = lambda nm: K(nm + hh)
                        q1, qk1 = quart()
                        MM(q1, X("BtT" + hh), X("AbT" + hh), True, True, [K("BtT" + hh), K("AbT" + hh)], [qk1])
                        q2, qk2 = quart()
                        MM(q2, X("AbT" + hh), X("BtT" + hh), True, True, [K("BtT" + hh), K("AbT" + hh)], [qk2])

                        def side_lak():
                            q, qk = quart()
                            MM(q, X("KtT" + hh), X("AbT" + hh), True, True, [K("KtT" + hh), K("AbT" + hh)], [qk])
                            V(tt(X("Lak" + hh), q, Msl, ALU.mult), [qk, "cst"], [K("Lak" + hh)])

                        def side_z():
                            q, qk = quart()
                            MM(q[:, 0:64], X("Lak" + hh), Ve[:, dhalf], True, True, [K("Lak" + hh), Vek], [qk])
                            A(cpa(X("Y" + hh + "0")[:, ohalf], q[:, 0:64]), [qk], [K("Y" + hh + "0")])

                        def side_m():
                            if own:
                                q, qk = quart()
                                MM(q, X("BtT" + hh), X("RbT"), True, True, [K("BtT" + hh), K("RbT")], [qk])
                                V(tt(X("Mrb" + hh), q, Mle, ALU.mult), [qk, "cst"], [K("Mrb" + hh)])
                                q, qk = quart()
                                MM(q, X("KtT" + hh), X("RbT"), True, True, [K("KtT" + hh), K("RbT")], [qk])
                                V(tt(X("Mrk" + hh), q, Mle, ALU.mult), [qk, "cst"], [K("Mrk" + hh)])

                        if samp:
                            V(tt(X("sXT0"), q1, ms_sl, ALU.mult), [qk1, "cst"], [K("sXT0")])
                            V(tt(X("sX0"), q2, ms_gt, ALU.mult), [qk2, "cst"], [K("sX0")])
                            yield
                            side_lak()
                            yield
                            side_z()
                            yield
                            side_m()
                            yield
                            nsteps = 3
                            for i in range(nsteps):
                                a, b = i % 2, (i + 1) % 2
                                Xa, XTa, Ya = X(f"sX{a}"), X(f"sXT{a}"), X(f"Y{hh}{a}")
                                Xb, XTb, Yb = X(f"sX{b}"), X(f"sXT{b}"), X(f"Y{hh}{b}")
                                q, qk = quart()
                                MM(q, XTa, Ya, True, True, [K(f"sXT{a}"), K(f"Y{hh}{a}")], [qk])
                                V(tt(Yb, q, r32(Ya), ALU.add), [qk, K(f"Y{hh}{a}")], [K(f"Y{hh}{b}")])
                                if i < nsteps - 1:
                                    q, qk = quart()
                                    MM(q, XTa, Xa, True, True, [K(f"sXT{a}"), K(f"sX{a}")], [qk])
                                    qq, qqk = quart()
                                    MM(qq, Xa, XTa, True, True, [K(f"sXT{a}"), K(f"sX{a}")], [qqk])
                                    A(cpa(Xb, q), [qk], [K(f"sX{b}")])
                                    A(cpa(XTb, qq), [qqk], [K(f"sXT{b}")])
                            Yf, Yfk = X(f"Y{hh}1"), K(f"Y{hh}1")
                            Yfin[hh] = (Yf, Yfk)
                            q, qk = quart()
                            TR(q, r32(Yf), [Yfk], [qk])
                            A(cpa(X("WT")[ps_, :], q[ps_, :]), [qk], [K("WT")])
                            V(cpv(X("WT" + hh)[ps_, :], q[ps_, :]), [qk], [K("WT" + hh)])
                            return
                        A(cpa(HX("LTb"), q1), [qk1], [HK("LTb")])
                        A(cpa(HX("Lb"), q2), [qk2], [HK("Lb")])
                        V(tt(HX("XT0"), HX("LTb"), ms_sl, ALU.mult), [HK("LTb"), "cst"], [HK("XT0")])
                        V(tt(HX("X0"), HX("Lb"), ms_gt, ALU.mult), [HK("Lb"), "cst"], [HK("X0")])
                        V(tt(HX("D0"), HX("X0"), ident, ALU.add), [HK("X0"), "cst"], [HK("D0")])
                        V(tt(HX("DT0"), HX("XT0"), ident, ALU.add), [HK("XT0"), "cst"], [HK("DT0")])

                        def mask_l(l):
                            if l < 3:
                                V(tt(HX(f"LoT{l}"), HX("LTb"), moffT[l], ALU.mult), [HK("LTb"), "cstb"], [HK(f"LoT{l}")])
                            V(tt(HX(f"Lo{l}"), HX("Lb"), moff[l], ALU.mult), [HK("Lb"), "cstb"], [HK(f"Lo{l}")])

                        side = [side_lak, lambda: mask_l(0), side_z, lambda: mask_l(1), side_m, lambda: mask_l(2), lambda: mask_l(3)]
                        cur = 0
                        for i in range(2):
                            a, b = i % 2, (i + 1) % 2
                            q, qk = quart()
                            MM(q, HX(f"XT{a}"), HX(f"X{a}"), True, True, [HK(f"XT{a}"), HK(f"X{a}")], [qk])
                            qq, qqk = quart()
                            MM(qq, HX(f"X{a}"), HX(f"XT{a}"), True, True, [HK(f"XT{a}"), HK(f"X{a}")], [qqk])
                            A(cpa(HX(f"X{b}"), q), [qk], [HK(f"X{b}")])
                            A(cpa(HX(f"XT{b}"), qq), [qqk], [HK(f"XT{b}")])
                            if side:
                                side.pop(0)()
                            yield
                            q, qk = quart()
                            MM(q, HX(f"XT{b}"), HX(f"D{cur}"), True, True, [HK(f"XT{b}"), HK(f"D{cur}")], [qk])
                            qq, qqk = quart()
                            MM(qq, HX(f"D{cur}"), HX(f"XT{b}"), True, True, [HK(f"XT{b}"), HK(f"D{cur}")], [qqk])
                            V(tt(HX(f"D{1 - cur}"), q, HX(f"D{cur}"), ALU.add), [qk, HK(f"D{cur}")], [HK(f"D{1 - cur}")])
                            V(tt(HX(f"DT{1 - cur}"), qq, HX(f"DT{cur}"), ALU.add), [qqk, HK(f"DT{cur}")], [HK(f"DT{1 - cur}")])
                            cur = 1 - cur
                            if side:
                                side.pop(0)()
                            yield
                        for l in range(4):
                            last = (l == 3)
                            while side and l + 2 > 7 - len(side):
                                side.pop(0)()
                            if not last:
                                q, qk = quart()
                                MM(q, HX(f"LoT{l}"), HX(f"D{cur}"), True, True, [HK(f"LoT{l}"), HK(f"D{cur}")], [qk])
                                A(cpa(HX("Pm"), q), [qk], [HK("Pm")])
                            qq, qqk = quart()
                            MM(qq, HX(f"Lo{l}"), HX(f"DT{cur}"), True, True, [HK(f"Lo{l}"), HK(f"DT{cur}")], [qqk])
                            A(cpa(HX("Qm"), qq), [qqk], [HK("Qm")])
                            if side:
                                side.pop(0)()
                            yield
                            if not last:
                                q, qk = quart()
                                MM(q, HX(f"DT{cur}"), HX("Pm"), True, True, [HK(f"DT{cur}"), HK("Pm")], [qk])
                                V(tt(HX(f"D{1 - cur}"), q, HX(f"D{cur}"), ALU.add), [qk, HK(f"D{cur}")], [HK(f"D{1 - cur}")])
                            qq, qqk = quart()
                            MM(qq, HX(f"D{cur}"), HX("Qm"), True, True, [HK(f"D{cur}"), HK("Qm")], [qqk])
                            if last:
                                V(tt(HX("DTf"), qq, HX(f"DT{cur}"), ALU.add), [qqk, HK(f"DT{cur}")], [HK("DTf")])
                            else:
                                V(tt(HX(f"DT{1 - cur}"), qq, HX(f"DT{cur}"), ALU.add), [qqk, HK(f"DT{cur}")], [HK(f"DT{1 - cur}")])
                            cur = 1 - cur
                            yield
                        while side:
                            side.pop(0)()
                        DT, DTk = HX("DTf"), HK("DTf")
                        Y0, Y0k = X(f"Y{hh}0"), K(f"Y{hh}0")
                        q, qk = quart()
                        MM(q, Y0, DT, True, True, [Y0k, DTk], [qk])
                        A(cpa(X("WT")[ps_, :], q[ps_, :]), [qk], [K("WT")])
                        q, qk = quart()
                        MM(q[:, 0:64], DT, Y0[:, ohalf], True, True, [Y0k, DTk], [qk])
                        A(cpa(X(f"Y{hh}1")[:, ohalf], q[:, 0:64]), [qk], [K(f"Y{hh}1")])
                        Yfin[hh] = (X(f"Y{hh}1"), K(f"Y{hh}1"))

                    if samp:
                        def seq():
                            for g_ in (head_gen("A", 0), head_gen("B", 64)):
                                for _ in g_:
                                    yield
                        return [seq()]
                    return [head_gen("A", 0), head_gen("B", 64)]

                def chainpost_gen(ti):
                    samp, K, X, cs, Msl, Mgt, Mle = env(ti)
                    Yfin = YF[ti]
                    hk = ("Hs", hp)
                    Hh = Hs[:, hp, :]
                    if not samp:
                        q, qk = quart()
                        MM(q, X("WT"), Hh, True, True, [K("WT"), hk], [qk])
                        V(tt(X("UeA")[:, 0:64], q[:, 0:64], r32(Yfin["A"][0][:, 64:128]), ALU.add), [qk, Yfin["A"][1]], [K("UeA")])
                        V(tt(X("UeB")[:, 64:128], q[:, 64:128], r32(Yfin["B"][0][:, 0:64]), ALU.add), [qk, Yfin["B"][1]], [K("UeB")])
                        if own:
                            q, qk = quart()
                            MM(q, Hh, X("RbT"), True, False, [hk, K("RbT")], [qk], signal=False)
                            MM(q, X("UeA"), X("MrbA"), False, False, [K("UeA"), K("MrbA")], [qk], signal=False)
                            MM(q, X("UeB"), X("MrbB"), False, False, [K("UeB"), K("MrbB")], [qk], signal=False)
                            MM(q, X("VeA"), X("MrkA"), False, False, [K("VeA"), K("MrkA")], [qk], signal=False)
                            MM(q, X("VeB"), X("MrkB"), False, True, [K("VeB"), K("MrkB")], [qk])
                            A(cpa(X("oT"), q), [qk], [K("oT")])
                        q, qk = quart()
                        MM(q[:, 0:64], X("Bt_tm"), X("UeA")[:, 0:64], True, False, [K("Bt_tm"), K("UeA")], [qk], signal=False)
                        MM(q[:, 0:64], X("Kt_tm"), X("VeA")[:, 0:64], False, True, [K("Kt_tm"), K("VeA")], [qk], signal=False)
                        MM(q[:, 64:128], X("Bt_tm"), X("UeB")[:, 64:128], True, False, [K("Bt_tm"), K("UeB")], [qk], signal=False)
                        MM(q[:, 64:128], X("Kt_tm"), X("VeB")[:, 64:128], False, True, [K("Kt_tm"), K("VeB")], [qk])
                        GC = X("G")[:, 127:128]
                        for ps_, hf in ((slice(0, 64), slice(0, 64)), (slice(64, 128), slice(64, 128))):
                            A(act(X("msq")[ps_, 0:64], r32(Hh[ps_, hf]), AF.Identity, scale=GC[ps_, :]), [hk, K("G")], [K("msq")])
                            V(stt(Hh[ps_, hf], q[ps_, hf], GC[ps_, :], X("msq")[ps_, 0:64], ALU.mult, ALU.add), [qk, K("G"), K("msq")], [hk])
                        if own and ti == 7:
                            P.dma("sp", DR["wkv_p"][hp], r32(Hh), reads=[hk], semres=("Hs", hp))
                    else:
                        P.dma("sp", big, DR["s0T"][hp].rearrange("p (b v) -> p b v", v=64), writes=["big"])
                        V(cpv(h0r, big), ["big"], ["h0r"])
                        b3m = maskTB.unsqueeze(2).broadcast_to([128, 16, 64])
                        G3 = X("G").rearrange("p (b t) -> p b t", t=8)[:, :, 7:8]
                        for hh, p0 in (("A", 0), ("B", 64)):
                            ps_ = slice(p0, p0 + 64)
                            dhalf = slice(0, 64) if hh == "A" else slice(64, 128)
                            ohalf = slice(64, 128) if hh == "A" else slice(0, 64)
                            Ue, Uek = X("Ue" + hh), K("Ue" + hh)
                            Ve, Vek = X("Ve" + hh), K("Ve" + hh)
                            for src, srck, dstv, dstk in ((X("WT" + hh), K("WT" + hh), u1, "u1"), (X("RbT" + hh), K("RbT" + hh), o0, "o0")):
                                for nb in range(2):
                                    pb, pk = bank()
                                    MM(pb, src, h0r[:, nb * 8:(nb + 1) * 8, :].rearrange("p b v -> p (b v)"), True, True, [srck, "h0r"], [pk])
                                    V(tt(big[:, nb * 8:(nb + 1) * 8, :], pb.rearrange("p (b v) -> p b v", v=64), b3m[:, nb * 8:(nb + 1) * 8, :], ALU.mult),
                                      [pk, "cst"], ["big"])
                                V(lambda dstv=dstv: nc.vector.tensor_reduce(dstv, big.rearrange("p b v -> p v b"), AX.X, ALU.add), ["big"], [dstk])
                            V(tt(Ue[:, dhalf], u1, r32(Yfin[hh][0][:, ohalf]), ALU.add), ["u1", Yfin[hh][1]], [Uek])
                            q, qk = quart()
                            MM(q[:, 0:64], X("Mrb" + hh), Ue[:, dhalf], True, False, [K("Mrb" + hh), Uek], [qk], signal=False)
                            MM(q[:, 0:64], X("Mrk" + hh), Ve[:, dhalf], False, True, [K("Mrk" + hh), Vek], [qk])
                            V(tt(X("otm")[:, dhalf], q[:, 0:64], o0, ALU.add), [qk, "o0"], [K("otm")])
                            ub = r32(Ue[:, dhalf]).unsqueeze(1).broadcast_to([128, 16, 64])
                            vb = r32(Ve[:, dhalf]).unsqueeze(1).broadcast_to([128, 16, 64])
                            V(tt(Ublk, ub, b3m, ALU.mult), [Uek, "cst"], ["Ublk"])
                            V(tt(Vblk, vb, b3m, ALU.mult), [Vek, "cst"], ["Vblk"])
                            for nb in range(2):
                                pb, pk = bank()
                                bs = slice(nb * 8, (nb + 1) * 8)
                                MM(pb, X("Bt_tm"), Ublk[:, bs, :].rearrange("p b v -> p (b v)"), True, False, [K("Bt_tm"), "Ublk"], [pk], signal=False)
                                MM(pb, X("Kt_tm"), Vblk[:, bs, :].rearrange("p b v -> p (b v)"), False, True, [K("Kt_tm"), "Vblk"], [pk])
                                V(tt(hsn[ps_, bs, :], pb[ps_, :].rearrange("p (b v) -> p b v", v=64), r32(h0r)[ps_, bs, :], ALU.add), [pk, "h0r"], ["hsn"])
                                V(tt(hsn[ps_, bs, :], hsn[ps_, bs, :], G3[ps_, bs, :].broadcast_to([64, 8, 64]), ALU.mult), ["hsn", K("G")], ["hsn"])
                        P.dma("sp", DR["wkv_s"][hp].rearrange("p (b v) -> p b v", v=64), hsn, reads=["hsn"], semres="hsn")
                        q, qk = quart()
                        TR(q, X("otm"), [K("otm")], [qk])
                        A(cpa(X("oT"), q), [qk], [K("oT")])
                    yield
                    if own:
                        q, qk = quart()
                        MM(q, blk1b, X("oT"), True, True, ["cstb", K("oT")], [qk])
                        A(act(X("o2"), X("oT"), AF.Square), [K("oT")], [K("o2")])
                        q2, qk2 = quart()
                        MM(q2, blk1b, X("o2"), True, True, ["cstb", K("o2")], [qk2])
                        A(act(X("mean"), q, AF.Identity, scale=1.0 / 64), [qk], [K("mean")])
                        V(tt(X("msq"), X("mean"), X("mean"), ALU.mult), [K("mean")], [K("msq")])
                        V(stt(X("varp"), q2, 1.0 / 64, X("msq"), ALU.mult, ALU.subtract), [qk2, K("msq")], [K("varp")])
                        A(act(X("varp"), X("varp"), AF.Ln, bias=epsc[:, 1:2]), [K("varp"), "epsc"], [K("varp")])
                        A(act(X("varp"), X("varp"), AF.Exp, scale=-0.5), [K("varp")], [K("varp")])
                        yield
                        V(tt(X("cen"), X("oT"), X("mean"), ALU.subtract), [K("oT"), K("mean")], [K("cen")])
                        V(tt(X("cen"), X("cen"), X("varp"), ALU.mult), [K("cen"), K("varp")], [K("cen")])
                        V(ts(X("cen"), X("cen"), sv(5, hp), sv(6, hp), ALU.mult, ALU.add), [K("cen"), "svec"], [K("cen")])
                        V(tt(X("cen"), X("cen"), X("bon"), ALU.add), [K("cen"), K("bon")], [K("cen")])
                        q, qk = quart()
                        MM(q, wgu[:, ch], sg[:, cs], True, True, ["wgu", "sg"], [qk])
                        V(tt(oaT[:, hp, cs], X("cen"), q, ALU.mult), [K("cen"), qk], [("oaT", hp)])
                    if hp == 0 and ti == 0:
                        ckpt("T1" if not own else "T2", [(nm, r32(pt[nm][0]) if pt[nm][0].dtype == F32R else pt[nm][0], [(nm, 0)]) for nm in
                                   ["lw", "cum", "aa", "kkn", "kp", "G", "AbT", "BtT", "KtT", "Bt_tm", "VeA", "VeB", "WT", "UeA", "UeB", "LakA", "YA0", "YA1", "YB1", "oT", "cen"]]
                             + [("Hs", r32(Hs), [("Hs", k) for k in range(8)]), ("zs1", zs[0][1], ["zs0_1"]), ("zs2", zs[0][2], ["zs0_2"])])

                YF = {}

                def rec(gen, qpool, bpool=(0,)):
                    QPOOL[0] = list(qpool)
                    BPOOL[0] = list(bpool)
                    r_ = P.record(gen)
                    QPOOL[0] = [2, 3, 4, 5, 6, 7]
                    BPOOL[0] = [0, 1, 2, 3]
                    return r_

                P.schedule([rec(prep_gen(0), (0, 1))])
                prev = None
                for ti in range(ntiles):
                    streams = []
                    if prev is not None:
                        streams.append(rec(chainpost_gen(prev), (2, 3)))
                    hg = scan_gens(ti)
                    if len(hg) == 1:
                        streams.append(rec(hg[0], (4, 5, 6, 7)))
                    else:
                        streams.append(rec(hg[0], (4, 5)))
                        streams.append(rec(hg[1], (6, 7)))
                    if ti + 1 < ntiles:
                        streams.append(rec(prep_gen(ti + 1), (0, 1)))
                    P.schedule(streams)
                    prev = ti
                P.schedule([rec(chainpost_gen(prev), (2, 3))])
            P.barrier()
            ckpt("C1" if not own else "C2", [("Hs", r32(Hs), [("Hs", k) for k in range(8)]), ("oaT", oaT, [("oaT", k) for k in range(8)])])

    mixer_pass(False)
    mixer_pass(True)
    P.dma("sp", DR["shp"], shp, reads=["shp"], semres="shp")
    P.dma("sp", DR["shs"].rearrange("p (j b) -> p j b", b=16), shs, reads=["shs"], semres="shs")
    P.barrier()
    es_mp.close()

    prodT = es_mix.enter_context(_sbt(nc, "prodT", [128, 8, T], BF16)).ap()
    with ExitStack() as es:
        al = lambda name, shape, dt=F32: es.enter_context(_sbt(nc, name, list(shape), dt)).ap()
        wsm = al("wsm", [128, 16, 128], BF16)
        with ExitStack() as es2:
            wsf = es2.enter_context(_sbt(nc, "wsf", [128, 8, 128], F32)).ap()
            P.dma("sp", wsf, DR["w_spT"].rearrange("g s t -> s g t"), writes=["wsf"])
            for g in range(8):
                V(tt(wsm[:, g, :], wsf[:, g, :], m_le, ALU.mult), ["wsf", "cst"], ["wsm"])
            P.dma("sp", wsf, DR["w_spTs"].rearrange("g s t -> s g t"), writes=["wsf"])
            for g in range(8):
                V(tt(wsm[:, 8 + g, :], wsf[:, g, :], ms_le, ALU.mult), ["wsf", "cst"], ["wsm"])
            P.barrier()
        gv = al("gv", [128, 9, 1024], BF16)
        vnb = gv
        lngb = al("lngb", [128, 1024]); lnbb = al("lnbb", [128, 1024])
        P.dma("sp", lngb, DR["ln_g"].partition_broadcast(128), writes=["lngb"])
        P.dma("sp", lnbb, DR["ln_b"].partition_broadcast(128), writes=["lnbb"])
        bspb = al("bspb", [128, 8, 128]); bspsb = al("bspsb", [128, 8, 128])
        P.dma("sp", bspb, DR["bsp"].rearrange("g t -> (g t)").partition_broadcast(128).rearrange("p (g t) -> p g t", t=128), writes=["bspb"])
        P.dma("sp", bspsb, DR["bsps"].rearrange("g t -> (g t)").partition_broadcast(128).rearrange("p (g t) -> p g t", t=128), writes=["bspsb"])
        for blk in range(4):
            wt, wk = load_w(DR["w_in"][:, 4352 + blk * 256: 4352 + (blk + 1) * 256], 16, 256)
            for ti in range(9):
                pb, pk = bank()
                for kc in range(16):
                    MM(pb[:, 0:256], hT[:, kc, ti * 128:(ti + 1) * 128], wt[:, kc, 0:256], kc == 0, kc == 15, [wk, ("hT", kc)], [pk], signal=(kc == 15))
                A(act(gv[:, ti, blk * 256:(blk + 1) * 256], pb[:, 0:256], AF.Gelu_apprx_tanh), [pk], [("gv", ti)])
        st6 = al("st6", [128, 12]); mv = al("mv", [128, 2]); vtmp = [al(f"vtmp{i}", [128, 1024]) for i in range(1)]
        for ti in range(9):
            s = 0
            V(lambda: nc.vector.bn_stats(st6[:, 0:6], gv[:, ti, 0:512]), [("gv", ti)], ["st6"])
            V(lambda: nc.vector.bn_stats(st6[:, 6:12], gv[:, ti, 512:1024]), [("gv", ti)], ["st6"])
            V(lambda: nc.vector.bn_aggr(mv, st6), ["st6"], ["mv"])
            A(act(mv[:, 1:2], mv[:, 1:2], AF.Sqrt, bias=epsc[:, 2:3]), ["mv", "epsc"], ["mv"])
            V(rcp(mv[:, 1:2], mv[:, 1:2]), ["mv"], ["mv"])
            V(ts(vtmp[s], gv[:, ti, :], mv[:, 0:1], mv[:, 1:2], ALU.subtract, ALU.mult), [("gv", ti), "mv"], [f"vtmp{s}"])
            V(tt(vtmp[s], vtmp[s], lngb, ALU.mult), [f"vtmp{s}", "lngb"], [f"vtmp{s}"])
            V(tt(vtmp[s], vtmp[s], lnbb, ALU.add), [f"vtmp{s}", "lnbb"], [f"vtmp{s}"])
            A(cpa(vnb[:, ti, :], vtmp[s]), [f"vtmp{s}"], [("gv", ti)])
            if ti == 8:
                P.dma("sp", DR["v_s"], vtmp[s], reads=[f"vtmp{s}"], semres=f"vtmp{s}")
        uT = [al(f"uT{i}", [128, T]) for i in range(1)]
        mx = [al(f"mx{i}", [128, 128]) for i in range(2)]
        for blk in range(4):
            wt, wk = load_w(DR["w_in"][:, 3328 + blk * 256: 3328 + (blk + 1) * 256], 16, 256)
            for jj in range(2):
                g = blk * 2 + jj
                us, uk = uT[0], "uT0"
                for g0 in range(0, T, 384):
                    dense_fm(wt, wk, jj * 128, 16, hT_fn, hT_keys, g0, 384,
                             lambda ps, pk, g0=g0: A(act(us[:, g0:g0 + 384], ps, AF.Gelu_apprx_tanh), [pk], [uk]))
                for ti in range(9):
                    wsel = g if ti < 8 else 8 + g
                    bsel = bspb if ti < 8 else bspsb
                    q, qk = quart()
                    MM(q, vnb[:, ti, g * 128:(g + 1) * 128], wsm[:, wsel, :], True, True, [("gv", ti), "wsm"], [qk])
                    m_ = mx[ti % 2]; mk = f"mx{ti % 2}"
                    V(tt(m_, q, bsel[:, g, :], ALU.add), [qk, "bspb", "bspsb"], [mk])
                    V(tt(prodT[:, g, ti * 128:(ti + 1) * 128], m_, us[:, ti * 128:(ti + 1) * 128], ALU.mult), [mk, uk], [("prodT", g)])
        P.barrier()
        ckpt("D", [("prodT", prodT, [("prodT", k) for k in range(8)]), ("gv", gv, [("gv", k) for k in range(9)])])

    es_mg = ExitStack()
    merged = es_mg.enter_context(_sbt(nc, "merged", [128, 16, T], BF16)).ap()
    with ExitStack() as es:
        al = lambda name, shape, dt=F32: es.enter_context(_sbt(nc, name, list(shape), dt)).ap()
        wba = al("wba", [128, 8, 256], BF16)
        sa = [al(f"sa{i}", [128, 384]) for i in range(1)]
        sb_ = [al(f"sb{i}", [128, 384]) for i in range(1)]
        wbb = al("wbb", [128, 8, 256], BF16)
        cnt = 0
        for blk in range(8):
            wta, wka = wslot()
            wtb, wkb = wslot()
            P.dma("pool", wta[:, :, 0:256], DR["w_in"][:, 5376 + blk * 256:5376 + (blk + 1) * 256].rearrange("(kc p) n -> p kc n", p=128), writes=[wka])
            P.dma("pool", wtb[:, :, 0:256], DR["w_in"][:, 7424 + blk * 256:7424 + (blk + 1) * 256].rearrange("(kc p) n -> p kc n", p=128), writes=[wkb])
            P.dma("pool", wba, DR["w_branch_a"][:, blk * 256:(blk + 1) * 256].rearrange("(kc p) n -> p kc n", p=128), writes=["wba"])
            P.dma("pool", wbb, DR["w_branch_b"][:, blk * 256:(blk + 1) * 256].rearrange("(kc p) n -> p kc n", p=128), writes=["wbb"])
            for jj in range(2):
                dc = blk * 2 + jj
                for g0 in range(0, T, 384):
                    s = 0
                    cs = slice(g0, g0 + 384)
                    dense_fm(wta, wka, jj * 128, 16, hT_fn, hT_keys, g0, 384,
                             lambda ps, pk: A(act(sa[s], ps, AF.Sigmoid), [pk], [f"sa{s}"]))
                    dense_fm(wtb, wkb, jj * 128, 16, hT_fn, hT_keys, g0, 384,
                             lambda ps, pk: A(act(sb_[s], ps, AF.Sigmoid), [pk], [f"sb{s}"]))
                    dense_fm(wba, "wba", jj * 128, 8, lambda kc: oaT[:, kc, :], lambda kc: [("oaT", kc)], g0, 384,
                             lambda ps, pk: V(tt(sa[s], sa[s], ps, ALU.mult), [f"sa{s}", pk], [f"sa{s}"]))
                    dense_fm(wbb, "wbb", jj * 128, 8, lambda kc: prodT[:, kc, :], lambda kc: [("prodT", kc)], g0, 384,
                             lambda ps, pk: V(tt(sb_[s], sb_[s], ps, ALU.mult), [f"sb{s}", pk], [f"sb{s}"]))
                    V(tt(merged[:, dc, cs], sa[s], sb_[s], ALU.add), [f"sa{s}", f"sb{s}"], [("merged", dc)])
        P.barrier()
        ckpt("E", [("merged", merged, [("merged", k) for k in range(16)])])
    es_mix.close()
    es_x = ExitStack()
    x1 = es_x.enter_context(_sbt(nc, "x1", [128, 16, T], F32)).ap()

    def resid_evac(ps, pk, dc, g0, n, gate_row):
        npr = max(0, min(TP, g0 + n) - g0)
        if npr > 0:
            V(stt(x1[:, dc, g0:g0 + npr], ps[:, 0:npr], modT[:, gate_row + dc, 0:1], x1[:, dc, g0:g0 + npr], ALU.mult, ALU.add),
              [pk, *MODK, ("x1", dc)], [("x1", dc)])
        if g0 + n > TP:
            a0 = max(g0, TP)
            v3 = lambda a: a.rearrange("p (b t) -> p b t", t=8)
            nb0 = (a0 - TP) // 8
            nb = (g0 + n - a0) // 8
            gb = modT[:, gate_row + dc, 1 + nb0:1 + nb0 + nb].unsqueeze(2).broadcast_to([128, nb, 8])
            V(tt(v3(ps[:, a0 - g0:n]), v3(ps[:, a0 - g0:n]), gb, ALU.mult), [pk, *MODK], [pk])
            V(tt(x1[:, dc, a0:g0 + n], ps[:, a0 - g0:n], x1[:, dc, a0:g0 + n], ALU.add), [pk, ("x1", dc)], [("x1", dc)])

    for blk in range(8):
        wt, wk = load_w(DR["w_out"][:, blk * 256:(blk + 1) * 256], 16, 256)
        for jj in range(2):
            dc = blk * 2 + jj
            P.dma("sp", x1[:, dc, :], DR["xT_own"][dc * 128:(dc + 1) * 128, :], writes=[("x1", dc)])
            for g0 in range(0, T, 384):
                dense_fm(wt, wk, jj * 128, 16, lambda kc: merged[:, kc, :], lambda kc: [("merged", kc)], g0, 384,
                         lambda ps, pk, dc=dc, g0=g0: resid_evac(ps, pk, dc, g0, 384, 32))
    P.barrier()
    ckpt("F", [("x1", x1, [("x1", k) for k in range(16)])])
    es_mg.close()

    with ExitStack() as es:
        al = lambda name, shape, dt=F32: es.enter_context(_sbt(nc, name, list(shape), dt)).ap()
        h2T = al("h2T", [128, 16, T], BF16)
        sq = [al(f"sq{i}", [128, 512], F32R) for i in range(2)]
        rstd = al("rstd", [128, 512]); tmpn = rstd
        tq = [al(f"tq{i}", [128, 512]) for i in range(2)]
        for g0 in range(0, T, 512):
            n = min(512, T - g0)
            rms_rstd((sq, rstd, tmpn), lambda dc: x1[:, dc, g0:g0 + n], n, lambda dc: [("x1", dc)])
            for dc in range(16):
                s = dc % 2
                if g0 >= TP:
                    b3 = lambda a: a.unsqueeze(2).broadcast_to([128, 16, 8])
                    v3 = lambda a: a.rearrange("p (b t) -> p b t", t=8)
                    V(tt(v3(tq[s][:, 0:n]), v3(x1[:, dc, g0:g0 + n]), b3(gf[:, dc, 1:17]), ALU.mult), [("x1", dc), "gf"], [f"tq{s}"])
                    V(tt(tq[s][:, 0:n], tq[s][:, 0:n], rstd[:, 0:n], ALU.mult), [f"tq{s}", "rstd"], [f"tq{s}"])
                    V(tt(v3(h2T[:, dc, g0:g0 + n]), v3(tq[s][:, 0:n]), b3(modT[:, 48 + dc, 1:17]), ALU.add), [f"tq{s}", *MODK], [("h2T", dc)])
                else:
                    V(stt(tq[s][:, 0:n], x1[:, dc, g0:g0 + n], gf[:, dc, 0:1], rstd[:, 0:n], ALU.mult, ALU.mult), [("x1", dc), "gf", "rstd"], [f"tq{s}"])
                    A(act(h2T[:, dc, g0:g0 + n], tq[s][:, 0:n], AF.Identity, bias=modT[:, 48 + dc, 0:1]), [f"tq{s}", *MODK], [("h2T", dc)])
        actT = al("actT", [128, 4, T], BF16)
        sl = [al(f"sl{i}", [128, 384]) for i in range(1)]
        wfo = [al(f"wfo{i}", [128, 4, 256], BF16) for i in range(2)]
        cnt = 0
        for qd in range(11):
            for jj in range(4):
                j = qd * 4 + jj
                wt, wk = wslot()
                P.dma("pool", wt[:, :, 0:128], DR["w_ffn_in"][:, j * 128:(j + 1) * 128].rearrange("(kc p) n -> p kc n", p=128), writes=[wk])
                P.dma("pool", wt[:, :, 128:256], DR["w_ffn_in"][:, DFF + j * 128:DFF + (j + 1) * 128].rearrange("(kc p) n -> p kc n", p=128), writes=[wk])
                for g0 in range(0, T, 384):
                    s = 0
                    dense_fm(wt, wk, 0, 16, lambda kc: h2T[:, kc, :], lambda kc: [("h2T", kc)], g0, 384,
                             lambda ps, pk: A(act(sl[s], ps, AF.Silu), [pk], [f"sl{s}"]))
                    dense_fm(wt, wk, 128, 16, lambda kc: h2T[:, kc, :], lambda kc: [("h2T", kc)], g0, 384,
                             lambda ps, pk, g0=g0, jj=jj: V(tt(actT[:, jj, g0:g0 + 384], sl[s], ps, ALU.mult), [f"sl{s}", pk], [("actT", jj)]))
            for blk in range(8):
                wo, wok = wfo[blk % 2], f"wfo{blk % 2}"
                P.dma("pool", wo, DR["w_ffn_out"][qd * 512:(qd + 1) * 512, blk * 256:(blk + 1) * 256].rearrange("(kc p) n -> p kc n", p=128), writes=[wok])
                for jj2 in range(2):
                    dc = blk * 2 + jj2
                    for g0 in range(0, T, 384):
                        dense_fm(wo, wok, jj2 * 128, 4, lambda kc: actT[:, kc, :], lambda kc: [("actT", kc)], g0, 384,
                                 lambda ps, pk, dc=dc, g0=g0: resid_evac(ps, pk, dc, g0, 384, 80))
        yo = tq
        for g0 in range(0, T, 512):
            n = min(512, T - g0)
            rms_rstd((sq, rstd, tmpn), lambda dc: x1[:, dc, g0:g0 + n], n, lambda dc: [("x1", dc)])
            for dc in range(16):
                s = dc % 2
                V(stt(yo[s][:, 0:n], x1[:, dc, g0:g0 + n], gvec[:, 32 + dc:33 + dc], rstd[:, 0:n], ALU.mult, ALU.mult), [("x1", dc), "gvec", "rstd"], [f"tq{s}"])
                P.dma("sp", DR["yT"][dc * 128:(dc + 1) * 128, g0:g0 + n], yo[s][:, 0:n], reads=[f"tq{s}"], semres=f"tq{s}")
        P.finish("sp")
    es_x.close()


def _consts():
    i = np.arange(128)
    r, c = i[:, None], i[None, :]
    same = (r // 8) == (c // 8)
    f = lambda m: m.astype(np.float32)
    parts = [np.eye(128, dtype=np.float32), f(r < c), f(r > c), f(r <= c),
             f((r < c) & same), f((r > c) & same), f((r <= c) & same),
             f(np.broadcast_to((c % 8) != 0, (128, 128))), f((r // 64) == (c // 64)), np.ones((128, 128), np.float32),
             f((r // 8) == np.arange(16)[None, :])]
    return np.ascontiguousarray(np.concatenate(parts, axis=1))


def _constsb():
    import ml_dtypes
    i = np.arange(128)
    r, c = i[:, None], i[None, :]
    parts = []
    for b in (8, 16, 32, 64):
        parts.append(((r // (2 * b)) == (c // (2 * b))) & ((r // b) != (c // b)) & (r > c))
    parts = parts + [p.T for p in parts]
    parts.append((r // 64) == (c // 64))
    return np.ascontiguousarray(np.concatenate(parts, axis=1).astype(np.float32).astype(ml_dtypes.bfloat16))


def _col(v, n):
    return np.ascontiguousarray(np.asarray(v, np.float32).reshape(n, 128).T)


_NC_CACHE = {}


def _prep(x_prompt, x_sample, state_wkv, state_shift, c_prompt, c_sample,
           w_ada, b_ada, norm_mix_g, w_in, mu_shift, w0, w_decay_up, a0, w_aaa_up,
           w_gate_up, k_k, k_a, r_k, gn_g, gn_b, ln_v_g, ln_v_b, w_spatial, b_spatial,
           w_branch_a, w_branch_b, w_out, norm_ffn_g, w_ffn_in, w_ffn_out, norm_final_g):
    f = lambda a: np.ascontiguousarray(np.asarray(a, np.float32))
    x_prompt, x_sample = f(x_prompt), f(x_sample)
    state_wkv, state_shift = f(state_wkv)[0], f(state_shift)[0]
    c_prompt, c_sample = f(c_prompt), f(c_sample)
    shared = {
        "w_ada": f(w_ada)[0], "badaT": _col(f(b_ada)[0], 96),
        "gvec": np.concatenate([_col(f(norm_mix_g)[0], 16), _col(f(norm_ffn_g)[0], 16), _col(f(norm_final_g), 16)], 1),
        "w_in": f(w_in)[0],
        "lora_up": np.ascontiguousarray(np.concatenate([f(w_decay_up)[0], f(w_aaa_up)[0]], 0)),
        "w_gate_up": f(w_gate_up)[0], "ln_g": f(ln_v_g)[0], "ln_b": f(ln_v_b)[0],
        "w_branch_a": f(w_branch_a)[0], "w_branch_b": f(w_branch_b)[0], "w_out": f(w_out)[0],
        "w_ffn_in": f(w_ffn_in)[0], "w_ffn_out": f(w_ffn_out)[0], "cst": _consts(), "cstb": _constsb(),
    }
    mu = f(mu_shift)[0]
    ka = f(k_a)[0]
    vecs = [f(w0)[0], f(a0)[0], f(k_k)[0], ka, ka, f(gn_g)[0], f(gn_b)[0], f(r_k)[0].reshape(-1), ka]
    sv = [_col(mu, 26)] + [_col(v, 8) for v in vecs]
    shared["svec"] = np.ascontiguousarray(np.concatenate(sv, 1))
    wsp = f(w_spatial)[0]
    shared["w_spT"] = np.ascontiguousarray(wsp.transpose(0, 2, 1))
    blkT = np.zeros((8, 128, 128), np.float32)
    for b in range(16):
        blkT[:, b * 8:(b + 1) * 8, b * 8:(b + 1) * 8] = wsp[:, :8, :8].transpose(0, 2, 1)
    shared["w_spTs"] = blkT
    bsp = f(b_spatial)[0]
    shared["bsp"] = bsp
    shared["bsps"] = np.ascontiguousarray(np.tile(bsp[:, :8], (1, 16)))
    in_maps = []
    for c in range(8):
        b, half = c // 2, c % 2
        xs = x_sample[16 * c:16 * (c + 1)].reshape(128, D)
        xo = np.concatenate([x_prompt[b, half * 1024:(half + 1) * 1024], xs], 0)
        xp = x_prompt[b, 0:1024]
        cc = np.concatenate([c_prompt[b:b + 1], c_sample[16 * c:16 * (c + 1)]], 0)
        sw = state_wkv[16 * c:16 * (c + 1)]
        s0T = sw.reshape(16, 8, 2, 64, 64).transpose(1, 2, 4, 0, 3).reshape(8, 128, 16 * 64)
        ssh = state_shift[16 * c:16 * (c + 1)]
        sshT = ssh.reshape(16, 26, 128).transpose(2, 1, 0).reshape(128, 26 * 16)
        m = dict(shared)
        m.update({"xT_own": np.ascontiguousarray(xo.T), "xT_prev": np.ascontiguousarray(xp.T),
                  "cT": np.ascontiguousarray(cc.T), "flag": np.full((128, 1), float(half), np.float32),
                  "s0T": np.ascontiguousarray(s0T), "sshT": np.ascontiguousarray(sshT)})
        in_maps.append(m)
    return in_maps


def kernel(x_prompt, x_sample, state_wkv, state_shift, c_prompt, c_sample,
           w_ada, b_ada, norm_mix_g, w_in, mu_shift, w0, w_decay_up, a0, w_aaa_up,
           w_gate_up, k_k, k_a, r_k, gn_g, gn_b, ln_v_g, ln_v_b, w_spatial, b_spatial,
           w_branch_a, w_branch_b, w_out, norm_ffn_g, w_ffn_in, w_ffn_out, norm_final_g):
    in_maps = _prep(x_prompt, x_sample, state_wkv, state_shift, c_prompt, c_sample,
                    w_ada, b_ada, norm_mix_g, w_in, mu_shift, w0, w_decay_up, a0, w_aaa_up,
                    w_gate_up, k_k, k_a, r_k, gn_g, gn_b, ln_v_g, ln_v_b, w_spatial, b_spatial,
                    w_branch_a, w_branch_b, w_out, norm_ffn_g, w_ffn_in, w_ffn_out, norm_final_g)
    if "nc" not in _NC_CACHE:
        _NC_CACHE["nc"] = build_nc()
    res = run_bass_kernel_spmd(_NC_CACHE["nc"], in_maps, core_ids=list(range(8)))
    R = res.results
    y_prompt = np.zeros((4, 2048, D), np.float32); y_sample = np.zeros((128, 8, D), np.float32)
    wkv_p = np.zeros((1, 4, 16, 64, 64), np.float32); shift_p = np.zeros((1, 4, 3328), np.float32)
    wkv_s = np.zeros((1, 128, 16, 64, 64), np.float32); shift_s = np.zeros((1, 128, 3328), np.float32)
    v_s = np.zeros((1, 128, 8, 1024), np.float32)
    for c in range(8):
        b, half = c // 2, c % 2
        yT = R[c]["yT"]
        y_prompt[b, half * 1024:(half + 1) * 1024] = yT[:, :1024].T
        y_sample[16 * c:16 * (c + 1)] = yT[:, 1024:].T.reshape(16, 8, D)
        if half == 1:
            hp = R[c]["wkv_p"]
            for p in range(8):
                for h2 in range(2):
                    wkv_p[0, b, 2 * p + h2] = hp[p, h2 * 64:(h2 + 1) * 64, h2 * 64:(h2 + 1) * 64].T
            shift_p[0, b] = R[c]["shp"].T.reshape(-1)
        ws = R[c]["wkv_s"].reshape(8, 2, 64, 16, 64)
        wkv_s[0, 16 * c:16 * (c + 1)] = ws.transpose(3, 0, 1, 4, 2).reshape(16, 16, 64, 64)
        shift_s[0, 16 * c:16 * (c + 1)] = R[c]["shs"].reshape(128, 26, 16).transpose(2, 1, 0).reshape(16, 3328)
        v_s[0, 16 * c:16 * (c + 1)] = R[c]["v_s"].reshape(16, 8, 1024)
    return (y_prompt, y_sample, wkv_p, shift_p, wkv_s, shift_s, v_s)
```

```python
import numpy as np
import concourse.bass as bass
import concourse.mybir as mybir
from concourse.bass_utils import run_bass_kernel_spmd
from contextlib import ExitStack

F32 = mybir.dt.float32
F32R = mybir.dt.float32r
BF16 = mybir.dt.bfloat16
AF = mybir.ActivationFunctionType
ALU = mybir.AluOpType
AX = mybir.AxisListType

EPOCH = 8192
D = 2048
T = 1152
TP = 1024
DFF = 5632
CIN = 9472
NCST = 10 * 128 + 16


class Prog:
    def __init__(self, nc):
        self.nc = nc
        self.eng = {"pe": nc.tensor, "act": nc.scalar, "dve": nc.vector,
                    "pool": nc.gpsimd, "sp": nc.sync}
        self.cnt = {e: 0 for e in self.eng}
        self.sems = {e: [] for e in self.eng}
        self.seen = {e: {} for e in self.eng}
        self.last_w = {}
        self.readers = {}
        self.dma_sems = {}
        self.pend_r = {e: [] for e in self.eng}
        self.pend_w = {e: [] for e in self.eng}
        self.nsem = 0
        self.ninstr = {e: 0 for e in self.eng}
        self.rec = None
        self.m_eng = {e: 0.0 for e in self.eng}
        self.m_key = {}

    def record(self, gen):
        self.rec = []
        for _ in gen:
            pass
        r, self.rec = self.rec, None
        return r

    def schedule(self, streams):
        from collections import Counter
        DUR = {"pe": 0.2, "act": 0.25, "dve": 0.22, "pool": 0.5, "sp": 0.05}
        LAT = 0.3
        isps = lambda k: isinstance(k, str) and k.startswith("psb")
        units = []
        for st in streams:
            us, cur = [], []
            for o in st:
                cur.append(o)
                if o[5]:
                    us.append(cur)
                    cur = []
            if cur:
                us.append(cur)
            units.append(us)
        pend_r = [Counter(k for u in us for o in u for k in o[3] if not isps(k)) for us in units]
        pend_w = [Counter(k for u in us for o in u for k in o[4] if not isps(k)) for us in units]
        idx = [0] * len(units)
        while True:
            best = None
            for j, us in enumerate(units):
                if idx[j] >= len(us):
                    continue
                u = us[idx[j]]
                keys = [k for o in u for k in (o[3] + o[4])]
                rk_ = [k for o in u for k in o[3] if not isps(k)]
                wk_ = [k for o in u for k in o[4] if not isps(k)]
                if any(pend_w[i][k] > 0 for k in rk_ for i in range(j)) or \
                   any(pend_w[i][k] > 0 or pend_r[i][k] > 0 for k in wk_ for i in range(j)):
                    continue
                F = u[0][1]
                t_ready = max([self.m_key.get(k, 0.0) for k in keys] + [0.0])
                start = max(self.m_eng[F], t_ready)
                if best is None or start < best[0]:
                    best = (start, j, u, keys, F)
            if best is None:
                break
            start, j, u, keys, F = best
            t = start
            for o in u:
                if o[0] == "op":
                    self.op(o[1], o[2], o[3], o[4], o[5])
                    t += (o[6] if len(o) > 6 and o[6] else DUR[o[1]])
                else:
                    out, in_, semres, kw = o[2]
                    self.dma(o[1], out, in_, o[3], o[4], semres, **kw)
                    t += DUR["sp"]
            self.m_eng[F] = t
            fin = t + LAT + ((u[0][6] or 2.0) if u[0][0] == "dma" else 0.0)
            for o in u:
                for k in o[4] + [k2 for k2 in o[3] if isps(k2)]:
                    self.m_key[k] = fin
            for o in u:
                for k in o[3]:
                    if not isps(k):
                        pend_r[j][k] -= 1
                for k in o[4]:
                    if not isps(k):
                        pend_w[j][k] -= 1
            idx[j] += 1

    def _newsem(self, name):
        self.nsem += 1
        return self.nc.alloc_semaphore(name)

    def _deps(self, F, reads, writes):
        deps = {}

        def add(tok, same_ok):
            if tok is None:
                return
            key, sem, val, eng = tok
            if eng == F and F == "pe":
                return
            if val > deps.get(key, (None, 0))[1]:
                deps[key] = (sem, val)

        for r in reads:
            add(self.last_w.get(r), True)
        for w in writes:
            add(self.last_w.get(w), True)
            for t in self.readers.get(w, ()):
                add(t, False)
        return deps

    def _emit_waits(self, F, deps):
        e = self.eng[F]
        for key, (sem, val) in deps.items():
            if self.seen[F].get(key, 0) >= val:
                continue
            e.wait_ge(sem, val)
            self.seen[F][key] = val

    def _register(self, tok, reads, writes):
        for r in reads:
            lst = self.readers.setdefault(r, [])
            lst[:] = [t for t in lst if t[0] != tok[0]]
            lst.append(tok)
        for w in writes:
            self.last_w[w] = tok
            self.readers[w] = []

    def op(self, F, fn, reads=(), writes=(), signal=True):
        reads = list(reads)
        writes = list(writes)
        if self.rec is not None:
            self.rec.append(("op", F, fn, reads, writes, signal, None))
            return None
        ex = [r for r in reads if isinstance(r, str) and r.startswith("psb")]
        if ex:
            reads = [r for r in reads if r not in ex]
            writes = writes + [r for r in ex if r not in writes]
        deps = self._deps(F, reads, writes)
        self._emit_waits(F, deps)
        ins = fn()
        self.ninstr[F] += 1
        if not signal:
            self.pend_r[F] += reads
            self.pend_w[F] += writes
            return ins
        i = self.cnt[F]
        self.cnt[F] += 1
        ep = i // EPOCH
        while len(self.sems[F]) <= ep:
            self.sems[F].append(self._newsem(f"s_{F}_{len(self.sems[F])}"))
        sem = self.sems[F][ep]
        val = i % EPOCH + 1
        ins.then_inc(sem, 1)
        tok = ((F, ep), sem, val, F)
        self._register(tok, reads + self.pend_r[F], writes + self.pend_w[F])
        self.pend_r[F] = []
        self.pend_w[F] = []
        return ins

    def dma(self, Q, out, in_, reads=(), writes=(), semres=None, est=None, **kw):
        reads = list(reads)
        writes = list(writes)
        if self.rec is not None:
            self.rec.append(("dma", Q, (out, in_, semres, kw), reads, writes, True, est))
            return None
        if semres is None:
            semres = (writes + reads)[0]
        deps = self._deps("dma", reads, writes)
        self._emit_waits(Q, deps)
        if semres not in self.dma_sems:
            self.dma_sems[semres] = [self._newsem(f"d_{len(self.dma_sems)}"), 0]
        ent = self.dma_sems[semres]
        ent[1] += 16
        ins = self.eng[Q].dma_start(out=out, in_=in_, **kw)
        ins.then_inc(ent[0], 16)
        self.ninstr[Q] += 1
        tok = (("dma", semres), ent[0], ent[1], "dma")
        self._register(tok, reads, writes)
        return ins

    def barrier(self):
        for F, e in self.eng.items():
            for semres, (sem, val) in self.dma_sems.items():
                key = ("dma", semres)
                if val > self.seen[F].get(key, 0):
                    e.wait_ge(sem, val)
                    self.seen[F][key] = val
            for E in self.eng:
                if E == F or self.cnt[E] == 0:
                    continue
                i = self.cnt[E] - 1
                key = (E, i // EPOCH)
                val = i % EPOCH + 1
                if val > self.seen[F].get(key, 0):
                    e.wait_ge(self.sems[E][i // EPOCH], val)
                    self.seen[F][key] = val

    def finish(self, F="sp"):
        e = self.eng[F]
        for semres, (sem, val) in self.dma_sems.items():
            if val > 0:
                e.wait_ge(sem, val)
        for E in self.eng:
            if self.cnt[E] > 0:
                i = self.cnt[E] - 1
                e.wait_ge(self.sems[E][i // EPOCH], i % EPOCH + 1)


_UC = [0]


def _uname(name):
    _UC[0] += 1
    return f"t{_UC[0]}_{name}"


class _Arena:
    def __init__(self):
        self.ap = None
        self.free = []

    def init(self, nc, name="arena", dt=F32, n=None):
        if n is None:
            nbytes = int(nc.sbuf_bytes_remaining) - 1024
            n = nbytes // 4
        self.ap = nc.alloc_sbuf_tensor(name, [128, n], dt).ap()
        self.free = [(0, n)]
        self.n = n

    def alloc(self, words):
        words = (words + 15) // 16 * 16
        for i, (st, sz) in enumerate(self.free):
            if sz >= words:
                if sz == words:
                    self.free.pop(i)
                else:
                    self.free[i] = (st + words, sz - words)
                return st, words
        raise MemoryError(f"arena full: need {words} words, free={self.free}")

    def release(self, st, words):
        self.free.append((st, words))
        self.free.sort()
        out = []
        for a, b in self.free:
            if out and out[-1][0] + out[-1][1] == a:
                out[-1] = (out[-1][0], out[-1][1] + b)
            else:
                out.append((a, b))
        self.free = out


_AR = _Arena()
_ARR = _Arena()


class _Tile:
    def __init__(self, shape, dt):
        self.shape = list(shape)
        self.dt = dt

    def __enter__(self):
        esz = 2 if self.dt == BF16 else 4
        per = 1
        for d in self.shape[1:]:
            per *= d
        words = (per * esz + 3) // 4
        self.ar = _ARR if self.dt == F32R else _AR
        self.st, self.words = self.ar.alloc(words)
        v = self.ar.ap[:, self.st:self.st + words]
        if self.dt == BF16:
            v = v.bitcast(self.dt)
        v = v[:, 0:per]
        if len(self.shape) == 3:
            v = v.rearrange("p (a b) -> p a b", b=self.shape[2])
        elif len(self.shape) == 4:
            v = v.rearrange("p (a b c) -> p a b c", b=self.shape[2], c=self.shape[3])
        if self.shape[0] != 128:
            v = v[0:self.shape[0]]
        self._ap = v
        return self

    def ap(self):
        return self._ap

    def __exit__(self, *a):
        self.ar.release(self.st, self.words)
        return False


def _sbt(nc, name, shape, dt):
    return _Tile(shape, dt)


def r32(ap):
    return ap.bitcast(F32)


class _Stop(Exception):
    pass


def build_nc(stop=None):
    nc = bass.Bass("TRN2", target_bir_lowering=False)
    P = Prog(nc)
    DR = {}
    try:
        _build(nc, P, DR, stop)
    except _Stop:
        pass
    print("ninstr", P.ninstr, "nsem", P.nsem, flush=True)
    return nc


def _build(nc, P, DR, stop):
    def ckpt(tag, dumps):
        if stop != tag:
            return
        for name, ap, keys in dumps:
            d = nc.dram_tensor("dbg_" + name, list(ap.shape), ap.dtype, kind="ExternalOutput").ap()
            P.dma("sp", d, ap, reads=keys, semres=("dbg", name))
        P.finish("sp")
        raise _Stop()

    cpa = lambda o, i: (lambda: nc.scalar.copy(o, i))
    cpv = lambda o, i: (lambda: nc.vector.tensor_copy(o, i))
    rcp = lambda o, i: (lambda: nc.vector.reciprocal(o, i))
    scn = lambda o, d0, d1, init, o0, o1: (lambda: nc.vector.tensor_tensor_scan(o, d0, d1, init, o0, o1))

    def din(name, shape):
        DR[name] = nc.dram_tensor(name, list(shape), F32, kind="ExternalInput").ap()

    def dout(name, shape):
        DR[name] = nc.dram_tensor(name, list(shape), F32, kind="ExternalOutput").ap()

    din("xT_own", [D, T]); din("xT_prev", [D, TP]); din("cT", [D, 17]); din("flag", [128, 1])
    din("s0T", [8, 128, 16 * 64]); din("sshT", [128, 26 * 16])
    din("w_ada", [D, 6 * D]); din("badaT", [128, 96])
    din("gvec", [128, 48])
    din("w_in", [D, CIN]); din("svec", [128, 26 + 9 * 8])
    din("lora_up", [128, 1024]); din("w_gate_up", [128, 1024])
    din("ln_g", [1024]); din("ln_b", [1024])
    din("w_spT", [8, 128, 128]); din("w_spTs", [8, 128, 128]); din("bsp", [8, 128]); din("bsps", [8, 128])
    din("w_branch_a", [1024, D]); din("w_branch_b", [1024, D]); din("w_out", [D, D])
    din("w_ffn_in", [D, 2 * DFF]); din("w_ffn_out", [DFF, D]); din("cst", [128, NCST])
    DR["cstb"] = nc.dram_tensor("cstb", [128, 1152], BF16, kind="ExternalInput").ap()
    dout("yT", [D, T]); dout("wkv_p", [8, 128, 128]); dout("shp", [128, 26])
    dout("wkv_s", [8, 128, 16 * 64]); dout("shs", [128, 26 * 16]); dout("v_s", [128, 1024])

    def sbp(name, shape, dt=F32):
        return nc.alloc_sbuf_tensor(_uname(name), list(shape), dt).ap()

    cst = sbp("cst", [128, NCST])
    P.dma("sp", cst, DR["cst"], writes=["cst"])
    ident = cst[:, 0:128]; m_sl = cst[:, 128:256]; m_gt = cst[:, 256:384]; m_le = cst[:, 384:512]
    ms_sl = cst[:, 512:640]; ms_gt = cst[:, 640:768]; ms_le = cst[:, 768:896]
    mreset = cst[:, 896:1024]; blk1 = cst[:, 1024:1152]; ones = cst[:, 1152:1280]
    maskTB = cst[:, 1280:1296]
    cstb = sbp("cstb", [128, 1152], BF16)
    blk1b = cstb[:, 1024:1152]
    P.dma("sp", cstb, DR["cstb"], writes=["cstb"])
    moff = [cstb[:, l * 128:(l + 1) * 128] for l in range(4)]
    moffT = [cstb[:, 512 + l * 128:512 + (l + 1) * 128] for l in range(4)]
    onesR = sbp("onesR", [128, 128], F32R)
    P.op("dve", cpv(onesR, ones), ["cst"], ["onesR"])
    epsc = sbp("epsc", [128, 8])
    P.op("dve", lambda: nc.vector.memset(epsc[:, 0:1], 1e-6), [], ["epsc"])
    P.op("dve", lambda: nc.vector.memset(epsc[:, 1:2], 64e-5), [], ["epsc"])
    P.op("dve", lambda: nc.vector.memset(epsc[:, 2:3], 1e-5), [], ["epsc"])
    P.op("dve", lambda: nc.vector.memset(epsc[:, 3:4], 1.0), [], ["epsc"])
    P.op("dve", lambda: nc.vector.memset(epsc[:, 4:5], -0.5), [], ["epsc"])
    flag = sbp("flag", [128, 1]); P.dma("sp", flag, DR["flag"], writes=["flag"])
    gvec = sbp("gvec", [128, 48]); P.dma("sp", gvec, DR["gvec"], writes=["gvec"])
    svec = sbp("svec", [128, 114]); P.dma("sp", svec[:, 0:98], DR["svec"], writes=["svec"])
    P.op("dve", lambda: nc.vector.tensor_scalar(svec[:, 98:114], svec[:, 26:42], -1.0, None, ALU.mult, ALU.bypass), ["svec"], ["svec"])
    P.op("dve", lambda: nc.vector.tensor_scalar(svec[:, 58:66], svec[:, 50:58], -1.0, 1.0, ALU.mult, ALU.add), ["svec"], ["svec"])
    zeros = sbp("zeros", [128, 128])
    P.op("dve", lambda: nc.vector.memset(zeros, 0.0), [], ["zeros"])
    muT = svec[:, 0:26]
    sv = lambda i, hp: svec[:, 26 + 8 * i + hp: 26 + 8 * i + hp + 1]
    badaT = sbp("badaT", [128, 96]); P.dma("sp", badaT, DR["badaT"], writes=["badaT"])
    modT = sbp("modT", [128, 96, 17])
    MODK = [("modT", k) for k in range(6)]
    gm = sbp("gm", [128, 16, 17]); gf = sbp("gf", [128, 16, 17])
    WB = [sbp(f"WB{i}", [128, 16, 256], BF16) for i in range(2)]
    wbc = [0]

    def wslot():
        i = wbc[0] % 2
        wbc[0] += 1
        return WB[i], f"WB{i}"

    psb = [nc.alloc_psum_tensor(f"psb{i}", [128, 512], F32).ap() for i in range(8)]
    bc = [0]; qc = [0]

    BPOOL = [[0, 1, 2, 3]]
    QPOOL = [[2, 3, 4, 5, 6, 7]]

    def bank():
        pool_ = BPOOL[0]
        i = pool_[bc[0] % len(pool_)]
        bc[0] += 1
        return psb[i], f"psb{i}"

    def quart():
        pool_ = QPOOL[0]
        i = pool_[qc[0] % len(pool_)]
        qc[0] += 1
        return psb[i][:, 0:128], f"psb{i}"

    V = lambda fn, r, w: P.op("dve", fn, r, w)
    A = lambda fn, r, w: P.op("act", fn, r, w)

    def MM(out, lhsT, rhs, start, stop, r, w, signal=True):
        return P.op("pe", lambda: nc.tensor.matmul(out, lhsT=lhsT, rhs=rhs, start=start, stop=stop), r, w, signal)

    def TR(out, in_, r, w):
        return P.op("pe", lambda: nc.tensor.transpose(out, in_, ident), list(r) + ["cst"], w)

    tt = lambda o, a, b, op: (lambda: nc.vector.tensor_tensor(o, a, b, op))
    ts = lambda o, a, s1, s2, o0, o1: (lambda: nc.vector.tensor_scalar(o, a, s1, s2, o0, o1))
    stt = lambda o, a, s, b, o0, o1: (lambda: nc.vector.scalar_tensor_tensor(o, a, s, b, o0, o1))
    act = lambda o, i, f, **kw: (lambda: nc.scalar.activation(o, i, f, **kw))

    def load_w(src_ap, nk, ncols, dst=None, key=None):
        if dst is None:
            dst, key = wslot()
        P.dma("pool", dst[:, 0:nk, 0:ncols], src_ap.rearrange("(kc p) n -> p kc n", p=128), writes=[key])
        return dst, key

    _ARR.init(nc, "arenaR", F32R, 10624)
    _AR.init(nc)
    scb = sbp("scb", [128, 16, 17], BF16)
    with ExitStack() as es:
        cTt = es.enter_context(_sbt(nc, "cTt", [128, 16, 17], F32)).ap()
        P.dma("sp", cTt, DR["cT"].rearrange("(kc p) n -> p kc n", p=128), writes=["cTt"])
        A(act(scb, cTt, AF.Silu), ["cTt"], ["scb"])
        P.barrier()

    ADA_SLOT = {}

    def ada_dma_gen(blk):
        dst, key = wslot()
        ADA_SLOT[blk] = (dst, key)
        P.dma("pool", dst[:, 0:16, 0:256], DR["w_ada"][:, blk * 256:(blk + 1) * 256].rearrange("(kc p) n -> p kc n", p=128),
              writes=[key], est=14.0)
        yield

    def ada_mm_gen(blk):
        wa_, wak = ADA_SLOT[blk]
        for jj in range(2):
            j = blk * 2 + jj
            pb, pk = bank()
            for kc in range(16):
                MM(pb[:, 0:17], wa_[:, kc, jj * 128:(jj + 1) * 128], scb[:, kc, :], kc == 0, kc == 15,
                   [wak, "scb"], [pk], signal=(kc == 15))
            A(act(modT[:, j, :], pb[:, 0:17], AF.Identity, bias=badaT[:, j:j + 1]), [pk, "badaT"], [("modT", j // 16)])
        yield

    def ada_block(blk):
        wa_, wak = load_w(DR["w_ada"][:, blk * 256:(blk + 1) * 256], 16, 256)
        for jj in range(2):
            j = blk * 2 + jj
            pb, pk = bank()
            for kc in range(16):
                MM(pb[:, 0:17], wa_[:, kc, jj * 128:(jj + 1) * 128], scb[:, kc, :], kc == 0, kc == 15,
                   [wak, "scb"], [pk], signal=(kc == 15))
            A(act(modT[:, j, :], pb[:, 0:17], AF.Identity, bias=badaT[:, j:j + 1]), [pk, "badaT"], [("modT", j // 16)])

    for blk in range(16):
        ada_block(blk)
    for dc in range(16):
        V(ts(gm[:, dc, :], modT[:, 16 + dc, :], 1.0, gvec[:, dc:dc + 1], ALU.add, ALU.mult), [("modT", 1), "gvec"], ["gm"])
    ckpt("A", [("modT", modT, MODK), ("gm", gm, ["gm"])])

    def rms_rstd(es_tiles, src_fn, n, srckeys):
        sq, rstd, tmpn = es_tiles
        pb, pk = bank()
        for dc in range(16):
            s = dc % 2
            A(act(sq[s][:, 0:n], src_fn(dc), AF.Square), srckeys(dc), [f"sq{s}"])
            MM(pb[:, 0:n], onesR, sq[s][:, 0:n], dc == 0, dc == 15, [f"sq{s}", "onesR"], [pk], signal=True)
        A(act(tmpn[:, 0:n], pb[:, 0:n], AF.Sqrt, bias=epsc[:, 0:1], scale=1.0 / D), [pk, "epsc"], ["rstd"])
        V(rcp(rstd[:, 0:n], tmpn[:, 0:n]), ["rstd"], ["rstd"])
        return rstd

    def build_hT(es, hT, xsrc, ncols, g_t, shift_base, with_sample):
        xg = es.enter_context(_sbt(nc, "xg", [128, 16, 512], F32)).ap()
        sq = [es.enter_context(_sbt(nc, f"sq{i}", [128, 512], F32R)).ap() for i in range(2)]
        rstd = es.enter_context(_sbt(nc, "rstd", [128, 512], F32)).ap()
        tmpn = es.enter_context(_sbt(nc, "tmpn", [128, 512], F32)).ap()
        tq = [es.enter_context(_sbt(nc, f"tq{i}", [128, 512], F32)).ap() for i in range(2)]
        for g0 in range(0, ncols, 512):
            n = min(512, ncols - g0)
            for dc in range(16):
                P.dma("sp", xg[:, dc, 0:n], xsrc[dc * 128:(dc + 1) * 128, g0:g0 + n], writes=[("xg", dc)])
            rms_rstd((sq, rstd, tmpn), lambda dc: xg[:, dc, 0:n], n, lambda dc: [("xg", dc)])
            for dc in range(16):
                s = dc % 2
                if with_sample and g0 >= TP:
                    b3 = lambda a: a.unsqueeze(2).broadcast_to([128, 16, 8])
                    v3 = lambda a: a.rearrange("p (b t) -> p b t", t=8)
                    V(tt(v3(tq[s][:, 0:n]), v3(xg[:, dc, 0:n]), b3(g_t[:, dc, 1:17]), ALU.mult), [("xg", dc), "gm", "gf"], [f"tq{s}"])
                    V(tt(tq[s][:, 0:n], tq[s][:, 0:n], rstd[:, 0:n], ALU.mult), [f"tq{s}", "rstd"], [f"tq{s}"])
                    V(tt(v3(hT[:, dc, g0:g0 + n]), v3(tq[s][:, 0:n]), b3(modT[:, shift_base + dc, 1:17]), ALU.add),
                      [f"tq{s}", *MODK], [("hT", dc)])
                else:
                    V(stt(tq[s][:, 0:n], xg[:, dc, 0:n], g_t[:, dc, 0:1], rstd[:, 0:n], ALU.mult, ALU.mult),
                      [("xg", dc), "gm", "gf", "rstd"], [f"tq{s}"])
                    A(act(hT[:, dc, g0:g0 + n], tq[s][:, 0:n], AF.Identity, bias=modT[:, shift_base + dc, 0:1]),
                      [f"tq{s}", *MODK], [("hT", dc)])

    def dense_fm(wt, wkey, wcol0, nk, act_fn, act_keys, c0, n, consumer):
        pb, pk = bank()
        for kc in range(nk):
            MM(pb[:, 0:n], wt[:, kc, wcol0:wcol0 + 128], act_fn(kc)[:, c0:c0 + n], kc == 0, kc == nk - 1,
               [wkey] + act_keys(kc), [pk], signal=(kc == nk - 1))
        consumer(pb[:, 0:n], pk)

    es_mix = ExitStack()
    es_mp = ExitStack()
    mpa = lambda name, shape, dt=F32: es_mp.enter_context(_sbt(nc, name, list(shape), dt)).ap()
    lora_d = mpa("lora_d", [128, 1024], BF16); lora_a = mpa("lora_a", [128, 1024], BF16)
    P.op("dve", lambda: nc.vector.memset(lora_d[64:128, :], 0.0), [], ["lora_up"])
    P.op("dve", lambda: nc.vector.memset(lora_a[0:64, :], 0.0), [], ["lora_up"])
    P.dma("pool", lora_d[0:64, :], DR["lora_up"][0:64, :], writes=["lora_up"])
    P.dma("pool", lora_a[64:128, :], DR["lora_up"][64:128, :], writes=["lora_up"])
    wgu = mpa("wgu", [128, 1024], BF16); P.dma("pool", wgu, DR["w_gate_up"], writes=["wgu"])
    sshT = mpa("sshT", [128, 26, 16]); P.dma("sp", sshT, DR["sshT"].rearrange("p (j b) -> p j b", b=16), writes=["sshT"])
    Hs = mpa("Hs", [128, 8, 128], F32R)
    hlast = mpa("hlast", [128, 16, 2], BF16)
    shp = mpa("shp", [128, 26]); shs = mpa("shs", [128, 26, 16])
    hT = es_mix.enter_context(_sbt(nc, "hT", [128, 16, T], BF16)).ap()
    hT_fn = lambda kc: hT[:, kc, :]
    hT_keys = lambda kc: [("hT", kc)]
    oaT = es_mix.enter_context(_sbt(nc, "oaT", [128, 8, T], BF16)).ap()

    def mixer_pass(own):
        ncols = T if own else TP
        ntiles = 9 if own else 8
        with ExitStack() as es:
            build_hT(es, hT, DR["xT_own"] if own else DR["xT_prev"], ncols, gm, 0, own)
        if not own:
            V(cpv(hlast, hT[:, :, TP - 2:TP]), [("hT", k) for k in range(16)], ["hlast"])
        P.barrier()
        ckpt("B1" if not own else "B2", [("hT", hT, [("hT", k) for k in range(16)])])
        with ExitStack() as es:
            al = lambda name, shape, dt=F32: es.enter_context(_sbt(nc, name, list(shape), dt)).ap()
            lor = al("lor", [128, T], BF16); sg = al("sg", [128, T], BF16)
            zraw = [al(f"zraw{i}", [128, 1 + T]) for i in range(1)]
            zrs = [al(f"zrs{i}", [128, 16, 8]) for i in range(1)]
            big = al("big", [128, 16, 64])
            dtl = [big[:, 0:8, :].rearrange("p b v -> p (b v)")]
            zs = [[al(f"zs{s}_{w}", [128, T]) for w in range(3)] for s in range(1)]
            wrkv = [al(f"wrkv{i}", [128, 16, 3, 128], BF16) for i in range(1)]
            zcnt = [0]

            def shift_evac(j, dst, dstkey, wt, wkey, wcol0):
                zi = 0
                zcnt[0] += 1
                zr, zk = zraw[zi], f"zraw{zi}"
                mu = muT[:, j:j + 1]
                if own:
                    pb, pk = bank()
                    for kc in range(16):
                        MM(pb[:, 0:2], wt[:, kc, wcol0:wcol0 + 128], hlast[:, kc, :], kc == 0, kc == 15, [wkey, "hlast"], [pk], signal=(kc == 15))
                    A(act(zr[:, 0:1], pb[:, 1:2], AF.Identity, scale=flag[:, 0:1]), [pk, "flag"], [zk])
                else:
                    V(lambda: nc.vector.memset(zr[:, 0:1], 0.0), [], [zk])
                for g0 in range(0, TP, 512):
                    def cons(ps, pk, g0=g0):
                        A(cpa(zr[:, 1 + g0:1 + g0 + 512], ps), [pk], [zk])
                        d = dtl[0]; dk = "big"
                        V(tt(d, zr[:, g0:g0 + 512], ps, ALU.subtract), [zk, pk], [dk])
                        V(stt(dst[:, g0:g0 + 512], d, mu, ps, ALU.mult, ALU.add), [dk, pk, "svec"], [dstkey])
                    dense_fm(wt, wkey, wcol0, 16, hT_fn, hT_keys, g0, 512, cons)
                if own:
                    V(cpv(shp[:, j:j + 1], zr[:, TP:TP + 1]), [zk], ["shp"])

                    def cons_s(ps, pk):
                        z3 = zrs[zi]; z3k = f"zrs{zi}"
                        p3 = ps.rearrange("p (b t) -> p b t", t=8)
                        A(cpa(z3[:, :, 1:8], p3[:, :, 0:7]), [pk], [z3k])
                        V(cpv(z3[:, :, 0:1], sshT[:, j, :].unsqueeze(2)), ["sshT"], [z3k])
                        V(cpv(shs[:, j, :].unsqueeze(2), p3[:, :, 7:8]), [pk], ["shs"])
                        d = dtl[0][:, 0:128]
                        V(tt(d, z3.rearrange("p b t -> p (b t)"), ps, ALU.subtract), [z3k, pk], ["big"])
                        V(stt(dst[:, TP:T], d, mu, ps, ALU.mult, ALU.add), ["big", pk, "svec"], [dstkey])
                    dense_fm(wt, wkey, wcol0, 16, hT_fn, hT_keys, TP, 128, cons_s)

            wl, wlk = load_w(DR["w_in"][:, 3072:3328], 16, 256)
            zl = zs[0][0]
            shift_evac(24, zl, "zs0_0", wl, wlk, 0)
            A(act(lor[0:64, 0:ncols], zl[0:64, 0:ncols], AF.Tanh), ["zs0_0"], ["lor"])
            V(cpv(lor[64:128, 0:ncols], zl[64:128, 0:ncols]), ["zs0_0"], ["lor"])
            if own:
                shift_evac(25, zl, "zs0_0", wl, wlk, 128)
                A(act(sg, zl, AF.Sigmoid), ["zs0_0"], ["sg"])

            ckpt("L1" if not own else "L2", [("lor", lor, ["lor"]), ("sg", sg, ["sg"])])
            NS = 1
            DBN_ = {"AbTA", "AbTB", "BtTA", "BtTB", "KtTA", "KtTB", "RbT", "Bt_tm", "Kt_tm", "VeA", "VeB", "YA0", "YB0", "G", "bon"}
            pt = {}
            for nm in ["oT", "o2", "kk2b", "rkb"]:
                pt[nm] = [al(f"p_{nm}0", [128, 128], BF16)]
            for nm in ["lw", "cum", "aa", "kk", "kkn", "kp", "G", "Gi", "Gm1", "bon",
                       "mean", "msq", "varp", "cen", "otm"]:
                pt[nm] = [al(f"p_{nm}{s}", [128, 128]) for s in range(2 if nm in DBN_ else 1)]
            for nm in ["AbT", "BtT", "KtT", "RbT", "Bt_tm", "Kt_tm", "VeA", "VeB", "UeA", "UeB", "WT",
                       "AbTA", "AbTB", "BtTA", "BtTB", "KtTA", "KtTB", "WTA", "WTB", "RbTA", "RbTB",
                       "sX0", "sX1", "sXT0", "sXT1", "YA0", "YA1", "LakA", "MrbA", "MrkA",
                       "YB0", "YB1", "LakB", "MrbB", "MrkB", "DTfA", "DTfB"]:
                pt[nm] = [al(f"p_{nm}{s}", [128, 128], F32R) for s in range(2 if nm in DBN_ else 1)]
            for hh_ in ("A", "B"):
                for nm in ["X0", "X1", "XT0", "XT1", "D0", "D1", "DT0", "DT1", "Pm", "Qm", "Lo0", "Lo1", "Lo2", "Lo3", "LoT0", "LoT1", "LoT2", "Lb", "LTb"]:
                    pt[nm + hh_] = [al(f"p_{nm}{hh_}{s}", [128, 128], BF16) for s in range(NS)]
            for nm in ["VeA", "VeB", "UeA", "UeB", "AbTA", "AbTB", "BtTA", "BtTB", "KtTA", "KtTB", "WTA", "WTB", "RbTA", "RbTB"]:
                for s in range(len(pt[nm])):
                    V((lambda a: (cpv(a, zeros)))(pt[nm][s]), ["zeros"], [(nm, s)])
            if own:
                h0r = al("h0r", [128, 16, 64], F32R)
                hsn = al("hsn", [128, 16, 64])
                u1 = al("u1", [128, 64]); o0 = al("o0", [128, 64])
                Ublk = al("Ublk", [128, 16, 64], F32R); Vblk = al("Vblk", [128, 16, 64], F32R)
            setc = [0]

            for hp in range(8):
                wi = 0
                wr, wrk = wrkv[wi], f"wrkv{wi}"
                which = [0, 1, 2] if own else [1, 2]
                for w in which:
                    c = w * 1024 + hp * 128
                    P.dma("pool", wr[:, :, w, :], DR["w_in"][:, c:c + 128].rearrange("(kc p) n -> p kc n", p=128), writes=[wrk])
                Z = zs[0]
                for w in which:
                    shift_evac(w * 8 + hp, Z[w], f"zs0_{w}", wr[:, :, w, :], wrk, 0)
                zr_, zk_, zv_ = Z
                kr, kk_, kv = [f"zs0_{w}" for w in range(3)]
                if own:
                    A(act(Hs[:, hp, :], r32(Hs[:, hp, :]), AF.Identity, scale=flag[:, 0:1]), [("Hs", hp), "flag"], [("Hs", hp)])
                else:
                    V((lambda a: (cpv(a, zeros)))(Hs[:, hp, :]), ["zeros"], [("Hs", hp)])
                ch = slice(hp * 128, (hp + 1) * 128)
                if hp == 0 and not own:
                    ckpt("T0", [("zs1", zs[0][1], ["zs0_1"]), ("zs2", zs[0][2], ["zs0_2"]), ("Hs", r32(Hs), [("Hs", 0)])])
                DBN = {"AbTA", "AbTB", "BtTA", "BtTB", "KtTA", "KtTB", "RbT", "Bt_tm", "Kt_tm", "VeA", "VeB", "YA0", "YB0", "G", "bon"}

                def env(ti):
                    samp = own and ti == 8
                    par = ti % 2
                    K = lambda nm: (nm, par if nm in DBN else 0)
                    X = lambda nm: pt[nm][par if nm in DBN else 0]
                    cs = slice(ti * 128, (ti + 1) * 128)
                    Msl, Mgt, Mle = (ms_sl, ms_gt, ms_le) if samp else (m_sl, m_gt, m_le)
                    return samp, K, X, cs, Msl, Mgt, Mle

                def prep_gen(ti):
                    samp, K, X, cs, Msl, Mgt, Mle = env(ti)
                    q, qk = quart()
                    MM(q, lora_d[:, ch], lor[:, cs], True, True, ["lora_up", "lor"], [qk])
                    A(act(X("lw"), q, AF.Exp, bias=svec[:, 98 + hp:99 + hp], scale=-1.0), [qk, "svec"], [K("lw")])
                    A(act(X("lw"), X("lw"), AF.Ln, bias=epsc[:, 3:4]), [K("lw"), "epsc"], [K("lw")])
                    A(act(X("lw"), X("lw"), AF.Exp, bias=epsc[:, 4:5], scale=-1.0), [K("lw"), "epsc"], [K("lw")])
                    yield
                    V(scn(X("cum"), mreset if samp else ones, X("lw"), 0.0, ALU.mult, ALU.add),
                      [K("lw"), "cst"], [K("cum")])
                    q, qk = quart()
                    MM(q, lora_a[:, ch], lor[:, cs], True, True, ["lora_up", "lor"], [qk])
                    A(act(X("aa"), q, AF.Exp, bias=svec[:, 106 + hp:107 + hp], scale=-1.0), [qk, "svec"], [K("aa")])
                    A(act(X("aa"), X("aa"), AF.Ln, bias=epsc[:, 3:4]), [K("aa"), "epsc"], [K("aa")])
                    A(act(X("aa"), X("aa"), AF.Exp, scale=-1.0), [K("aa")], [K("aa")])
                    yield
                    V(ts(X("kk"), zk_[:, cs], sv(2, hp), None, ALU.mult, ALU.bypass), [kk_, "svec"], [K("kk")])
                    A(act(X("kk2b"), X("kk"), AF.Square), [K("kk")], [K("kk2b")])
                    q, qk = quart()
                    MM(q, blk1b, X("kk2b"), True, True, ["cstb", K("kk2b")], [qk])
                    V(ts(X("Gm1"), q, 1e-19, None, ALU.max, ALU.bypass), [qk], [K("Gm1")])
                    A(act(X("Gm1"), X("Gm1"), AF.Ln), [K("Gm1")], [K("Gm1")])
                    A(act(X("Gm1"), X("Gm1"), AF.Exp, scale=-0.5), [K("Gm1")], [K("Gm1")])
                    V(tt(X("kkn"), X("kk"), X("Gm1"), ALU.mult), [K("kk"), K("Gm1")], [K("kkn")])
                    yield
                    V(ts(X("kp"), X("aa"), sv(3, hp), sv(4, hp), ALU.mult, ALU.add), [K("aa"), "svec"], [K("kp")])
                    V(tt(X("kp"), zk_[:, cs], X("kp"), ALU.mult), [kk_, K("kp")], [K("kp")])
                    A(act(X("G"), X("cum"), AF.Exp, scale=-1.0), [K("cum")], [K("G")])
                    A(act(X("Gi"), X("cum"), AF.Exp), [K("cum")], [K("Gi")])
                    V(tt(X("cum"), X("cum"), X("lw"), ALU.subtract), [K("cum"), K("lw")], [K("cum")])
                    A(act(X("Gm1"), X("cum"), AF.Exp, scale=-1.0), [K("cum")], [K("Gm1")])
                    yield
                    V(stt(X("AbT"), X("kkn"), -1.0, X("Gm1"), ALU.mult, ALU.mult), [K("kkn"), K("Gm1")], [K("AbT")])
                    V(tt(X("kk"), X("kkn"), X("aa"), ALU.mult), [K("kkn"), K("aa")], [K("kk")])
                    V(tt(X("BtT"), X("kk"), X("Gi"), ALU.mult), [K("kk"), K("Gi")], [K("BtT")])
                    V(tt(X("KtT"), X("kp"), X("Gi"), ALU.mult), [K("kp"), K("Gi")], [K("KtT")])
                    for nm in ("AbT", "BtT", "KtT"):
                        A((lambda nm=nm: (cpa(X(nm + "A")[0:64, :], r32(X(nm))[0:64, :])))(), [K(nm)], [K(nm + "A")])
                        V((lambda nm=nm: (cpv(X(nm + "B")[64:128, :], r32(X(nm))[64:128, :])))(), [K(nm)], [K(nm + "B")])
                    if own:
                        V(tt(X("RbT"), zr_[:, cs], X("G"), ALU.mult), [kr, K("G")], [K("RbT")])
                        if samp:
                            A(cpa(X("RbTA")[0:64, :], r32(X("RbT"))[0:64, :]), [K("RbT")], [K("RbTA")])
                            V(cpv(X("RbTB")[64:128, :], r32(X("RbT"))[64:128, :]), [K("RbT")], [K("RbTB")])
                        V(stt(X("rkb"), zr_[:, cs], sv(7, hp), X("kp"), ALU.mult, ALU.mult), [kr, "svec", K("kp")], [K("rkb")])
                        q, qk = quart()
                        MM(q, blk1b, X("rkb"), True, True, ["cstb", K("rkb")], [qk])
                        V(tt(X("bon"), q, zv_[:, cs], ALU.mult), [qk, kv], [K("bon")])
                    yield
                    q, qk = quart()
                    TR(q, r32(X("AbT")), [K("AbT")], [qk])
                    A(cpa(X("YA0")[:, 0:64], q[:, 0:64]), [qk], [K("YA0")])
                    A(cpa(X("YB0")[:, 64:128], q[:, 64:128]), [qk], [K("YB0")])
                    q, qk = quart()
                    TR(q, r32(X("BtT")), [K("BtT")], [qk])
                    A(cpa(X("Bt_tm"), q), [qk], [K("Bt_tm")])
                    yield
                    q, qk = quart()
                    TR(q, r32(X("KtT")), [K("KtT")], [qk])
                    A(cpa(X("Kt_tm"), q), [qk], [K("Kt_tm")])
                    q, qk = quart()
                    TR(q, zv_[:, cs], [kv], [qk])
                    A(cpa(X("VeA")[:, 0:64], q[:, 0:64]), [qk], [K("VeA")])
                    A(cpa(X("VeB")[:, 64:128], q[:, 64:128]), [qk], [K("VeB")])

                def scan_gens(ti):
                    samp, K, X, cs, Msl, Mgt, Mle = env(ti)
                    Yfin = YF.setdefault(ti, {})

                    def head_gen(hh, p0):
                        ps_ = slice(p0, p0 + 64)
                        dhalf = slice(0, 64) if hh == "A" else slice(64, 128)
                        ohalf = slice(64, 128) if hh == "A" else slice(0, 64)
                        Ve = X("Ve" + hh); Vek = K("Ve" + hh)
                        HX = lambda nm: X(nm + hh)
                        HK = lambda nm: K(nm + hh)
                        q1, qk1 = quart()
                        MM(q1, X("BtT" + hh), X("AbT" + hh), True, True, [K("BtT" + hh), K("AbT" + hh)], [qk1])
                        q2, qk2 = quart()
                        MM(q2, X("AbT" + hh), X("BtT" + hh), True, True, [K("BtT" + hh), K("AbT" + hh)], [qk2])

                        def side_lak():
                            q, qk = quart()
                            MM(q, X("KtT" + hh), X("AbT" + hh), True, True, [K("KtT" + hh), K("AbT" + hh)], [qk])
                            V(tt(X("Lak" + hh), q, Msl, ALU.mult), [qk, "cst"], [K("Lak" + hh)])

                        def side_z():
                            q, qk = quart()
                            MM(q[:, 0:64], X("Lak" + hh), Ve[:, dhalf], True, True, [K("Lak" + hh), Vek], [qk])
                            A(cpa(X("Y" + hh + "0")[:, ohalf], q[:, 0:64]), [qk], [K("Y" + hh + "0")])

                        def side_m():
                            if own:
                                q, qk = quart()
                                MM(q, X("BtT" + hh), X("RbT"), True, True, [K("BtT" + hh), K("RbT")], [qk])
                                V(tt(X("Mrb" + hh), q, Mle, ALU.mult), [qk, "cst"], [K("Mrb" + hh)])
                                q, qk = quart()
                                MM(q, X("KtT" + hh), X("RbT"), True, True, [K("KtT" + hh), K("RbT")], [qk])
                                V(tt(X("Mrk" + hh), q, Mle, ALU.mult), [qk, "cst"], [K("Mrk" + hh)])

                        if samp:
                            V(tt(X("sXT0"), q1, ms_sl, ALU.mult), [qk1, "cst"], [K("sXT0")])
                            V(tt(X("sX0"), q2, ms_gt, ALU.mult), [qk2, "cst"], [K("sX0")])
                            yield
                            side_lak()
                            yield
                            side_z()
                            yield
                            side_m()
                            yield
                            nsteps = 3
                            for i in range(nsteps):
                                a, b = i % 2, (i + 1) % 2
                                Xa, XTa, Ya = X(f"sX{a}"), X(f"sXT{a}"), X(f"Y{hh}{a}")
                                Xb, XTb, Yb = X(f"sX{b}"), X(f"sXT{b}"), X(f"Y{hh}{b}")
                                q, qk = quart()
                                MM(q, XTa, Ya, True, True, [K(f"sXT{a}"), K(f"Y{hh}{a}")], [qk])
                                V(tt(Yb, q, r32(Ya), ALU.add), [qk, K(f"Y{hh}{a}")], [K(f"Y{hh}{b}")])
                                if i < nsteps - 1:
                                    q, qk = quart()
                                    MM(q, XTa, Xa, True, True, [K(f"sXT{a}"), K(f"sX{a}")], [qk])
                                    qq, qqk = quart()
                                    MM(qq, Xa, XTa, True, True, [K(f"sXT{a}"), K(f"sX{a}")], [qqk])
                                    A(cpa(Xb, q), [qk], [K(f"sX{b}")])
                                    A(cpa(XTb, qq), [qqk], [K(f"sXT{b}")])
                            Yf, Yfk = X(f"Y{hh}1"), K(f"Y{hh}1")
                            Yfin[hh] = (Yf, Yfk)
                            q, qk = quart()
                            TR(q, r32(Yf), [Yfk], [qk])
                            A(cpa(X("WT")[ps_, :], q[ps_, :]), [qk], [K("WT")])
                            V(cpv(X("WT" + hh)[ps_, :], q[ps_, :]), [qk], [K("WT" + hh)])
                            return
                        A(cpa(HX("LTb"), q1), [qk1], [HK("LTb")])
                        A(cpa(HX("Lb"), q2), [qk2], [HK("Lb")])
                        V(tt(HX("XT0"), HX("LTb"), ms_sl, ALU.mult), [HK("LTb"), "cst"], [HK("XT0")])
                        V(tt(HX("X0"), HX("Lb"), ms_gt, ALU.mult), [HK("Lb"), "cst"], [HK("X0")])
                        V(tt(HX("D0"), HX("X0"), ident, ALU.add), [HK("X0"), "cst"], [HK("D0")])
                        V(tt(HX("DT0"), HX("XT0"), ident, ALU.add), [HK("XT0"), "cst"], [HK("DT0")])

                        def mask_l(l):
                            if l < 3:
                                V(tt(HX(f"LoT{l}"), HX("LTb"), moffT[l], ALU.mult), [HK("LTb"), "cstb"], [HK(f"LoT{l}")])
                            V(tt(HX(f"Lo{l}"), HX("Lb"), moff[l], ALU.mult), [HK("Lb"), "cstb"], [HK(f"Lo{l}")])

                        side = [side_lak, lambda: mask_l(0), side_z, lambda: mask_l(1), side_m, lambda: mask_l(2), lambda: mask_l(3)]
                        cur = 0
                        for i in range(2):
                            a, b = i % 2, (i + 1) % 2
                            q, qk = quart()
                            MM(q, HX(f"XT{a}"), HX(f"X{a}"), True, True, [HK(f"XT{a}"), HK(f"X{a}")], [qk])
                            qq, qqk = quart()
                            MM(qq, HX(f"X{a}"), HX(f"XT{a}"), True, True, [HK(f"XT{a}"), HK(f"X{a}")], [qqk])
                            A(cpa(HX(f"X{b}"), q), [qk], [HK(f"X{b}")])
                            A(cpa(HX(f"XT{b}"), qq), [qqk], [HK(f"XT{b}")])
                            if side:
                                side.pop(0)()
                            yield
                            q, qk = quart()
                            MM(q, HX(f"XT{b}"), HX(f"D{cur}"), True, True, [HK(f"XT{b}"), HK(f"D{cur}")], [qk])
                            qq, qqk = quart()
                            MM(qq, HX(f"D{cur}"), HX(f"XT{b}"), True, True, [HK(f"XT{b}"), HK(f"D{cur}")], [qqk])
                            V(tt(HX(f"D{1 - cur}"), q, HX(f"D{cur}"), ALU.add), [qk, HK(f"D{cur}")], [HK(f"D{1 - cur}")])
                            V(tt(HX(f"DT{1 - cur}"), qq, HX(f"DT{cur}"), ALU.add), [qqk, HK(f"DT{cur}")], [HK(f"DT{1 - cur}")])
                            cur = 1 - cur
                            if side:
                                side.pop(0)()
                            yield
                        for l in range(4):
                            last = (l == 3)
                            while side and l + 2 > 7 - len(side):
                                side.pop(0)()
                            if not last:
                                q, qk = quart()
                                MM(q, HX(f"LoT{l}"), HX(f"D{cur}"), True, True, [HK(f"LoT{l}"), HK(f"D{cur}")], [qk])
                                A(cpa(HX("Pm"), q), [qk], [HK("Pm")])
                            qq, qqk = quart()
                            MM(qq, HX(f"Lo{l}"), HX(f"DT{cur}"), True, True, [HK(f"Lo{l}"), HK(f"DT{cur}")], [qqk])
                            A(cpa(HX("Qm"), qq), [qqk], [HK("Qm")])
                            if side:
                                side.pop(0)()
                            yield
                            if not last:
                                q, qk = quart()
                                MM(q, HX(f"DT{cur}"), HX("Pm"), True, True, [HK(f"DT{cur}"), HK("Pm")], [qk])
                                V(tt(HX(f"D{1 - cur}"), q, HX(f"D{cur}"), ALU.add), [qk, HK(f"D{cur}")], [HK(f"D{1 - cur}")])
                            qq, qqk = quart()
                            MM(qq, HX(f"D{cur}"), HX("Qm"), True, True, [HK(f"D{cur}"), HK("Qm")], [qqk])
                            if last:
                                V(tt(HX("DTf"), qq, HX(f"DT{cur}"), ALU.add), [qqk, HK(f"DT{cur}")], [HK("DTf")])
                            else:
                                V(tt(HX(f"DT{1 - cur}"), qq, HX(f"DT{cur}"), ALU.add), [qqk, HK(f"DT{cur}")], [HK(f"DT{1 - cur}")])
                            cur = 1 - cur
                            yield
                        while side:
                            side.pop(0)()
                        DT, DTk = HX("DTf"), HK("DTf")
                        Y0, Y0k = X(f"Y{hh}0"), K(f"Y{hh}0")
                        q, qk = quart()
                        MM(q, Y0, DT, True, True, [Y0k, DTk], [qk])
                        A(cpa(X("WT")[ps_, :], q[ps_, :]), [qk], [K("WT")])
                        q, qk = quart()
                        MM(q[:, 0:64], DT, Y0[:, ohalf], True, True, [Y0k, DTk], [qk])
                        A(cpa(X(f"Y{hh}1")[:, ohalf], q[:, 0:64]), [qk], [K(f"Y{hh}1")])
                        Yfin[hh] = (X(f"Y{hh}1"), K(f"Y{hh}1"))

                    if samp:
                        def seq():
                            for g_ in (head_gen("A", 0), head_gen("B", 64)):
                                for _ in g_:
                                    yield
                        return [seq()]
                    return [head_gen("A", 0), head_gen("B", 64)]

                def chainpost_gen(ti):
                    samp, K, X, cs, Msl, Mgt, Mle = env(ti)
                    Yfin = YF[ti]
                    hk = ("Hs", hp)
                    Hh = Hs[:, hp, :]
                    if not samp:
                        q, qk = quart()
                        MM(q, X("WT"), Hh, True, True, [K("WT"), hk], [qk])
                        V(tt(X("UeA")[:, 0:64], q[:, 0:64], r32(Yfin["A"][0][:, 64:128]), ALU.add), [qk, Yfin["A"][1]], [K("UeA")])
                        V(tt(X("UeB")[:, 64:128], q[:, 64:128], r32(Yfin["B"][0][:, 0:64]), ALU.add), [qk, Yfin["B"][1]], [K("UeB")])
                        if own:
                            q, qk = quart()
                            MM(q, Hh, X("RbT"), True, False, [hk, K("RbT")], [qk], signal=False)
                            MM(q, X("UeA"), X("MrbA"), False, False, [K("UeA"), K("MrbA")], [qk], signal=False)
                            MM(q, X("UeB"), X("MrbB"), False, False, [K("UeB"), K("MrbB")], [qk], signal=False)
                            MM(q, X("VeA"), X("MrkA"), False, False, [K("VeA"), K("MrkA")], [qk], signal=False)
                            MM(q, X("VeB"), X("MrkB"), False, True, [K("VeB"), K("MrkB")], [qk])
                            A(cpa(X("oT"), q), [qk], [K("oT")])
                        q, qk = quart()
                        MM(q[:, 0:64], X("Bt_tm"), X("UeA")[:, 0:64], True, False, [K("Bt_tm"), K("UeA")], [qk], signal=False)
                        MM(q[:, 0:64], X("Kt_tm"), X("VeA")[:, 0:64], False, True, [K("Kt_tm"), K("VeA")], [qk], signal=False)
                        MM(q[:, 64:128], X("Bt_tm"), X("UeB")[:, 64:128], True, False, [K("Bt_tm"), K("UeB")], [qk], signal=False)
                        MM(q[:, 64:128], X("Kt_tm"), X("VeB")[:, 64:128], False, True, [K("Kt_tm"), K("VeB")], [qk])
                        GC = X("G")[:, 127:128]
                        for ps_, hf in ((slice(0, 64), slice(0, 64)), (slice(64, 128), slice(64, 128))):
                            A(act(X("msq")[ps_, 0:64], r32(Hh[ps_, hf]), AF.Identity, scale=GC[ps_, :]), [hk, K("G")], [K("msq")])
                            V(stt(Hh[ps_, hf], q[ps_, hf], GC[ps_, :], X("msq")[ps_, 0:64], ALU.mult, ALU.add), [qk, K("G"), K("msq")], [hk])
                        if own and ti == 7:
                            P.dma("sp", DR["wkv_p"][hp], r32(Hh), reads=[hk], semres=("Hs", hp))
                    else:
                        P.dma("sp", big, DR["s0T"][hp].rearrange("p (b v) -> p b v", v=64), writes=["big"])
                        V(cpv(h0r, big), ["big"], ["h0r"])
                        b3m = maskTB.unsqueeze(2).broadcast_to([128, 16, 64])
                        G3 = X("G").rearrange("p (b t) -> p b t", t=8)[:, :, 7:8]
                        for hh, p0 in (("A", 0), ("B", 64)):
                            ps_ = slice(p0, p0 + 64)
                            dhalf = slice(0, 64) if hh == "A" else slice(64, 128)
                            ohalf = slice(64, 128) if hh == "A" else slice(0, 64)
                            Ue, Uek = X("Ue" + hh), K("Ue" + hh)
                            Ve, Vek = X("Ve" + hh), K("Ve" + hh)
                            for src, srck, dstv, dstk in ((X("WT" + hh), K("WT" + hh), u1, "u1"), (X("RbT" + hh), K("RbT" + hh), o0, "o0")):
                                for nb in range(2):
                                    pb, pk = bank()
                                    MM(pb, src, h0r[:, nb * 8:(nb + 1) * 8, :].rearrange("p b v -> p (b v)"), True, True, [srck, "h0r"], [pk])
                                    V(tt(big[:, nb * 8:(nb + 1) * 8, :], pb.rearrange("p (b v) -> p b v", v=64), b3m[:, nb * 8:(nb + 1) * 8, :], ALU.mult),
                                      [pk, "cst"], ["big"])
                                V(lambda dstv=dstv: nc.vector.tensor_reduce(dstv, big.rearrange("p b v -> p v b"), AX.X, ALU.add), ["big"], [dstk])
                            V(tt(Ue[:, dhalf], u1, r32(Yfin[hh][0][:, ohalf]), ALU.add), ["u1", Yfin[hh][1]], [Uek])
                            q, qk = quart()
                            MM(q[:, 0:64], X("Mrb" + hh), Ue[:, dhalf], True, False, [K("Mrb" + hh), Uek], [qk], signal=False)
                            MM(q[:, 0:64], X("Mrk" + hh), Ve[:, dhalf], False, True, [K("Mrk" + hh), Vek], [qk])
                            V(tt(X("otm")[:, dhalf], q[:, 0:64], o0, ALU.add), [qk, "o0"], [K("otm")])
                            ub = r32(Ue[:, dhalf]).unsqueeze(1).broadcast_to([128, 16, 64])
                            vb = r32(Ve[:, dhalf]).unsqueeze(1).broadcast_to([128, 16, 64])
                            V(tt(Ublk, ub, b3m, ALU.mult), [Uek, "cst"], ["Ublk"])
                            V(tt(Vblk, vb, b3m, ALU.mult), [Vek, "cst"], ["Vblk"])
                            for nb in range(2):
                                pb, pk = bank()
                                bs = slice(nb * 8, (nb + 1) * 8)
                                MM(pb, X("Bt_tm"), Ublk[:, bs, :].rearrange("p b v -> p (b v)"), True, False, [K("Bt_tm"), "Ublk"], [pk], signal=False)
                                MM(pb, X("Kt_tm"), Vblk[:, bs, :].rearrange("p b v -> p (b v)"), False, True, [K("Kt_tm"), "Vblk"], [pk])
                                V(tt(hsn[ps_, bs, :], pb[ps_, :].rearrange("p (b v) -> p b v", v=64), r32(h0r)[ps_, bs, :], ALU.add), [pk, "h0r"], ["hsn"])
                                V(tt(hsn[ps_, bs, :], hsn[ps_, bs, :], G3[ps_, bs, :].broadcast_to([64, 8, 64]), ALU.mult), ["hsn", K("G")], ["hsn"])
                        P.dma("sp", DR["wkv_s"][hp].rearrange("p (b v) -> p b v", v=64), hsn, reads=["hsn"], semres="hsn")
                        q, qk = quart()
                        TR(q, X("otm"), [K("otm")], [qk])
                        A(cpa(X("oT"), q), [qk], [K("oT")])
                    yield
                    if own:
                        q, qk = quart()
                        MM(q, blk1b, X("oT"), True, True, ["cstb", K("oT")], [qk])
                        A(act(X("o2"), X("oT"), AF.Square), [K("oT")], [K("o2")])
                        q2, qk2 = quart()
                        MM(q2, blk1b, X("o2"), True, True, ["cstb", K("o2")], [qk2])
                        A(act(X("mean"), q, AF.Identity, scale=1.0 / 64), [qk], [K("mean")])
                        V(tt(X("msq"), X("mean"), X("mean"), ALU.mult), [K("mean")], [K("msq")])
                        V(stt(X("varp"), q2, 1.0 / 64, X("msq"), ALU.mult, ALU.subtract), [qk2, K("msq")], [K("varp")])
                        A(act(X("varp"), X("varp"), AF.Ln, bias=epsc[:, 1:2]), [K("varp"), "epsc"], [K("varp")])
                        A(act(X("varp"), X("varp"), AF.Exp, scale=-0.5), [K("varp")], [K("varp")])
                        yield
                        V(tt(X("cen"), X("oT"), X("mean"), ALU.subtract), [K("oT"), K("mean")], [K("cen")])
                        V(tt(X("cen"), X("cen"), X("varp"), ALU.mult), [K("cen"), K("varp")], [K("cen")])
                        V(ts(X("cen"), X("cen"), sv(5, hp), sv(6, hp), ALU.mult, ALU.add), [K("cen"), "svec"], [K("cen")])
                        V(tt(X("cen"), X("cen"), X("bon"), ALU.add), [K("cen"), K("bon")], [K("cen")])
                        q, qk = quart()
                        MM(q, wgu[:, ch], sg[:, cs], True, True, ["wgu", "sg"], [qk])
                        V(tt(oaT[:, hp, cs], X("cen"), q, ALU.mult), [K("cen"), qk], [("oaT", hp)])
                    if hp == 0 and ti == 0:
                        ckpt("T1" if not own else "T2", [(nm, r32(pt[nm][0]) if pt[nm][0].dtype == F32R else pt[nm][0], [(nm, 0)]) for nm in
                                   ["lw", "cum", "aa", "kkn", "kp", "G", "AbT", "BtT", "KtT", "Bt_tm", "VeA", "VeB", "WT", "UeA", "UeB", "LakA", "YA0", "YA1", "YB1", "oT", "cen"]]
                             + [("Hs", r32(Hs), [("Hs", k) for k in range(8)]), ("zs1", zs[0][1], ["zs0_1"]), ("zs2", zs[0][2], ["zs0_2"])])

                YF = {}

                def rec(gen, qpool, bpool=(0,)):
                    QPOOL[0] = list(qpool)
                    BPOOL[0] = list(bpool)
                    r_ = P.record(gen)
                    QPOOL[0] = [2, 3, 4, 5, 6, 7]
                    BPOOL[0] = [0, 1, 2, 3]
                    return r_

                P.schedule([rec(prep_gen(0), (0, 1))])
                prev = None
                for ti in range(ntiles):
                    streams = []
                    if prev is not None:
                        streams.append(rec(chainpost_gen(prev), (2, 3) if own else (2,)))
                    hg = scan_gens(ti)
                    if len(hg) == 1:
                        streams.append(rec(hg[0], (4, 5, 6, 7)))
                    else:
                        streams.append(rec(hg[0], (4, 5)))
                        streams.append(rec(hg[1], (6, 7)))
                    if ti + 1 < ntiles:
                        streams.append(rec(prep_gen(ti + 1), (0, 1)))
                    if not own:
                        if 1 <= ti <= 4:
                            streams.append(rec(ada_mm_gen(16 + hp * 4 + ti - 1), (3,), (3,)))
                        if ti <= 3:
                            streams.append(rec(ada_dma_gen(16 + hp * 4 + ti), (3,), (3,)))
                    P.schedule(streams)
                    prev = ti
                P.schedule([rec(chainpost_gen(prev), (2, 3) if own else (2,))])
            if not own:
                for dc_ in range(16):
                    V(ts(gf[:, dc_, :], modT[:, 64 + dc_, :], 1.0, gvec[:, 16 + dc_:17 + dc_], ALU.add, ALU.mult), [("modT", 4), "gvec"], ["gf"])
            P.barrier()
            ckpt("C1" if not own else "C2", [("Hs", r32(Hs), [("Hs", k) for k in range(8)]), ("oaT", oaT, [("oaT", k) for k in range(8)])])

    mixer_pass(False)
    mixer_pass(True)
    P.dma("sp", DR["shp"], shp, reads=["shp"], semres="shp")
    P.dma("sp", DR["shs"].rearrange("p (j b) -> p j b", b=16), shs, reads=["shs"], semres="shs")
    P.barrier()
    es_mp.close()

    prodT = es_mix.enter_context(_sbt(nc, "prodT", [128, 8, T], BF16)).ap()
    with ExitStack() as es:
        al = lambda name, shape, dt=F32: es.enter_context(_sbt(nc, name, list(shape), dt)).ap()
        wsm = al("wsm", [128, 16, 128], BF16)
        with ExitStack() as es2:
            wsf = es2.enter_context(_sbt(nc, "wsf", [128, 8, 128], F32)).ap()
            P.dma("sp", wsf, DR["w_spT"].rearrange("g s t -> s g t"), writes=["wsf"])
            for g in range(8):
                V(tt(wsm[:, g, :], wsf[:, g, :], m_le, ALU.mult), ["wsf", "cst"], ["wsm"])
            P.dma("sp", wsf, DR["w_spTs"].rearrange("g s t -> s g t"), writes=["wsf"])
            for g in range(8):
                V(tt(wsm[:, 8 + g, :], wsf[:, g, :], ms_le, ALU.mult), ["wsf", "cst"], ["wsm"])
            P.barrier()
        gv = al("gv", [128, 9, 1024], BF16)
        vnb = gv
        lngb = al("lngb", [128, 1024]); lnbb = al("lnbb", [128, 1024])
        P.dma("sp", lngb, DR["ln_g"].partition_broadcast(128), writes=["lngb"])
        P.dma("sp", lnbb, DR["ln_b"].partition_broadcast(128), writes=["lnbb"])
        bspb = al("bspb", [128, 8, 128]); bspsb = al("bspsb", [128, 8, 128])
        P.dma("sp", bspb, DR["bsp"].rearrange("g t -> (g t)").partition_broadcast(128).rearrange("p (g t) -> p g t", t=128), writes=["bspb"])
        P.dma("sp", bspsb, DR["bsps"].rearrange("g t -> (g t)").partition_broadcast(128).rearrange("p (g t) -> p g t", t=128), writes=["bspsb"])
        for blk in range(4):
            wt, wk = load_w(DR["w_in"][:, 4352 + blk * 256: 4352 + (blk + 1) * 256], 16, 256)
            for ti in range(9):
                pb, pk = bank()
                for kc in range(16):
                    MM(pb[:, 0:256], hT[:, kc, ti * 128:(ti + 1) * 128], wt[:, kc, 0:256], kc == 0, kc == 15, [wk, ("hT", kc)], [pk], signal=(kc == 15))
                A(act(gv[:, ti, blk * 256:(blk + 1) * 256], pb[:, 0:256], AF.Gelu_apprx_tanh), [pk], [("gv", ti)])
        st6 = al("st6", [128, 12]); mv = al("mv", [128, 2]); vtmp = [al(f"vtmp{i}", [128, 1024]) for i in range(1)]
        for ti in range(9):
            s = 0
            V(lambda: nc.vector.bn_stats(st6[:, 0:6], gv[:, ti, 0:512]), [("gv", ti)], ["st6"])
            V(lambda: nc.vector.bn_stats(st6[:, 6:12], gv[:, ti, 512:1024]), [("gv", ti)], ["st6"])
            V(lambda: nc.vector.bn_aggr(mv, st6), ["st6"], ["mv"])
            A(act(mv[:, 1:2], mv[:, 1:2], AF.Sqrt, bias=epsc[:, 2:3]), ["mv", "epsc"], ["mv"])
            V(rcp(mv[:, 1:2], mv[:, 1:2]), ["mv"], ["mv"])
            V(ts(vtmp[s], gv[:, ti, :], mv[:, 0:1], mv[:, 1:2], ALU.subtract, ALU.mult), [("gv", ti), "mv"], [f"vtmp{s}"])
            V(tt(vtmp[s], vtmp[s], lngb, ALU.mult), [f"vtmp{s}", "lngb"], [f"vtmp{s}"])
            V(tt(vtmp[s], vtmp[s], lnbb, ALU.add), [f"vtmp{s}", "lnbb"], [f"vtmp{s}"])
            A(cpa(vnb[:, ti, :], vtmp[s]), [f"vtmp{s}"], [("gv", ti)])
            if ti == 8:
                P.dma("sp", DR["v_s"], vtmp[s], reads=[f"vtmp{s}"], semres=f"vtmp{s}")
        uT = [al(f"uT{i}", [128, T]) for i in range(1)]
        mx = [al(f"mx{i}", [128, 128]) for i in range(2)]
        for blk in range(4):
            wt, wk = load_w(DR["w_in"][:, 3328 + blk * 256: 3328 + (blk + 1) * 256], 16, 256)
            for jj in range(2):
                g = blk * 2 + jj
                us, uk = uT[0], "uT0"
                for g0 in range(0, T, 384):
                    dense_fm(wt, wk, jj * 128, 16, hT_fn, hT_keys, g0, 384,
                             lambda ps, pk, g0=g0: A(act(us[:, g0:g0 + 384], ps, AF.Gelu_apprx_tanh), [pk], [uk]))
                for ti in range(9):
                    wsel = g if ti < 8 else 8 + g
                    bsel = bspb if ti < 8 else bspsb
                    q, qk = quart()
                    MM(q, vnb[:, ti, g * 128:(g + 1) * 128], wsm[:, wsel, :], True, True, [("gv", ti), "wsm"], [qk])
                    m_ = mx[ti % 2]; mk = f"mx{ti % 2}"
                    V(tt(m_, q, bsel[:, g, :], ALU.add), [qk, "bspb", "bspsb"], [mk])
                    V(tt(prodT[:, g, ti * 128:(ti + 1) * 128], m_, us[:, ti * 128:(ti + 1) * 128], ALU.mult), [mk, uk], [("prodT", g)])
        P.barrier()
        ckpt("D", [("prodT", prodT, [("prodT", k) for k in range(8)]), ("gv", gv, [("gv", k) for k in range(9)])])

    es_mg = ExitStack()
    merged = es_mg.enter_context(_sbt(nc, "merged", [128, 16, T], BF16)).ap()
    with ExitStack() as es:
        al = lambda name, shape, dt=F32: es.enter_context(_sbt(nc, name, list(shape), dt)).ap()
        wba = al("wba", [128, 8, 256], BF16)
        sa = [al(f"sa{i}", [128, 384]) for i in range(1)]
        sb_ = [al(f"sb{i}", [128, 384]) for i in range(1)]
        wbb = al("wbb", [128, 8, 256], BF16)
        cnt = 0
        for blk in range(8):
            wta, wka = wslot()
            wtb, wkb = wslot()
            P.dma("pool", wta[:, :, 0:256], DR["w_in"][:, 5376 + blk * 256:5376 + (blk + 1) * 256].rearrange("(kc p) n -> p kc n", p=128), writes=[wka])
            P.dma("pool", wtb[:, :, 0:256], DR["w_in"][:, 7424 + blk * 256:7424 + (blk + 1) * 256].rearrange("(kc p) n -> p kc n", p=128), writes=[wkb])
            P.dma("pool", wba, DR["w_branch_a"][:, blk * 256:(blk + 1) * 256].rearrange("(kc p) n -> p kc n", p=128), writes=["wba"])
            P.dma("pool", wbb, DR["w_branch_b"][:, blk * 256:(blk + 1) * 256].rearrange("(kc p) n -> p kc n", p=128), writes=["wbb"])
            for jj in range(2):
                dc = blk * 2 + jj
                for g0 in range(0, T, 384):
                    s = 0
                    cs = slice(g0, g0 + 384)
                    dense_fm(wta, wka, jj * 128, 16, hT_fn, hT_keys, g0, 384,
                             lambda ps, pk: A(act(sa[s], ps, AF.Sigmoid), [pk], [f"sa{s}"]))
                    dense_fm(wtb, wkb, jj * 128, 16, hT_fn, hT_keys, g0, 384,
                             lambda ps, pk: A(act(sb_[s], ps, AF.Sigmoid), [pk], [f"sb{s}"]))
                    dense_fm(wba, "wba", jj * 128, 8, lambda kc: oaT[:, kc, :], lambda kc: [("oaT", kc)], g0, 384,
                             lambda ps, pk: V(tt(sa[s], sa[s], ps, ALU.mult), [f"sa{s}", pk], [f"sa{s}"]))
                    dense_fm(wbb, "wbb", jj * 128, 8, lambda kc: prodT[:, kc, :], lambda kc: [("prodT", kc)], g0, 384,
                             lambda ps, pk: V(tt(sb_[s], sb_[s], ps, ALU.mult), [f"sb{s}", pk], [f"sb{s}"]))
                    V(tt(merged[:, dc, cs], sa[s], sb_[s], ALU.add), [f"sa{s}", f"sb{s}"], [("merged", dc)])
        P.barrier()
        ckpt("E", [("merged", merged, [("merged", k) for k in range(16)])])
    es_mix.close()
    es_x = ExitStack()
    x1 = es_x.enter_context(_sbt(nc, "x1", [128, 16, T], F32)).ap()

    def resid_evac(ps, pk, dc, g0, n, gate_row):
        npr = max(0, min(TP, g0 + n) - g0)
        if npr > 0:
            V(stt(x1[:, dc, g0:g0 + npr], ps[:, 0:npr], modT[:, gate_row + dc, 0:1], x1[:, dc, g0:g0 + npr], ALU.mult, ALU.add),
              [pk, *MODK, ("x1", dc)], [("x1", dc)])
        if g0 + n > TP:
            a0 = max(g0, TP)
            v3 = lambda a: a.rearrange("p (b t) -> p b t", t=8)
            nb0 = (a0 - TP) // 8
            nb = (g0 + n - a0) // 8
            gb = modT[:, gate_row + dc, 1 + nb0:1 + nb0 + nb].unsqueeze(2).broadcast_to([128, nb, 8])
            V(tt(v3(ps[:, a0 - g0:n]), v3(ps[:, a0 - g0:n]), gb, ALU.mult), [pk, *MODK], [pk])
            V(tt(x1[:, dc, a0:g0 + n], ps[:, a0 - g0:n], x1[:, dc, a0:g0 + n], ALU.add), [pk, ("x1", dc)], [("x1", dc)])

    for blk in range(8):
        wt, wk = load_w(DR["w_out"][:, blk * 256:(blk + 1) * 256], 16, 256)
        for jj in range(2):
            dc = blk * 2 + jj
            P.dma("sp", x1[:, dc, :], DR["xT_own"][dc * 128:(dc + 1) * 128, :], writes=[("x1", dc)])
            for g0 in range(0, T, 384):
                dense_fm(wt, wk, jj * 128, 16, lambda kc: merged[:, kc, :], lambda kc: [("merged", kc)], g0, 384,
                         lambda ps, pk, dc=dc, g0=g0: resid_evac(ps, pk, dc, g0, 384, 32))
    P.barrier()
    ckpt("F", [("x1", x1, [("x1", k) for k in range(16)])])
    es_mg.close()

    with ExitStack() as es:
        al = lambda name, shape, dt=F32: es.enter_context(_sbt(nc, name, list(shape), dt)).ap()
        h2T = al("h2T", [128, 16, T], BF16)
        sq = [al(f"sq{i}", [128, 512], F32R) for i in range(2)]
        rstd = al("rstd", [128, 512]); tmpn = rstd
        tq = [al(f"tq{i}", [128, 512]) for i in range(2)]
        for g0 in range(0, T, 512):
            n = min(512, T - g0)
            rms_rstd((sq, rstd, tmpn), lambda dc: x1[:, dc, g0:g0 + n], n, lambda dc: [("x1", dc)])
            for dc in range(16):
                s = dc % 2
                if g0 >= TP:
                    b3 = lambda a: a.unsqueeze(2).broadcast_to([128, 16, 8])
                    v3 = lambda a: a.rearrange("p (b t) -> p b t", t=8)
                    V(tt(v3(tq[s][:, 0:n]), v3(x1[:, dc, g0:g0 + n]), b3(gf[:, dc, 1:17]), ALU.mult), [("x1", dc), "gf"], [f"tq{s}"])
                    V(tt(tq[s][:, 0:n], tq[s][:, 0:n], rstd[:, 0:n], ALU.mult), [f"tq{s}", "rstd"], [f"tq{s}"])
                    V(tt(v3(h2T[:, dc, g0:g0 + n]), v3(tq[s][:, 0:n]), b3(modT[:, 48 + dc, 1:17]), ALU.add), [f"tq{s}", *MODK], [("h2T", dc)])
                else:
                    V(stt(tq[s][:, 0:n], x1[:, dc, g0:g0 + n], gf[:, dc, 0:1], rstd[:, 0:n], ALU.mult, ALU.mult), [("x1", dc), "gf", "rstd"], [f"tq{s}"])
                    A(act(h2T[:, dc, g0:g0 + n], tq[s][:, 0:n], AF.Identity, bias=modT[:, 48 + dc, 0:1]), [f"tq{s}", *MODK], [("h2T", dc)])
        actT = al("actT", [128, 4, T], BF16)
        sl = [al(f"sl{i}", [128, 384]) for i in range(1)]
        wfo = [al(f"wfo{i}", [128, 4, 256], BF16) for i in range(2)]
        cnt = 0
        for qd in range(11):
            for jj in range(4):
                j = qd * 4 + jj
                wt, wk = wslot()
                P.dma("pool", wt[:, :, 0:128], DR["w_ffn_in"][:, j * 128:(j + 1) * 128].rearrange("(kc p) n -> p kc n", p=128), writes=[wk])
                P.dma("pool", wt[:, :, 128:256], DR["w_ffn_in"][:, DFF + j * 128:DFF + (j + 1) * 128].rearrange("(kc p) n -> p kc n", p=128), writes=[wk])
                for g0 in range(0, T, 384):
                    s = 0
                    dense_fm(wt, wk, 0, 16, lambda kc: h2T[:, kc, :], lambda kc: [("h2T", kc)], g0, 384,
                             lambda ps, pk: A(act(sl[s], ps, AF.Silu), [pk], [f"sl{s}"]))
                    dense_fm(wt, wk, 128, 16, lambda kc: h2T[:, kc, :], lambda kc: [("h2T", kc)], g0, 384,
                             lambda ps, pk, g0=g0, jj=jj: V(tt(actT[:, jj, g0:g0 + 384], sl[s], ps, ALU.mult), [f"sl{s}", pk], [("actT", jj)]))
            for blk in range(8):
                wo, wok = wfo[blk % 2], f"wfo{blk % 2}"
                P.dma("pool", wo, DR["w_ffn_out"][qd * 512:(qd + 1) * 512, blk * 256:(blk + 1) * 256].rearrange("(kc p) n -> p kc n", p=128), writes=[wok])
                for jj2 in range(2):
                    dc = blk * 2 + jj2
                    for g0 in range(0, T, 384):
                        dense_fm(wo, wok, jj2 * 128, 4, lambda kc: actT[:, kc, :], lambda kc: [("actT", kc)], g0, 384,
                                 lambda ps, pk, dc=dc, g0=g0: resid_evac(ps, pk, dc, g0, 384, 80))
        yo = tq
        for g0 in range(0, T, 512):
            n = min(512, T - g0)
            rms_rstd((sq, rstd, tmpn), lambda dc: x1[:, dc, g0:g0 + n], n, lambda dc: [("x1", dc)])
            for dc in range(16):
                s = dc % 2
                V(stt(yo[s][:, 0:n], x1[:, dc, g0:g0 + n], gvec[:, 32 + dc:33 + dc], rstd[:, 0:n], ALU.mult, ALU.mult), [("x1", dc), "gvec", "rstd"], [f"tq{s}"])
                P.dma("sp", DR["yT"][dc * 128:(dc + 1) * 128, g0:g0 + n], yo[s][:, 0:n], reads=[f"tq{s}"], semres=f"tq{s}")
        P.finish("sp")
    es_x.close()


def _consts():
    i = np.arange(128)
    r, c = i[:, None], i[None, :]
    same = (r // 8) == (c // 8)
    f = lambda m: m.astype(np.float32)
    parts = [np.eye(128, dtype=np.float32), f(r < c), f(r > c), f(r <= c),
             f((r < c) & same), f((r > c) & same), f((r <= c) & same),
             f(np.broadcast_to((c % 8) != 0, (128, 128))), f((r // 64) == (c // 64)), np.ones((128, 128), np.float32),
             f((r // 8) == np.arange(16)[None, :])]
    return np.ascontiguousarray(np.concatenate(parts, axis=1))


def _constsb():
    import ml_dtypes
    i = np.arange(128)
    r, c = i[:, None], i[None, :]
    parts = []
    for b in (8, 16, 32, 64):
        parts.append(((r // (2 * b)) == (c // (2 * b))) & ((r // b) != (c // b)) & (r > c))
    parts = parts + [p.T for p in parts]
    parts.append((r // 64) == (c // 64))
    return np.ascontiguousarray(np.concatenate(parts, axis=1).astype(np.float32).astype(ml_dtypes.bfloat16))


def _col(v, n):
    return np.ascontiguousarray(np.asarray(v, np.float32).reshape(n, 128).T)


_NC_CACHE = {}


def _prep(x_prompt, x_sample, state_wkv, state_shift, c_prompt, c_sample,
           w_ada, b_ada, norm_mix_g, w_in, mu_shift, w0, w_decay_up, a0, w_aaa_up,
           w_gate_up, k_k, k_a, r_k, gn_g, gn_b, ln_v_g, ln_v_b, w_spatial, b_spatial,
           w_branch_a, w_branch_b, w_out, norm_ffn_g, w_ffn_in, w_ffn_out, norm_final_g):
    f = lambda a: np.ascontiguousarray(np.asarray(a, np.float32))
    x_prompt, x_sample = f(x_prompt), f(x_sample)
    state_wkv, state_shift = f(state_wkv)[0], f(state_shift)[0]
    c_prompt, c_sample = f(c_prompt), f(c_sample)
    shared = {
        "w_ada": f(w_ada)[0], "badaT": _col(f(b_ada)[0], 96),
        "gvec": np.concatenate([_col(f(norm_mix_g)[0], 16), _col(f(norm_ffn_g)[0], 16), _col(f(norm_final_g), 16)], 1),
        "w_in": f(w_in)[0],
        "lora_up": np.ascontiguousarray(np.concatenate([f(w_decay_up)[0], f(w_aaa_up)[0]], 0)),
        "w_gate_up": f(w_gate_up)[0], "ln_g": f(ln_v_g)[0], "ln_b": f(ln_v_b)[0],
        "w_branch_a": f(w_branch_a)[0], "w_branch_b": f(w_branch_b)[0], "w_out": f(w_out)[0],
        "w_ffn_in": f(w_ffn_in)[0], "w_ffn_out": f(w_ffn_out)[0], "cst": _consts(), "cstb": _constsb(),
    }
    mu = f(mu_shift)[0]
    ka = f(k_a)[0]
    vecs = [f(w0)[0], f(a0)[0], f(k_k)[0], ka, ka, f(gn_g)[0], f(gn_b)[0], f(r_k)[0].reshape(-1), ka]
    sv = [_col(mu, 26)] + [_col(v, 8) for v in vecs]
    shared["svec"] = np.ascontiguousarray(np.concatenate(sv, 1))
    wsp = f(w_spatial)[0]
    shared["w_spT"] = np.ascontiguousarray(wsp.transpose(0, 2, 1))
    blkT = np.zeros((8, 128, 128), np.float32)
    for b in range(16):
        blkT[:, b * 8:(b + 1) * 8, b * 8:(b + 1) * 8] = wsp[:, :8, :8].transpose(0, 2, 1)
    shared["w_spTs"] = blkT
    bsp = f(b_spatial)[0]
    shared["bsp"] = bsp
    shared["bsps"] = np.ascontiguousarray(np.tile(bsp[:, :8], (1, 16)))
    in_maps = []
    for c in range(8):
        b, half = c // 2, c % 2
        xs = x_sample[16 * c:16 * (c + 1)].reshape(128, D)
        xo = np.concatenate([x_prompt[b, half * 1024:(half + 1) * 1024], xs], 0)
        xp = x_prompt[b, 0:1024]
        cc = np.concatenate([c_prompt[b:b + 1], c_sample[16 * c:16 * (c + 1)]], 0)
        sw = state_wkv[16 * c:16 * (c + 1)]
        s0T = sw.reshape(16, 8, 2, 64, 64).transpose(1, 2, 4, 0, 3).reshape(8, 128, 16 * 64)
        ssh = state_shift[16 * c:16 * (c + 1)]
        sshT = ssh.reshape(16, 26, 128).transpose(2, 1, 0).reshape(128, 26 * 16)
        m = dict(shared)
        m.update({"xT_own": np.ascontiguousarray(xo.T), "xT_prev": np.ascontiguousarray(xp.T),
                  "cT": np.ascontiguousarray(cc.T), "flag": np.full((128, 1), float(half), np.float32),
                  "s0T": np.ascontiguousarray(s0T), "sshT": np.ascontiguousarray(sshT)})
        in_maps.append(m)
    return in_maps


def kernel(x_prompt, x_sample, state_wkv, state_shift, c_prompt, c_sample,
           w_ada, b_ada, norm_mix_g, w_in, mu_shift, w0, w_decay_up, a0, w_aaa_up,
           w_gate_up, k_k, k_a, r_k, gn_g, gn_b, ln_v_g, ln_v_b, w_spatial, b_spatial,
           w_branch_a, w_branch_b, w_out, norm_ffn_g, w_ffn_in, w_ffn_out, norm_final_g):
    in_maps = _prep(x_prompt, x_sample, state_wkv, state_shift, c_prompt, c_sample,
                    w_ada, b_ada, norm_mix_g, w_in, mu_shift, w0, w_decay_up, a0, w_aaa_up,
                    w_gate_up, k_k, k_a, r_k, gn_g, gn_b, ln_v_g, ln_v_b, w_spatial, b_spatial,
                    w_branch_a, w_branch_b, w_out, norm_ffn_g, w_ffn_in, w_ffn_out, norm_final_g)
    if "nc" not in _NC_CACHE:
        _NC_CACHE["nc"] = build_nc()
    res = run_bass_kernel_spmd(_NC_CACHE["nc"], in_maps, core_ids=list(range(8)))
    R = res.results
    y_prompt = np.zeros((4, 2048, D), np.float32); y_sample = np.zeros((128, 8, D), np.float32)
    wkv_p = np.zeros((1, 4, 16, 64, 64), np.float32); shift_p = np.zeros((1, 4, 3328), np.float32)
    wkv_s = np.zeros((1, 128, 16, 64, 64), np.float32); shift_s = np.zeros((1, 128, 3328), np.float32)
    v_s = np.zeros((1, 128, 8, 1024), np.float32)
    for c in range(8):
        b, half = c // 2, c % 2
        yT = R[c]["yT"]
        y_prompt[b, half * 1024:(half + 1) * 1024] = yT[:, :1024].T
        y_sample[16 * c:16 * (c + 1)] = yT[:, 1024:].T.reshape(16, 8, D)
        if half == 1:
            hp = R[c]["wkv_p"]
            for p in range(8):
                for h2 in range(2):
                    wkv_p[0, b, 2 * p + h2] = hp[p, h2 * 64:(h2 + 1) * 64, h2 * 64:(h2 + 1) * 64].T
            shift_p[0, b] = R[c]["shp"].T.reshape(-1)
        ws = R[c]["wkv_s"].reshape(8, 2, 64, 16, 64)
        wkv_s[0, 16 * c:16 * (c + 1)] = ws.transpose(3, 0, 1, 4, 2).reshape(16, 16, 64, 64)
        shift_s[0, 16 * c:16 * (c + 1)] = R[c]["shs"].reshape(128, 26, 16).transpose(2, 1, 0).reshape(16, 3328)
        v_s[0, 16 * c:16 * (c + 1)] = R[c]["v_s"].reshape(16, 8, 1024)
    return (y_prompt, y_sample, wkv_p, shift_p, wkv_s, shift_s, v_s)
```

```python
import numpy as np
import concourse.bass as bass
import concourse.mybir as mybir
from concourse.bass_utils import run_bass_kernel_spmd
from contextlib import ExitStack

F32 = mybir.dt.float32
F32R = mybir.dt.float32r
BF16 = mybir.dt.bfloat16
AF = mybir.ActivationFunctionType
ALU = mybir.AluOpType
AX = mybir.AxisListType

EPOCH = 8192
D = 2048
T = 1152
TP = 1024
DFF = 5632
CIN = 9472
NCST = 10 * 128 + 16


class Prog:
    def __init__(self, nc):
        self.nc = nc
        self.eng = {"pe": nc.tensor, "act": nc.scalar, "dve": nc.vector,
                    "pool": nc.gpsimd, "sp": nc.sync}
        self.cnt = {e: 0 for e in self.eng}
        self.sems = {e: [] for e in self.eng}
        self.seen = {e: {} for e in self.eng}
        self.last_w = {}
        self.readers = {}
        self.dma_sems = {}
        self.pend_r = {e: [] for e in self.eng}
        self.pend_w = {e: [] for e in self.eng}
        self.nsem = 0
        self.ninstr = {e: 0 for e in self.eng}
        self.rec = None
        self.m_eng = {e: 0.0 for e in self.eng}
        self.m_key = {}

    def record(self, gen):
        self.rec = []
        for _ in gen:
            pass
        r, self.rec = self.rec, None
        return r

    def schedule(self, streams):
        from collections import Counter
        DUR = {"pe": 0.2, "act": 0.25, "dve": 0.22, "pool": 0.5, "sp": 0.05}
        LAT = 0.3
        isps = lambda k: isinstance(k, str) and k.startswith("psb")
        units = []
        for st in streams:
            us, cur = [], []
            for o in st:
                cur.append(o)
                if o[5]:
                    us.append(cur)
                    cur = []
            if cur:
                us.append(cur)
            units.append(us)
        pend_r = [Counter(k for u in us for o in u for k in o[3] if not isps(k)) for us in units]
        pend_w = [Counter(k for u in us for o in u for k in o[4] if not isps(k)) for us in units]
        idx = [0] * len(units)
        while True:
            best = None
            for j, us in enumerate(units):
                if idx[j] >= len(us):
                    continue
                u = us[idx[j]]
                keys = [k for o in u for k in (o[3] + o[4])]
                rk_ = [k for o in u for k in o[3] if not isps(k)]
                wk_ = [k for o in u for k in o[4] if not isps(k)]
                if any(pend_w[i][k] > 0 for k in rk_ for i in range(j)) or \
                   any(pend_w[i][k] > 0 or pend_r[i][k] > 0 for k in wk_ for i in range(j)):
                    continue
                F = u[0][1]
                t_ready = max([self.m_key.get(k, 0.0) for k in keys] + [0.0])
                start = max(self.m_eng[F], t_ready)
                if best is None or start < best[0]:
                    best = (start, j, u, keys, F)
            if best is None:
                break
            start, j, u, keys, F = best
            t = start
            for o in u:
                if o[0] == "op":
                    self.op(o[1], o[2], o[3], o[4], o[5])
                    t += (o[6] if len(o) > 6 and o[6] else DUR[o[1]])
                else:
                    out, in_, semres, kw = o[2]
                    self.dma(o[1], out, in_, o[3], o[4], semres, **kw)
                    t += DUR["sp"]
            self.m_eng[F] = t
            fin = t + LAT + ((u[0][6] or 2.0) if u[0][0] == "dma" else 0.0)
            for o in u:
                for k in o[4] + [k2 for k2 in o[3] if isps(k2)]:
                    self.m_key[k] = fin
            for o in u:
                for k in o[3]:
                    if not isps(k):
                        pend_r[j][k] -= 1
                for k in o[4]:
                    if not isps(k):
                        pend_w[j][k] -= 1
            idx[j] += 1

    def _newsem(self, name):
        self.nsem += 1
        return self.nc.alloc_semaphore(name)

    def _deps(self, F, reads, writes):
        deps = {}

        def add(tok, same_ok):
            if tok is None:
                return
            key, sem, val, eng = tok
            if eng == F and F == "pe":
                return
            if val > deps.get(key, (None, 0))[1]:
                deps[key] = (sem, val)

        for r in reads:
            add(self.last_w.get(r), True)
        for w in writes:
            add(self.last_w.get(w), True)
            for t in self.readers.get(w, ()):
                add(t, False)
        return deps

    def _emit_waits(self, F, deps):
        e = self.eng[F]
        for key, (sem, val) in deps.items():
            if self.seen[F].get(key, 0) >= val:
                continue
            e.wait_ge(sem, val)
            self.seen[F][key] = val

    def _register(self, tok, reads, writes):
        for r in reads:
            lst = self.readers.setdefault(r, [])
            lst[:] = [t for t in lst if t[0] != tok[0]]
            lst.append(tok)
        for w in writes:
            self.last_w[w] = tok
            self.readers[w] = []

    def op(self, F, fn, reads=(), writes=(), signal=True):
        reads = list(reads)
        writes = list(writes)
        if self.rec is not None:
            self.rec.append(("op", F, fn, reads, writes, signal, None))
            return None
        ex = [r for r in reads if isinstance(r, str) and r.startswith("psb")]
        if ex:
            reads = [r for r in reads if r not in ex]
            writes = writes + [r for r in ex if r not in writes]
        deps = self._deps(F, reads, writes)
        self._emit_waits(F, deps)
        ins = fn()
        self.ninstr[F] += 1
        if not signal:
            self.pend_r[F] += reads
            self.pend_w[F] += writes
            return ins
        i = self.cnt[F]
        self.cnt[F] += 1
        ep = i // EPOCH
        while len(self.sems[F]) <= ep:
            self.sems[F].append(self._newsem(f"s_{F}_{len(self.sems[F])}"))
        sem = self.sems[F][ep]
        val = i % EPOCH + 1
        ins.then_inc(sem, 1)
        tok = ((F, ep), sem, val, F)
        self._register(tok, reads + self.pend_r[F], writes + self.pend_w[F])
        self.pend_r[F] = []
        self.pend_w[F] = []
        return ins

    def dma(self, Q, out, in_, reads=(), writes=(), semres=None, est=None, **kw):
        reads = list(reads)
        writes = list(writes)
        if self.rec is not None:
            self.rec.append(("dma", Q, (out, in_, semres, kw), reads, writes, True, est))
            return None
        if semres is None:
            semres = (writes + reads)[0]
        deps = self._deps("dma", reads, writes)
        self._emit_waits(Q, deps)
        if semres not in self.dma_sems:
            self.dma_sems[semres] = [self._newsem(f"d_{len(self.dma_sems)}"), 0]
        ent = self.dma_sems[semres]
        ent[1] += 16
        ins = self.eng[Q].dma_start(out=out, in_=in_, **kw)
        ins.then_inc(ent[0], 16)
        self.ninstr[Q] += 1
        tok = (("dma", semres), ent[0], ent[1], "dma")
        self._register(tok, reads, writes)
        return ins

    def barrier(self):
        for F, e in self.eng.items():
            for semres, (sem, val) in self.dma_sems.items():
                key = ("dma", semres)
                if val > self.seen[F].get(key, 0):
                    e.wait_ge(sem, val)
                    self.seen[F][key] = val
            for E in self.eng:
                if E == F or self.cnt[E] == 0:
                    continue
                i = self.cnt[E] - 1
                key = (E, i // EPOCH)
                val = i % EPOCH + 1
                if val > self.seen[F].get(key, 0):
                    e.wait_ge(self.sems[E][i // EPOCH], val)
                    self.seen[F][key] = val

    def finish(self, F="sp"):
        e = self.eng[F]
        for semres, (sem, val) in self.dma_sems.items():
            if val > 0:
                e.wait_ge(sem, val)
        for E in self.eng:
            if self.cnt[E] > 0:
                i = self.cnt[E] - 1
                e.wait_ge(self.sems[E][i // EPOCH], i % EPOCH + 1)


_UC = [0]


def _uname(name):
    _UC[0] += 1
    return f"t{_UC[0]}_{name}"


class _Arena:
    def __init__(self):
        self.ap = None
        self.free = []

    def init(self, nc, name="arena", dt=F32, n=None):
        if n is None:
            nbytes = int(nc.sbuf_bytes_remaining) - 1024
            n = nbytes // 4
        self.ap = nc.alloc_sbuf_tensor(name, [128, n], dt).ap()
        self.free = [(0, n)]
        self.n = n

    def alloc(self, words):
        words = (words + 15) // 16 * 16
        for i, (st, sz) in enumerate(self.free):
            if sz >= words:
                if sz == words:
                    self.free.pop(i)
                else:
                    self.free[i] = (st + words, sz - words)
                return st, words
        raise MemoryError(f"arena full: need {words} words, free={self.free}")

    def release(self, st, words):
        self.free.append((st, words))
        self.free.sort()
        out = []
        for a, b in self.free:
            if out and out[-1][0] + out[-1][1] == a:
                out[-1] = (out[-1][0], out[-1][1] + b)
            else:
                out.append((a, b))
        self.free = out


_AR = _Arena()
_ARR = _Arena()


class _Tile:
    def __init__(self, shape, dt):
        self.shape = list(shape)
        self.dt = dt

    def __enter__(self):
        esz = 2 if self.dt == BF16 else 4
        per = 1
        for d in self.shape[1:]:
            per *= d
        words = (per * esz + 3) // 4
        self.ar = _ARR if self.dt == F32R else _AR
        self.st, self.words = self.ar.alloc(words)
        v = self.ar.ap[:, self.st:self.st + words]
        if self.dt == BF16:
            v = v.bitcast(self.dt)
        v = v[:, 0:per]
        if len(self.shape) == 3:
            v = v.rearrange("p (a b) -> p a b", b=self.shape[2])
        elif len(self.shape) == 4:
            v = v.rearrange("p (a b c) -> p a b c", b=self.shape[2], c=self.shape[3])
        if self.shape[0] != 128:
            v = v[0:self.shape[0]]
        self._ap = v
        return self

    def ap(self):
        return self._ap

    def __exit__(self, *a):
        self.ar.release(self.st, self.words)
        return False


def _sbt(nc, name, shape, dt):
    return _Tile(shape, dt)


def r32(ap):
    return ap.bitcast(F32)


class _Stop(Exception):
    pass


def build_nc(stop=None):
    nc = bass.Bass("TRN2", target_bir_lowering=False)
    P = Prog(nc)
    DR = {}
    try:
        _build(nc, P, DR, stop)
    except _Stop:
        pass
    print("ninstr", P.ninstr, "nsem", P.nsem, flush=True)
    return nc


def _build(nc, P, DR, stop):
    def ckpt(tag, dumps):
        if stop != tag:
            return
        for name, ap, keys in dumps:
            d = nc.dram_tensor("dbg_" + name, list(ap.shape), ap.dtype, kind="ExternalOutput").ap()
            P.dma("sp", d, ap, reads=keys, semres=("dbg", name))
        P.finish("sp")
        raise _Stop()

    cpa = lambda o, i: (lambda: nc.scalar.copy(o, i))
    cpv = lambda o, i: (lambda: nc.vector.tensor_copy(o, i))
    rcp = lambda o, i: (lambda: nc.vector.reciprocal(o, i))
    scn = lambda o, d0, d1, init, o0, o1: (lambda: nc.vector.tensor_tensor_scan(o, d0, d1, init, o0, o1))

    def din(name, shape):
        DR[name] = nc.dram_tensor(name, list(shape), F32, kind="ExternalInput").ap()

    def dout(name, shape):
        DR[name] = nc.dram_tensor(name, list(shape), F32, kind="ExternalOutput").ap()

    din("xT_own", [D, T]); din("xT_prev", [D, TP]); din("cT", [D, 17]); din("flag", [128, 1])
    din("s0T", [8, 128, 16 * 64]); din("sshT", [128, 26 * 16])
    din("w_ada", [D, 6 * D]); din("badaT", [128, 96])
    din("gvec", [128, 48])
    din("w_in", [D, CIN]); din("svec", [128, 26 + 9 * 8])
    din("lora_up", [128, 1024]); din("w_gate_up", [128, 1024])
    din("ln_g", [1024]); din("ln_b", [1024])
    din("w_spT", [8, 128, 128]); din("w_spTs", [8, 128, 128]); din("bsp", [8, 128]); din("bsps", [8, 128])
    din("w_branch_a", [1024, D]); din("w_branch_b", [1024, D]); din("w_out", [D, D])
    din("w_ffn_in", [D, 2 * DFF]); din("w_ffn_out", [DFF, D]); din("cst", [128, NCST])
    DR["cstb"] = nc.dram_tensor("cstb", [128, 1152], BF16, kind="ExternalInput").ap()
    dout("yT", [D, T]); dout("wkv_p", [8, 128, 128]); dout("shp", [128, 26])
    dout("wkv_s", [8, 128, 16 * 64]); dout("shs", [128, 26 * 16]); dout("v_s", [128, 1024])

    def sbp(name, shape, dt=F32):
        return nc.alloc_sbuf_tensor(_uname(name), list(shape), dt).ap()

    cst = sbp("cst", [128, NCST])
    P.dma("sp", cst, DR["cst"], writes=["cst"])
    ident = cst[:, 0:128]; m_sl = cst[:, 128:256]; m_gt = cst[:, 256:384]; m_le = cst[:, 384:512]
    ms_sl = cst[:, 512:640]; ms_gt = cst[:, 640:768]; ms_le = cst[:, 768:896]
    mreset = cst[:, 896:1024]; blk1 = cst[:, 1024:1152]; ones = cst[:, 1152:1280]
    maskTB = cst[:, 1280:1296]
    cstb = sbp("cstb", [128, 1152], BF16)
    blk1b = cstb[:, 1024:1152]
    P.dma("sp", cstb, DR["cstb"], writes=["cstb"])
    moff = [cstb[:, l * 128:(l + 1) * 128] for l in range(4)]
    moffT = [cstb[:, 512 + l * 128:512 + (l + 1) * 128] for l in range(4)]
    onesR = sbp("onesR", [128, 128], F32R)
    P.op("dve", cpv(onesR, ones), ["cst"], ["onesR"])
    epsc = sbp("epsc", [128, 8])
    P.op("dve", lambda: nc.vector.memset(epsc[:, 0:1], 1e-6), [], ["epsc"])
    P.op("dve", lambda: nc.vector.memset(epsc[:, 1:2], 64e-5), [], ["epsc"])
    P.op("dve", lambda: nc.vector.memset(epsc[:, 2:3], 1e-5), [], ["epsc"])
    P.op("dve", lambda: nc.vector.memset(epsc[:, 3:4], 1.0), [], ["epsc"])
    P.op("dve", lambda: nc.vector.memset(epsc[:, 4:5], -0.5), [], ["epsc"])
    flag = sbp("flag", [128, 1]); P.dma("sp", flag, DR["flag"], writes=["flag"])
    gvec = sbp("gvec", [128, 48]); P.dma("sp", gvec, DR["gvec"], writes=["gvec"])
    svec = sbp("svec", [128, 114]); P.dma("sp", svec[:, 0:98], DR["svec"], writes=["svec"])
    P.op("dve", lambda: nc.vector.tensor_scalar(svec[:, 98:114], svec[:, 26:42], -1.0, None, ALU.mult, ALU.bypass), ["svec"], ["svec"])
    P.op("dve", lambda: nc.vector.tensor_scalar(svec[:, 58:66], svec[:, 50:58], -1.0, 1.0, ALU.mult, ALU.add), ["svec"], ["svec"])
    zeros = sbp("zeros", [128, 128])
    P.op("dve", lambda: nc.vector.memset(zeros, 0.0), [], ["zeros"])
    muT = svec[:, 0:26]
    sv = lambda i, hp: svec[:, 26 + 8 * i + hp: 26 + 8 * i + hp + 1]
    badaT = sbp("badaT", [128, 96]); P.dma("sp", badaT, DR["badaT"], writes=["badaT"])
    modT = sbp("modT", [128, 96, 17])
    MODK = [("modT", k) for k in range(6)]
    gm = sbp("gm", [128, 16, 17]); gf = sbp("gf", [128, 16, 17])
    WB = [sbp(f"WB{i}", [128, 16, 256], BF16) for i in range(2)]
    wbc = [0]

    def wslot():
        i = wbc[0] % 2
        wbc[0] += 1
        return WB[i], f"WB{i}"

    psb = [nc.alloc_psum_tensor(f"psb{i}", [128, 512], F32).ap() for i in range(8)]
    bc = [0]; qc = [0]

    BPOOL = [[0, 1, 2, 3]]
    QPOOL = [[2, 3, 4, 5, 6, 7]]

    def bank():
        pool_ = BPOOL[0]
        i = pool_[bc[0] % len(pool_)]
        bc[0] += 1
        return psb[i], f"psb{i}"

    def quart():
        pool_ = QPOOL[0]
        i = pool_[qc[0] % len(pool_)]
        qc[0] += 1
        return psb[i][:, 0:128], f"psb{i}"

    V = lambda fn, r, w: P.op("dve", fn, r, w)
    A = lambda fn, r, w: P.op("act", fn, r, w)

    def MM(out, lhsT, rhs, start, stop, r, w, signal=True):
        return P.op("pe", lambda: nc.tensor.matmul(out, lhsT=lhsT, rhs=rhs, start=start, stop=stop), r, w, signal)

    def TR(out, in_, r, w):
        return P.op("pe", lambda: nc.tensor.transpose(out, in_, ident), list(r) + ["cst"], w)

    tt = lambda o, a, b, op: (lambda: nc.vector.tensor_tensor(o, a, b, op))
    ts = lambda o, a, s1, s2, o0, o1: (lambda: nc.vector.tensor_scalar(o, a, s1, s2, o0, o1))
    stt = lambda o, a, s, b, o0, o1: (lambda: nc.vector.scalar_tensor_tensor(o, a, s, b, o0, o1))
    act = lambda o, i, f, **kw: (lambda: nc.scalar.activation(o, i, f, **kw))

    def load_w(src_ap, nk, ncols, dst=None, key=None):
        if dst is None:
            dst, key = wslot()
        P.dma("pool", dst[:, 0:nk, 0:ncols], src_ap.rearrange("(kc p) n -> p kc n", p=128), writes=[key])
        return dst, key

    _ARR.init(nc, "arenaR", F32R, 10624)
    _AR.init(nc)
    scb = sbp("scb", [128, 16, 17], BF16)
    with ExitStack() as es:
        cTt = es.enter_context(_sbt(nc, "cTt", [128, 16, 17], F32)).ap()
        P.dma("sp", cTt, DR["cT"].rearrange("(kc p) n -> p kc n", p=128), writes=["cTt"])
        A(act(scb, cTt, AF.Silu), ["cTt"], ["scb"])
        P.barrier()

    ADA_SLOT = {}

    def ada_dma_gen(blk):
        dst, key = wslot()
        ADA_SLOT[blk] = (dst, key)
        P.dma("pool", dst[:, 0:16, 0:256], DR["w_ada"][:, blk * 256:(blk + 1) * 256].rearrange("(kc p) n -> p kc n", p=128),
              writes=[key], est=14.0)
        yield

    def ada_mm_gen(blk):
        wa_, wak = ADA_SLOT[blk]
        for jj in range(2):
            j = blk * 2 + jj
            pb, pk = bank()
            for kc in range(16):
                MM(pb[:, 0:17], wa_[:, kc, jj * 128:(jj + 1) * 128], scb[:, kc, :], kc == 0, kc == 15,
                   [wak, "scb"], [pk], signal=(kc == 15))
            A(act(modT[:, j, :], pb[:, 0:17], AF.Identity, bias=badaT[:, j:j + 1]), [pk, "badaT"], [("modT", j // 16)])
        yield

    def ada_block(blk):
        wa_, wak = load_w(DR["w_ada"][:, blk * 256:(blk + 1) * 256], 16, 256)
        for jj in range(2):
            j = blk * 2 + jj
            pb, pk = bank()
            for kc in range(16):
                MM(pb[:, 0:17], wa_[:, kc, jj * 128:(jj + 1) * 128], scb[:, kc, :], kc == 0, kc == 15,
                   [wak, "scb"], [pk], signal=(kc == 15))
            A(act(modT[:, j, :], pb[:, 0:17], AF.Identity, bias=badaT[:, j:j + 1]), [pk, "badaT"], [("modT", j // 16)])

    for blk in range(16):
        ada_block(blk)
    for dc in range(16):
        V(ts(gm[:, dc, :], modT[:, 16 + dc, :], 1.0, gvec[:, dc:dc + 1], ALU.add, ALU.mult), [("modT", 1), "gvec"], ["gm"])
    ckpt("A", [("modT", modT, MODK), ("gm", gm, ["gm"])])

    def rms_rstd(es_tiles, src_fn, n, srckeys):
        sq, rstd, tmpn = es_tiles
        pb, pk = bank()
        for dc in range(16):
            s = dc % 2
            A(act(sq[s][:, 0:n], src_fn(dc), AF.Square), srckeys(dc), [f"sq{s}"])
            MM(pb[:, 0:n], onesR, sq[s][:, 0:n], dc == 0, dc == 15, [f"sq{s}", "onesR"], [pk], signal=True)
        A(act(tmpn[:, 0:n], pb[:, 0:n], AF.Sqrt, bias=epsc[:, 0:1], scale=1.0 / D), [pk, "epsc"], ["rstd"])
        V(rcp(rstd[:, 0:n], tmpn[:, 0:n]), ["rstd"], ["rstd"])
        return rstd

    def build_hT(es, hT, xsrc, ncols, g_t, shift_base, with_sample):
        xg = es.enter_context(_sbt(nc, "xg", [128, 16, 512], F32)).ap()
        sq = [es.enter_context(_sbt(nc, f"sq{i}", [128, 512], F32R)).ap() for i in range(2)]
        rstd = es.enter_context(_sbt(nc, "rstd", [128, 512], F32)).ap()
        tmpn = es.enter_context(_sbt(nc, "tmpn", [128, 512], F32)).ap()
        tq = [es.enter_context(_sbt(nc, f"tq{i}", [128, 512], F32)).ap() for i in range(2)]
        for g0 in range(0, ncols, 512):
            n = min(512, ncols - g0)
            for dc in range(16):
                P.dma("sp", xg[:, dc, 0:n], xsrc[dc * 128:(dc + 1) * 128, g0:g0 + n], writes=[("xg", dc)])
            rms_rstd((sq, rstd, tmpn), lambda dc: xg[:, dc, 0:n], n, lambda dc: [("xg", dc)])
            for dc in range(16):
                s = dc % 2
                if with_sample and g0 >= TP:
                    b3 = lambda a: a.unsqueeze(2).broadcast_to([128, 16, 8])
                    v3 = lambda a: a.rearrange("p (b t) -> p b t", t=8)
                    V(tt(v3(tq[s][:, 0:n]), v3(xg[:, dc, 0:n]), b3(g_t[:, dc, 1:17]), ALU.mult), [("xg", dc), "gm", "gf"], [f"tq{s}"])
                    V(tt(tq[s][:, 0:n], tq[s][:, 0:n], rstd[:, 0:n], ALU.mult), [f"tq{s}", "rstd"], [f"tq{s}"])
                    V(tt(v3(hT[:, dc, g0:g0 + n]), v3(tq[s][:, 0:n]), b3(modT[:, shift_base + dc, 1:17]), ALU.add),
                      [f"tq{s}", *MODK], [("hT", dc)])
                else:
                    V(stt(tq[s][:, 0:n], xg[:, dc, 0:n], g_t[:, dc, 0:1], rstd[:, 0:n], ALU.mult, ALU.mult),
                      [("xg", dc), "gm", "gf", "rstd"], [f"tq{s}"])
                    A(act(hT[:, dc, g0:g0 + n], tq[s][:, 0:n], AF.Identity, bias=modT[:, shift_base + dc, 0:1]),
                      [f"tq{s}", *MODK], [("hT", dc)])

    def dense_fm(wt, wkey, wcol0, nk, act_fn, act_keys, c0, n, consumer):
        pb, pk = bank()
        for kc in range(nk):
            MM(pb[:, 0:n], wt[:, kc, wcol0:wcol0 + 128], act_fn(kc)[:, c0:c0 + n], kc == 0, kc == nk - 1,
               [wkey] + act_keys(kc), [pk], signal=(kc == nk - 1))
        consumer(pb[:, 0:n], pk)

    es_mix = ExitStack()
    es_mp = ExitStack()
    mpa = lambda name, shape, dt=F32: es_mp.enter_context(_sbt(nc, name, list(shape), dt)).ap()
    lora_d = mpa("lora_d", [128, 1024], BF16); lora_a = mpa("lora_a", [128, 1024], BF16)
    P.op("dve", lambda: nc.vector.memset(lora_d[64:128, :], 0.0), [], ["lora_up"])
    P.op("dve", lambda: nc.vector.memset(lora_a[0:64, :], 0.0), [], ["lora_up"])
    P.dma("pool", lora_d[0:64, :], DR["lora_up"][0:64, :], writes=["lora_up"])
    P.dma("pool", lora_a[64:128, :], DR["lora_up"][64:128, :], writes=["lora_up"])
    wgu = mpa("wgu", [128, 1024], BF16); P.dma("pool", wgu, DR["w_gate_up"], writes=["wgu"])
    sshT = mpa("sshT", [128, 26, 16]); P.dma("sp", sshT, DR["sshT"].rearrange("p (j b) -> p j b", b=16), writes=["sshT"])
    Hs = mpa("Hs", [128, 8, 128], F32R)
    hlast = mpa("hlast", [128, 16, 2], BF16)
    shp = mpa("shp", [128, 26]); shs = mpa("shs", [128, 26, 16])
    hT = es_mix.enter_context(_sbt(nc, "hT", [128, 16, T], BF16)).ap()
    hT_fn = lambda kc: hT[:, kc, :]
    hT_keys = lambda kc: [("hT", kc)]
    oaT = es_mix.enter_context(_sbt(nc, "oaT", [128, 8, T], BF16)).ap()

    def mixer_pass(own):
        ncols = T if own else TP
        ntiles = 9 if own else 8
        with ExitStack() as es:
            build_hT(es, hT, DR["xT_own"] if own else DR["xT_prev"], ncols, gm, 0, own)
        if not own:
            V(cpv(hlast, hT[:, :, TP - 2:TP]), [("hT", k) for k in range(16)], ["hlast"])
        P.barrier()
        ckpt("B1" if not own else "B2", [("hT", hT, [("hT", k) for k in range(16)])])
        with ExitStack() as es:
            al = lambda name, shape, dt=F32: es.enter_context(_sbt(nc, name, list(shape), dt)).ap()
            lor = al("lor", [128, T], BF16); sg = al("sg", [128, T], BF16)
            zraw = [al(f"zraw{i}", [128, 1 + T]) for i in range(1)]
            zrs = [al(f"zrs{i}", [128, 16, 8]) for i in range(1)]
            big = al("big", [128, 16, 64])
            dtl = [big[:, 0:8, :].rearrange("p b v -> p (b v)")]
            zs = [[al(f"zs{s}_{w}", [128, T]) for w in range(3)] for s in range(1)]
            wrkv = [al(f"wrkv{i}", [128, 16, 3, 128], BF16) for i in range(1)]
            zcnt = [0]

            def shift_evac(j, dst, dstkey, wt, wkey, wcol0):
                zi = 0
                zcnt[0] += 1
                zr, zk = zraw[zi], f"zraw{zi}"
                mu = muT[:, j:j + 1]
                if own:
                    pb, pk = bank()
                    for kc in range(16):
                        MM(pb[:, 0:2], wt[:, kc, wcol0:wcol0 + 128], hlast[:, kc, :], kc == 0, kc == 15, [wkey, "hlast"], [pk], signal=(kc == 15))
                    A(act(zr[:, 0:1], pb[:, 1:2], AF.Identity, scale=flag[:, 0:1]), [pk, "flag"], [zk])
                else:
                    V(lambda: nc.vector.memset(zr[:, 0:1], 0.0), [], [zk])
                for g0 in range(0, TP, 512):
                    def cons(ps, pk, g0=g0):
                        A(cpa(zr[:, 1 + g0:1 + g0 + 512], ps), [pk], [zk])
                        d = dtl[0]; dk = "big"
                        V(tt(d, zr[:, g0:g0 + 512], ps, ALU.subtract), [zk, pk], [dk])
                        V(stt(dst[:, g0:g0 + 512], d, mu, ps, ALU.mult, ALU.add), [dk, pk, "svec"], [dstkey])
                    dense_fm(wt, wkey, wcol0, 16, hT_fn, hT_keys, g0, 512, cons)
                if own:
                    V(cpv(shp[:, j:j + 1], zr[:, TP:TP + 1]), [zk], ["shp"])

                    def cons_s(ps, pk):
                        z3 = zrs[zi]; z3k = f"zrs{zi}"
                        p3 = ps.rearrange("p (b t) -> p b t", t=8)
                        A(cpa(z3[:, :, 1:8], p3[:, :, 0:7]), [pk], [z3k])
                        V(cpv(z3[:, :, 0:1], sshT[:, j, :].unsqueeze(2)), ["sshT"], [z3k])
                        V(cpv(shs[:, j, :].unsqueeze(2), p3[:, :, 7:8]), [pk], ["shs"])
                        d = dtl[0][:, 0:128]
                        V(tt(d, z3.rearrange("p b t -> p (b t)"), ps, ALU.subtract), [z3k, pk], ["big"])
                        V(stt(dst[:, TP:T], d, mu, ps, ALU.mult, ALU.add), ["big", pk, "svec"], [dstkey])
                    dense_fm(wt, wkey, wcol0, 16, hT_fn, hT_keys, TP, 128, cons_s)

            wl, wlk = load_w(DR["w_in"][:, 3072:3328], 16, 256)
            zl = zs[0][0]
            shift_evac(24, zl, "zs0_0", wl, wlk, 0)
            A(act(lor[0:64, 0:ncols], zl[0:64, 0:ncols], AF.Tanh), ["zs0_0"], ["lor"])
            V(cpv(lor[64:128, 0:ncols], zl[64:128, 0:ncols]), ["zs0_0"], ["lor"])
            if own:
                shift_evac(25, zl, "zs0_0", wl, wlk, 128)
                A(act(sg, zl, AF.Sigmoid), ["zs0_0"], ["sg"])

            ckpt("L1" if not own else "L2", [("lor", lor, ["lor"]), ("sg", sg, ["sg"])])
            NS = 1
            DBN_ = {"AbTA", "AbTB", "BtTA", "BtTB", "KtTA", "KtTB", "RbT", "Bt_tm", "Kt_tm", "VeA", "VeB", "YA0", "YB0", "G", "bon"}
            pt = {}
            for nm in ["oT", "o2", "kk2b", "rkb"]:
                pt[nm] = [al(f"p_{nm}0", [128, 128], BF16)]
            for nm in ["lw", "cum", "aa", "kk", "kkn", "kp", "G", "Gi", "Gm1", "bon",
                       "mean", "msq", "varp", "cen", "otm"]:
                pt[nm] = [al(f"p_{nm}{s}", [128, 128]) for s in range(2 if nm in DBN_ else 1)]
            for nm in ["AbT", "BtT", "KtT", "RbT", "Bt_tm", "Kt_tm", "VeA", "VeB", "UeA", "UeB", "WT",
                       "AbTA", "AbTB", "BtTA", "BtTB", "KtTA", "KtTB", "WTA", "WTB", "RbTA", "RbTB",
                       "sX0", "sX1", "sXT0", "sXT1", "YA0", "YA1", "LakA", "MrbA", "MrkA",
                       "YB0", "YB1", "LakB", "MrbB", "MrkB", "DTfA", "DTfB"]:
                pt[nm] = [al(f"p_{nm}{s}", [128, 128], F32R) for s in range(2 if nm in DBN_ else 1)]
            for hh_ in ("A", "B"):
                for nm in ["X0", "X1", "XT0", "XT1", "D0", "D1", "DT0", "DT1", "Pm", "Qm", "Lo0", "Lo1", "Lo2", "Lo3", "LoT0", "LoT1", "LoT2", "Lb", "LTb"]:
                    pt[nm + hh_] = [al(f"p_{nm}{hh_}{s}", [128, 128], BF16) for s in range(NS)]
            for nm in ["VeA", "VeB", "UeA", "UeB", "AbTA", "AbTB", "BtTA", "BtTB", "KtTA", "KtTB", "WTA", "WTB", "RbTA", "RbTB"]:
                for s in range(len(pt[nm])):
                    V((lambda a: (cpv(a, zeros)))(pt[nm][s]), ["zeros"], [(nm, s)])
            if own:
                h0r = al("h0r", [128, 16, 64], F32R)
                hsn = al("hsn", [128, 16, 64])
                u1 = al("u1", [128, 64]); o0 = al("o0", [128, 64])
                Ublk = al("Ublk", [128, 16, 64], F32R); Vblk = al("Vblk", [128, 16, 64], F32R)
            setc = [0]

            CARRY = [None]
            for hp in range(8):
                wi = 0
                wr, wrk = wrkv[wi], f"wrkv{wi}"
                which = [0, 1, 2] if own else [1, 2]
                Z = zs[0]

                def proj_gen():
                    for w in which:
                        c = w * 1024 + hp * 128
                        P.dma("pool", wr[:, :, w, :], DR["w_in"][:, c:c + 128].rearrange("(kc p) n -> p kc n", p=128), writes=[wrk], est=4.0)
                    for w in which:
                        shift_evac(w * 8 + hp, Z[w], f"zs0_{w}", wr[:, :, w, :], wrk, 0)
                    if own:
                        A(act(Hs[:, hp, :], r32(Hs[:, hp, :]), AF.Identity, scale=flag[:, 0:1]), [("Hs", hp), "flag"], [("Hs", hp)])
                    else:
                        V((lambda a: (cpv(a, zeros)))(Hs[:, hp, :]), ["zeros"], [("Hs", hp)])
                    yield

                zr_, zk_, zv_ = Z
                kr, kk_, kv = [f"zs0_{w}" for w in range(3)]
                ch = slice(hp * 128, (hp + 1) * 128)
                DBN = {"AbTA", "AbTB", "BtTA", "BtTB", "KtTA", "KtTB", "RbT", "Bt_tm", "Kt_tm", "VeA", "VeB", "YA0", "YB0", "G", "bon"}

                def env(ti):
                    samp = own and ti == 8
                    par = ti % 2
                    K = lambda nm: (nm, par if nm in DBN else 0)
                    X = lambda nm: pt[nm][par if nm in DBN else 0]
                    cs = slice(ti * 128, (ti + 1) * 128)
                    Msl, Mgt, Mle = (ms_sl, ms_gt, ms_le) if samp else (m_sl, m_gt, m_le)
                    return samp, K, X, cs, Msl, Mgt, Mle

                def prep_gen(ti):
                    samp, K, X, cs, Msl, Mgt, Mle = env(ti)
                    q, qk = quart()
                    MM(q, lora_d[:, ch], lor[:, cs], True, True, ["lora_up", "lor"], [qk])
                    A(act(X("lw"), q, AF.Exp, bias=svec[:, 98 + hp:99 + hp], scale=-1.0), [qk, "svec"], [K("lw")])
                    A(act(X("lw"), X("lw"), AF.Ln, bias=epsc[:, 3:4]), [K("lw"), "epsc"], [K("lw")])
                    A(act(X("lw"), X("lw"), AF.Exp, bias=epsc[:, 4:5], scale=-1.0), [K("lw"), "epsc"], [K("lw")])
                    yield
                    V(scn(X("cum"), mreset if samp else ones, X("lw"), 0.0, ALU.mult, ALU.add),
                      [K("lw"), "cst"], [K("cum")])
                    q, qk = quart()
                    MM(q, lora_a[:, ch], lor[:, cs], True, True, ["lora_up", "lor"], [qk])
                    A(act(X("aa"), q, AF.Exp, bias=svec[:, 106 + hp:107 + hp], scale=-1.0), [qk, "svec"], [K("aa")])
                    A(act(X("aa"), X("aa"), AF.Ln, bias=epsc[:, 3:4]), [K("aa"), "epsc"], [K("aa")])
                    A(act(X("aa"), X("aa"), AF.Exp, scale=-1.0), [K("aa")], [K("aa")])
                    yield
                    V(ts(X("kk"), zk_[:, cs], sv(2, hp), None, ALU.mult, ALU.bypass), [kk_, "svec"], [K("kk")])
                    A(act(X("kk2b"), X("kk"), AF.Square), [K("kk")], [K("kk2b")])
                    q, qk = quart()
                    MM(q, blk1b, X("kk2b"), True, True, ["cstb", K("kk2b")], [qk])
                    V(ts(X("Gm1"), q, 1e-19, None, ALU.max, ALU.bypass), [qk], [K("Gm1")])
                    A(act(X("Gm1"), X("Gm1"), AF.Ln), [K("Gm1")], [K("Gm1")])
                    A(act(X("Gm1"), X("Gm1"), AF.Exp, scale=-0.5), [K("Gm1")], [K("Gm1")])
                    V(tt(X("kkn"), X("kk"), X("Gm1"), ALU.mult), [K("kk"), K("Gm1")], [K("kkn")])
                    yield
                    V(ts(X("kp"), X("aa"), sv(3, hp), sv(4, hp), ALU.mult, ALU.add), [K("aa"), "svec"], [K("kp")])
                    V(tt(X("kp"), zk_[:, cs], X("kp"), ALU.mult), [kk_, K("kp")], [K("kp")])
                    A(act(X("G"), X("cum"), AF.Exp, scale=-1.0), [K("cum")], [K("G")])
                    A(act(X("Gi"), X("cum"), AF.Exp), [K("cum")], [K("Gi")])
                    V(tt(X("cum"), X("cum"), X("lw"), ALU.subtract), [K("cum"), K("lw")], [K("cum")])
                    A(act(X("Gm1"), X("cum"), AF.Exp, scale=-1.0), [K("cum")], [K("Gm1")])
                    yield
                    V(stt(X("AbT"), X("kkn"), -1.0, X("Gm1"), ALU.mult, ALU.mult), [K("kkn"), K("Gm1")], [K("AbT")])
                    V(tt(X("kk"), X("kkn"), X("aa"), ALU.mult), [K("kkn"), K("aa")], [K("kk")])
                    V(tt(X("BtT"), X("kk"), X("Gi"), ALU.mult), [K("kk"), K("Gi")], [K("BtT")])
                    V(tt(X("KtT"), X("kp"), X("Gi"), ALU.mult), [K("kp"), K("Gi")], [K("KtT")])
                    for nm in ("AbT", "BtT", "KtT"):
                        A((lambda nm=nm: (cpa(X(nm + "A")[0:64, :], r32(X(nm))[0:64, :])))(), [K(nm)], [K(nm + "A")])
                        V((lambda nm=nm: (cpv(X(nm + "B")[64:128, :], r32(X(nm))[64:128, :])))(), [K(nm)], [K(nm + "B")])
                    if own:
                        V(tt(X("RbT"), zr_[:, cs], X("G"), ALU.mult), [kr, K("G")], [K("RbT")])
                        if samp:
                            A(cpa(X("RbTA")[0:64, :], r32(X("RbT"))[0:64, :]), [K("RbT")], [K("RbTA")])
                            V(cpv(X("RbTB")[64:128, :], r32(X("RbT"))[64:128, :]), [K("RbT")], [K("RbTB")])
                        V(stt(X("rkb"), zr_[:, cs], sv(7, hp), X("kp"), ALU.mult, ALU.mult), [kr, "svec", K("kp")], [K("rkb")])
                        q, qk = quart()
                        MM(q, blk1b, X("rkb"), True, True, ["cstb", K("rkb")], [qk])
                        V(tt(X("bon"), q, zv_[:, cs], ALU.mult), [qk, kv], [K("bon")])
                    yield
                    q, qk = quart()
                    TR(q, r32(X("AbT")), [K("AbT")], [qk])
                    A(cpa(X("YA0")[:, 0:64], q[:, 0:64]), [qk], [K("YA0")])
                    A(cpa(X("YB0")[:, 64:128], q[:, 64:128]), [qk], [K("YB0")])
                    q, qk = quart()
                    TR(q, r32(X("BtT")), [K("BtT")], [qk])
                    A(cpa(X("Bt_tm"), q), [qk], [K("Bt_tm")])
                    yield
                    q, qk = quart()
                    TR(q, r32(X("KtT")), [K("KtT")], [qk])
                    A(cpa(X("Kt_tm"), q), [qk], [K("Kt_tm")])
                    q, qk = quart()
                    TR(q, zv_[:, cs], [kv], [qk])
                    A(cpa(X("VeA")[:, 0:64], q[:, 0:64]), [qk], [K("VeA")])
                    A(cpa(X("VeB")[:, 64:128], q[:, 64:128]), [qk], [K("VeB")])

                def scan_gens(ti):
                    samp, K, X, cs, Msl, Mgt, Mle = env(ti)
                    Yfin = YF.setdefault(ti, {})

                    def head_gen(hh, p0):
                        ps_ = slice(p0, p0 + 64)
                        dhalf = slice(0, 64) if hh == "A" else slice(64, 128)
                        ohalf = slice(64, 128) if hh == "A" else slice(0, 64)
                        Ve = X("Ve" + hh); Vek = K("Ve" + hh)
                        HX = lambda nm: X(nm + hh)
                        HK = lambda nm: K(nm + hh)
                        q1, qk1 = quart()
                        MM(q1, X("BtT" + hh), X("AbT" + hh), True, True, [K("BtT" + hh), K("AbT" + hh)], [qk1])
                        q2, qk2 = quart()
                        MM(q2, X("AbT" + hh), X("BtT" + hh), True, True, [K("BtT" + hh), K("AbT" + hh)], [qk2])

                        def side_lak():
                            q, qk = quart()
                            MM(q, X("KtT" + hh), X("AbT" + hh), True, True, [K("KtT" + hh), K("AbT" + hh)], [qk])
                            V(tt(X("Lak" + hh), q, Msl, ALU.mult), [qk, "cst"], [K("Lak" + hh)])

                        def side_z():
                            q, qk = quart()
                            MM(q[:, 0:64], X("Lak" + hh), Ve[:, dhalf], True, True, [K("Lak" + hh), Vek], [qk])
                            A(cpa(X("Y" + hh + "0")[:, ohalf], q[:, 0:64]), [qk], [K("Y" + hh + "0")])

                        def side_m():
                            if own:
                                q, qk = quart()
                                MM(q, X("BtT" + hh), X("RbT"), True, True, [K("BtT" + hh), K("RbT")], [qk])
                                V(tt(X("Mrb" + hh), q, Mle, ALU.mult), [qk, "cst"], [K("Mrb" + hh)])
                                q, qk = quart()
                                MM(q, X("KtT" + hh), X("RbT"), True, True, [K("KtT" + hh), K("RbT")], [qk])
                                V(tt(X("Mrk" + hh), q, Mle, ALU.mult), [qk, "cst"], [K("Mrk" + hh)])

                        if samp:
                            V(tt(X("sXT0"), q1, ms_sl, ALU.mult), [qk1, "cst"], [K("sXT0")])
                            V(tt(X("sX0"), q2, ms_gt, ALU.mult), [qk2, "cst"], [K("sX0")])
                            yield
                            side_lak()
                            yield
                            side_z()
                            yield
                            side_m()
                            yield
                            nsteps = 3
                            for i in range(nsteps):
                                a, b = i % 2, (i + 1) % 2
                                Xa, XTa, Ya = X(f"sX{a}"), X(f"sXT{a}"), X(f"Y{hh}{a}")
                                Xb, XTb, Yb = X(f"sX{b}"), X(f"sXT{b}"), X(f"Y{hh}{b}")
                                q, qk = quart()
                                MM(q, XTa, Ya, True, True, [K(f"sXT{a}"), K(f"Y{hh}{a}")], [qk])
                                V(tt(Yb, q, r32(Ya), ALU.add), [qk, K(f"Y{hh}{a}")], [K(f"Y{hh}{b}")])
                                if i < nsteps - 1:
                                    q, qk = quart()
                                    MM(q, XTa, Xa, True, True, [K(f"sXT{a}"), K(f"sX{a}")], [qk])
                                    qq, qqk = quart()
                                    MM(qq, Xa, XTa, True, True, [K(f"sXT{a}"), K(f"sX{a}")], [qqk])
                                    A(cpa(Xb, q), [qk], [K(f"sX{b}")])
                                    A(cpa(XTb, qq), [qqk], [K(f"sXT{b}")])
                            Yf, Yfk = X(f"Y{hh}1"), K(f"Y{hh}1")
                            Yfin[hh] = (Yf, Yfk)
                            q, qk = quart()
                            TR(q, r32(Yf), [Yfk], [qk])
                            A(cpa(X("WT")[ps_, :], q[ps_, :]), [qk], [K("WT")])
                            V(cpv(X("WT" + hh)[ps_, :], q[ps_, :]), [qk], [K("WT" + hh)])
                            return
                        A(cpa(HX("LTb"), q1), [qk1], [HK("LTb")])
                        A(cpa(HX("Lb"), q2), [qk2], [HK("Lb")])
                        V(tt(HX("XT0"), HX("LTb"), ms_sl, ALU.mult), [HK("LTb"), "cst"], [HK("XT0")])
                        V(tt(HX("X0"), HX("Lb"), ms_gt, ALU.mult), [HK("Lb"), "cst"], [HK("X0")])
                        V(tt(HX("D0"), HX("X0"), ident, ALU.add), [HK("X0"), "cst"], [HK("D0")])
                        V(tt(HX("DT0"), HX("XT0"), ident, ALU.add), [HK("XT0"), "cst"], [HK("DT0")])

                        def mask_l(l):
                            if l < 3:
                                V(tt(HX(f"LoT{l}"), HX("LTb"), moffT[l], ALU.mult), [HK("LTb"), "cstb"], [HK(f"LoT{l}")])
                            V(tt(HX(f"Lo{l}"), HX("Lb"), moff[l], ALU.mult), [HK("Lb"), "cstb"], [HK(f"Lo{l}")])

                        side = [side_lak, lambda: mask_l(0), side_z, lambda: mask_l(1), side_m, lambda: mask_l(2), lambda: mask_l(3)]
                        cur = 0
                        for i in range(2):
                            a, b = i % 2, (i + 1) % 2
                            q, qk = quart()
                            MM(q, HX(f"XT{a}"), HX(f"X{a}"), True, True, [HK(f"XT{a}"), HK(f"X{a}")], [qk])
                            qq, qqk = quart()
                            MM(qq, HX(f"X{a}"), HX(f"XT{a}"), True, True, [HK(f"XT{a}"), HK(f"X{a}")], [qqk])
                            A(cpa(HX(f"X{b}"), q), [qk], [HK(f"X{b}")])
                            A(cpa(HX(f"XT{b}"), qq), [qqk], [HK(f"XT{b}")])
                            if side:
                                side.pop(0)()
                            yield
                            q, qk = quart()
                            MM(q, HX(f"XT{b}"), HX(f"D{cur}"), True, True, [HK(f"XT{b}"), HK(f"D{cur}")], [qk])
                            qq, qqk = quart()
                            MM(qq, HX(f"D{cur}"), HX(f"XT{b}"), True, True, [HK(f"XT{b}"), HK(f"D{cur}")], [qqk])
                            V(tt(HX(f"D{1 - cur}"), q, HX(f"D{cur}"), ALU.add), [qk, HK(f"D{cur}")], [HK(f"D{1 - cur}")])
                            V(tt(HX(f"DT{1 - cur}"), qq, HX(f"DT{cur}"), ALU.add), [qqk, HK(f"DT{cur}")], [HK(f"DT{1 - cur}")])
                            cur = 1 - cur
                            if side:
                                side.pop(0)()
                            yield
                        for l in range(4):
                            last = (l == 3)
                            while side and l + 2 > 7 - len(side):
                                side.pop(0)()
                            if not last:
                                q, qk = quart()
                                MM(q, HX(f"LoT{l}"), HX(f"D{cur}"), True, True, [HK(f"LoT{l}"), HK(f"D{cur}")], [qk])
                                A(cpa(HX("Pm"), q), [qk], [HK("Pm")])
                            qq, qqk = quart()
                            MM(qq, HX(f"Lo{l}"), HX(f"DT{cur}"), True, True, [HK(f"Lo{l}"), HK(f"DT{cur}")], [qqk])
                            A(cpa(HX("Qm"), qq), [qqk], [HK("Qm")])
                            if side:
                                side.pop(0)()
                            yield
                            if not last:
                                q, qk = quart()
                                MM(q, HX(f"DT{cur}"), HX("Pm"), True, True, [HK(f"DT{cur}"), HK("Pm")], [qk])
                                V(tt(HX(f"D{1 - cur}"), q, HX(f"D{cur}"), ALU.add), [qk, HK(f"D{cur}")], [HK(f"D{1 - cur}")])
                            qq, qqk = quart()
                            MM(qq, HX(f"D{cur}"), HX("Qm"), True, True, [HK(f"D{cur}"), HK("Qm")], [qqk])
                            if last:
                                V(tt(HX("DTf"), qq, HX(f"DT{cur}"), ALU.add), [qqk, HK(f"DT{cur}")], [HK("DTf")])
                            else:
                                V(tt(HX(f"DT{1 - cur}"), qq, HX(f"DT{cur}"), ALU.add), [qqk, HK(f"DT{cur}")], [HK(f"DT{1 - cur}")])
                            cur = 1 - cur
                            yield
                        while side:
                            side.pop(0)()
                        DT, DTk = HX("DTf"), HK("DTf")
                        Y0, Y0k = X(f"Y{hh}0"), K(f"Y{hh}0")
                        q, qk = quart()
                        MM(q, Y0, DT, True, True, [Y0k, DTk], [qk])
                        A(cpa(X("WT")[ps_, :], q[ps_, :]), [qk], [K("WT")])
                        q, qk = quart()
                        MM(q[:, 0:64], DT, Y0[:, ohalf], True, True, [Y0k, DTk], [qk])
                        A(cpa(X(f"Y{hh}1")[:, ohalf], q[:, 0:64]), [qk], [K(f"Y{hh}1")])
                        Yfin[hh] = (X(f"Y{hh}1"), K(f"Y{hh}1"))

                    if samp:
                        def seq():
                            for g_ in (head_gen("A", 0), head_gen("B", 64)):
                                for _ in g_:
                                    yield
                        return [seq()]
                    return [head_gen("A", 0), head_gen("B", 64)]

                def chainpost_gen(ti):
                    samp, K, X, cs, Msl, Mgt, Mle = env(ti)
                    Yfin = YF[ti]
                    hk = ("Hs", hp)
                    Hh = Hs[:, hp, :]
                    if not samp:
                        q, qk = quart()
                        MM(q, X("WT"), Hh, True, True, [K("WT"), hk], [qk])
                        V(tt(X("UeA")[:, 0:64], q[:, 0:64], r32(Yfin["A"][0][:, 64:128]), ALU.add), [qk, Yfin["A"][1]], [K("UeA")])
                        V(tt(X("UeB")[:, 64:128], q[:, 64:128], r32(Yfin["B"][0][:, 0:64]), ALU.add), [qk, Yfin["B"][1]], [K("UeB")])
                        if own:
                            q, qk = quart()
                            MM(q, Hh, X("RbT"), True, False, [hk, K("RbT")], [qk], signal=False)
                            MM(q, X("UeA"), X("MrbA"), False, False, [K("UeA"), K("MrbA")], [qk], signal=False)
                            MM(q, X("UeB"), X("MrbB"), False, False, [K("UeB"), K("MrbB")], [qk], signal=False)
                            MM(q, X("VeA"), X("MrkA"), False, False, [K("VeA"), K("MrkA")], [qk], signal=False)
                            MM(q, X("VeB"), X("MrkB"), False, True, [K("VeB"), K("MrkB")], [qk])
                            A(cpa(X("oT"), q), [qk], [K("oT")])
                        q, qk = quart()
                        MM(q[:, 0:64], X("Bt_tm"), X("UeA")[:, 0:64], True, False, [K("Bt_tm"), K("UeA")], [qk], signal=False)
                        MM(q[:, 0:64], X("Kt_tm"), X("VeA")[:, 0:64], False, True, [K("Kt_tm"), K("VeA")], [qk], signal=False)
                        MM(q[:, 64:128], X("Bt_tm"), X("UeB")[:, 64:128], True, False, [K("Bt_tm"), K("UeB")], [qk], signal=False)
                        MM(q[:, 64:128], X("Kt_tm"), X("VeB")[:, 64:128], False, True, [K("Kt_tm"), K("VeB")], [qk])
                        GC = X("G")[:, 127:128]
                        for ps_, hf in ((slice(0, 64), slice(0, 64)), (slice(64, 128), slice(64, 128))):
                            A(act(X("msq")[ps_, 0:64], r32(Hh[ps_, hf]), AF.Identity, scale=GC[ps_, :]), [hk, K("G")], [K("msq")])
                            V(stt(Hh[ps_, hf], q[ps_, hf], GC[ps_, :], X("msq")[ps_, 0:64], ALU.mult, ALU.add), [qk, K("G"), K("msq")], [hk])
                        if own and ti == 7:
                            P.dma("sp", DR["wkv_p"][hp], r32(Hh), reads=[hk], semres=("Hs", hp))
                    else:
                        P.dma("sp", big, DR["s0T"][hp].rearrange("p (b v) -> p b v", v=64), writes=["big"])
                        V(cpv(h0r, big), ["big"], ["h0r"])
                        b3m = maskTB.unsqueeze(2).broadcast_to([128, 16, 64])
                        G3 = X("G").rearrange("p (b t) -> p b t", t=8)[:, :, 7:8]
                        for hh, p0 in (("A", 0), ("B", 64)):
                            ps_ = slice(p0, p0 + 64)
                            dhalf = slice(0, 64) if hh == "A" else slice(64, 128)
                            ohalf = slice(64, 128) if hh == "A" else slice(0, 64)
                            Ue, Uek = X("Ue" + hh), K("Ue" + hh)
                            Ve, Vek = X("Ve" + hh), K("Ve" + hh)
                            for src, srck, dstv, dstk in ((X("WT" + hh), K("WT" + hh), u1, "u1"), (X("RbT" + hh), K("RbT" + hh), o0, "o0")):
                                for nb in range(2):
                                    pb, pk = bank()
                                    MM(pb, src, h0r[:, nb * 8:(nb + 1) * 8, :].rearrange("p b v -> p (b v)"), True, True, [srck, "h0r"], [pk])
                                    V(tt(big[:, nb * 8:(nb + 1) * 8, :], pb.rearrange("p (b v) -> p b v", v=64), b3m[:, nb * 8:(nb + 1) * 8, :], ALU.mult),
                                      [pk, "cst"], ["big"])
                                V(lambda dstv=dstv: nc.vector.tensor_reduce(dstv, big.rearrange("p b v -> p v b"), AX.X, ALU.add), ["big"], [dstk])
                            V(tt(Ue[:, dhalf], u1, r32(Yfin[hh][0][:, ohalf]), ALU.add), ["u1", Yfin[hh][1]], [Uek])
                            q, qk = quart()
                            MM(q[:, 0:64], X("Mrb" + hh), Ue[:, dhalf], True, False, [K("Mrb" + hh), Uek], [qk], signal=False)
                            MM(q[:, 0:64], X("Mrk" + hh), Ve[:, dhalf], False, True, [K("Mrk" + hh), Vek], [qk])
                            V(tt(X("otm")[:, dhalf], q[:, 0:64], o0, ALU.add), [qk, "o0"], [K("otm")])
                            ub = r32(Ue[:, dhalf]).unsqueeze(1).broadcast_to([128, 16, 64])
                            vb = r32(Ve[:, dhalf]).unsqueeze(1).broadcast_to([128, 16, 64])
                            V(tt(Ublk, ub, b3m, ALU.mult), [Uek, "cst"], ["Ublk"])
                            V(tt(Vblk, vb, b3m, ALU.mult), [Vek, "cst"], ["Vblk"])
                            for nb in range(2):
                                pb, pk = bank()
                                bs = slice(nb * 8, (nb + 1) * 8)
                                MM(pb, X("Bt_tm"), Ublk[:, bs, :].rearrange("p b v -> p (b v)"), True, False, [K("Bt_tm"), "Ublk"], [pk], signal=False)
                                MM(pb, X("Kt_tm"), Vblk[:, bs, :].rearrange("p b v -> p (b v)"), False, True, [K("Kt_tm"), "Vblk"], [pk])
                                V(tt(hsn[ps_, bs, :], pb[ps_, :].rearrange("p (b v) -> p b v", v=64), r32(h0r)[ps_, bs, :], ALU.add), [pk, "h0r"], ["hsn"])
                                V(tt(hsn[ps_, bs, :], hsn[ps_, bs, :], G3[ps_, bs, :].broadcast_to([64, 8, 64]), ALU.mult), ["hsn", K("G")], ["hsn"])
                        P.dma("sp", DR["wkv_s"][hp].rearrange("p (b v) -> p b v", v=64), hsn, reads=["hsn"], semres="hsn")
                        q, qk = quart()
                        TR(q, X("otm"), [K("otm")], [qk])
                        A(cpa(X("oT"), q), [qk], [K("oT")])
                    yield
                    if own:
                        q, qk = quart()
                        MM(q, blk1b, X("oT"), True, True, ["cstb", K("oT")], [qk])
                        A(act(X("o2"), X("oT"), AF.Square), [K("oT")], [K("o2")])
                        q2, qk2 = quart()
                        MM(q2, blk1b, X("o2"), True, True, ["cstb", K("o2")], [qk2])
                        A(act(X("mean"), q, AF.Identity, scale=1.0 / 64), [qk], [K("mean")])
                        V(tt(X("msq"), X("mean"), X("mean"), ALU.mult), [K("mean")], [K("msq")])
                        V(stt(X("varp"), q2, 1.0 / 64, X("msq"), ALU.mult, ALU.subtract), [qk2, K("msq")], [K("varp")])
                        A(act(X("varp"), X("varp"), AF.Ln, bias=epsc[:, 1:2]), [K("varp"), "epsc"], [K("varp")])
                        A(act(X("varp"), X("varp"), AF.Exp, scale=-0.5), [K("varp")], [K("varp")])
                        yield
                        V(tt(X("cen"), X("oT"), X("mean"), ALU.subtract), [K("oT"), K("mean")], [K("cen")])
                        V(tt(X("cen"), X("cen"), X("varp"), ALU.mult), [K("cen"), K("varp")], [K("cen")])
                        V(ts(X("cen"), X("cen"), sv(5, hp), sv(6, hp), ALU.mult, ALU.add), [K("cen"), "svec"], [K("cen")])
                        V(tt(X("cen"), X("cen"), X("bon"), ALU.add), [K("cen"), K("bon")], [K("cen")])
                        q, qk = quart()
                        MM(q, wgu[:, ch], sg[:, cs], True, True, ["wgu", "sg"], [qk])
                        V(tt(oaT[:, hp, cs], X("cen"), q, ALU.mult), [K("cen"), qk], [("oaT", hp)])
                    if hp == 0 and ti == 0:
                        ckpt("T1" if not own else "T2", [(nm, r32(pt[nm][0]) if pt[nm][0].dtype == F32R else pt[nm][0], [(nm, 0)]) for nm in
                                   ["lw", "cum", "aa", "kkn", "kp", "G", "AbT", "BtT", "KtT", "Bt_tm", "VeA", "VeB", "WT", "UeA", "UeB", "LakA", "YA0", "YA1", "YB1", "oT", "cen"]]
                             + [("Hs", r32(Hs), [("Hs", k) for k in range(8)]), ("zs1", zs[0][1], ["zs0_1"]), ("zs2", zs[0][2], ["zs0_2"])])

                YF = {}

                def rec(gen, qpool, bpool=(0,)):
                    QPOOL[0] = list(qpool)
                    BPOOL[0] = list(bpool)
                    r_ = P.record(gen)
                    QPOOL[0] = [2, 3, 4, 5, 6, 7]
                    BPOOL[0] = [0, 1, 2, 3]
                    return r_

                streams0 = []
                if CARRY[0] is not None:
                    streams0.append(CARRY[0])
                    CARRY[0] = None
                streams0.append(rec(proj_gen(), (4, 5, 6, 7), (4, 5, 6, 7)))
                streams0.append(rec(prep_gen(0), (0, 1)))
                P.schedule(streams0)
                prev = None
                for ti in range(ntiles):
                    streams = []
                    if prev is not None:
                        streams.append(rec(chainpost_gen(prev), (2, 3) if own else (2,)))
                    hg = scan_gens(ti)
                    if len(hg) == 1:
                        streams.append(rec(hg[0], (4, 5, 6, 7)))
                    else:
                        streams.append(rec(hg[0], (4, 5)))
                        streams.append(rec(hg[1], (6, 7)))
                    if ti + 1 < ntiles:
                        streams.append(rec(prep_gen(ti + 1), (0, 1)))
                    if not own:
                        if 1 <= ti <= 4:
                            streams.append(rec(ada_mm_gen(16 + hp * 4 + ti - 1), (3,), (3,)))
                        if ti <= 3:
                            streams.append(rec(ada_dma_gen(16 + hp * 4 + ti), (3,), (3,)))
                    P.schedule(streams)
                    prev = ti
                CARRY[0] = rec(chainpost_gen(prev), (2, 3) if own else (2,))
            P.schedule([CARRY[0]])
            if not own:
                for dc_ in range(16):
                    V(ts(gf[:, dc_, :], modT[:, 64 + dc_, :], 1.0, gvec[:, 16 + dc_:17 + dc_], ALU.add, ALU.mult), [("modT", 4), "gvec"], ["gf"])
            P.barrier()
            ckpt("C1" if not own else "C2", [("Hs", r32(Hs), [("Hs", k) for k in range(8)]), ("oaT", oaT, [("oaT", k) for k in range(8)])])

    mixer_pass(False)
    mixer_pass(True)
    P.dma("sp", DR["shp"], shp, reads=["shp"], semres="shp")
    P.dma("sp", DR["shs"].rearrange("p (j b) -> p j b", b=16), shs, reads=["shs"], semres="shs")
    P.barrier()
    es_mp.close()

    prodT = es_mix.enter_context(_sbt(nc, "prodT", [128, 8, T], BF16)).ap()
    with ExitStack() as es:
        al = lambda name, shape, dt=F32: es.enter_context(_sbt(nc, name, list(shape), dt)).ap()
        wsm = al("wsm", [128, 16, 128], BF16)
        with ExitStack() as es2:
            wsf = es2.enter_context(_sbt(nc, "wsf", [128, 8, 128], F32)).ap()
            P.dma("sp", wsf, DR["w_spT"].rearrange("g s t -> s g t"), writes=["wsf"])
            for g in range(8):
                V(tt(wsm[:, g, :], wsf[:, g, :], m_le, ALU.mult), ["wsf", "cst"], ["wsm"])
            P.dma("sp", wsf, DR["w_spTs"].rearrange("g s t -> s g t"), writes=["wsf"])
            for g in range(8):
                V(tt(wsm[:, 8 + g, :], wsf[:, g, :], ms_le, ALU.mult), ["wsf", "cst"], ["wsm"])
            P.barrier()
        gv = al("gv", [128, 9, 1024], BF16)
        vnb = gv
        lngb = al("lngb", [128, 1024]); lnbb = al("lnbb", [128, 1024])
        P.dma("sp", lngb, DR["ln_g"].partition_broadcast(128), writes=["lngb"])
        P.dma("sp", lnbb, DR["ln_b"].partition_broadcast(128), writes=["lnbb"])
        bspb = al("bspb", [128, 8, 128]); bspsb = al("bspsb", [128, 8, 128])
        P.dma("sp", bspb, DR["bsp"].rearrange("g t -> (g t)").partition_broadcast(128).rearrange("p (g t) -> p g t", t=128), writes=["bspb"])
        P.dma("sp", bspsb, DR["bsps"].rearrange("g t -> (g t)").partition_broadcast(128).rearrange("p (g t) -> p g t", t=128), writes=["bspsb"])
        for blk in range(4):
            wt, wk = load_w(DR["w_in"][:, 4352 + blk * 256: 4352 + (blk + 1) * 256], 16, 256)
            for ti in range(9):
                pb, pk = bank()
                for kc in range(16):
                    MM(pb[:, 0:256], hT[:, kc, ti * 128:(ti + 1) * 128], wt[:, kc, 0:256], kc == 0, kc == 15, [wk, ("hT", kc)], [pk], signal=(kc == 15))
                A(act(gv[:, ti, blk * 256:(blk + 1) * 256], pb[:, 0:256], AF.Gelu_apprx_tanh), [pk], [("gv", ti)])
        st6 = al("st6", [128, 12]); mv = al("mv", [128, 2]); vtmp = [al(f"vtmp{i}", [128, 1024]) for i in range(1)]
        for ti in range(9):
            s = 0
            V(lambda: nc.vector.bn_stats(st6[:, 0:6], gv[:, ti, 0:512]), [("gv", ti)], ["st6"])
            V(lambda: nc.vector.bn_stats(st6[:, 6:12], gv[:, ti, 512:1024]), [("gv", ti)], ["st6"])
            V(lambda: nc.vector.bn_aggr(mv, st6), ["st6"], ["mv"])
            A(act(mv[:, 1:2], mv[:, 1:2], AF.Sqrt, bias=epsc[:, 2:3]), ["mv", "epsc"], ["mv"])
            V(rcp(mv[:, 1:2], mv[:, 1:2]), ["mv"], ["mv"])
            V(ts(vtmp[s], gv[:, ti, :], mv[:, 0:1], mv[:, 1:2], ALU.subtract, ALU.mult), [("gv", ti), "mv"], [f"vtmp{s}"])
            V(tt(vtmp[s], vtmp[s], lngb, ALU.mult), [f"vtmp{s}", "lngb"], [f"vtmp{s}"])
            V(tt(vtmp[s], vtmp[s], lnbb, ALU.add), [f"vtmp{s}", "lnbb"], [f"vtmp{s}"])
            A(cpa(vnb[:, ti, :], vtmp[s]), [f"vtmp{s}"], [("gv", ti)])
            if ti == 8:
                P.dma("sp", DR["v_s"], vtmp[s], reads=[f"vtmp{s}"], semres=f"vtmp{s}")
        uT = [al(f"uT{i}", [128, T]) for i in range(1)]
        mx = [al(f"mx{i}", [128, 128]) for i in range(2)]
        for blk in range(4):
            wt, wk = load_w(DR["w_in"][:, 3328 + blk * 256: 3328 + (blk + 1) * 256], 16, 256)
            for jj in range(2):
                g = blk * 2 + jj
                us, uk = uT[0], "uT0"
                for g0 in range(0, T, 384):
                    dense_fm(wt, wk, jj * 128, 16, hT_fn, hT_keys, g0, 384,
                             lambda ps, pk, g0=g0: A(act(us[:, g0:g0 + 384], ps, AF.Gelu_apprx_tanh), [pk], [uk]))
                for ti in range(9):
                    wsel = g if ti < 8 else 8 + g
                    bsel = bspb if ti < 8 else bspsb
                    q, qk = quart()
                    MM(q, vnb[:, ti, g * 128:(g + 1) * 128], wsm[:, wsel, :], True, True, [("gv", ti), "wsm"], [qk])
                    m_ = mx[ti % 2]; mk = f"mx{ti % 2}"
                    V(tt(m_, q, bsel[:, g, :], ALU.add), [qk, "bspb", "bspsb"], [mk])
                    V(tt(prodT[:, g, ti * 128:(ti + 1) * 128], m_, us[:, ti * 128:(ti + 1) * 128], ALU.mult), [mk, uk], [("prodT", g)])
        P.barrier()
        ckpt("D", [("prodT", prodT, [("prodT", k) for k in range(8)]), ("gv", gv, [("gv", k) for k in range(9)])])

    es_mg = ExitStack()
    merged = es_mg.enter_context(_sbt(nc, "merged", [128, 16, T], BF16)).ap()
    with ExitStack() as es:
        al = lambda name, shape, dt=F32: es.enter_context(_sbt(nc, name, list(shape), dt)).ap()
        wba = al("wba", [128, 8, 256], BF16)
        sa = [al(f"sa{i}", [128, 384]) for i in range(1)]
        sb_ = [al(f"sb{i}", [128, 384]) for i in range(1)]
        wbb = al("wbb", [128, 8, 256], BF16)
        cnt = 0
        for blk in range(8):
            wta, wka = wslot()
            wtb, wkb = wslot()
            P.dma("pool", wta[:, :, 0:256], DR["w_in"][:, 5376 + blk * 256:5376 + (blk + 1) * 256].rearrange("(kc p) n -> p kc n", p=128), writes=[wka])
            P.dma("pool", wtb[:, :, 0:256], DR["w_in"][:, 7424 + blk * 256:7424 + (blk + 1) * 256].rearrange("(kc p) n -> p kc n", p=128), writes=[wkb])
            P.dma("pool", wba, DR["w_branch_a"][:, blk * 256:(blk + 1) * 256].rearrange("(kc p) n -> p kc n", p=128), writes=["wba"])
            P.dma("pool", wbb, DR["w_branch_b"][:, blk * 256:(blk + 1) * 256].rearrange("(kc p) n -> p kc n", p=128), writes=["wbb"])
            for jj in range(2):
                dc = blk * 2 + jj
                for g0 in range(0, T, 384):
                    s = 0
                    cs = slice(g0, g0 + 384)
                    dense_fm(wta, wka, jj * 128, 16, hT_fn, hT_keys, g0, 384,
                             lambda ps, pk: A(act(sa[s], ps, AF.Sigmoid), [pk], [f"sa{s}"]))
                    dense_fm(wtb, wkb, jj * 128, 16, hT_fn, hT_keys, g0, 384,
                             lambda ps, pk: A(act(sb_[s], ps, AF.Sigmoid), [pk], [f"sb{s}"]))
                    dense_fm(wba, "wba", jj * 128, 8, lambda kc: oaT[:, kc, :], lambda kc: [("oaT", kc)], g0, 384,
                             lambda ps, pk: V(tt(sa[s], sa[s], ps, ALU.mult), [f"sa{s}", pk], [f"sa{s}"]))
                    dense_fm(wbb, "wbb", jj * 128, 8, lambda kc: prodT[:, kc, :], lambda kc: [("prodT", kc)], g0, 384,
                             lambda ps, pk: V(tt(sb_[s], sb_[s], ps, ALU.mult), [f"sb{s}", pk], [f"sb{s}"]))
                    V(tt(merged[:, dc, cs], sa[s], sb_[s], ALU.add), [f"sa{s}", f"sb{s}"], [("merged", dc)])
        P.barrier()
        ckpt("E", [("merged", merged, [("merged", k) for k in range(16)])])
    es_mix.close()
    es_x = ExitStack()
    x1 = es_x.enter_context(_sbt(nc, "x1", [128, 16, T], F32)).ap()

    def resid_evac(ps, pk, dc, g0, n, gate_row):
        npr = max(0, min(TP, g0 + n) - g0)
        if npr > 0:
            V(stt(x1[:, dc, g0:g0 + npr], ps[:, 0:npr], modT[:, gate_row + dc, 0:1], x1[:, dc, g0:g0 + npr], ALU.mult, ALU.add),
              [pk, *MODK, ("x1", dc)], [("x1", dc)])
        if g0 + n > TP:
            a0 = max(g0, TP)
            v3 = lambda a: a.rearrange("p (b t) -> p b t", t=8)
            nb0 = (a0 - TP) // 8
            nb = (g0 + n - a0) // 8
            gb = modT[:, gate_row + dc, 1 + nb0:1 + nb0 + nb].unsqueeze(2).broadcast_to([128, nb, 8])
            V(tt(v3(ps[:, a0 - g0:n]), v3(ps[:, a0 - g0:n]), gb, ALU.mult), [pk, *MODK], [pk])
            V(tt(x1[:, dc, a0:g0 + n], ps[:, a0 - g0:n], x1[:, dc, a0:g0 + n], ALU.add), [pk, ("x1", dc)], [("x1", dc)])

    for blk in range(8):
        wt, wk = load_w(DR["w_out"][:, blk * 256:(blk + 1) * 256], 16, 256)
        for jj in range(2):
            dc = blk * 2 + jj
            P.dma("sp", x1[:, dc, :], DR["xT_own"][dc * 128:(dc + 1) * 128, :], writes=[("x1", dc)])
            for g0 in range(0, T, 384):
                dense_fm(wt, wk, jj * 128, 16, lambda kc: merged[:, kc, :], lambda kc: [("merged", kc)], g0, 384,
                         lambda ps, pk, dc=dc, g0=g0: resid_evac(ps, pk, dc, g0, 384, 32))
    P.barrier()
    ckpt("F", [("x1", x1, [("x1", k) for k in range(16)])])
    es_mg.close()

    with ExitStack() as es:
        al = lambda name, shape, dt=F32: es.enter_context(_sbt(nc, name, list(shape), dt)).ap()
        h2T = al("h2T", [128, 16, T], BF16)
        sq = [al(f"sq{i}", [128, 512], F32R) for i in range(2)]
        rstd = al("rstd", [128, 512]); tmpn = rstd
        tq = [al(f"tq{i}", [128, 512]) for i in range(2)]
        for g0 in range(0, T, 512):
            n = min(512, T - g0)
            rms_rstd((sq, rstd, tmpn), lambda dc: x1[:, dc, g0:g0 + n], n, lambda dc: [("x1", dc)])
            for dc in range(16):
                s = dc % 2
                if g0 >= TP:
                    b3 = lambda a: a.unsqueeze(2).broadcast_to([128, 16, 8])
                    v3 = lambda a: a.rearrange("p (b t) -> p b t", t=8)
                    V(tt(v3(tq[s][:, 0:n]), v3(x1[:, dc, g0:g0 + n]), b3(gf[:, dc, 1:17]), ALU.mult), [("x1", dc), "gf"], [f"tq{s}"])
                    V(tt(tq[s][:, 0:n], tq[s][:, 0:n], rstd[:, 0:n], ALU.mult), [f"tq{s}", "rstd"], [f"tq{s}"])
                    V(tt(v3(h2T[:, dc, g0:g0 + n]), v3(tq[s][:, 0:n]), b3(modT[:, 48 + dc, 1:17]), ALU.add), [f"tq{s}", *MODK], [("h2T", dc)])
                else:
                    V(stt(tq[s][:, 0:n], x1[:, dc, g0:g0 + n], gf[:, dc, 0:1], rstd[:, 0:n], ALU.mult, ALU.mult), [("x1", dc), "gf", "rstd"], [f"tq{s}"])
                    A(act(h2T[:, dc, g0:g0 + n], tq[s][:, 0:n], AF.Identity, bias=modT[:, 48 + dc, 0:1]), [f"tq{s}", *MODK], [("h2T", dc)])
        actT = al("actT", [128, 4, T], BF16)
        sl = [al(f"sl{i}", [128, 384]) for i in range(1)]
        wfo = [al(f"wfo{i}", [128, 4, 256], BF16) for i in range(2)]
        cnt = 0
        for qd in range(11):
            for jj in range(4):
                j = qd * 4 + jj
                wt, wk = wslot()
                P.dma("pool", wt[:, :, 0:128], DR["w_ffn_in"][:, j * 128:(j + 1) * 128].rearrange("(kc p) n -> p kc n", p=128), writes=[wk])
                P.dma("pool", wt[:, :, 128:256], DR["w_ffn_in"][:, DFF + j * 128:DFF + (j + 1) * 128].rearrange("(kc p) n -> p kc n", p=128), writes=[wk])
                for g0 in range(0, T, 384):
                    s = 0
                    dense_fm(wt, wk, 0, 16, lambda kc: h2T[:, kc, :], lambda kc: [("h2T", kc)], g0, 384,
                             lambda ps, pk: A(act(sl[s], ps, AF.Silu), [pk], [f"sl{s}"]))
                    dense_fm(wt, wk, 128, 16, lambda kc: h2T[:, kc, :], lambda kc: [("h2T", kc)], g0, 384,
                             lambda ps, pk, g0=g0, jj=jj: V(tt(actT[:, jj, g0:g0 + 384], sl[s], ps, ALU.mult), [f"sl{s}", pk], [("actT", jj)]))
            for blk in range(8):
                wo, wok = wfo[blk % 2], f"wfo{blk % 2}"
                P.dma("pool", wo, DR["w_ffn_out"][qd * 512:(qd + 1) * 512, blk * 256:(blk + 1) * 256].rearrange("(kc p) n -> p kc n", p=128), writes=[wok])
                for jj2 in range(2):
                    dc = blk * 2 + jj2
                    for g0 in range(0, T, 384):
                        dense_fm(wo, wok, jj2 * 128, 4, lambda kc: actT[:, kc, :], lambda kc: [("actT", kc)], g0, 384,
                                 lambda ps, pk, dc=dc, g0=g0: resid_evac(ps, pk, dc, g0, 384, 80))
        yo = tq
        for g0 in range(0, T, 512):
            n = min(512, T - g0)
            rms_rstd((sq, rstd, tmpn), lambda dc: x1[:, dc, g0:g0 + n], n, lambda dc: [("x1", dc)])
            for dc in range(16):
                s = dc % 2
                V(stt(yo[s][:, 0:n], x1[:, dc, g0:g0 + n], gvec[:, 32 + dc:33 + dc], rstd[:, 0:n], ALU.mult, ALU.mult), [("x1", dc), "gvec", "rstd"], [f"tq{s}"])
                P.dma("sp", DR["yT"][dc * 128:(dc + 1) * 128, g0:g0 + n], yo[s][:, 0:n], reads=[f"tq{s}"], semres=f"tq{s}")
        P.finish("sp")
    es_x.close()


def _consts():
    i = np.arange(128)
    r, c = i[:, None], i[None, :]
    same = (r // 8) == (c // 8)
    f = lambda m: m.astype(np.float32)
    parts = [np.eye(128, dtype=np.float32), f(r < c), f(r > c), f(r <= c),
             f((r < c) & same), f((r > c) & same), f((r <= c) & same),
             f(np.broadcast_to((c % 8) != 0, (128, 128))), f((r // 64) == (c // 64)), np.ones((128, 128), np.float32),
             f((r // 8) == np.arange(16)[None, :])]
    return np.ascontiguousarray(np.concatenate(parts, axis=1))


def _constsb():
    import ml_dtypes
    i = np.arange(128)
    r, c = i[:, None], i[None, :]
    parts = []
    for b in (8, 16, 32, 64):
        parts.append(((r // (2 * b)) == (c // (2 * b))) & ((r // b) != (c // b)) & (r > c))
    parts = parts + [p.T for p in parts]
    parts.append((r // 64) == (c // 64))
    return np.ascontiguousarray(np.concatenate(parts, axis=1).astype(np.float32).astype(ml_dtypes.bfloat16))


def _col(v, n):
    return np.ascontiguousarray(np.asarray(v, np.float32).reshape(n, 128).T)


_NC_CACHE = {}


def _prep(x_prompt, x_sample, state_wkv, state_shift, c_prompt, c_sample,
           w_ada, b_ada, norm_mix_g, w_in, mu_shift, w0, w_decay_up, a0, w_aaa_up,
           w_gate_up, k_k, k_a, r_k, gn_g, gn_b, ln_v_g, ln_v_b, w_spatial, b_spatial,
           w_branch_a, w_branch_b, w_out, norm_ffn_g, w_ffn_in, w_ffn_out, norm_final_g):
    f = lambda a: np.ascontiguousarray(np.asarray(a, np.float32))
    x_prompt, x_sample = f(x_prompt), f(x_sample)
    state_wkv, state_shift = f(state_wkv)[0], f(state_shift)[0]
    c_prompt, c_sample = f(c_prompt), f(c_sample)
    shared = {
        "w_ada": f(w_ada)[0], "badaT": _col(f(b_ada)[0], 96),
        "gvec": np.concatenate([_col(f(norm_mix_g)[0], 16), _col(f(norm_ffn_g)[0], 16), _col(f(norm_final_g), 16)], 1),
        "w_in": f(w_in)[0],
        "lora_up": np.ascontiguousarray(np.concatenate([f(w_decay_up)[0], f(w_aaa_up)[0]], 0)),
        "w_gate_up": f(w_gate_up)[0], "ln_g": f(ln_v_g)[0], "ln_b": f(ln_v_b)[0],
        "w_branch_a": f(w_branch_a)[0], "w_branch_b": f(w_branch_b)[0], "w_out": f(w_out)[0],
        "w_ffn_in": f(w_ffn_in)[0], "w_ffn_out": f(w_ffn_out)[0], "cst": _consts(), "cstb": _constsb(),
    }
    mu = f(mu_shift)[0]
    ka = f(k_a)[0]
    vecs = [f(w0)[0], f(a0)[0], f(k_k)[0], ka, ka, f(gn_g)[0], f(gn_b)[0], f(r_k)[0].reshape(-1), ka]
    sv = [_col(mu, 26)] + [_col(v, 8) for v in vecs]
    shared["svec"] = np.ascontiguousarray(np.concatenate(sv, 1))
    wsp = f(w_spatial)[0]
    shared["w_spT"] = np.ascontiguousarray(wsp.transpose(0, 2, 1))
    blkT = np.zeros((8, 128, 128), np.float32)
    for b in range(16):
        blkT[:, b * 8:(b + 1) * 8, b * 8:(b + 1) * 8] = wsp[:, :8, :8].transpose(0, 2, 1)
    shared["w_spTs"] = blkT
    bsp = f(b_spatial)[0]
    shared["bsp"] = bsp
    shared["bsps"] = np.ascontiguousarray(np.tile(bsp[:, :8], (1, 16)))
    in_maps = []
    for c in range(8):
        b, half = c // 2, c % 2
        xs = x_sample[16 * c:16 * (c + 1)].reshape(128, D)
        xo = np.concatenate([x_prompt[b, half * 1024:(half + 1) * 1024], xs], 0)
        xp = x_prompt[b, 0:1024]
        cc = np.concatenate([c_prompt[b:b + 1], c_sample[16 * c:16 * (c + 1)]], 0)
        sw = state_wkv[16 * c:16 * (c + 1)]
        s0T = sw.reshape(16, 8, 2, 64, 64).transpose(1, 2, 4, 0, 3).reshape(8, 128, 16 * 64)
        ssh = state_shift[16 * c:16 * (c + 1)]
        sshT = ssh.reshape(16, 26, 128).transpose(2, 1, 0).reshape(128, 26 * 16)
        m = dict(shared)
        m.update({"xT_own": np.ascontiguousarray(xo.T), "xT_prev": np.ascontiguousarray(xp.T),
                  "cT": np.ascontiguousarray(cc.T), "flag": np.full((128, 1), float(half), np.float32),
                  "s0T": np.ascontiguousarray(s0T), "sshT": np.ascontiguousarray(sshT)})
        in_maps.append(m)
    return in_maps


def kernel(x_prompt, x_sample, state_wkv, state_shift, c_prompt, c_sample,
           w_ada, b_ada, norm_mix_g, w_in, mu_shift, w0, w_decay_up, a0, w_aaa_up,
           w_gate_up, k_k, k_a, r_k, gn_g, gn_b, ln_v_g, ln_v_b, w_spatial, b_spatial,
           w_branch_a, w_branch_b, w_out, norm_ffn_g, w_ffn_in, w_ffn_out, norm_final_g):
    in_maps = _prep(x_prompt, x_sample, state_wkv, state_shift, c_prompt, c_sample,
                    w_ada, b_ada, norm_mix_g, w_in, mu_shift, w0, w_decay_up, a0, w_aaa_up,
                    w_gate_up, k_k, k_a, r_k, gn_g, gn_b, ln_v_g, ln_v_b, w_spatial, b_spatial,
                    w_branch_a, w_branch_b, w_out, norm_ffn_g, w_ffn_in, w_ffn_out, norm_final_g)
    if "nc" not in _NC_CACHE:
        _NC_CACHE["nc"] = build_nc()
    res = run_bass_kernel_spmd(_NC_CACHE["nc"], in_maps, core_ids=list(range(8)))
    R = res.results
    y_prompt = np.zeros((4, 2048, D), np.float32); y_sample = np.zeros((128, 8, D), np.float32)
    wkv_p = np.zeros((1, 4, 16, 64, 64), np.float32); shift_p = np.zeros((1, 4, 3328), np.float32)
    wkv_s = np.zeros((1, 128, 16, 64, 64), np.float32); shift_s = np.zeros((1, 128, 3328), np.float32)
    v_s = np.zeros((1, 128, 8, 1024), np.float32)
    for c in range(8):
        b, half = c // 2, c % 2
        yT = R[c]["yT"]
        y_prompt[b, half * 1024:(half + 1) * 1024] = yT[:, :1024].T
        y_sample[16 * c:16 * (c + 1)] = yT[:, 1024:].T.reshape(16, 8, D)
        if half == 1:
            hp = R[c]["wkv_p"]
            for p in range(8):
                for h2 in range(2):
                    wkv_p[0, b, 2 * p + h2] = hp[p, h2 * 64:(h2 + 1) * 64, h2 * 64:(h2 + 1) * 64].T
            shift_p[0, b] = R[c]["shp"].T.reshape(-1)
        ws = R[c]["wkv_s"].reshape(8, 2, 64, 16, 64)
        wkv_s[0, 16 * c:16 * (c + 1)] = ws.transpose(3, 0, 1, 4, 2).reshape(16, 16, 64, 64)
        shift_s[0, 16 * c:16 * (c + 1)] = R[c]["shs"].reshape(128, 26, 16).transpose(2, 1, 0).reshape(16, 3328)
        v_s[0, 16 * c:16 * (c + 1)] = R[c]["v_s"].reshape(16, 8, 1024)
    return (y_prompt, y_sample, wkv_p, shift_p, wkv_s, shift_s, v_s)
```

```python
import numpy as np
import concourse.bass as bass
import concourse.mybir as mybir
from concourse.bass_utils import run_bass_kernel_spmd
from contextlib import ExitStack

F32 = mybir.dt.float32
F32R = mybir.dt.float32r
BF16 = mybir.dt.bfloat16
AF = mybir.ActivationFunctionType
ALU = mybir.AluOpType
AX = mybir.AxisListType

EPOCH = 8192
D = 2048
T = 1152
TP = 1024
DFF = 5632
CIN = 9472
NCST = 10 * 128 + 16


class Prog:
    def __init__(self, nc):
        self.nc = nc
        self.eng = {"pe": nc.tensor, "act": nc.scalar, "dve": nc.vector,
                    "pool": nc.gpsimd, "sp": nc.sync}
        self.cnt = {e: 0 for e in self.eng}
        self.sems = {e: [] for e in self.eng}
        self.seen = {e: {} for e in self.eng}
        self.last_w = {}
        self.readers = {}
        self.dma_sems = {}
        self.pend_r = {e: [] for e in self.eng}
        self.pend_w = {e: [] for e in self.eng}
        self.nsem = 0
        self.ninstr = {e: 0 for e in self.eng}
        self.rec = None
        self.junk = None
        self.m_eng = {e: 0.0 for e in self.eng}
        self.m_key = {}

    def record(self, gen):
        self.rec = []
        for _ in gen:
            pass
        r, self.rec = self.rec, None
        return r

    def schedule(self, streams):
        from collections import Counter
        DUR = {"pe": 0.2, "act": 0.25, "dve": 0.22, "pool": 0.5, "sp": 0.05}
        LAT = 0.3
        isps = lambda k: isinstance(k, str) and k.startswith("psb")
        units = []
        for st in streams:
            us, cur = [], []
            for o in st:
                cur.append(o)
                if o[5]:
                    us.append(cur)
                    cur = []
            if cur:
                us.append(cur)
            units.append(us)
        pend_r = [Counter(k for u in us for o in u for k in o[3] if not isps(k)) for us in units]
        pend_w = [Counter(k for u in us for o in u for k in o[4] if not isps(k)) for us in units]
        idx = [0] * len(units)
        while True:
            best = None
            for j, us in enumerate(units):
                if idx[j] >= len(us):
                    continue
                u = us[idx[j]]
                keys = [k for o in u for k in (o[3] + o[4])]
                rk_ = [k for o in u for k in o[3] if not isps(k)]
                wk_ = [k for o in u for k in o[4] if not isps(k)]
                if any(pend_w[i][k] > 0 for k in rk_ for i in range(j)) or \
                   any(pend_w[i][k] > 0 or pend_r[i][k] > 0 for k in wk_ for i in range(j)):
                    continue
                F = u[0][1]
                t_ready = max([self.m_key.get(k, 0.0) for k in keys] + [0.0])
                start = max(self.m_eng[F], t_ready)
                if best is None or start < best[0]:
                    best = (start, j, u, keys, F)
            if best is None:
                break
            start, j, u, keys, F = best
            t = start
            for o in u:
                if o[0] == "op":
                    self.op(o[1], o[2], o[3], o[4], o[5])
                    t += (o[6] if len(o) > 6 and o[6] else DUR[o[1]])
                else:
                    out, in_, semres, kw = o[2]
                    self.dma(o[1], out, in_, o[3], o[4], semres, **kw)
                    t += DUR["sp"]
            self.m_eng[F] = t
            fin = t + LAT + ((u[0][6] or 2.0) if u[0][0] == "dma" else 0.0)
            for o in u:
                for k in o[4] + [k2 for k2 in o[3] if isps(k2)]:
                    self.m_key[k] = fin
            for o in u:
                for k in o[3]:
                    if not isps(k):
                        pend_r[j][k] -= 1
                for k in o[4]:
                    if not isps(k):
                        pend_w[j][k] -= 1
            idx[j] += 1

    def _newsem(self, name):
        self.nsem += 1
        return self.nc.alloc_semaphore(name)

    def _deps(self, F, reads, writes):
        deps = {}

        def add(tok, same_ok):
            if tok is None:
                return
            key, sem, val, eng = tok
            if eng == F and F == "pe":
                return
            if val > deps.get(key, (None, 0))[1]:
                deps[key] = (sem, val)

        for r in reads:
            add(self.last_w.get(r), True)
        for w in writes:
            add(self.last_w.get(w), True)
            for t in self.readers.get(w, ()):
                add(t, False)
        return deps

    def _emit_waits(self, F, deps):
        e = self.eng[F]
        for key, (sem, val) in deps.items():
            if self.seen[F].get(key, 0) >= val:
                continue
            e.wait_ge(sem, val)
            self.seen[F][key] = val

    def _register(self, tok, reads, writes):
        for r in reads:
            lst = self.readers.setdefault(r, [])
            lst[:] = [t for t in lst if t[0] != tok[0]]
            lst.append(tok)
        for w in writes:
            self.last_w[w] = tok
            self.readers[w] = []

    def op(self, F, fn, reads=(), writes=(), signal=True):
        reads = list(reads)
        writes = list(writes)
        if self.rec is not None:
            self.rec.append(("op", F, fn, reads, writes, signal, None))
            return None
        ex = [r for r in reads if isinstance(r, str) and r.startswith("psb")]
        if ex:
            reads = [r for r in reads if r not in ex]
            writes = writes + [r for r in ex if r not in writes]
        deps = self._deps(F, reads, writes)
        self._emit_waits(F, deps)
        ins = fn()
        self.ninstr[F] += 1
        if F == "pe" and signal and self.junk is not None:
            self.junk()
        if not signal:
            self.pend_r[F] += reads
            self.pend_w[F] += writes
            return ins
        i = self.cnt[F]
        self.cnt[F] += 1
        ep = i // EPOCH
        while len(self.sems[F]) <= ep:
            self.sems[F].append(self._newsem(f"s_{F}_{len(self.sems[F])}"))
        sem = self.sems[F][ep]
        val = i % EPOCH + 1
        ins.then_inc(sem, 1)
        tok = ((F, ep), sem, val, F)
        self._register(tok, reads + self.pend_r[F], writes + self.pend_w[F])
        self.pend_r[F] = []
        self.pend_w[F] = []
        return ins

    def dma(self, Q, out, in_, reads=(), writes=(), semres=None, est=None, **kw):
        reads = list(reads)
        writes = list(writes)
        if self.rec is not None:
            self.rec.append(("dma", Q, (out, in_, semres, kw), reads, writes, True, est))
            return None
        if semres is None:
            semres = (writes + reads)[0]
        deps = self._deps("dma", reads, writes)
        self._emit_waits(Q, deps)
        if semres not in self.dma_sems:
            self.dma_sems[semres] = [self._newsem(f"d_{len(self.dma_sems)}"), 0]
        ent = self.dma_sems[semres]
        ent[1] += 16
        ins = self.eng[Q].dma_start(out=out, in_=in_, **kw)
        ins.then_inc(ent[0], 16)
        self.ninstr[Q] += 1
        tok = (("dma", semres), ent[0], ent[1], "dma")
        self._register(tok, reads, writes)
        return ins

    def barrier(self):
        for F, e in self.eng.items():
            for semres, (sem, val) in self.dma_sems.items():
                key = ("dma", semres)
                if val > self.seen[F].get(key, 0):
                    e.wait_ge(sem, val)
                    self.seen[F][key] = val
            for E in self.eng:
                if E == F or self.cnt[E] == 0:
                    continue
                i = self.cnt[E] - 1
                key = (E, i // EPOCH)
                val = i % EPOCH + 1
                if val > self.seen[F].get(key, 0):
                    e.wait_ge(self.sems[E][i // EPOCH], val)
                    self.seen[F][key] = val

    def finish(self, F="sp"):
        e = self.eng[F]
        for semres, (sem, val) in self.dma_sems.items():
            if val > 0:
                e.wait_ge(sem, val)
        for E in self.eng:
            if self.cnt[E] > 0:
                i = self.cnt[E] - 1
                e.wait_ge(self.sems[E][i // EPOCH], i % EPOCH + 1)


_UC = [0]


def _uname(name):
    _UC[0] += 1
    return f"t{_UC[0]}_{name}"


class _Arena:
    def __init__(self):
        self.ap = None
        self.free = []

    def init(self, nc, name="arena", dt=F32, n=None):
        if n is None:
            nbytes = int(nc.sbuf_bytes_remaining) - 1024
            n = nbytes // 4
        self.ap = nc.alloc_sbuf_tensor(name, [128, n], dt).ap()
        self.free = [(0, n)]
        self.n = n

    def alloc(self, words):
        words = (words + 15) // 16 * 16
        for i, (st, sz) in enumerate(self.free):
            if sz >= words:
                if sz == words:
                    self.free.pop(i)
                else:
                    self.free[i] = (st + words, sz - words)
                return st, words
        raise MemoryError(f"arena full: need {words} words, free={self.free}")

    def release(self, st, words):
        self.free.append((st, words))
        self.free.sort()
        out = []
        for a, b in self.free:
            if out and out[-1][0] + out[-1][1] == a:
                out[-1] = (out[-1][0], out[-1][1] + b)
            else:
                out.append((a, b))
        self.free = out


_AR = _Arena()
_ARR = _Arena()


class _Tile:
    def __init__(self, shape, dt):
        self.shape = list(shape)
        self.dt = dt

    def __enter__(self):
        esz = 2 if self.dt == BF16 else 4
        per = 1
        for d in self.shape[1:]:
            per *= d
        words = (per * esz + 3) // 4
        self.ar = _ARR if self.dt == F32R else _AR
        self.st, self.words = self.ar.alloc(words)
        v = self.ar.ap[:, self.st:self.st + words]
        if self.dt == BF16:
            v = v.bitcast(self.dt)
        v = v[:, 0:per]
        if len(self.shape) == 3:
            v = v.rearrange("p (a b) -> p a b", b=self.shape[2])
        elif len(self.shape) == 4:
            v = v.rearrange("p (a b c) -> p a b c", b=self.shape[2], c=self.shape[3])
        if self.shape[0] != 128:
            v = v[0:self.shape[0]]
        self._ap = v
        return self

    def ap(self):
        return self._ap

    def __exit__(self, *a):
        self.ar.release(self.st, self.words)
        return False


def _sbt(nc, name, shape, dt):
    return _Tile(shape, dt)


def r32(ap):
    return ap.bitcast(F32)


class _Stop(Exception):
    pass


def build_nc(stop=None):
    nc = bass.Bass("TRN2", target_bir_lowering=False)
    P = Prog(nc)
    DR = {}
    try:
        _build(nc, P, DR, stop)
    except _Stop:
        pass
    print("ninstr", P.ninstr, "nsem", P.nsem, flush=True)
    return nc


NJUNK = 256


def _build(nc, P, DR, stop):
    def ckpt(tag, dumps):
        if stop != tag:
            return
        for name, ap, keys in dumps:
            d = nc.dram_tensor("dbg_" + name, list(ap.shape), ap.dtype, kind="ExternalOutput").ap()
            P.dma("sp", d, ap, reads=keys, semres=("dbg", name))
        P.finish("sp")
        raise _Stop()

    cpa = lambda o, i: (lambda: nc.scalar.copy(o, i))
    cpv = lambda o, i: (lambda: nc.vector.tensor_copy(o, i))
    rcp = lambda o, i: (lambda: nc.vector.reciprocal(o, i))
    scn = lambda o, d0, d1, init, o0, o1: (lambda: nc.vector.tensor_tensor_scan(o, d0, d1, init, o0, o1))

    def din(name, shape):
        DR[name] = nc.dram_tensor(name, list(shape), F32, kind="ExternalInput").ap()

    def dout(name, shape):
        DR[name] = nc.dram_tensor(name, list(shape), F32, kind="ExternalOutput").ap()

    din("xT_own", [D, T]); din("xT_prev", [D, TP]); din("cT", [D, 17]); din("flag", [128, 1])
    din("s0T", [8, 128, 16 * 64]); din("sshT", [128, 26 * 16])
    din("w_ada", [D, 6 * D]); din("badaT", [128, 96])
    din("gvec", [128, 48])
    din("w_in", [D, CIN]); din("svec", [128, 26 + 9 * 8])
    din("lora_up", [128, 1024]); din("w_gate_up", [128, 1024])
    din("ln_g", [1024]); din("ln_b", [1024])
    din("w_spT", [8, 128, 128]); din("w_spTs", [8, 128, 128]); din("bsp", [8, 128]); din("bsps", [8, 128])
    din("w_branch_a", [1024, D]); din("w_branch_b", [1024, D]); din("w_out", [D, D])
    din("w_ffn_in", [D, 2 * DFF]); din("w_ffn_out", [DFF, D]); din("cst", [128, NCST])
    DR["cstb"] = nc.dram_tensor("cstb", [128, 1152], BF16, kind="ExternalInput").ap()
    dout("yT", [D, T]); dout("wkv_p", [8, 128, 128]); dout("shp", [128, 26])
    dout("wkv_s", [8, 128, 16 * 64]); dout("shs", [128, 26 * 16]); dout("v_s", [128, 1024])

    def sbp(name, shape, dt=F32):
        return nc.alloc_sbuf_tensor(_uname(name), list(shape), dt).ap()

    cst = sbp("cst", [128, NCST])
    P.dma("sp", cst, DR["cst"], writes=["cst"])
    ident = cst[:, 0:128]; m_sl = cst[:, 128:256]; m_gt = cst[:, 256:384]; m_le = cst[:, 384:512]
    ms_sl = cst[:, 512:640]; ms_gt = cst[:, 640:768]; ms_le = cst[:, 768:896]
    mreset = cst[:, 896:1024]; blk1 = cst[:, 1024:1152]; ones = cst[:, 1152:1280]
    maskTB = cst[:, 1280:1296]
    cstb = sbp("cstb", [128, 1152], BF16)
    blk1b = cstb[:, 1024:1152]
    P.dma("sp", cstb, DR["cstb"], writes=["cstb"])
    moff = [cstb[:, l * 128:(l + 1) * 128] for l in range(4)]
    moffT = [cstb[:, 512 + l * 128:512 + (l + 1) * 128] for l in range(4)]
    onesR = sbp("onesR", [128, 128], F32R)
    P.op("dve", cpv(onesR, ones), ["cst"], ["onesR"])
    epsc = sbp("epsc", [128, 8])
    P.op("dve", lambda: nc.vector.memset(epsc[:, 0:1], 1e-6), [], ["epsc"])
    P.op("dve", lambda: nc.vector.memset(epsc[:, 1:2], 64e-5), [], ["epsc"])
    P.op("dve", lambda: nc.vector.memset(epsc[:, 2:3], 1e-5), [], ["epsc"])
    P.op("dve", lambda: nc.vector.memset(epsc[:, 3:4], 1.0), [], ["epsc"])
    P.op("dve", lambda: nc.vector.memset(epsc[:, 4:5], -0.5), [], ["epsc"])
    flag = sbp("flag", [128, 1]); P.dma("sp", flag, DR["flag"], writes=["flag"])
    gvec = sbp("gvec", [128, 48]); P.dma("sp", gvec, DR["gvec"], writes=["gvec"])
    svec = sbp("svec", [128, 114]); P.dma("sp", svec[:, 0:98], DR["svec"], writes=["svec"])
    P.op("dve", lambda: nc.vector.tensor_scalar(svec[:, 98:114], svec[:, 26:42], -1.0, None, ALU.mult, ALU.bypass), ["svec"], ["svec"])
    P.op("dve", lambda: nc.vector.tensor_scalar(svec[:, 58:66], svec[:, 50:58], -1.0, 1.0, ALU.mult, ALU.add), ["svec"], ["svec"])
    zeros = sbp("zeros", [128, 128])
    P.op("dve", lambda: nc.vector.memset(zeros, 0.0), [], ["zeros"])
    muT = svec[:, 0:26]
    sv = lambda i, hp: svec[:, 26 + 8 * i + hp: 26 + 8 * i + hp + 1]
    badaT = sbp("badaT", [128, 96]); P.dma("sp", badaT, DR["badaT"], writes=["badaT"])
    modT = sbp("modT", [128, 96, 17])
    MODK = [("modT", k) for k in range(6)]
    gm = sbp("gm", [128, 16, 17]); gf = sbp("gf", [128, 16, 17])
    WB = [sbp(f"WB{i}", [128, 16, 256], BF16) for i in range(2)]
    wbc = [0]

    def wslot():
        i = wbc[0] % 2
        wbc[0] += 1
        return WB[i], f"WB{i}"

    psb = [nc.alloc_psum_tensor(f"psb{i}", [128, 512], F32).ap() for i in range(8)]
    bc = [0]; qc = [0]

    BPOOL = [[0, 1, 2, 3]]
    QPOOL = [[2, 3, 4, 5, 6, 7]]

    def bank():
        pool_ = BPOOL[0]
        i = pool_[bc[0] % len(pool_)]
        bc[0] += 1
        return psb[i], f"psb{i}"

    def quart():
        pool_ = QPOOL[0]
        i = pool_[qc[0] % len(pool_)]
        qc[0] += 1
        return psb[i][:, 0:128], f"psb{i}"

    V = lambda fn, r, w: P.op("dve", fn, r, w)
    A = lambda fn, r, w: P.op("act", fn, r, w)

    def MM(out, lhsT, rhs, start, stop, r, w, signal=True):
        return P.op("pe", lambda: nc.tensor.matmul(out, lhsT=lhsT, rhs=rhs, start=start, stop=stop), r, w, signal)

    def TR(out, in_, r, w):
        return P.op("pe", lambda: nc.tensor.transpose(out, in_, ident), list(r) + ["cst"], w)

    tt = lambda o, a, b, op: (lambda: nc.vector.tensor_tensor(o, a, b, op))
    ts = lambda o, a, s1, s2, o0, o1: (lambda: nc.vector.tensor_scalar(o, a, s1, s2, o0, o1))
    stt = lambda o, a, s, b, o0, o1: (lambda: nc.vector.scalar_tensor_tensor(o, a, s, b, o0, o1))
    act = lambda o, i, f, **kw: (lambda: nc.scalar.activation(o, i, f, **kw))

    def load_w(src_ap, nk, ncols, dst=None, key=None):
        if dst is None:
            dst, key = wslot()
        P.dma("pool", dst[:, 0:nk, 0:ncols], src_ap.rearrange("(kc p) n -> p kc n", p=128), writes=[key])
        return dst, key

    _ARR.init(nc, "arenaR", F32R, 10624)
    _AR.init(nc)
    scb = sbp("scb", [128, 16, 17], BF16)
    with ExitStack() as es:
        cTt = es.enter_context(_sbt(nc, "cTt", [128, 16, 17], F32)).ap()
        P.dma("sp", cTt, DR["cT"].rearrange("(kc p) n -> p kc n", p=128), writes=["cTt"])
        A(act(scb, cTt, AF.Silu), ["cTt"], ["scb"])
        P.barrier()

    ADA_SLOT = {}

    def ada_dma_gen(blk):
        dst, key = wslot()
        ADA_SLOT[blk] = (dst, key)
        P.dma("pool", dst[:, 0:16, 0:256], DR["w_ada"][:, blk * 256:(blk + 1) * 256].rearrange("(kc p) n -> p kc n", p=128),
              writes=[key], est=14.0)
        yield

    def ada_mm_gen(blk):
        wa_, wak = ADA_SLOT[blk]
        for jj in range(2):
            j = blk * 2 + jj
            pb, pk = bank()
            for kc in range(16):
                MM(pb[:, 0:17], wa_[:, kc, jj * 128:(jj + 1) * 128], scb[:, kc, :], kc == 0, kc == 15,
                   [wak, "scb"], [pk], signal=(kc == 15))
            A(act(modT[:, j, :], pb[:, 0:17], AF.Identity, bias=badaT[:, j:j + 1]), [pk, "badaT"], [("modT", j // 16)])
        yield

    def ada_block(blk):
        wa_, wak = load_w(DR["w_ada"][:, blk * 256:(blk + 1) * 256], 16, 256)
        for jj in range(2):
            j = blk * 2 + jj
            pb, pk = bank()
            for kc in range(16):
                MM(pb[:, 0:17], wa_[:, kc, jj * 128:(jj + 1) * 128], scb[:, kc, :], kc == 0, kc == 15,
                   [wak, "scb"], [pk], signal=(kc == 15))
            A(act(modT[:, j, :], pb[:, 0:17], AF.Identity, bias=badaT[:, j:j + 1]), [pk, "badaT"], [("modT", j // 16)])

    for blk in range(16):
        ada_block(blk)
    for dc in range(16):
        V(ts(gm[:, dc, :], modT[:, 16 + dc, :], 1.0, gvec[:, dc:dc + 1], ALU.add, ALU.mult), [("modT", 1), "gvec"], ["gm"])
    ckpt("A", [("modT", modT, MODK), ("gm", gm, ["gm"])])

    def rms_rstd(es_tiles, src_fn, n, srckeys):
        sq, rstd, tmpn = es_tiles
        pb, pk = bank()
        for dc in range(16):
            s = dc % 2
            A(act(sq[s][:, 0:n], src_fn(dc), AF.Square), srckeys(dc), [f"sq{s}"])
            MM(pb[:, 0:n], onesR, sq[s][:, 0:n], dc == 0, dc == 15, [f"sq{s}", "onesR"], [pk], signal=True)
        A(act(tmpn[:, 0:n], pb[:, 0:n], AF.Sqrt, bias=epsc[:, 0:1], scale=1.0 / D), [pk, "epsc"], ["rstd"])
        V(rcp(rstd[:, 0:n], tmpn[:, 0:n]), ["rstd"], ["rstd"])
        return rstd

    def build_hT(es, hT, xsrc, ncols, g_t, shift_base, with_sample):
        xg = es.enter_context(_sbt(nc, "xg", [128, 16, 512], F32)).ap()
        sq = [es.enter_context(_sbt(nc, f"sq{i}", [128, 512], F32R)).ap() for i in range(2)]
        rstd = es.enter_context(_sbt(nc, "rstd", [128, 512], F32)).ap()
        tmpn = es.enter_context(_sbt(nc, "tmpn", [128, 512], F32)).ap()
        tq = [es.enter_context(_sbt(nc, f"tq{i}", [128, 512], F32)).ap() for i in range(2)]
        for g0 in range(0, ncols, 512):
            n = min(512, ncols - g0)
            for dc in range(16):
                P.dma("sp", xg[:, dc, 0:n], xsrc[dc * 128:(dc + 1) * 128, g0:g0 + n], writes=[("xg", dc)])
            rms_rstd((sq, rstd, tmpn), lambda dc: xg[:, dc, 0:n], n, lambda dc: [("xg", dc)])
            for dc in range(16):
                s = dc % 2
                if with_sample and g0 >= TP:
                    b3 = lambda a: a.unsqueeze(2).broadcast_to([128, 16, 8])
                    v3 = lambda a: a.rearrange("p (b t) -> p b t", t=8)
                    V(tt(v3(tq[s][:, 0:n]), v3(xg[:, dc, 0:n]), b3(g_t[:, dc, 1:17]), ALU.mult), [("xg", dc), "gm", "gf"], [f"tq{s}"])
                    V(tt(tq[s][:, 0:n], tq[s][:, 0:n], rstd[:, 0:n], ALU.mult), [f"tq{s}", "rstd"], [f"tq{s}"])
                    V(tt(v3(hT[:, dc, g0:g0 + n]), v3(tq[s][:, 0:n]), b3(modT[:, shift_base + dc, 1:17]), ALU.add),
                      [f"tq{s}", *MODK], [("hT", dc)])
                else:
                    V(stt(tq[s][:, 0:n], xg[:, dc, 0:n], g_t[:, dc, 0:1], rstd[:, 0:n], ALU.mult, ALU.mult),
                      [("xg", dc), "gm", "gf", "rstd"], [f"tq{s}"])
                    A(act(hT[:, dc, g0:g0 + n], tq[s][:, 0:n], AF.Identity, bias=modT[:, shift_base + dc, 0:1]),
                      [f"tq{s}", *MODK], [("hT", dc)])

    def dense_fm(wt, wkey, wcol0, nk, act_fn, act_keys, c0, n, consumer):
        pb, pk = bank()
        for kc in range(nk):
            MM(pb[:, 0:n], wt[:, kc, wcol0:wcol0 + 128], act_fn(kc)[:, c0:c0 + n], kc == 0, kc == nk - 1,
               [wkey] + act_keys(kc), [pk], signal=(kc == nk - 1))
        consumer(pb[:, 0:n], pk)

    es_mix = ExitStack()
    es_mp = ExitStack()
    mpa = lambda name, shape, dt=F32: es_mp.enter_context(_sbt(nc, name, list(shape), dt)).ap()
    lora_d = mpa("lora_d", [128, 1024], BF16); lora_a = mpa("lora_a", [128, 1024], BF16)
    P.op("dve", lambda: nc.vector.memset(lora_d[64:128, :], 0.0), [], ["lora_up"])
    P.op("dve", lambda: nc.vector.memset(lora_a[0:64, :], 0.0), [], ["lora_up"])
    P.dma("pool", lora_d[0:64, :], DR["lora_up"][0:64, :], writes=["lora_up"])
    P.dma("pool", lora_a[64:128, :], DR["lora_up"][64:128, :], writes=["lora_up"])
    wgu = mpa("wgu", [128, 1024], BF16); P.dma("pool", wgu, DR["w_gate_up"], writes=["wgu"])
    sshT = mpa("sshT", [128, 26, 16]); P.dma("sp", sshT, DR["sshT"].rearrange("p (j b) -> p j b", b=16), writes=["sshT"])
    Hs = mpa("Hs", [128, 8, 128], F32R)
    hlast = mpa("hlast", [128, 16, 2], BF16)
    shp = mpa("shp", [128, 26]); shs = mpa("shs", [128, 26, 16])
    hT = es_mix.enter_context(_sbt(nc, "hT", [128, 16, T], BF16)).ap()
    hT_fn = lambda kc: hT[:, kc, :]
    hT_keys = lambda kc: [("hT", kc)]
    oaT = es_mix.enter_context(_sbt(nc, "oaT", [128, 8, T], BF16)).ap()

    def mixer_pass(own):
        ncols = T if own else TP
        ntiles = 9 if own else 8
        with ExitStack() as es:
            build_hT(es, hT, DR["xT_own"] if own else DR["xT_prev"], ncols, gm, 0, own)
        if not own:
            V(cpv(hlast, hT[:, :, TP - 2:TP]), [("hT", k) for k in range(16)], ["hlast"])
        P.barrier()
        ckpt("B1" if not own else "B2", [("hT", hT, [("hT", k) for k in range(16)])])
        with ExitStack() as es:
            al = lambda name, shape, dt=F32: es.enter_context(_sbt(nc, name, list(shape), dt)).ap()
            lor = al("lor", [128, T], BF16); sg = al("sg", [128, T], BF16)
            zraw = [al(f"zraw{i}", [128, 1 + T]) for i in range(1)]
            zrs = [al(f"zrs{i}", [128, 16, 8]) for i in range(1)]
            big = al("big", [128, 16, 64])
            dtl = [big[:, 0:8, :].rearrange("p b v -> p (b v)")]
            zs = [[al(f"zs{s}_{w}", [128, T]) for w in range(3)] for s in range(1)]
            wrkv = [al(f"wrkv{i}", [128, 16, 3, 128], BF16) for i in range(1)]
            zcnt = [0]

            def shift_evac(j, dst, dstkey, wt, wkey, wcol0):
                zi = 0
                zcnt[0] += 1
                zr, zk = zraw[zi], f"zraw{zi}"
                mu = muT[:, j:j + 1]
                if own:
                    pb, pk = bank()
                    for kc in range(16):
                        MM(pb[:, 0:2], wt[:, kc, wcol0:wcol0 + 128], hlast[:, kc, :], kc == 0, kc == 15, [wkey, "hlast"], [pk], signal=(kc == 15))
                    A(act(zr[:, 0:1], pb[:, 1:2], AF.Identity, scale=flag[:, 0:1]), [pk, "flag"], [zk])
                else:
                    V(lambda: nc.vector.memset(zr[:, 0:1], 0.0), [], [zk])
                for g0 in range(0, TP, 512):
                    def cons(ps, pk, g0=g0):
                        A(cpa(zr[:, 1 + g0:1 + g0 + 512], ps), [pk], [zk])
                        d = dtl[0]; dk = "big"
                        V(tt(d, zr[:, g0:g0 + 512], ps, ALU.subtract), [zk, pk], [dk])
                        V(stt(dst[:, g0:g0 + 512], d, mu, ps, ALU.mult, ALU.add), [dk, pk, "svec"], [dstkey])
                    dense_fm(wt, wkey, wcol0, 16, hT_fn, hT_keys, g0, 512, cons)
                if own:
                    V(cpv(shp[:, j:j + 1], zr[:, TP:TP + 1]), [zk], ["shp"])

                    def cons_s(ps, pk):
                        z3 = zrs[zi]; z3k = f"zrs{zi}"
                        p3 = ps.rearrange("p (b t) -> p b t", t=8)
                        A(cpa(z3[:, :, 1:8], p3[:, :, 0:7]), [pk], [z3k])
                        V(cpv(z3[:, :, 0:1], sshT[:, j, :].unsqueeze(2)), ["sshT"], [z3k])
                        V(cpv(shs[:, j, :].unsqueeze(2), p3[:, :, 7:8]), [pk], ["shs"])
                        d = dtl[0][:, 0:128]
                        V(tt(d, z3.rearrange("p b t -> p (b t)"), ps, ALU.subtract), [z3k, pk], ["big"])
                        V(stt(dst[:, TP:T], d, mu, ps, ALU.mult, ALU.add), ["big", pk, "svec"], [dstkey])
                    dense_fm(wt, wkey, wcol0, 16, hT_fn, hT_keys, TP, 128, cons_s)

            wl, wlk = load_w(DR["w_in"][:, 3072:3328], 16, 256)
            zl = zs[0][0]
            shift_evac(24, zl, "zs0_0", wl, wlk, 0)
            A(act(lor[0:64, 0:ncols], zl[0:64, 0:ncols], AF.Tanh), ["zs0_0"], ["lor"])
            V(cpv(lor[64:128, 0:ncols], zl[64:128, 0:ncols]), ["zs0_0"], ["lor"])
            if own:
                shift_evac(25, zl, "zs0_0", wl, wlk, 128)
                A(act(sg, zl, AF.Sigmoid), ["zs0_0"], ["sg"])

            ckpt("L1" if not own else "L2", [("lor", lor, ["lor"]), ("sg", sg, ["sg"])])
            NS = 1
            DBN_ = {"AbTA", "AbTB", "BtTA", "BtTB", "KtTA", "KtTB", "RbT", "Bt_tm", "Kt_tm", "VeA", "VeB", "YA0", "YB0", "G", "bon"}
            pt = {}
            for nm in ["oT", "o2", "kk2b", "rkb"]:
                pt[nm] = [al(f"p_{nm}0", [128, 128], BF16)]
            for nm in ["lw", "cum", "aa", "kk", "kkn", "kp", "G", "Gi", "Gm1", "bon",
                       "mean", "msq", "varp", "cen", "otm"]:
                pt[nm] = [al(f"p_{nm}{s}", [128, 128]) for s in range(2 if nm in DBN_ else 1)]
            for nm in ["AbT", "BtT", "KtT", "RbT", "Bt_tm", "Kt_tm", "VeA", "VeB", "UeA", "UeB", "WT",
                       "AbTA", "AbTB", "BtTA", "BtTB", "KtTA", "KtTB", "WTA", "WTB", "RbTA", "RbTB",
                       "sX0", "sX1", "sXT0", "sXT1", "YA0", "YA1", "LakA", "MrbA", "MrkA",
                       "YB0", "YB1", "LakB", "MrbB", "MrkB", "DTfA", "DTfB"]:
                pt[nm] = [al(f"p_{nm}{s}", [128, 128], F32R) for s in range(2 if nm in DBN_ else 1)]
            for hh_ in ("A", "B"):
                for nm in ["X0", "X1", "XT0", "XT1", "D0", "D1", "DT0", "DT1", "Pm", "Qm", "Lo0", "Lo1", "Lo2", "Lo3", "LoT0", "LoT1", "LoT2", "Lb", "LTb"]:
                    pt[nm + hh_] = [al(f"p_{nm}{hh_}{s}", [128, 128], BF16) for s in range(NS)]
            for nm in ["VeA", "VeB", "UeA", "UeB", "AbTA", "AbTB", "BtTA", "BtTB", "KtTA", "KtTB", "WTA", "WTB", "RbTA", "RbTB"]:
                for s in range(len(pt[nm])):
                    V((lambda a: (cpv(a, zeros)))(pt[nm][s]), ["zeros"], [(nm, s)])
            if own:
                h0r = al("h0r", [128, 16, 64], F32R)
                hsn = al("hsn", [128, 16, 64])
                u1 = al("u1", [128, 64]); o0 = al("o0", [128, 64])
                Ublk = al("Ublk", [128, 16, 64], F32R); Vblk = al("Vblk", [128, 16, 64], F32R)
            setc = [0]

            CARRY = [None]
            if NJUNK and own:
                P.junk = lambda: nc.tensor.matmul(psb[3][:, 0:NJUNK], lhsT=cstb[:, 0:128], rhs=cstb[:, 0:NJUNK], start=True, stop=True)
            for hp in range(8):
                wi = 0
                wr, wrk = wrkv[wi], f"wrkv{wi}"
                which = [0, 1, 2] if own else [1, 2]
                Z = zs[0]

                def proj_gen():
                    for w in which:
                        c = w * 1024 + hp * 128
                        P.dma("pool", wr[:, :, w, :], DR["w_in"][:, c:c + 128].rearrange("(kc p) n -> p kc n", p=128), writes=[wrk], est=4.0)
                    for w in which:
                        shift_evac(w * 8 + hp, Z[w], f"zs0_{w}", wr[:, :, w, :], wrk, 0)
                    if own:
                        A(act(Hs[:, hp, :], r32(Hs[:, hp, :]), AF.Identity, scale=flag[:, 0:1]), [("Hs", hp), "flag"], [("Hs", hp)])
                    else:
                        V((lambda a: (cpv(a, zeros)))(Hs[:, hp, :]), ["zeros"], [("Hs", hp)])
                    yield

                zr_, zk_, zv_ = Z
                kr, kk_, kv = [f"zs0_{w}" for w in range(3)]
                ch = slice(hp * 128, (hp + 1) * 128)
                DBN = {"AbTA", "AbTB", "BtTA", "BtTB", "KtTA", "KtTB", "RbT", "Bt_tm", "Kt_tm", "VeA", "VeB", "YA0", "YB0", "G", "bon"}

                def env(ti):
                    samp = own and ti == 8
                    par = ti % 2
                    K = lambda nm: (nm, par if nm in DBN else 0)
                    X = lambda nm: pt[nm][par if nm in DBN else 0]
                    cs = slice(ti * 128, (ti + 1) * 128)
                    Msl, Mgt, Mle = (ms_sl, ms_gt, ms_le) if samp else (m_sl, m_gt, m_le)
                    return samp, K, X, cs, Msl, Mgt, Mle

                def prep_gen(ti):
                    samp, K, X, cs, Msl, Mgt, Mle = env(ti)
                    q, qk = quart()
                    MM(q, lora_d[:, ch], lor[:, cs], True, True, ["lora_up", "lor"], [qk])
                    A(act(X("lw"), q, AF.Exp, bias=svec[:, 98 + hp:99 + hp], scale=-1.0), [qk, "svec"], [K("lw")])
                    A(act(X("lw"), X("lw"), AF.Ln, bias=epsc[:, 3:4]), [K("lw"), "epsc"], [K("lw")])
                    A(act(X("lw"), X("lw"), AF.Exp, bias=epsc[:, 4:5], scale=-1.0), [K("lw"), "epsc"], [K("lw")])
                    yield
                    V(scn(X("cum"), mreset if samp else ones, X("lw"), 0.0, ALU.mult, ALU.add),
                      [K("lw"), "cst"], [K("cum")])
                    q, qk = quart()
                    MM(q, lora_a[:, ch], lor[:, cs], True, True, ["lora_up", "lor"], [qk])
                    A(act(X("aa"), q, AF.Exp, bias=svec[:, 106 + hp:107 + hp], scale=-1.0), [qk, "svec"], [K("aa")])
                    A(act(X("aa"), X("aa"), AF.Ln, bias=epsc[:, 3:4]), [K("aa"), "epsc"], [K("aa")])
                    A(act(X("aa"), X("aa"), AF.Exp, scale=-1.0), [K("aa")], [K("aa")])
                    yield
                    V(ts(X("kk"), zk_[:, cs], sv(2, hp), None, ALU.mult, ALU.bypass), [kk_, "svec"], [K("kk")])
                    A(act(X("kk2b"), X("kk"), AF.Square), [K("kk")], [K("kk2b")])
                    q, qk = quart()
                    MM(q, blk1b, X("kk2b"), True, True, ["cstb", K("kk2b")], [qk])
                    V(ts(X("Gm1"), q, 1e-19, None, ALU.max, ALU.bypass), [qk], [K("Gm1")])
                    A(act(X("Gm1"), X("Gm1"), AF.Ln), [K("Gm1")], [K("Gm1")])
                    A(act(X("Gm1"), X("Gm1"), AF.Exp, scale=-0.5), [K("Gm1")], [K("Gm1")])
                    V(tt(X("kkn"), X("kk"), X("Gm1"), ALU.mult), [K("kk"), K("Gm1")], [K("kkn")])
                    yield
                    V(ts(X("kp"), X("aa"), sv(3, hp), sv(4, hp), ALU.mult, ALU.add), [K("aa"), "svec"], [K("kp")])
                    V(tt(X("kp"), zk_[:, cs], X("kp"), ALU.mult), [kk_, K("kp")], [K("kp")])
                    A(act(X("G"), X("cum"), AF.Exp, scale=-1.0), [K("cum")], [K("G")])
                    A(act(X("Gi"), X("cum"), AF.Exp), [K("cum")], [K("Gi")])
                    V(tt(X("cum"), X("cum"), X("lw"), ALU.subtract), [K("cum"), K("lw")], [K("cum")])
                    A(act(X("Gm1"), X("cum"), AF.Exp, scale=-1.0), [K("cum")], [K("Gm1")])
                    yield
                    V(stt(X("AbT"), X("kkn"), -1.0, X("Gm1"), ALU.mult, ALU.mult), [K("kkn"), K("Gm1")], [K("AbT")])
                    V(tt(X("kk"), X("kkn"), X("aa"), ALU.mult), [K("kkn"), K("aa")], [K("kk")])
                    V(tt(X("BtT"), X("kk"), X("Gi"), ALU.mult), [K("kk"), K("Gi")], [K("BtT")])
                    V(tt(X("KtT"), X("kp"), X("Gi"), ALU.mult), [K("kp"), K("Gi")], [K("KtT")])
                    for nm in ("AbT", "BtT", "KtT"):
                        A((lambda nm=nm: (cpa(X(nm + "A")[0:64, :], r32(X(nm))[0:64, :])))(), [K(nm)], [K(nm + "A")])
                        V((lambda nm=nm: (cpv(X(nm + "B")[64:128, :], r32(X(nm))[64:128, :])))(), [K(nm)], [K(nm + "B")])
                    if own:
                        V(tt(X("RbT"), zr_[:, cs], X("G"), ALU.mult), [kr, K("G")], [K("RbT")])
                        if samp:
                            A(cpa(X("RbTA")[0:64, :], r32(X("RbT"))[0:64, :]), [K("RbT")], [K("RbTA")])
                            V(cpv(X("RbTB")[64:128, :], r32(X("RbT"))[64:128, :]), [K("RbT")], [K("RbTB")])
                        V(stt(X("rkb"), zr_[:, cs], sv(7, hp), X("kp"), ALU.mult, ALU.mult), [kr, "svec", K("kp")], [K("rkb")])
                        q, qk = quart()
                        MM(q, blk1b, X("rkb"), True, True, ["cstb", K("rkb")], [qk])
                        V(tt(X("bon"), q, zv_[:, cs], ALU.mult), [qk, kv], [K("bon")])
                    yield
                    q, qk = quart()
                    TR(q, r32(X("AbT")), [K("AbT")], [qk])
                    A(cpa(X("YA0")[:, 0:64], q[:, 0:64]), [qk], [K("YA0")])
                    A(cpa(X("YB0")[:, 64:128], q[:, 64:128]), [qk], [K("YB0")])
                    q, qk = quart()
                    TR(q, r32(X("BtT")), [K("BtT")], [qk])
                    A(cpa(X("Bt_tm"), q), [qk], [K("Bt_tm")])
                    yield
                    q, qk = quart()
                    TR(q, r32(X("KtT")), [K("KtT")], [qk])
                    A(cpa(X("Kt_tm"), q), [qk], [K("Kt_tm")])
                    q, qk = quart()
                    TR(q, zv_[:, cs], [kv], [qk])
                    A(cpa(X("VeA")[:, 0:64], q[:, 0:64]), [qk], [K("VeA")])
                    A(cpa(X("VeB")[:, 64:128], q[:, 64:128]), [qk], [K("VeB")])

                def scan_gens(ti):
                    samp, K, X, cs, Msl, Mgt, Mle = env(ti)
                    Yfin = YF.setdefault(ti, {})

                    def head_gen(hh, p0):
                        ps_ = slice(p0, p0 + 64)
                        dhalf = slice(0, 64) if hh == "A" else slice(64, 128)
                        ohalf = slice(64, 128) if hh == "A" else slice(0, 64)
                        Ve = X("Ve" + hh); Vek = K("Ve" + hh)
                        HX = lambda nm: X(nm + hh)
                        HK = lambda nm: K(nm + hh)
                        q1, qk1 = quart()
                        MM(q1, X("BtT" + hh), X("AbT" + hh), True, True, [K("BtT" + hh), K("AbT" + hh)], [qk1])
                        q2, qk2 = quart()
                        MM(q2, X("AbT" + hh), X("BtT" + hh), True, True, [K("BtT" + hh), K("AbT" + hh)], [qk2])

                        def side_lak():
                            q, qk = quart()
                            MM(q, X("KtT" + hh), X("AbT" + hh), True, True, [K("KtT" + hh), K("AbT" + hh)], [qk])
                            V(tt(X("Lak" + hh), q, Msl, ALU.mult), [qk, "cst"], [K("Lak" + hh)])

                        def side_z():
                            q, qk = quart()
                            MM(q[:, 0:64], X("Lak" + hh), Ve[:, dhalf], True, True, [K("Lak" + hh), Vek], [qk])
                            A(cpa(X("Y" + hh + "0")[:, ohalf], q[:, 0:64]), [qk], [K("Y" + hh + "0")])

                        def side_m():
                            if own:
                                q, qk = quart()
                                MM(q, X("BtT" + hh), X("RbT"), True, True, [K("BtT" + hh), K("RbT")], [qk])
                                V(tt(X("Mrb" + hh), q, Mle, ALU.mult), [qk, "cst"], [K("Mrb" + hh)])
                                q, qk = quart()
                                MM(q, X("KtT" + hh), X("RbT"), True, True, [K("KtT" + hh), K("RbT")], [qk])
                                V(tt(X("Mrk" + hh), q, Mle, ALU.mult), [qk, "cst"], [K("Mrk" + hh)])

                        if samp:
                            V(tt(X("sXT0"), q1, ms_sl, ALU.mult), [qk1, "cst"], [K("sXT0")])
                            V(tt(X("sX0"), q2, ms_gt, ALU.mult), [qk2, "cst"], [K("sX0")])
                            yield
                            side_lak()
                            yield
                            side_z()
                            yield
                            side_m()
                            yield
                            nsteps = 3
                            for i in range(nsteps):
                                a, b = i % 2, (i + 1) % 2
                                Xa, XTa, Ya = X(f"sX{a}"), X(f"sXT{a}"), X(f"Y{hh}{a}")
                                Xb, XTb, Yb = X(f"sX{b}"), X(f"sXT{b}"), X(f"Y{hh}{b}")
                                q, qk = quart()
                                MM(q, XTa, Ya, True, True, [K(f"sXT{a}"), K(f"Y{hh}{a}")], [qk])
                                V(tt(Yb, q, r32(Ya), ALU.add), [qk, K(f"Y{hh}{a}")], [K(f"Y{hh}{b}")])
                                if i < nsteps - 1:
                                    q, qk = quart()
                                    MM(q, XTa, Xa, True, True, [K(f"sXT{a}"), K(f"sX{a}")], [qk])
                                    qq, qqk = quart()
                                    MM(qq, Xa, XTa, True, True, [K(f"sXT{a}"), K(f"sX{a}")], [qqk])
                                    A(cpa(Xb, q), [qk], [K(f"sX{b}")])
                                    A(cpa(XTb, qq), [qqk], [K(f"sXT{b}")])
                            Yf, Yfk = X(f"Y{hh}1"), K(f"Y{hh}1")
                            Yfin[hh] = (Yf, Yfk)
                            q, qk = quart()
                            TR(q, r32(Yf), [Yfk], [qk])
                            A(cpa(X("WT")[ps_, :], q[ps_, :]), [qk], [K("WT")])
                            V(cpv(X("WT" + hh)[ps_, :], q[ps_, :]), [qk], [K("WT" + hh)])
                            return
                        A(cpa(HX("LTb"), q1), [qk1], [HK("LTb")])
                        A(cpa(HX("Lb"), q2), [qk2], [HK("Lb")])
                        V(tt(HX("XT0"), HX("LTb"), ms_sl, ALU.mult), [HK("LTb"), "cst"], [HK("XT0")])
                        V(tt(HX("X0"), HX("Lb"), ms_gt, ALU.mult), [HK("Lb"), "cst"], [HK("X0")])
                        V(tt(HX("D0"), HX("X0"), ident, ALU.add), [HK("X0"), "cst"], [HK("D0")])
                        V(tt(HX("DT0"), HX("XT0"), ident, ALU.add), [HK("XT0"), "cst"], [HK("DT0")])

                        def mask_l(l):
                            if l < 3:
                                V(tt(HX(f"LoT{l}"), HX("LTb"), moffT[l], ALU.mult), [HK("LTb"), "cstb"], [HK(f"LoT{l}")])
                            V(tt(HX(f"Lo{l}"), HX("Lb"), moff[l], ALU.mult), [HK("Lb"), "cstb"], [HK(f"Lo{l}")])

                        side = [side_lak, lambda: mask_l(0), side_z, lambda: mask_l(1), side_m, lambda: mask_l(2), lambda: mask_l(3)]
                        cur = 0
                        for i in range(2):
                            a, b = i % 2, (i + 1) % 2
                            q, qk = quart()
                            MM(q, HX(f"XT{a}"), HX(f"X{a}"), True, True, [HK(f"XT{a}"), HK(f"X{a}")], [qk])
                            qq, qqk = quart()
                            MM(qq, HX(f"X{a}"), HX(f"XT{a}"), True, True, [HK(f"XT{a}"), HK(f"X{a}")], [qqk])
                            A(cpa(HX(f"X{b}"), q), [qk], [HK(f"X{b}")])
                            A(cpa(HX(f"XT{b}"), qq), [qqk], [HK(f"XT{b}")])
                            if side:
                                side.pop(0)()
                            yield
                            q, qk = quart()
                            MM(q, HX(f"XT{b}"), HX(f"D{cur}"), True, True, [HK(f"XT{b}"), HK(f"D{cur}")], [qk])
                            qq, qqk = quart()
                            MM(qq, HX(f"D{cur}"), HX(f"XT{b}"), True, True, [HK(f"XT{b}"), HK(f"D{cur}")], [qqk])
                            V(tt(HX(f"D{1 - cur}"), q, HX(f"D{cur}"), ALU.add), [qk, HK(f"D{cur}")], [HK(f"D{1 - cur}")])
                            V(tt(HX(f"DT{1 - cur}"), qq, HX(f"DT{cur}"), ALU.add), [qqk, HK(f"DT{cur}")], [HK(f"DT{1 - cur}")])
                            cur = 1 - cur
                            if side:
                                side.pop(0)()
                            yield
                        for l in range(4):
                            last = (l == 3)
                            while side and l + 2 > 7 - len(side):
                                side.pop(0)()
                            if not last:
                                q, qk = quart()
                                MM(q, HX(f"LoT{l}"), HX(f"D{cur}"), True, True, [HK(f"LoT{l}"), HK(f"D{cur}")], [qk])
                                A(cpa(HX("Pm"), q), [qk], [HK("Pm")])
                            qq, qqk = quart()
                            MM(qq, HX(f"Lo{l}"), HX(f"DT{cur}"), True, True, [HK(f"Lo{l}"), HK(f"DT{cur}")], [qqk])
                            A(cpa(HX("Qm"), qq), [qqk], [HK("Qm")])
                            if side:
                                side.pop(0)()
                            yield
                            if not last:
                                q, qk = quart()
                                MM(q, HX(f"DT{cur}"), HX("Pm"), True, True, [HK(f"DT{cur}"), HK("Pm")], [qk])
                                V(tt(HX(f"D{1 - cur}"), q, HX(f"D{cur}"), ALU.add), [qk, HK(f"D{cur}")], [HK(f"D{1 - cur}")])
                            qq, qqk = quart()
                            MM(qq, HX(f"D{cur}"), HX("Qm"), True, True, [HK(f"D{cur}"), HK("Qm")], [qqk])
                            if last:
                                V(tt(HX("DTf"), qq, HX(f"DT{cur}"), ALU.add), [qqk, HK(f"DT{cur}")], [HK("DTf")])
                            else:
                                V(tt(HX(f"DT{1 - cur}"), qq, HX(f"DT{cur}"), ALU.add), [qqk, HK(f"DT{cur}")], [HK(f"DT{1 - cur}")])
                            cur = 1 - cur
                            yield
                        while side:
                            side.pop(0)()
                        DT, DTk = HX("DTf"), HK("DTf")
                        Y0, Y0k = X(f"Y{hh}0"), K(f"Y{hh}0")
                        q, qk = quart()
                        MM(q, Y0, DT, True, True, [Y0k, DTk], [qk])
                        A(cpa(X("WT")[ps_, :], q[ps_, :]), [qk], [K("WT")])
                        q, qk = quart()
                        MM(q[:, 0:64], DT, Y0[:, ohalf], True, True, [Y0k, DTk], [qk])
                        A(cpa(X(f"Y{hh}1")[:, ohalf], q[:, 0:64]), [qk], [K(f"Y{hh}1")])
                        Yfin[hh] = (X(f"Y{hh}1"), K(f"Y{hh}1"))

                    if samp:
                        def seq():
                            for g_ in (head_gen("A", 0), head_gen("B", 64)):
                                for _ in g_:
                                    yield
                        return [seq()]
                    return [head_gen("A", 0), head_gen("B", 64)]

                def chainpost_gen(ti):
                    samp, K, X, cs, Msl, Mgt, Mle = env(ti)
                    Yfin = YF[ti]
                    hk = ("Hs", hp)
                    Hh = Hs[:, hp, :]
                    if not samp:
                        q, qk = quart()
                        MM(q, X("WT"), Hh, True, True, [K("WT"), hk], [qk])
                        V(tt(X("UeA")[:, 0:64], q[:, 0:64], r32(Yfin["A"][0][:, 64:128]), ALU.add), [qk, Yfin["A"][1]], [K("UeA")])
                        V(tt(X("UeB")[:, 64:128], q[:, 64:128], r32(Yfin["B"][0][:, 0:64]), ALU.add), [qk, Yfin["B"][1]], [K("UeB")])
                        if own:
                            q, qk = quart()
                            MM(q, Hh, X("RbT"), True, False, [hk, K("RbT")], [qk], signal=False)
                            MM(q, X("UeA"), X("MrbA"), False, False, [K("UeA"), K("MrbA")], [qk], signal=False)
                            MM(q, X("UeB"), X("MrbB"), False, False, [K("UeB"), K("MrbB")], [qk], signal=False)
                            MM(q, X("VeA"), X("MrkA"), False, False, [K("VeA"), K("MrkA")], [qk], signal=False)
                            MM(q, X("VeB"), X("MrkB"), False, True, [K("VeB"), K("MrkB")], [qk])
                            A(cpa(X("oT"), q), [qk], [K("oT")])
                        q, qk = quart()
                        MM(q[:, 0:64], X("Bt_tm"), X("UeA")[:, 0:64], True, False, [K("Bt_tm"), K("UeA")], [qk], signal=False)
                        MM(q[:, 0:64], X("Kt_tm"), X("VeA")[:, 0:64], False, True, [K("Kt_tm"), K("VeA")], [qk], signal=False)
                        MM(q[:, 64:128], X("Bt_tm"), X("UeB")[:, 64:128], True, False, [K("Bt_tm"), K("UeB")], [qk], signal=False)
                        MM(q[:, 64:128], X("Kt_tm"), X("VeB")[:, 64:128], False, True, [K("Kt_tm"), K("VeB")], [qk])
                        GC = X("G")[:, 127:128]
                        for ps_, hf in ((slice(0, 64), slice(0, 64)), (slice(64, 128), slice(64, 128))):
                            A(act(X("msq")[ps_, 0:64], r32(Hh[ps_, hf]), AF.Identity, scale=GC[ps_, :]), [hk, K("G")], [K("msq")])
                            V(stt(Hh[ps_, hf], q[ps_, hf], GC[ps_, :], X("msq")[ps_, 0:64], ALU.mult, ALU.add), [qk, K("G"), K("msq")], [hk])
                        if own and ti == 7:
                            P.dma("sp", DR["wkv_p"][hp], r32(Hh), reads=[hk], semres=("Hs", hp))
                    else:
                        P.dma("sp", big, DR["s0T"][hp].rearrange("p (b v) -> p b v", v=64), writes=["big"])
                        V(cpv(h0r, big), ["big"], ["h0r"])
                        b3m = maskTB.unsqueeze(2).broadcast_to([128, 16, 64])
                        G3 = X("G").rearrange("p (b t) -> p b t", t=8)[:, :, 7:8]
                        for hh, p0 in (("A", 0), ("B", 64)):
                            ps_ = slice(p0, p0 + 64)
                            dhalf = slice(0, 64) if hh == "A" else slice(64, 128)
                            ohalf = slice(64, 128) if hh == "A" else slice(0, 64)
                            Ue, Uek = X("Ue" + hh), K("Ue" + hh)
                            Ve, Vek = X("Ve" + hh), K("Ve" + hh)
                            for src, srck, dstv, dstk in ((X("WT" + hh), K("WT" + hh), u1, "u1"), (X("RbT" + hh), K("RbT" + hh), o0, "o0")):
                                for nb in range(2):
                                    pb, pk = bank()
                                    MM(pb, src, h0r[:, nb * 8:(nb + 1) * 8, :].rearrange("p b v -> p (b v)"), True, True, [srck, "h0r"], [pk])
                                    V(tt(big[:, nb * 8:(nb + 1) * 8, :], pb.rearrange("p (b v) -> p b v", v=64), b3m[:, nb * 8:(nb + 1) * 8, :], ALU.mult),
                                      [pk, "cst"], ["big"])
                                V(lambda dstv=dstv: nc.vector.tensor_reduce(dstv, big.rearrange("p b v -> p v b"), AX.X, ALU.add), ["big"], [dstk])
                            V(tt(Ue[:, dhalf], u1, r32(Yfin[hh][0][:, ohalf]), ALU.add), ["u1", Yfin[hh][1]], [Uek])
                            q, qk = quart()
                            MM(q[:, 0:64], X("Mrb" + hh), Ue[:, dhalf], True, False, [K("Mrb" + hh), Uek], [qk], signal=False)
                            MM(q[:, 0:64], X("Mrk" + hh), Ve[:, dhalf], False, True, [K("Mrk" + hh), Vek], [qk])
                            V(tt(X("otm")[:, dhalf], q[:, 0:64], o0, ALU.add), [qk, "o0"], [K("otm")])
                            ub = r32(Ue[:, dhalf]).unsqueeze(1).broadcast_to([128, 16, 64])
                            vb = r32(Ve[:, dhalf]).unsqueeze(1).broadcast_to([128, 16, 64])
                            V(tt(Ublk, ub, b3m, ALU.mult), [Uek, "cst"], ["Ublk"])
                            V(tt(Vblk, vb, b3m, ALU.mult), [Vek, "cst"], ["Vblk"])
                            for nb in range(2):
                                pb, pk = bank()
                                bs = slice(nb * 8, (nb + 1) * 8)
                                MM(pb, X("Bt_tm"), Ublk[:, bs, :].rearrange("p b v -> p (b v)"), True, False, [K("Bt_tm"), "Ublk"], [pk], signal=False)
                                MM(pb, X("Kt_tm"), Vblk[:, bs, :].rearrange("p b v -> p (b v)"), False, True, [K("Kt_tm"), "Vblk"], [pk])
                                V(tt(hsn[ps_, bs, :], pb[ps_, :].rearrange("p (b v) -> p b v", v=64), r32(h0r)[ps_, bs, :], ALU.add), [pk, "h0r"], ["hsn"])
                                V(tt(hsn[ps_, bs, :], hsn[ps_, bs, :], G3[ps_, bs, :].broadcast_to([64, 8, 64]), ALU.mult), ["hsn", K("G")], ["hsn"])
                        P.dma("sp", DR["wkv_s"][hp].rearrange("p (b v) -> p b v", v=64), hsn, reads=["hsn"], semres="hsn")
                        q, qk = quart()
                        TR(q, X("otm"), [K("otm")], [qk])
                        A(cpa(X("oT"), q), [qk], [K("oT")])
                    yield
                    if own:
                        q, qk = quart()
                        MM(q, blk1b, X("oT"), True, True, ["cstb", K("oT")], [qk])
                        A(act(X("mean"), q, AF.Identity, scale=1.0 / 64), [qk], [K("mean")])
                        A(act(X("o2"), X("oT"), AF.Square), [K("oT")], [K("o2")])
                        q2, qk2 = quart()
                        MM(q2, blk1b, X("o2"), True, True, ["cstb", K("o2")], [qk2])
                        V(tt(X("msq"), X("mean"), X("mean"), ALU.mult), [K("mean")], [K("msq")])
                        V(stt(X("varp"), q2, 1.0 / 64, X("msq"), ALU.mult, ALU.subtract), [qk2, K("msq")], [K("varp")])
                        A(act(X("varp"), X("varp"), AF.Ln, bias=epsc[:, 1:2]), [K("varp"), "epsc"], [K("varp")])
                        A(act(X("varp"), X("varp"), AF.Exp, scale=-0.5), [K("varp")], [K("varp")])
                        yield
                        V(tt(X("cen"), X("oT"), X("mean"), ALU.subtract), [K("oT"), K("mean")], [K("cen")])
                        V(tt(X("cen"), X("cen"), X("varp"), ALU.mult), [K("cen"), K("varp")], [K("cen")])
                        V(ts(X("cen"), X("cen"), sv(5, hp), sv(6, hp), ALU.mult, ALU.add), [K("cen"), "svec"], [K("cen")])
                        V(tt(X("cen"), X("cen"), X("bon"), ALU.add), [K("cen"), K("bon")], [K("cen")])
                        q, qk = quart()
                        MM(q, wgu[:, ch], sg[:, cs], True, True, ["wgu", "sg"], [qk])
                        V(tt(oaT[:, hp, cs], X("cen"), q, ALU.mult), [K("cen"), qk], [("oaT", hp)])
                    if hp == 0 and ti == 0:
                        ckpt("T1" if not own else "T2", [(nm, r32(pt[nm][0]) if pt[nm][0].dtype == F32R else pt[nm][0], [(nm, 0)]) for nm in
                                   ["lw", "cum", "aa", "kkn", "kp", "G", "AbT", "BtT", "KtT", "Bt_tm", "VeA", "VeB", "WT", "UeA", "UeB", "LakA", "YA0", "YA1", "YB1", "oT", "cen"]]
                             + [("Hs", r32(Hs), [("Hs", k) for k in range(8)]), ("zs1", zs[0][1], ["zs0_1"]), ("zs2", zs[0][2], ["zs0_2"])])

                YF = {}

                def rec(gen, qpool, bpool=(0,)):
                    QPOOL[0] = list(qpool)
                    BPOOL[0] = list(bpool)
                    r_ = P.record(gen)
                    QPOOL[0] = [2, 3, 4, 5, 6, 7]
                    BPOOL[0] = [0, 1, 2, 3]
                    return r_

                streams0 = []
                if CARRY[0] is not None:
                    streams0.append(CARRY[0])
                    CARRY[0] = None
                streams0.append(rec(proj_gen(), (4, 5, 6, 7), (4, 5, 6, 7)))
                streams0.append(rec(prep_gen(0), (0, 1)))
                P.schedule(streams0)
                prev = None
                for ti in range(ntiles):
                    streams = []
                    if prev is not None:
                        streams.append(rec(chainpost_gen(prev), (2,), (2,)))
                    hg = scan_gens(ti)
                    if len(hg) == 1:
                        streams.append(rec(hg[0], (4, 5, 6, 7)))
                    else:
                        streams.append(rec(hg[0], (4, 5)))
                        streams.append(rec(hg[1], (6, 7)))
                    if ti + 1 < ntiles:
                        streams.append(rec(prep_gen(ti + 1), (0, 1)))
                    if not own:
                        if 1 <= ti <= 4:
                            streams.append(rec(ada_mm_gen(16 + hp * 4 + ti - 1), (3,), (3,)))
                        if ti <= 3:
                            streams.append(rec(ada_dma_gen(16 + hp * 4 + ti), (3,), (3,)))
                    P.schedule(streams)
                    prev = ti
                CARRY[0] = rec(chainpost_gen(prev), (2,), (2,))
            P.schedule([CARRY[0]])
            P.junk = None
            if not own:
                for dc_ in range(16):
                    V(ts(gf[:, dc_, :], modT[:, 64 + dc_, :], 1.0, gvec[:, 16 + dc_:17 + dc_], ALU.add, ALU.mult), [("modT", 4), "gvec"], ["gf"])
            P.barrier()
            ckpt("C1" if not own else "C2", [("Hs", r32(Hs), [("Hs", k) for k in range(8)]), ("oaT", oaT, [("oaT", k) for k in range(8)])])

    mixer_pass(False)
    mixer_pass(True)
    P.dma("sp", DR["shp"], shp, reads=["shp"], semres="shp")
    P.dma("sp", DR["shs"].rearrange("p (j b) -> p j b", b=16), shs, reads=["shs"], semres="shs")
    P.barrier()
    es_mp.close()

    prodT = es_mix.enter_context(_sbt(nc, "prodT", [128, 8, T], BF16)).ap()
    with ExitStack() as es:
        al = lambda name, shape, dt=F32: es.enter_context(_sbt(nc, name, list(shape), dt)).ap()
        wsm = al("wsm", [128, 16, 128], BF16)
        with ExitStack() as es2:
            wsf = es2.enter_context(_sbt(nc, "wsf", [128, 8, 128], F32)).ap()
            P.dma("sp", wsf, DR["w_spT"].rearrange("g s t -> s g t"), writes=["wsf"])
            for g in range(8):
                V(tt(wsm[:, g, :], wsf[:, g, :], m_le, ALU.mult), ["wsf", "cst"], ["wsm"])
            P.dma("sp", wsf, DR["w_spTs"].rearrange("g s t -> s g t"), writes=["wsf"])
            for g in range(8):
                V(tt(wsm[:, 8 + g, :], wsf[:, g, :], ms_le, ALU.mult), ["wsf", "cst"], ["wsm"])
            P.barrier()
        gv = al("gv", [128, 9, 1024], BF16)
        vnb = gv
        lngb = al("lngb", [128, 1024]); lnbb = al("lnbb", [128, 1024])
        P.dma("sp", lngb, DR["ln_g"].partition_broadcast(128), writes=["lngb"])
        P.dma("sp", lnbb, DR["ln_b"].partition_broadcast(128), writes=["lnbb"])
        bspb = al("bspb", [128, 8, 128]); bspsb = al("bspsb", [128, 8, 128])
        P.dma("sp", bspb, DR["bsp"].rearrange("g t -> (g t)").partition_broadcast(128).rearrange("p (g t) -> p g t", t=128), writes=["bspb"])
        P.dma("sp", bspsb, DR["bsps"].rearrange("g t -> (g t)").partition_broadcast(128).rearrange("p (g t) -> p g t", t=128), writes=["bspsb"])
        for blk in range(4):
            wt, wk = load_w(DR["w_in"][:, 4352 + blk * 256: 4352 + (blk + 1) * 256], 16, 256)
            for ti in range(9):
                pb, pk = bank()
                for kc in range(16):
                    MM(pb[:, 0:256], hT[:, kc, ti * 128:(ti + 1) * 128], wt[:, kc, 0:256], kc == 0, kc == 15, [wk, ("hT", kc)], [pk], signal=(kc == 15))
                A(act(gv[:, ti, blk * 256:(blk + 1) * 256], pb[:, 0:256], AF.Gelu_apprx_tanh), [pk], [("gv", ti)])
        st6 = al("st6", [128, 12]); mv = al("mv", [128, 2]); vtmp = [al(f"vtmp{i}", [128, 1024]) for i in range(1)]
        for ti in range(9):
            s = 0
            V(lambda: nc.vector.bn_stats(st6[:, 0:6], gv[:, ti, 0:512]), [("gv", ti)], ["st6"])
            V(lambda: nc.vector.bn_stats(st6[:, 6:12], gv[:, ti, 512:1024]), [("gv", ti)], ["st6"])
            V(lambda: nc.vector.bn_aggr(mv, st6), ["st6"], ["mv"])
            A(act(mv[:, 1:2], mv[:, 1:2], AF.Sqrt, bias=epsc[:, 2:3]), ["mv", "epsc"], ["mv"])
            V(rcp(mv[:, 1:2], mv[:, 1:2]), ["mv"], ["mv"])
            V(ts(vtmp[s], gv[:, ti, :], mv[:, 0:1], mv[:, 1:2], ALU.subtract, ALU.mult), [("gv", ti), "mv"], [f"vtmp{s}"])
            V(tt(vtmp[s], vtmp[s], lngb, ALU.mult), [f"vtmp{s}", "lngb"], [f"vtmp{s}"])
            V(tt(vtmp[s], vtmp[s], lnbb, ALU.add), [f"vtmp{s}", "lnbb"], [f"vtmp{s}"])
            A(cpa(vnb[:, ti, :], vtmp[s]), [f"vtmp{s}"], [("gv", ti)])
            if ti == 8:
                P.dma("sp", DR["v_s"], vtmp[s], reads=[f"vtmp{s}"], semres=f"vtmp{s}")
        uT = [al(f"uT{i}", [128, T]) for i in range(1)]
        mx = [al(f"mx{i}", [128, 128]) for i in range(2)]
        for blk in range(4):
            wt, wk = load_w(DR["w_in"][:, 3328 + blk * 256: 3328 + (blk + 1) * 256], 16, 256)
            for jj in range(2):
                g = blk * 2 + jj
                us, uk = uT[0], "uT0"
                for g0 in range(0, T, 384):
                    dense_fm(wt, wk, jj * 128, 16, hT_fn, hT_keys, g0, 384,
                             lambda ps, pk, g0=g0: A(act(us[:, g0:g0 + 384], ps, AF.Gelu_apprx_tanh), [pk], [uk]))
                for ti in range(9):
                    wsel = g if ti < 8 else 8 + g
                    bsel = bspb if ti < 8 else bspsb
                    q, qk = quart()
                    MM(q, vnb[:, ti, g * 128:(g + 1) * 128], wsm[:, wsel, :], True, True, [("gv", ti), "wsm"], [qk])
                    m_ = mx[ti % 2]; mk = f"mx{ti % 2}"
                    V(tt(m_, q, bsel[:, g, :], ALU.add), [qk, "bspb", "bspsb"], [mk])
                    V(tt(prodT[:, g, ti * 128:(ti + 1) * 128], m_, us[:, ti * 128:(ti + 1) * 128], ALU.mult), [mk, uk], [("prodT", g)])
        P.barrier()
        ckpt("D", [("prodT", prodT, [("prodT", k) for k in range(8)]), ("gv", gv, [("gv", k) for k in range(9)])])

    es_mg = ExitStack()
    merged = es_mg.enter_context(_sbt(nc, "merged", [128, 16, T], BF16)).ap()
    with ExitStack() as es:
        al = lambda name, shape, dt=F32: es.enter_context(_sbt(nc, name, list(shape), dt)).ap()
        wba = al("wba", [128, 8, 256], BF16)
        sa = [al(f"sa{i}", [128, 384]) for i in range(1)]
        sb_ = [al(f"sb{i}", [128, 384]) for i in range(1)]
        wbb = al("wbb", [128, 8, 256], BF16)
        cnt = 0
        for blk in range(8):
            wta, wka = wslot()
            wtb, wkb = wslot()
            P.dma("pool", wta[:, :, 0:256], DR["w_in"][:, 5376 + blk * 256:5376 + (blk + 1) * 256].rearrange("(kc p) n -> p kc n", p=128), writes=[wka])
            P.dma("pool", wtb[:, :, 0:256], DR["w_in"][:, 7424 + blk * 256:7424 + (blk + 1) * 256].rearrange("(kc p) n -> p kc n", p=128), writes=[wkb])
            P.dma("pool", wba, DR["w_branch_a"][:, blk * 256:(blk + 1) * 256].rearrange("(kc p) n -> p kc n", p=128), writes=["wba"])
            P.dma("pool", wbb, DR["w_branch_b"][:, blk * 256:(blk + 1) * 256].rearrange("(kc p) n -> p kc n", p=128), writes=["wbb"])
            for jj in range(2):
                dc = blk * 2 + jj
                for g0 in range(0, T, 384):
                    s = 0
                    cs = slice(g0, g0 + 384)
                    dense_fm(wta, wka, jj * 128, 16, hT_fn, hT_keys, g0, 384,
                             lambda ps, pk: A(act(sa[s], ps, AF.Sigmoid), [pk], [f"sa{s}"]))
                    dense_fm(wtb, wkb, jj * 128, 16, hT_fn, hT_keys, g0, 384,
                             lambda ps, pk: A(act(sb_[s], ps, AF.Sigmoid), [pk], [f"sb{s}"]))
                    dense_fm(wba, "wba", jj * 128, 8, lambda kc: oaT[:, kc, :], lambda kc: [("oaT", kc)], g0, 384,
                             lambda ps, pk: V(tt(sa[s], sa[s], ps, ALU.mult), [f"sa{s}", pk], [f"sa{s}"]))
                    dense_fm(wbb, "wbb", jj * 128, 8, lambda kc: prodT[:, kc, :], lambda kc: [("prodT", kc)], g0, 384,
                             lambda ps, pk: V(tt(sb_[s], sb_[s], ps, ALU.mult), [f"sb{s}", pk], [f"sb{s}"]))
                    V(tt(merged[:, dc, cs], sa[s], sb_[s], ALU.add), [f"sa{s}", f"sb{s}"], [("merged", dc)])
        P.barrier()
        ckpt("E", [("merged", merged, [("merged", k) for k in range(16)])])
    es_mix.close()
    es_x = ExitStack()
    x1 = es_x.enter_context(_sbt(nc, "x1", [128, 16, T], F32)).ap()

    def resid_evac(ps, pk, dc, g0, n, gate_row):
        npr = max(0, min(TP, g0 + n) - g0)
        if npr > 0:
            V(stt(x1[:, dc, g0:g0 + npr], ps[:, 0:npr], modT[:, gate_row + dc, 0:1], x1[:, dc, g0:g0 + npr], ALU.mult, ALU.add),
              [pk, *MODK, ("x1", dc)], [("x1", dc)])
        if g0 + n > TP:
            a0 = max(g0, TP)
            v3 = lambda a: a.rearrange("p (b t) -> p b t", t=8)
            nb0 = (a0 - TP) // 8
            nb = (g0 + n - a0) // 8
            gb = modT[:, gate_row + dc, 1 + nb0:1 + nb0 + nb].unsqueeze(2).broadcast_to([128, nb, 8])
            V(tt(v3(ps[:, a0 - g0:n]), v3(ps[:, a0 - g0:n]), gb, ALU.mult), [pk, *MODK], [pk])
            V(tt(x1[:, dc, a0:g0 + n], ps[:, a0 - g0:n], x1[:, dc, a0:g0 + n], ALU.add), [pk, ("x1", dc)], [("x1", dc)])

    for blk in range(8):
        wt, wk = load_w(DR["w_out"][:, blk * 256:(blk + 1) * 256], 16, 256)
        for jj in range(2):
            dc = blk * 2 + jj
            P.dma("sp", x1[:, dc, :], DR["xT_own"][dc * 128:(dc + 1) * 128, :], writes=[("x1", dc)])
            for g0 in range(0, T, 384):
                dense_fm(wt, wk, jj * 128, 16, lambda kc: merged[:, kc, :], lambda kc: [("merged", kc)], g0, 384,
                         lambda ps, pk, dc=dc, g0=g0: resid_evac(ps, pk, dc, g0, 384, 32))
    P.barrier()
    ckpt("F", [("x1", x1, [("x1", k) for k in range(16)])])
    es_mg.close()

    with ExitStack() as es:
        al = lambda name, shape, dt=F32: es.enter_context(_sbt(nc, name, list(shape), dt)).ap()
        h2T = al("h2T", [128, 16, T], BF16)
        sq = [al(f"sq{i}", [128, 512], F32R) for i in range(2)]
        rstd = al("rstd", [128, 512]); tmpn = rstd
        tq = [al(f"tq{i}", [128, 512]) for i in range(2)]
        for g0 in range(0, T, 512):
            n = min(512, T - g0)
            rms_rstd((sq, rstd, tmpn), lambda dc: x1[:, dc, g0:g0 + n], n, lambda dc: [("x1", dc)])
            for dc in range(16):
                s = dc % 2
                if g0 >= TP:
                    b3 = lambda a: a.unsqueeze(2).broadcast_to([128, 16, 8])
                    v3 = lambda a: a.rearrange("p (b t) -> p b t", t=8)
                    V(tt(v3(tq[s][:, 0:n]), v3(x1[:, dc, g0:g0 + n]), b3(gf[:, dc, 1:17]), ALU.mult), [("x1", dc), "gf"], [f"tq{s}"])
                    V(tt(tq[s][:, 0:n], tq[s][:, 0:n], rstd[:, 0:n], ALU.mult), [f"tq{s}", "rstd"], [f"tq{s}"])
                    V(tt(v3(h2T[:, dc, g0:g0 + n]), v3(tq[s][:, 0:n]), b3(modT[:, 48 + dc, 1:17]), ALU.add), [f"tq{s}", *MODK], [("h2T", dc)])
                else:
                    V(stt(tq[s][:, 0:n], x1[:, dc, g0:g0 + n], gf[:, dc, 0:1], rstd[:, 0:n], ALU.mult, ALU.mult), [("x1", dc), "gf", "rstd"], [f"tq{s}"])
                    A(act(h2T[:, dc, g0:g0 + n], tq[s][:, 0:n], AF.Identity, bias=modT[:, 48 + dc, 0:1]), [f"tq{s}", *MODK], [("h2T", dc)])
        actT = al("actT", [128, 4, T], BF16)
        sl = [al(f"sl{i}", [128, 384]) for i in range(1)]
        wfo = [al(f"wfo{i}", [128, 4, 256], BF16) for i in range(2)]
        cnt = 0
        for qd in range(11):
            for jj in range(4):
                j = qd * 4 + jj
                wt, wk = wslot()
                P.dma("pool", wt[:, :, 0:128], DR["w_ffn_in"][:, j * 128:(j + 1) * 128].rearrange("(kc p) n -> p kc n", p=128), writes=[wk])
                P.dma("pool", wt[:, :, 128:256], DR["w_ffn_in"][:, DFF + j * 128:DFF + (j + 1) * 128].rearrange("(kc p) n -> p kc n", p=128), writes=[wk])
                for g0 in range(0, T, 384):
                    s = 0
                    dense_fm(wt, wk, 0, 16, lambda kc: h2T[:, kc, :], lambda kc: [("h2T", kc)], g0, 384,
                             lambda ps, pk: A(act(sl[s], ps, AF.Silu), [pk], [f"sl{s}"]))
                    dense_fm(wt, wk, 128, 16, lambda kc: h2T[:, kc, :], lambda kc: [("h2T", kc)], g0, 384,
                             lambda ps, pk, g0=g0, jj=jj: V(tt(actT[:, jj, g0:g0 + 384], sl[s], ps, ALU.mult), [f"sl{s}", pk], [("actT", jj)]))
            for blk in range(8):
                wo, wok = wfo[blk % 2], f"wfo{blk % 2}"
                P.dma("pool", wo, DR["w_ffn_out"][qd * 512:(qd + 1) * 512, blk * 256:(blk + 1) * 256].rearrange("(kc p) n -> p kc n", p=128), writes=[wok])
                for jj2 in range(2):
                    dc = blk * 2 + jj2
                    for g0 in range(0, T, 384):
                        dense_fm(wo, wok, jj2 * 128, 4, lambda kc: actT[:, kc, :], lambda kc: [("actT", kc)], g0, 384,
                                 lambda ps, pk, dc=dc, g0=g0: resid_evac(ps, pk, dc, g0, 384, 80))
        yo = tq
        for g0 in range(0, T, 512):
            n = min(512, T - g0)
            rms_rstd((sq, rstd, tmpn), lambda dc: x1[:, dc, g0:g0 + n], n, lambda dc: [("x1", dc)])
            for dc in range(16):
                s = dc % 2
                V(stt(yo[s][:, 0:n], x1[:, dc, g0:g0 + n], gvec[:, 32 + dc:33 + dc], rstd[:, 0:n], ALU.mult, ALU.mult), [("x1", dc), "gvec", "rstd"], [f"tq{s}"])
                P.dma("sp", DR["yT"][dc * 128:(dc + 1) * 128, g0:g0 + n], yo[s][:, 0:n], reads=[f"tq{s}"], semres=f"tq{s}")
        P.finish("sp")
    es_x.close()


def _consts():
    i = np.arange(128)
    r, c = i[:, None], i[None, :]
    same = (r // 8) == (c // 8)
    f = lambda m: m.astype(np.float32)
    parts = [np.eye(128, dtype=np.float32), f(r < c), f(r > c), f(r <= c),
             f((r < c) & same), f((r > c) & same), f((r <= c) & same),
             f(np.broadcast_to((c % 8) != 0, (128, 128))), f((r // 64) == (c // 64)), np.ones((128, 128), np.float32),
             f((r // 8) == np.arange(16)[None, :])]
    return np.ascontiguousarray(np.concatenate(parts, axis=1))


def _constsb():
    import ml_dtypes
    i = np.arange(128)
    r, c = i[:, None], i[None, :]
    parts = []
    for b in (8, 16, 32, 64):
        parts.append(((r // (2 * b)) == (c // (2 * b))) & ((r // b) != (c // b)) & (r > c))
    parts = parts + [p.T for p in parts]
    parts.append((r // 64) == (c // 64))
    return np.ascontiguousarray(np.concatenate(parts, axis=1).astype(np.float32).astype(ml_dtypes.bfloat16))


def _col(v, n):
    return np.ascontiguousarray(np.asarray(v, np.float32).reshape(n, 128).T)


_NC_CACHE = {}


def _prep(x_prompt, x_sample, state_wkv, state_shift, c_prompt, c_sample,
           w_ada, b_ada, norm_mix_g, w_in, mu_shift, w0, w_decay_up, a0, w_aaa_up,
           w_gate_up, k_k, k_a, r_k, gn_g, gn_b, ln_v_g, ln_v_b, w_spatial, b_spatial,
           w_branch_a, w_branch_b, w_out, norm_ffn_g, w_ffn_in, w_ffn_out, norm_final_g):
    f = lambda a: np.ascontiguousarray(np.asarray(a, np.float32))
    x_prompt, x_sample = f(x_prompt), f(x_sample)
    state_wkv, state_shift = f(state_wkv)[0], f(state_shift)[0]
    c_prompt, c_sample = f(c_prompt), f(c_sample)
    shared = {
        "w_ada": f(w_ada)[0], "badaT": _col(f(b_ada)[0], 96),
        "gvec": np.concatenate([_col(f(norm_mix_g)[0], 16), _col(f(norm_ffn_g)[0], 16), _col(f(norm_final_g), 16)], 1),
        "w_in": f(w_in)[0],
        "lora_up": np.ascontiguousarray(np.concatenate([f(w_decay_up)[0], f(w_aaa_up)[0]], 0)),
        "w_gate_up": f(w_gate_up)[0], "ln_g": f(ln_v_g)[0], "ln_b": f(ln_v_b)[0],
        "w_branch_a": f(w_branch_a)[0], "w_branch_b": f(w_branch_b)[0], "w_out": f(w_out)[0],
        "w_ffn_in": f(w_ffn_in)[0], "w_ffn_out": f(w_ffn_out)[0], "cst": _consts(), "cstb": _constsb(),
    }
    mu = f(mu_shift)[0]
    ka = f(k_a)[0]
    vecs = [f(w0)[0], f(a0)[0], f(k_k)[0], ka, ka, f(gn_g)[0], f(gn_b)[0], f(r_k)[0].reshape(-1), ka]
    sv = [_col(mu, 26)] + [_col(v, 8) for v in vecs]
    shared["svec"] = np.ascontiguousarray(np.concatenate(sv, 1))
    wsp = f(w_spatial)[0]
    shared["w_spT"] = np.ascontiguousarray(wsp.transpose(0, 2, 1))
    blkT = np.zeros((8, 128, 128), np.float32)
    for b in range(16):
        blkT[:, b * 8:(b + 1) * 8, b * 8:(b + 1) * 8] = wsp[:, :8, :8].transpose(0, 2, 1)
    shared["w_spTs"] = blkT
    bsp = f(b_spatial)[0]
    shared["bsp"] = bsp
    shared["bsps"] = np.ascontiguousarray(np.tile(bsp[:, :8], (1, 16)))
    in_maps = []
    for c in range(8):
        b, half = c // 2, c % 2
        xs = x_sample[16 * c:16 * (c + 1)].reshape(128, D)
        xo = np.concatenate([x_prompt[b, half * 1024:(half + 1) * 1024], xs], 0)
        xp = x_prompt[b, 0:1024]
        cc = np.concatenate([c_prompt[b:b + 1], c_sample[16 * c:16 * (c + 1)]], 0)
        sw = state_wkv[16 * c:16 * (c + 1)]
        s0T = sw.reshape(16, 8, 2, 64, 64).transpose(1, 2, 4, 0, 3).reshape(8, 128, 16 * 64)
        ssh = state_shift[16 * c:16 * (c + 1)]
        sshT = ssh.reshape(16, 26, 128).transpose(2, 1, 0).reshape(128, 26 * 16)
        m = dict(shared)
        m.update({"xT_own": np.ascontiguousarray(xo.T), "xT_prev": np.ascontiguousarray(xp.T),
                  "cT": np.ascontiguousarray(cc.T), "flag": np.full((128, 1), float(half), np.float32),
                  "s0T": np.ascontiguousarray(s0T), "sshT": np.ascontiguousarray(sshT)})
        in_maps.append(m)
    return in_maps


def kernel(x_prompt, x_sample, state_wkv, state_shift, c_prompt, c_sample,
           w_ada, b_ada, norm_mix_g, w_in, mu_shift, w0, w_decay_up, a0, w_aaa_up,
           w_gate_up, k_k, k_a, r_k, gn_g, gn_b, ln_v_g, ln_v_b, w_spatial, b_spatial,
           w_branch_a, w_branch_b, w_out, norm_ffn_g, w_ffn_in, w_ffn_out, norm_final_g):
    in_maps = _prep(x_prompt, x_sample, state_wkv, state_shift, c_prompt, c_sample,
                    w_ada, b_ada, norm_mix_g, w_in, mu_shift, w0, w_decay_up, a0, w_aaa_up,
                    w_gate_up, k_k, k_a, r_k, gn_g, gn_b, ln_v_g, ln_v_b, w_spatial, b_spatial,
                    w_branch_a, w_branch_b, w_out, norm_ffn_g, w_ffn_in, w_ffn_out, norm_final_g)
    if "nc" not in _NC_CACHE:
        _NC_CACHE["nc"] = build_nc()
    res = run_bass_kernel_spmd(_NC_CACHE["nc"], in_maps, core_ids=list(range(8)))
    R = res.results
    y_prompt = np.zeros((4, 2048, D), np.float32); y_sample = np.zeros((128, 8, D), np.float32)
    wkv_p = np.zeros((1, 4, 16, 64, 64), np.float32); shift_p = np.zeros((1, 4, 3328), np.float32)
    wkv_s = np.zeros((1, 128, 16, 64, 64), np.float32); shift_s = np.zeros((1, 128, 3328), np.float32)
    v_s = np.zeros((1, 128, 8, 1024), np.float32)
    for c in range(8):
        b, half = c // 2, c % 2
        yT = R[c]["yT"]
        y_prompt[b, half * 1024:(half + 1) * 1024] = yT[:, :1024].T
        y_sample[16 * c:16 * (c + 1)] = yT[:, 1024:].T.reshape(16, 8, D)
        if half == 1:
            hp = R[c]["wkv_p"]
            for p in range(8):
                for h2 in range(2):
                    wkv_p[0, b, 2 * p + h2] = hp[p, h2 * 64:(h2 + 1) * 64, h2 * 64:(h2 + 1) * 64].T
            shift_p[0, b] = R[c]["shp"].T.reshape(-1)
        ws = R[c]["wkv_s"].reshape(8, 2, 64, 16, 64)
        wkv_s[0, 16 * c:16 * (c + 1)] = ws.transpose(3, 0, 1, 4, 2).reshape(16, 16, 64, 64)
        shift_s[0, 16 * c:16 * (c + 1)] = R[c]["shs"].reshape(128, 26, 16).transpose(2, 1, 0).reshape(16, 3328)
        v_s[0, 16 * c:16 * (c + 1)] = R[c]["v_s"].reshape(16, 8, 1024)
    return (y_prompt, y_sample, wkv_p, shift_p, wkv_s, shift_s, v_s)
```
